# Optimizing a Trainium2 kernel written in Bass

```python
import functools
import jax
import jax.numpy as jnp
from jax import lax
import numpy as np

D_MODEL = 2048
BATCH = 4
SEQ = 4096
DEPTH = 4

CTX_LEN = 256
GRID_W = 64
ROPE_THETA = 10000.0
NORM_EPS = 1e-6
N_BRANCH = 4
MIX_WIDTH = D_MODEL // N_BRANCH
HEAD_DIM = 128
ATTN_BLOCK = 128
WINDOW = 128
GA_HEADS = MIX_WIDTH // HEAD_DIM
GA_KV_HEADS = GA_HEADS // 2
WA_HEADS = MIX_WIDTH // HEAD_DIM
WA_KV_HEADS = WA_HEADS // 2
RWKV_HEAD_SIZE = 64
RWKV_DIM = MIX_WIDTH
RWKV_HEADS = RWKV_DIM // RWKV_HEAD_SIZE
RWKV_DECAY_LORA = max(32, int(round(1.8 * D_MODEL ** 0.5 / 32)) * 32)
RWKV_ICLR_LORA = max(32, int(round(1.8 * D_MODEL ** 0.5 / 32)) * 32)
RWKV_GATE_LORA = max(32, int(round(0.6 * D_MODEL ** 0.8 / 32)) * 32)
RWKV_LNX_EPS = 64e-5
RWKV_MU_WIDTH = 3 * RWKV_DIM + RWKV_GATE_LORA + RWKV_DECAY_LORA + RWKV_ICLR_LORA
MLA_HEADS = MIX_WIDTH // 128
MLA_NOPE = 128
MLA_ROPE = 64
MLA_V = 128
MLA_Q_LORA = 384
MLA_KV_LORA = 512
D_FF = ((8 * D_MODEL // 3 + 255) // 256) * 256
FFN_CONV = 3
IN_SIZES = (GA_HEADS * HEAD_DIM, GA_KV_HEADS * HEAD_DIM, GA_KV_HEADS * HEAD_DIM,
            WA_HEADS * HEAD_DIM, WA_KV_HEADS * HEAD_DIM, WA_KV_HEADS * HEAD_DIM,
            3 * RWKV_DIM + RWKV_GATE_LORA, 2 * RWKV_DECAY_LORA, 2 * RWKV_ICLR_LORA,
            MLA_Q_LORA, MLA_KV_LORA, MLA_ROPE,
            N_BRANCH * D_MODEL)
IN_WIDTH = sum(IN_SIZES)
IN_OFFSETS = tuple(sum(IN_SIZES[:i + 1]) for i in range(len(IN_SIZES) - 1))

kernel_name = 'hybrid_flow_backbone'


def rms_norm(x, g):
    xf = x.astype(jnp.float32)
    y = xf * lax.rsqrt(jnp.mean(xf * xf, axis=-1, keepdims=True) + NORM_EPS)
    return (y * g).astype(x.dtype)


def split_heads(z, n_heads):
    b, t, _ = z.shape
    return z.reshape(b, t, n_heads, -1).transpose(0, 2, 1, 3)


def group_q(q, n_kv):
    b, h, t, d = q.shape
    return q.reshape(b, n_kv, h // n_kv, t, d)


def merge_heads(o):
    b, t, d = o.shape[0], o.shape[-2], o.shape[-1]
    o = o.reshape(b, -1, t, d)
    return o.transpose(0, 2, 1, 3).reshape(b, t, -1)


def axial_rope_tables(n_tokens, rot_dim):
    rows = n_tokens // GRID_W
    row = jnp.repeat(jnp.arange(rows), GRID_W).astype(jnp.float32)
    col = jnp.tile(jnp.arange(GRID_W), rows).astype(jnp.float32)
    quarter = rot_dim // 4
    inv_freq = ROPE_THETA ** (-jnp.arange(quarter, dtype=jnp.float32) / quarter)
    ang_r = row[:, None] * inv_freq
    ang_c = col[:, None] * inv_freq
    ang = jnp.concatenate([ang_r, ang_r, ang_c, ang_c], axis=-1)
    return jnp.cos(ang), jnp.sin(ang)


def apply_rope(x, cos, sin):
    half = x.shape[-1] // 2
    quarter = half // 2

    def rot_half(p):
        return jnp.concatenate([-p[..., quarter:], p[..., :quarter]], axis=-1)

    xr = jnp.concatenate([rot_half(x[..., :half]), rot_half(x[..., half:])], axis=-1)
    return (x * cos + xr * sin).astype(x.dtype)


def softmax_with_sink(s, sink):
    sink = jnp.broadcast_to(sink.astype(jnp.float32), s.shape[:-1] + (1,))
    return jax.nn.softmax(jnp.concatenate([s, sink], axis=-1), axis=-1)[..., :-1]


def context_attention(qc, kc, vc, sink=None):
    s = jnp.einsum('bhgqd,bhkd->bhgqk', qc, kc, preferred_element_type=jnp.float32) * qc.shape[-1] ** -0.5
    p = jax.nn.softmax(s, axis=-1) if sink is None else softmax_with_sink(s, sink)
    return jnp.einsum('bhgqk,bhkd->bhgqd', p.astype(vc.dtype), vc)


def dense_attention(q, k, v, k_ctx, v_ctx):
    b, hk, g, t, dq = q.shape
    n_blocks = t // ATTN_BLOCK
    keys = jnp.concatenate([k_ctx, k], axis=2)
    vals = jnp.concatenate([v_ctx, v], axis=2)
    q_blocks = jnp.moveaxis(q.reshape(b, hk, g, n_blocks, ATTN_BLOCK, dq), 3, 0)
    scale = dq ** -0.5

    def one_block(qb):
        s = jnp.einsum('bhgqd,bhkd->bhgqk', qb, keys, preferred_element_type=jnp.float32) * scale
        p = jax.nn.softmax(s, axis=-1).astype(vals.dtype)
        return jnp.einsum('bhgqk,bhkd->bhgqd', p, vals)

    o = lax.map(one_block, q_blocks)
    return jnp.moveaxis(o, 0, 3).reshape(b, hk, g, t, -1)


def window_attention(q, k, v, k_ctx, v_ctx, sink):
    b, hk, g, t, d = q.shape
    n_blocks = t // ATTN_BLOCK
    n_ctx = k_ctx.shape[2]
    band = jnp.arange(n_blocks)[:, None] * ATTN_BLOCK + jnp.arange(3 * ATTN_BLOCK)[None, :]
    pad = ((0, 0), (0, 0), (ATTN_BLOCK, ATTN_BLOCK), (0, 0))
    k_band = jnp.pad(k, pad)[:, :, band]
    v_band = jnp.pad(v, pad)[:, :, band]
    qb = q.reshape(b, hk, g, n_blocks, ATTN_BLOCK, d)
    scale = d ** -0.5
    s_ctx = jnp.einsum('bhgnqd,bhkd->bhgnqk', qb, k_ctx, preferred_element_type=jnp.float32) * scale
    s_loc = jnp.einsum('bhgnqd,bhnkd->bhgnqk', qb, k_band, preferred_element_type=jnp.float32) * scale
    q_pos = jnp.arange(n_blocks)[:, None, None] * ATTN_BLOCK + jnp.arange(ATTN_BLOCK)[None, :, None]
    k_pos = (band - ATTN_BLOCK)[:, None, :]
    valid = (jnp.abs(q_pos - k_pos) <= WINDOW) & (k_pos >= 0) & (k_pos < t)
    s_loc = jnp.where(valid, s_loc, -jnp.inf)
    p = softmax_with_sink(jnp.concatenate([s_ctx, s_loc], axis=-1), sink[None, :, :, None, None, None])
    p = p.astype(v.dtype)
    o = (jnp.einsum('bhgnqk,bhkd->bhgnqd', p[..., :n_ctx], v_ctx)
         + jnp.einsum('bhgnqk,bhnkd->bhgnqd', p[..., n_ctx:], v_band))
    return o.reshape(b, hk, g, t, d)


def mixer_global(zq, zk, zv, zq_c, zk_c, zv_c, g_q, g_k, cos, sin, with_ctx):
    hk = GA_KV_HEADS
    q = group_q(apply_rope(rms_norm(split_heads(zq, GA_HEADS), g_q), cos, sin), hk)
    k = apply_rope(rms_norm(split_heads(zk, hk), g_k), cos, sin)
    v = split_heads(zv, hk)
    kc = rms_norm(split_heads(zk_c, hk), g_k)
    vc = split_heads(zv_c, hk)
    y = merge_heads(dense_attention(q, k, v, kc, vc))
    y_c = None
    if with_ctx:
        qc = group_q(rms_norm(split_heads(zq_c, GA_HEADS), g_q), hk)
        y_c = merge_heads(context_attention(qc, kc, vc))
    return y, y_c


def mixer_window(zq, zk, zv, zq_c, zk_c, zv_c, g_q, g_k, sink, cos, sin, with_ctx):
    hk = WA_KV_HEADS
    sink = sink.reshape(hk, WA_HEADS // hk)
    q = group_q(apply_rope(rms_norm(split_heads(zq, WA_HEADS), g_q), cos, sin), hk)
    k = apply_rope(rms_norm(split_heads(zk, hk), g_k), cos, sin)
    v = split_heads(zv, hk)
    kc = rms_norm(split_heads(zk_c, hk), g_k)
    vc = split_heads(zv_c, hk)
    y = merge_heads(window_attention(q, k, v, kc, vc, sink))
    y_c = None
    if with_ctx:
        qc = group_q(rms_norm(split_heads(zq_c, WA_HEADS), g_q), hk)
        y_c = merge_heads(context_attention(qc, kc, vc, sink[None, :, :, None, None]))
    return y, y_c


def to_heads(t):
    return t.reshape(*t.shape[:-1], RWKV_HEADS, RWKV_HEAD_SIZE)


def rwkv_prepare(z_rkvg, z_w, z_a, mu, w0, w2, a0, a2, g2, k_k, k_a):
    c_dim = RWKV_DIM
    u = jnp.stack([jnp.concatenate([z_rkvg,
                                    z_w[..., d * RWKV_DECAY_LORA:(d + 1) * RWKV_DECAY_LORA],
                                    z_a[..., d * RWKV_ICLR_LORA:(d + 1) * RWKV_ICLR_LORA]], axis=-1)
                   for d in range(2)])
    prev = jnp.stack([jnp.pad(u[0], ((0, 0), (1, 0), (0, 0)))[:, :-1],
                      jnp.pad(u[1], ((0, 0), (0, 1), (0, 0)))[:, 1:]])
    u = (u + (prev - u) * mu[:, None, None, :]).astype(jnp.float32)
    r, k, v, gd, wd, ad = jnp.split(u, [c_dim, 2 * c_dim, 3 * c_dim, 3 * c_dim + RWKV_GATE_LORA,
                                        3 * c_dim + RWKV_GATE_LORA + RWKV_DECAY_LORA], axis=-1)
    w = -jax.nn.softplus(-(w0[:, None, None, :] + jnp.einsum('zbtr,zrc->zbtc', jnp.tanh(wd), w2))) - 0.5
    decay = jnp.exp(-jnp.exp(w))
    a = jax.nn.sigmoid(a0[:, None, None, :] + jnp.einsum('zbtr,zrc->zbtc', ad, a2))
    g = jax.nn.sigmoid(gd) @ g2
    kk = to_heads(k * k_k)
    kk = kk / jnp.maximum(jnp.linalg.norm(kk, axis=-1, keepdims=True), 1e-12)
    k = k * (1.0 + (a - 1.0) * k_a)
    return to_heads(r), to_heads(decay), to_heads(k), to_heads(v), kk, kk * to_heads(a), g


def rwkv_scan(state0, r, decay, k, v, kk, kka):
    def time_major(t):
        t = jnp.stack([t[0], t[1][:, ::-1]])
        return jnp.moveaxis(t, 2, 0)

    def step(s, inp):
        r_t, w_t, k_t, v_t, kk_t, kka_t = inp
        sa = jnp.einsum('zbhij,zbhj->zbhi', s, kk_t)
        s = s * w_t[..., None, :] - sa[..., None] * kka_t[..., None, :] + v_t[..., None] * k_t[..., None, :]
        return s, jnp.einsum('zbhij,zbhj->zbhi', s, r_t)

    s_fin, ys = lax.scan(step, state0, tuple(time_major(t) for t in (r, decay, k, v, kk, kka)))
    ys = jnp.moveaxis(ys, 0, 2)
    return jnp.stack([ys[0], ys[1][:, ::-1]]), s_fin


def rwkv_output(y, r, k, v, g, r_k, lnx_w, lnx_b, dtype):
    mean = jnp.mean(y, axis=-1, keepdims=True)
    var = jnp.mean(jnp.square(y - mean), axis=-1, keepdims=True)
    yn = ((y - mean) * lax.rsqrt(var + RWKV_LNX_EPS)).reshape(*y.shape[:-2], RWKV_DIM) * lnx_w + lnx_b
    bonus = (jnp.sum(r * k * r_k, axis=-1, keepdims=True) * v).reshape(*v.shape[:-2], RWKV_DIM)
    return jnp.sum((yn + bonus) * g, axis=0).astype(dtype)


def mixer_rwkv(z_rkvg, z_w, z_a, zc_rkvg, zc_w, zc_a, mu, w0, w2, a0, a2, g2, k_k, k_a, r_k,
               lnx_w, lnx_b, with_ctx):
    prep = functools.partial(rwkv_prepare, mu=mu, w0=w0, w2=w2, a0=a0, a2=a2, g2=g2, k_k=k_k, k_a=k_a)
    post = functools.partial(rwkv_output, r_k=r_k, lnx_w=lnx_w, lnx_b=lnx_b, dtype=z_rkvg.dtype)
    b = z_rkvg.shape[0]
    state0 = jnp.zeros((2, b, RWKV_HEADS, RWKV_HEAD_SIZE, RWKV_HEAD_SIZE), jnp.float32)
    rc, dc, kc, vc, kkc, kkac, gc = prep(zc_rkvg, zc_w, zc_a)
    yc, state_ctx = rwkv_scan(state0, rc, dc, kc, vc, kkc, kkac)
    r, d, k, v, kk, kka, g = prep(z_rkvg, z_w, z_a)
    y, _ = rwkv_scan(state_ctx, r, d, k, v, kk, kka)
    out = post(y, r, k, v, g)
    out_c = post(yc, rc, kc, vc, gc) if with_ctx else None
    return out, out_c


def mixer_mla(zcq, zckv, zkr, zcq_c, zckv_c, zkr_c, g_cq, g_ckv, w_uq, w_ukv, g_qn, g_qr, g_kn, g_kr,
              cos, sin, with_ctx):
    def project(cq, ckv, kr, rope):
        q = split_heads(rms_norm(cq, g_cq) @ w_uq, MLA_HEADS)
        kv = split_heads(rms_norm(ckv, g_ckv) @ w_ukv, MLA_HEADS)
        q_nope = rms_norm(q[..., :MLA_NOPE], g_qn)
        q_rope = rms_norm(q[..., MLA_NOPE:], g_qr)
        k_nope = rms_norm(kv[..., :MLA_NOPE], g_kn)
        k_rope = rms_norm(kr[:, None], g_kr)
        if rope is not None:
            q_rope = apply_rope(q_rope, *rope)
            k_rope = apply_rope(k_rope, *rope)
        q = jnp.concatenate([q_nope, q_rope], axis=-1)
        k = jnp.concatenate([k_nope, jnp.broadcast_to(k_rope, k_nope.shape[:-1] + (MLA_ROPE,))], axis=-1)
        return q, k, kv[..., MLA_NOPE:]

    q, k, v = project(zcq, zckv, zkr, (cos, sin))
    qc, kc, vc = project(zcq_c, zckv_c, zkr_c, None)
    y = merge_heads(dense_attention(q[:, :, None], k, v, kc, vc))
    y_c = merge_heads(context_attention(qc[:, :, None], kc, vc)) if with_ctx else None
    return y, y_c


def merge_branches(ys, gate_logits, w_branch):
    gates = jax.nn.sigmoid(gate_logits.reshape(*gate_logits.shape[:-1], N_BRANCH, D_MODEL))
    merged = gates[..., 0, :] * (ys[0] @ w_branch[0])
    for i in range(1, N_BRANCH):
        merged = merged + gates[..., i, :] * (ys[i] @ w_branch[i])
    return merged


def conv_ffn(h, w_up, conv_w, conv_b, w_down):
    a, b = jnp.split(h @ w_up, 2, axis=-1)
    ap = jnp.pad(a, ((0, 0), (1, 1), (0, 0)))
    a = ap[:, :-2] * conv_w[0] + ap[:, 1:-1] * conv_w[1] + ap[:, 2:] * conv_w[2] + conv_b
    return (jax.nn.gelu(a) * b) @ w_down


def hybrid_layer(x, xc, c_act, cc_act, rope_attn, rope_mla, with_ctx,
                 ada_w, ada_b, norm1_g, norm2_g, w_in,
                 ga_q_norm, ga_k_norm, wa_q_norm, wa_k_norm, wa_sink,
                 rwkv_mu, rwkv_w0, rwkv_w2, rwkv_a0, rwkv_a2, rwkv_g2, rwkv_k_k, rwkv_k_a, rwkv_r_k,
                 rwkv_lnx_w, rwkv_lnx_b,
                 mla_cq_norm, mla_ckv_norm, mla_w_uq, mla_w_ukv, mla_qn_norm, mla_qr_norm,
                 mla_kn_norm, mla_kr_norm,
                 w_branch, w_out, ffn_up, ffn_conv_w, ffn_conv_b, ffn_down):
    cos_a, sin_a = rope_attn
    cos_m, sin_m = rope_mla
    mod = (c_act @ ada_w + ada_b)[:, None, :]
    modc = cc_act @ ada_w + ada_b
    sh1, sc1, gt1, sh2, sc2, gt2 = jnp.split(mod, 6, axis=-1)
    sh1c, sc1c, gt1c, sh2c, sc2c, gt2c = jnp.split(modc, 6, axis=-1)
    h = rms_norm(x, norm1_g) * (1.0 + sc1) + sh1
    hc = rms_norm(xc, norm1_g) * (1.0 + sc1c) + sh1c
    z = jnp.split(h @ w_in, IN_OFFSETS, axis=-1)
    zc = jnp.split(hc @ w_in, IN_OFFSETS, axis=-1)
    y_ga, yc_ga = mixer_global(z[0], z[1], z[2], zc[0], zc[1], zc[2], ga_q_norm, ga_k_norm,
                               cos_a, sin_a, with_ctx)
    y_wa, yc_wa = mixer_window(z[3], z[4], z[5], zc[3], zc[4], zc[5], wa_q_norm, wa_k_norm, wa_sink,
                               cos_a, sin_a, with_ctx)
    y_rw, yc_rw = mixer_rwkv(z[6], z[7], z[8], zc[6], zc[7], zc[8], rwkv_mu, rwkv_w0, rwkv_w2,
                             rwkv_a0, rwkv_a2, rwkv_g2, rwkv_k_k, rwkv_k_a, rwkv_r_k,
                             rwkv_lnx_w, rwkv_lnx_b, with_ctx)
    y_ml, yc_ml = mixer_mla(z[9], z[10], z[11], zc[9], zc[10], zc[11], mla_cq_norm, mla_ckv_norm,
                            mla_w_uq, mla_w_ukv, mla_qn_norm, mla_qr_norm, mla_kn_norm, mla_kr_norm,
                            cos_m, sin_m, with_ctx)
    x = x + gt1 * (merge_branches((y_ga, y_wa, y_rw, y_ml), z[12], w_branch) @ w_out)
    h2 = rms_norm(x, norm2_g) * (1.0 + sc2) + sh2
    x = x + gt2 * conv_ffn(h2, ffn_up, ffn_conv_w, ffn_conv_b, ffn_down)
    if with_ctx:
        xc = xc + gt1c * (merge_branches((yc_ga, yc_wa, yc_rw, yc_ml), zc[12], w_branch) @ w_out)
        h2c = rms_norm(xc, norm2_g) * (1.0 + sc2c) + sh2c
        xc = xc + gt2c * conv_ffn(h2c, ffn_up, ffn_conv_w, ffn_conv_b, ffn_down)
    return x, xc


def setup_inputs(seed: int = 0) -> dict:
    key = jax.random.key(seed)
    ks = iter(jax.random.split(key, 39))
    L = DEPTH

    def normal(shape, scale):
        return jax.random.normal(next(ks), shape, jnp.float32) * scale

    def gain(shape):
        return 1.0 + 0.05 * jax.random.normal(next(ks), shape, jnp.float32)

    def uniform(shape, lo, hi):
        return jax.random.uniform(next(ks), shape, jnp.float32, lo, hi)

    return {
        'x': normal((BATCH, SEQ, D_MODEL), 1.0),
        'c': normal((BATCH, D_MODEL), 1.0),
        'ctx': normal((BATCH, CTX_LEN, D_MODEL), 1.0),
        'c_ctx': normal((D_MODEL,), 1.0),
        'ada_w': normal((L, D_MODEL, 6 * D_MODEL), 0.5 * D_MODEL ** -0.5),
        'ada_b': normal((L, 6 * D_MODEL), 0.02),
        'norm1_g': gain((L, D_MODEL)),
        'norm2_g': gain((L, D_MODEL)),
        'w_in': normal((L, D_MODEL, IN_WIDTH), D_MODEL ** -0.5),
        'ga_q_norm': gain((L, HEAD_DIM)),
        'ga_k_norm': gain((L, HEAD_DIM)),
        'wa_q_norm': gain((L, HEAD_DIM)),
        'wa_k_norm': gain((L, HEAD_DIM)),
        'wa_sink': normal((L, WA_HEADS), 0.5),
        'rwkv_mu': uniform((L, 2, RWKV_MU_WIDTH), 0.0, 1.0),
        'rwkv_w0': uniform((L, 2, RWKV_DIM), -6.0, -0.5),
        'rwkv_w2': normal((L, 2, RWKV_DECAY_LORA, RWKV_DIM), 0.1 * RWKV_DECAY_LORA ** -0.5),
        'rwkv_a0': normal((L, 2, RWKV_DIM), 0.5),
        'rwkv_a2': normal((L, 2, RWKV_ICLR_LORA, RWKV_DIM), 0.5 * RWKV_ICLR_LORA ** -0.5),
        'rwkv_g2': normal((L, RWKV_GATE_LORA, RWKV_DIM), RWKV_GATE_LORA ** -0.5),
        'rwkv_k_k': 0.85 + normal((L, RWKV_DIM), 0.05),
        'rwkv_k_a': gain((L, RWKV_DIM)),
        'rwkv_r_k': normal((L, RWKV_HEADS, RWKV_HEAD_SIZE), 0.1),
        'rwkv_lnx_w': gain((L, RWKV_DIM)),
        'rwkv_lnx_b': normal((L, RWKV_DIM), 0.01),
        'mla_cq_norm': gain((L, MLA_Q_LORA)),
        'mla_ckv_norm': gain((L, MLA_KV_LORA)),
        'mla_w_uq': normal((L, MLA_Q_LORA, MLA_HEADS * (MLA_NOPE + MLA_ROPE)), MLA_Q_LORA ** -0.5),
        'mla_w_ukv': normal((L, MLA_KV_LORA, MLA_HEADS * (MLA_NOPE + MLA_V)), MLA_KV_LORA ** -0.5),
        'mla_qn_norm': gain((L, MLA_NOPE)),
        'mla_qr_norm': gain((L, MLA_ROPE)),
        'mla_kn_norm': gain((L, MLA_NOPE)),
        'mla_kr_norm': gain((L, MLA_ROPE)),
        'w_branch': normal((L, N_BRANCH, MIX_WIDTH, D_MODEL), MIX_WIDTH ** -0.5),
        'w_out': normal((L, D_MODEL, D_MODEL), D_MODEL ** -0.5),
        'ffn_up': normal((L, D_MODEL, 2 * D_FF), D_MODEL ** -0.5),
        'ffn_conv_w': normal((L, FFN_CONV, D_FF), FFN_CONV ** -0.5),
        'ffn_conv_b': normal((L, D_FF), 0.01),
        'ffn_down': normal((L, D_FF, D_MODEL), D_FF ** -0.5),
    }


def reference(x, c, ctx, c_ctx, ada_w, ada_b, norm1_g, norm2_g, w_in,
              ga_q_norm, ga_k_norm, wa_q_norm, wa_k_norm, wa_sink,
              rwkv_mu, rwkv_w0, rwkv_w2, rwkv_a0, rwkv_a2, rwkv_g2, rwkv_k_k, rwkv_k_a, rwkv_r_k,
              rwkv_lnx_w, rwkv_lnx_b,
              mla_cq_norm, mla_ckv_norm, mla_w_uq, mla_w_ukv, mla_qn_norm, mla_qr_norm,
              mla_kn_norm, mla_kr_norm,
              w_branch, w_out, ffn_up, ffn_conv_w, ffn_conv_b, ffn_down):
    n_tokens = x.shape[1]
    rope_attn = axial_rope_tables(n_tokens, HEAD_DIM)
    rope_mla = axial_rope_tables(n_tokens, MLA_ROPE)
    c_act = jax.nn.silu(c)
    cc_act = jax.nn.silu(c_ctx)
    x_lat, x_ctx = x, ctx
    for layer in range(DEPTH):
        x_lat, x_ctx = hybrid_layer(
            x_lat, x_ctx, c_act, cc_act, rope_attn, rope_mla, layer < DEPTH - 1,
            ada_w[layer], ada_b[layer], norm1_g[layer], norm2_g[layer], w_in[layer],
            ga_q_norm[layer], ga_k_norm[layer], wa_q_norm[layer], wa_k_norm[layer], wa_sink[layer],
            rwkv_mu[layer], rwkv_w0[layer], rwkv_w2[layer], rwkv_a0[layer], rwkv_a2[layer],
            rwkv_g2[layer], rwkv_k_k[layer], rwkv_k_a[layer], rwkv_r_k[layer],
            rwkv_lnx_w[layer], rwkv_lnx_b[layer],
            mla_cq_norm[layer], mla_ckv_norm[layer], mla_w_uq[layer], mla_w_ukv[layer],
            mla_qn_norm[layer], mla_qr_norm[layer], mla_kn_norm[layer], mla_kr_norm[layer],
            w_branch[layer], w_out[layer], ffn_up[layer], ffn_conv_w[layer], ffn_conv_b[layer],
            ffn_down[layer])
    return x_lat
```

```python
from contextlib import ExitStack
import numpy as np
import concourse.bass as bass
import concourse.mybir as mybir
from concourse.bass_utils import run_bass_kernel_spmd

F32 = mybir.dt.float32
AF = mybir.ActivationFunctionType
ALU = mybir.AluOpType
AX = mybir.AxisListType

D = 2048
GRID_W = 64
EPS = 1e-6
IN_W = 13376
DFF = 5632
O_GAQ, O_GAK, O_GAV, O_WAQ, O_WAK, O_WAV = 0, 512, 768, 1024, 1536, 1792
O_RKVG, O_ZW, O_ZA, O_CQ, O_CKV, O_KR, O_GATE = 2048, 3840, 4032, 4224, 4608, 5120, 5184
LNX_EPS = 64e-5


class Buf:
    __slots__ = ("name", "w", "r")

    def __init__(self, name=""):
        self.name = name
        self.w = None
        self.r = {}


class Sync:
    MAXC = 30000

    def __init__(self, nc, n_dma_sems=48, same_engine_sync=True):
        self.nc = nc
        self.hw = {"pe": nc.tensor, "dve": nc.vector, "act": nc.scalar, "pool": nc.gpsimd, "sp": nc.sync}
        self.gen = {k: 0 for k in self.hw}
        self.sem = {k: nc.alloc_semaphore("sem_%s_0" % k) for k in self.hw}
        self.count = {k: 0 for k in self.hw}
        self.seen = {k: {} for k in self.hw}
        self.dma_sems = [nc.alloc_semaphore("dsem%d" % i) for i in range(n_dma_sems)]
        self.dma_uses = [0] * n_dma_sems
        self.dma_next = 0
        self.same_engine_sync = same_engine_sync
        self.latest = {}

    def _wait(self, eng, dep):
        key, sem, val = dep
        if self.seen[eng].get(key, 0) >= val:
            return
        self.hw[eng].wait_ge(sem, val)
        self.seen[eng][key] = val

    @staticmethod
    def _add(deps, d):
        if d is not None and deps.get(d[0], (0, 0, 0))[2] < d[2]:
            deps[d[0]] = d

    def _deps(self, reads, writes):
        deps = {}
        for b in reads:
            self._add(deps, b.w)
        for b in writes:
            self._add(deps, b.w)
            for d in b.r.values():
                self._add(deps, d)
        return deps

    def _mark(self, dep, reads, writes):
        for b in reads:
            b.r[dep[0]] = dep
        for b in writes:
            b.w = dep
            b.r = {}
        self.latest[dep[0]] = dep

    def op(self, eng, fn, reads=(), writes=()):
        for key, d in self._deps(reads, writes).items():
            if isinstance(key, tuple) and key[0] == eng and (eng == "pe" or not self.same_engine_sync):
                continue
            self._wait(eng, d)
        ins = fn(self.hw[eng])
        self.count[eng] += 1
        ins.then_inc(self.sem[eng], 1)
        self._mark(((eng, self.gen[eng]), self.sem[eng], self.count[eng]), reads, writes)
        if self.count[eng] >= self.MAXC:
            self.gen[eng] += 1
            self.sem[eng] = self.nc.alloc_semaphore("sem_%s_%d" % (eng, self.gen[eng]))
            self.count[eng] = 0
        return ins

    def dma(self, q, out, in_, reads=(), writes=(), **kw):
        i = self.dma_next
        self.dma_next = (self.dma_next + 1) % len(self.dma_sems)
        sem = self.dma_sems[i]
        key = "d%d" % i
        if self.dma_uses[i] > 0:
            self._wait(q, (key, sem, 16 * self.dma_uses[i]))
        for k, d in self._deps(reads, writes).items():
            self._wait(q, d)
        self.dma_uses[i] += 1
        ins = self.hw[q].dma_start(out=out, in_=in_, **kw)
        ins.then_inc(sem, 16)
        self._mark((key, sem, 16 * self.dma_uses[i]), reads, writes)
        return ins

    def barrier(self):
        for eng in self.hw:
            for dep in list(self.latest.values()):
                key = dep[0]
                if isinstance(key, tuple) and key[0] == "pe" and eng == "pe":
                    continue
                self._wait(eng, dep)


class Tl:
    def __init__(self, t, name):
        self.t = t
        self.b = Buf(name)

    def __getitem__(self, idx):
        return self.t[idx]


class Ctx:
    pass


def sb(g, st, name, shape, dtype=F32):
    g.uid += 1
    nm = "%s_%d" % (name, g.uid)
    return Tl(st.enter_context(g.nc.sbuf_tensor(nm, list(shape), dtype)), nm)


def evac_engine(g):
    g.ev += 1
    return "dve" if g.ev % 2 else "act"


def copy_op(g, eng, out, in_, reads, writes):
    if eng == "act":
        return g.S.op("act", lambda e: e.copy(out=out, in_=in_), reads=reads, writes=writes)
    return g.S.op(eng, lambda e: e.tensor_copy(out=out, in_=in_), reads=reads, writes=writes)


def next_psum(g):
    g.pi = (g.pi + 1) % len(g.psums)
    return g.psums[g.pi]


def linear(g, x, w, y, T, K, N, G=8, NB=512, pro=None, epi=None, segs=None, wlist=None):
    S, nc = g.S, g.nc
    KC = K // 128
    assert K % 128 == 0 and T % 128 == 0
    NT = T // 128
    XW = min(K, 2048)
    if segs is None:
        segs = [(0, KC)]
    with ExitStack() as st:
        xt = sb(g, st, "xt", [128, KC, G * 128])
        xin = [sb(g, st, "xin", [128, XW]) for _ in range(2)]
        wt = [sb(g, st, "wt", [128, KC, NB]) for _ in range(2)]
        ot = [sb(g, st, "ot", [128, NB]) for _ in range(3)]
        nxin = nw = no = 0
        for g0 in range(0, NT, G):
            gn = min(G, NT - g0)
            for gi in range(gn):
                ti = g0 + gi
                for c0 in range(0, K, XW):
                    cw = min(XW, K - c0)
                    xi = xin[nxin % 2]
                    nxin += 1
                    S.dma("sp", xi[:, 0:cw], x[ti * 128:(ti + 1) * 128, c0:c0 + cw], writes=[xi.b])
                    if pro is not None:
                        pro(xi, ti, cw)
                    for k4 in range(0, cw // 128, 4):
                        ps = next_psum(g)
                        nk = min(4, cw // 128 - k4)
                        for j in range(nk):
                            kk = k4 + j
                            S.op("pe", lambda e: e.transpose(ps[:, j * 128:(j + 1) * 128], xi[:, kk * 128:(kk + 1) * 128], g.ident[:]),
                                 reads=[xi.b, g.ident.b], writes=[ps.b])
                        kc0 = c0 // 128 + k4
                        copy_op(g, evac_engine(g), xt[:, kc0:kc0 + nk, gi * 128:(gi + 1) * 128],
                                ps[:, 0:nk * 128].rearrange("p (k t) -> p k t", k=nk), [ps.b], [xt.b])
            for n0 in range(0, N, NB):
                nb = min(NB, N - n0)
                wi = wt[nw % 2]
                nw += 1
                if wlist is None:
                    S.dma("sp", wi[:, :, 0:nb], w.rearrange("(kc p) n -> p kc n", p=128)[:, :, n0:n0 + nb], writes=[wi.b])
                else:
                    for (wa, k0, k1) in wlist:
                        S.dma("sp", wi[:, k0:k1, 0:nb], wa.rearrange("(kc p) n -> p kc n", p=128)[:, :, n0:n0 + nb], writes=[wi.b])
                for gi in range(gn):
                    ti = g0 + gi
                    pss = []
                    for (k0, k1) in segs:
                        ps = next_psum(g)
                        pss.append(ps)
                        for kc in range(k0, k1):
                            S.op("pe", lambda e: e.matmul(ps[:, 0:nb], lhsT=xt[:, kc, gi * 128:(gi + 1) * 128], rhs=wi[:, kc, 0:nb],
                                                          start=(kc == k0), stop=(kc == k1 - 1)),
                                 reads=[xt.b, wi.b], writes=[ps.b])
                    o = ot[no % 3]
                    no += 1
                    if epi is None:
                        copy_op(g, evac_engine(g), o[:, 0:nb], pss[0][:, 0:nb], [pss[0].b], [o.b])
                        S.dma("pool", y[ti * 128:(ti + 1) * 128, n0:n0 + nb], o[:, 0:nb], reads=[o.b])
                    else:
                        epi(pss, o, ti, n0, nb)
    S.barrier()


def rms_rstd(g, st_tiles, x_ap, n, width, reads_b, sq, ss, rstd, eps=EPS):
    S = g.S
    S.op("act", lambda e: e.activation(out=sq[:, 0:n * width].rearrange("p (n w) -> p n w", n=n), in_=x_ap, func=AF.Square),
         reads=[reads_b], writes=[sq.b])
    S.op("dve", lambda e: e.tensor_reduce(out=ss[:, 0:n], in_=sq[:, 0:n * width].rearrange("p (n w) -> p n w", n=n), axis=AX.X, op=ALU.add),
         reads=[sq.b], writes=[ss.b])
    S.op("dve", lambda e: e.tensor_scalar(out=ss[:, 0:n], in0=ss[:, 0:n], scalar1=1.0 / width, scalar2=eps, op0=ALU.mult, op1=ALU.add),
         reads=[ss.b], writes=[ss.b])
    S.op("act", lambda e: e.activation(out=ss[:, 0:n], in_=ss[:, 0:n], func=AF.Sqrt), reads=[ss.b], writes=[ss.b])
    S.op("dve", lambda e: e.reciprocal(out=rstd[:, 0:n], in_=ss[:, 0:n]), reads=[ss.b], writes=[rstd.b])


def bcast_row(g, q, tile_ap, buf, dram_row_ap):
    g.S.dma(q, tile_ap, dram_row_ap.partition_broadcast(128), writes=[buf])


def stage_mod(g, l):
    S, nc = g.S, g.nc
    with ExitStack() as st:
        wt = [sb(g, st, "mw", [128, 16, 512]) for _ in range(2)]
        bt = [sb(g, st, "mb", [128, 512]) for _ in range(2)]
        ot = [sb(g, st, "mo", [128, 512]) for _ in range(3)]
        no = 0
        for bi, n0 in enumerate(range(0, 6 * D, 512)):
            wi = wt[bi % 2]
            bb = bt[bi % 2]
            S.dma("sp", wi[:, :, :], g.ada_w[l].rearrange("(kc p) n -> p kc n", p=128)[:, :, n0:n0 + 512], writes=[wi.b])
            bcast_row(g, "sp", bb[:, :], bb.b, g.ada_b[l, n0:n0 + 512])
            for r in range(2):
                ps = next_psum(g)
                for kc in range(16):
                    S.op("pe", lambda e: e.matmul(ps[:, :], lhsT=g.cb[r][:, kc, :], rhs=wi[:, kc, :], start=(kc == 0), stop=(kc == 15)),
                         reads=[g.cb[r].b, wi.b], writes=[ps.b])
                o = ot[no % 3]
                no += 1
                S.op("dve", lambda e: e.tensor_tensor(out=o[:, :], in0=ps[:, :], in1=bb[:, :], op=ALU.add), reads=[ps.b, bb.b], writes=[o.b])
                S.dma("pool", g.MODB[r, :, n0:n0 + 512], o[:, :], reads=[o.b])
    S.barrier()


def make_norm_pro(g, st, l, gain_dram, sc_off, sh_off):
    S = g.S
    A = [sb(g, st, "nA", [128, D]) for _ in range(2)]
    B = [sb(g, st, "nB", [128, D]) for _ in range(2)]
    sq = sb(g, st, "nsq", [128, D])
    ss = sb(g, st, "nss", [128, 1])
    rstd = sb(g, st, "nrs", [128, 1])
    with ExitStack() as st2:
        gt = sb(g, st2, "ng", [128, D])
        bcast_row(g, "sp", gt[:, :], gt.b, gain_dram)
        for r in range(2):
            S.dma("sp", A[r][:, :], g.MODB[r, :, sc_off:sc_off + D], writes=[A[r].b])
            S.dma("sp", B[r][:, :], g.MODB[r, :, sh_off:sh_off + D], writes=[B[r].b])
            S.op("dve", lambda e: e.scalar_tensor_tensor(out=A[r][:, :], in0=A[r][:, :], scalar=1.0, in1=gt[:, :], op0=ALU.add, op1=ALU.mult),
                 reads=[A[r].b, gt.b], writes=[A[r].b])
        S.barrier()

    def pro(xi, ti, cw):
        r = 1 if ti < g.NTC else 0
        rms_rstd(g, None, xi[:, 0:D].rearrange("p (n w) -> p n w", n=1), 1, D, xi.b, sq, ss, rstd)
        S.op("dve", lambda e: e.scalar_tensor_tensor(out=xi[:, 0:D], in0=xi[:, 0:D], scalar=rstd[:, 0:1], in1=A[r][:, :], op0=ALU.mult, op1=ALU.mult),
             reads=[xi.b, rstd.b, A[r].b], writes=[xi.b])
        S.op("dve", lambda e: e.tensor_tensor(out=xi[:, 0:D], in0=xi[:, 0:D], in1=B[r][:, :], op=ALU.add), reads=[xi.b, B[r].b], writes=[xi.b])
    return pro


def make_resid_epi(g, st, gate_off):
    S = g.S
    gts = [sb(g, st, "rg", [128, D]) for _ in range(2)]
    for r in range(2):
        S.dma("sp", gts[r][:, :], g.MODB[r, :, gate_off:gate_off + D], writes=[gts[r].b])
    xb = [sb(g, st, "rx", [128, 512]) for _ in range(3)]
    cnt = [0]

    def epi(pss, o, ti, n0, nb):
        r = 1 if ti < g.NTC else 0
        x = xb[cnt[0] % 3]
        cnt[0] += 1
        S.dma("sp", x[:, 0:nb], g.XS[ti * 128:(ti + 1) * 128, n0:n0 + nb], writes=[x.b])
        S.op("dve", lambda e: e.tensor_tensor(out=o[:, 0:nb], in0=pss[0][:, 0:nb], in1=gts[r][:, n0:n0 + nb], op=ALU.mult),
             reads=[pss[0].b, gts[r].b], writes=[o.b])
        S.op("dve", lambda e: e.tensor_tensor(out=o[:, 0:nb], in0=o[:, 0:nb], in1=x[:, 0:nb], op=ALU.add), reads=[o.b, x.b], writes=[o.b])
        S.dma("pool", g.XS[ti * 128:(ti + 1) * 128, n0:n0 + nb], o[:, 0:nb], reads=[o.b])
    return epi


def norm_rope(g, src, src_b, n, w, gain_ap, gain_b, out, out_b, tmp, sq, ss, rstd, rope=None):
    S = g.S
    rms_rstd(g, None, src, n, w, src_b, sq, ss, rstd)
    dst = out if rope is None else tmp[:, 0:n * w].rearrange("p (n w) -> p n w", n=n)
    dst_b = out_b if rope is None else tmp.b
    S.op("dve", lambda e: e.tensor_tensor(out=dst, in0=src, in1=rstd[:, 0:n].unsqueeze(2).broadcast_to([128, n, w]), op=ALU.mult),
         reads=[src_b, rstd.b], writes=[dst_b])
    S.op("dve", lambda e: e.tensor_tensor(out=dst, in0=dst, in1=gain_ap, op=ALU.mult), reads=[dst_b, gain_b], writes=[dst_b])
    if rope is None:
        return
    cos, ssin, blk = rope
    nh = w // (2 * blk)
    xv = dst.rearrange("p n (h b k) -> p n h b k", h=nh, b=2, k=blk)
    sv = ssin[:, 0:w].rearrange("p (h b k) -> p h b k", h=nh, b=2, k=blk)
    sw = sq[:, 0:n * w].rearrange("p (n h b k) -> p n h b k", n=n, h=nh, b=2, k=blk)
    for n_i in range(n):
        for b in range(2):
            S.op("dve", lambda e: e.tensor_tensor(out=sw[:, n_i, :, b, :], in0=xv[:, n_i, :, 1 - b, :], in1=sv[:, :, b, :], op=ALU.mult),
                 reads=[dst_b, ssin.b], writes=[sq.b])
    S.op("dve", lambda e: e.tensor_tensor(out=dst, in0=dst, in1=cos[:, 0:w].unsqueeze(1).broadcast_to([128, n, w]), op=ALU.mult),
         reads=[dst_b, cos.b], writes=[dst_b])
    S.op("dve", lambda e: e.tensor_tensor(out=out, in0=dst, in1=sq[:, 0:n * w].rearrange("p (n w) -> p n w", n=n), op=ALU.add),
         reads=[dst_b, sq.b], writes=[out_b])


def transpose_slots(g, src_tl, slots, w, dst_tl, dst_slot0):
    S = g.S
    for i0 in range(0, len(slots), 4):
        grp = slots[i0:i0 + 4]
        ps = next_psum(g)
        for j, s_ in enumerate(grp):
            S.op("pe", lambda e: e.transpose(ps[0:w, j * 128:(j + 1) * 128], src_tl[:, s_, 0:w], g.ident[:]),
                 reads=[src_tl.b, g.ident.b], writes=[ps.b])
        copy_op(g, evac_engine(g), dst_tl[0:w, dst_slot0 + i0:dst_slot0 + i0 + len(grp), :],
                ps[0:w, 0:len(grp) * 128].rearrange("p (k t) -> p k t", k=len(grp)), [ps.b], [dst_tl.b])


def stage_attn_prep(g, l):
    S = g.S
    with ExitStack() as st:
        gain = sb(g, st, "apg", [128, 2, 6, 128])
        for gi, (qn, kn) in enumerate([(g.ga_q_norm, g.ga_k_norm), (g.wa_q_norm, g.wa_k_norm)]):
            for s_ in range(6):
                bcast_row(g, "sp", gain[:, gi, s_, :], gain.b, (qn if s_ < 4 else kn)[l, :])
        zt = [sb(g, st, "apz", [128, 16, 128]) for _ in range(2)]
        xr = [sb(g, st, "apx", [128, 12, 128]) for _ in range(2)]
        tT = [sb(g, st, "apt", [128, 12, 128]) for _ in range(2)]
        cs = [sb(g, st, "apc", [128, 128]) for _ in range(2)]
        sn = [sb(g, st, "aps", [128, 128]) for _ in range(2)]
        tmp = sb(g, st, "aptmp", [128, 6 * 128])
        sq = sb(g, st, "apsq", [128, 6 * 128])
        ss = sb(g, st, "apss", [128, 8])
        rstd = sb(g, st, "aprs", [128, 8])
        for ti in range(g.NT):
            z = zt[ti % 2]
            x = xr[ti % 2]
            t_ = tT[ti % 2]
            S.dma("sp", z[:, :, :], g.Z[ti * 128:(ti + 1) * 128, 0:2048].rearrange("p (s d) -> p s d", s=16), writes=[z.b])
            rope = None
            if ti >= g.NTC:
                c_, s_t = cs[ti % 2], sn[ti % 2]
                p0 = (ti - g.NTC) * 128
                S.dma("sp", c_[:, :], g.ropeA[0, p0:p0 + 128, :], writes=[c_.b])
                S.dma("sp", s_t[:, :], g.ropeA[1, p0:p0 + 128, :], writes=[s_t.b])
                rope = (c_, s_t, 32)
            for gi, base in enumerate([0, 8]):
                norm_rope(g, z[:, base:base + 6, :], z.b, 6, 128, gain[:, gi, :, :], gain.b,
                          x[:, gi * 6:(gi + 1) * 6, :], x.b, tmp, sq, ss, rstd, rope=rope)
            transpose_slots(g, x, list(range(12)), 128, t_, 0)
            S.dma("pool", g.QKT[:, :, ti * 128:(ti + 1) * 128].rearrange("s d t -> d s t"), t_[:, :, :], reads=[t_.b])
    S.barrier()


def attention(g, heads, scale, sink_tl=None):
    S = g.S
    T, NT = g.T, g.NT
    po = g.psums[0:4]
    sps = g.psums[4:8]
    with ExitStack() as st:
        kts = [sb(g, st, "akt", [128, T]) for _ in range(2)]
        vt = sb(g, st, "avt", [128, NT, 129])
        qts = [[sb(g, st, "aqt", [128, 512]) for _ in range(2)] for _ in range(2)]
        pts = [sb(g, st, "apt", [128, 512]) for _ in range(3)]
        yts = [sb(g, st, "ayt", [128, 128]) for _ in range(3)]
        den = sb(g, st, "aden", [128, 4])
        S.op("dve", lambda e: e.memset(vt[:, :, 128:129], 1.0), writes=[vt.b])
        nq = npt = ny = nsp = 0
        for hd in heads:
            for pi_, (kd_ap, kd) in enumerate(hd["kparts"]):
                S.dma("sp", kts[pi_][0:kd, :], kd_ap, writes=[kts[pi_].b])
            S.dma("sp", vt[:, :, 0:128], hd["v"].rearrange("(n p) d -> p n d", p=128), writes=[vt.b])
            for qh, qparts in enumerate(hd["qparts"]):
                for (q0, qlen, keys) in hd["blocks"]:
                    nqb = qlen // 128
                    qt = qts[nq % 2]
                    nq += 1
                    for pi_, (qd_ap, kd) in enumerate(qparts):
                        S.dma("sp", qt[pi_][0:kd, 0:qlen], qd_ap[:, q0:q0 + qlen], writes=[qt[pi_].b])
                    for idx, (kt, mask) in enumerate(keys):
                        ps = sps[nsp % 4]
                        nsp += 1
                        npart = len(qparts)
                        for pi_, (qd_ap, kd) in enumerate(qparts):
                            S.op("pe", lambda e: e.matmul(ps[:, 0:qlen], lhsT=kts[pi_][0:kd, kt * 128:(kt + 1) * 128], rhs=qt[pi_][0:kd, 0:qlen],
                                                          start=(pi_ == 0), stop=(pi_ == npart - 1)),
                                 reads=[kts[pi_].b, qt[pi_].b], writes=[ps.b])
                        pt = pts[npt % 3]
                        npt += 1
                        S.op("act", lambda e: e.activation(out=pt[:, 0:qlen], in_=ps[:, 0:qlen], func=AF.Exp, scale=scale),
                             reads=[ps.b], writes=[pt.b])
                        if mask is not None:
                            S.op("dve", lambda e: e.tensor_tensor(out=pt[:, 0:qlen], in0=pt[:, 0:qlen], in1=g.wmask[:, mask, :], op=ALU.mult),
                                 reads=[pt.b, g.wmask.b], writes=[pt.b])
                        for qb in range(nqb):
                            S.op("pe", lambda e: e.matmul(po[qb][:, 0:129], lhsT=pt[:, qb * 128:(qb + 1) * 128], rhs=vt[:, kt, :],
                                                          start=(idx == 0), stop=(idx == len(keys) - 1)),
                                 reads=[pt.b, vt.b], writes=[po[qb].b])
                    for qb in range(nqb):
                        y = yts[ny % 3]
                        ny += 1
                        if hd.get("sink_idx") is not None:
                            si = hd["sink_idx"][qh]
                            S.op("dve", lambda e: e.tensor_tensor(out=den[:, 0:1], in0=po[qb][:, 128:129], in1=sink_tl[:, si:si + 1], op=ALU.add),
                                 reads=[po[qb].b, sink_tl.b], writes=[den.b])
                            S.op("dve", lambda e: e.reciprocal(out=den[:, 0:1], in_=den[:, 0:1]), reads=[den.b], writes=[den.b])
                        else:
                            S.op("dve", lambda e: e.reciprocal(out=den[:, 0:1], in_=po[qb][:, 128:129]), reads=[po[qb].b], writes=[den.b])
                        S.op("dve", lambda e: e.tensor_scalar(out=y[:, :], in0=po[qb][:, 0:128], scalar1=den[:, 0:1], scalar2=None, op0=ALU.mult),
                             reads=[po[qb].b, den.b], writes=[y.b])
                        r0 = q0 + qb * 128
                        yc = hd["ycols"][qh]
                        S.dma("pool", g.YMIX[r0:r0 + 128, yc:yc + 128], y[:, :], reads=[y.b])
    S.barrier()


def dense_blocks(g):
    blocks = [(0, g.LC, [(kt, None) for kt in range(g.NTC)])]
    for q0 in range(g.LC, g.T, 512):
        blocks.append((q0, min(512, g.T - q0), [(kt, None) for kt in range(g.NT)]))
    return blocks


def window_blocks(g):
    blocks = [(0, g.LC, [(kt, None) for kt in range(g.NTC)])]
    nblk = g.TL // 128
    for n in range(nblk):
        keys = [(kt, None) for kt in range(g.NTC)]
        if n > 0:
            keys.append((g.NTC + n - 1, 0))
        keys.append((g.NTC + n, None))
        if n < nblk - 1:
            keys.append((g.NTC + n + 1, 1))
        blocks.append((g.LC + n * 128, 128, keys))
    return blocks


def stage_attn_gawa(g, l):
    S = g.S
    heads = []
    for hk in range(2):
        heads.append(dict(kparts=[(g.QKT[4 + hk], 128)], qparts=[[(g.QKT[2 * hk + j], 128)] for j in range(2)],
                          v=g.Z[:, O_GAV + hk * 128:O_GAV + (hk + 1) * 128], ycols=[(2 * hk + j) * 128 for j in range(2)],
                          blocks=dense_blocks(g)))
    attention(g, heads, 128 ** -0.5)
    with ExitStack() as st:
        sink = sb(g, st, "sink", [128, 4])
        bcast_row(g, "sp", sink[:, :], sink.b, g.wa_sink[l, :])
        S.op("act", lambda e: e.activation(out=sink[:, :], in_=sink[:, :], func=AF.Exp), reads=[sink.b], writes=[sink.b])
        heads = []
        for hk in range(2):
            heads.append(dict(kparts=[(g.QKT[10 + hk], 128)], qparts=[[(g.QKT[6 + 2 * hk + j], 128)] for j in range(2)],
                              v=g.Z[:, O_WAV + hk * 128:O_WAV + (hk + 1) * 128], ycols=[512 + (2 * hk + j) * 128 for j in range(2)],
                              blocks=window_blocks(g), sink_idx=[2 * hk, 2 * hk + 1]))
        attention(g, heads, 128 ** -0.5, sink_tl=sink)


def make_rms_pro(g, st, gain_dram, K):
    S = g.S
    gt = sb(g, st, "pg", [128, K])
    sq = sb(g, st, "psq", [128, K])
    ss = sb(g, st, "pss", [128, 1])
    rstd = sb(g, st, "prs", [128, 1])
    bcast_row(g, "sp", gt[:, :], gt.b, gain_dram)

    def pro(xi, ti, cw):
        rms_rstd(g, None, xi[:, 0:K].rearrange("p (n w) -> p n w", n=1), 1, K, xi.b, sq, ss, rstd)
        S.op("dve", lambda e: e.scalar_tensor_tensor(out=xi[:, 0:K], in0=xi[:, 0:K], scalar=rstd[:, 0:1], in1=gt[:, :], op0=ALU.mult, op1=ALU.mult),
             reads=[xi.b, rstd.b, gt.b], writes=[xi.b])
    return pro


def stage_mla(g, l):
    S = g.S
    T = g.T
    with ExitStack() as st:
        pro = make_rms_pro(g, st, g.mla_cq_norm[l, :], 384)
        linear(g, g.Z[:, O_CQ:O_CQ + 384], g.mla_w_uq[l], g.QML, T, 384, 768, G=17, NB=512, pro=pro)
    with ExitStack() as st:
        pro = make_rms_pro(g, st, g.mla_ckv_norm[l, :], 512)
        linear(g, g.Z[:, O_CKV:O_CKV + 512], g.mla_w_ukv[l], g.KVML, T, 512, 1024, G=17, NB=512, pro=pro)
    with ExitStack() as st:
        gq_n = sb(g, st, "mgqn", [128, 128])
        gq_r = sb(g, st, "mgqr", [128, 64])
        gk_n = sb(g, st, "mgkn", [128, 128])
        gk_r = sb(g, st, "mgkr", [128, 64])
        bcast_row(g, "sp", gq_n[:, :], gq_n.b, g.mla_qn_norm[l, :])
        bcast_row(g, "sp", gq_r[:, :], gq_r.b, g.mla_qr_norm[l, :])
        bcast_row(g, "sp", gk_n[:, :], gk_n.b, g.mla_kn_norm[l, :])
        bcast_row(g, "sp", gk_r[:, :], gk_r.b, g.mla_kr_norm[l, :])
        qin = [sb(g, st, "mq", [128, 4, 192]) for _ in range(2)]
        kvin = [sb(g, st, "mkv", [128, 4, 256]) for _ in range(2)]
        krin = [sb(g, st, "mkr", [128, 1, 64]) for _ in range(2)]
        xqn = [sb(g, st, "xqn", [128, 4, 128]) for _ in range(2)]
        xqr = [sb(g, st, "xqr", [128, 4, 64]) for _ in range(2)]
        xkn = [sb(g, st, "xkn", [128, 4, 128]) for _ in range(2)]
        xkr = [sb(g, st, "xkr", [128, 1, 64]) for _ in range(2)]
        tqn = [sb(g, st, "tqn", [128, 4, 128]) for _ in range(2)]
        tqr = [sb(g, st, "tqr", [64, 4, 128]) for _ in range(2)]
        tkn = [sb(g, st, "tkn", [128, 4, 128]) for _ in range(2)]
        tkr = [sb(g, st, "tkr", [64, 1, 128]) for _ in range(2)]
        cs = [sb(g, st, "mc", [128, 64]) for _ in range(2)]
        sn = [sb(g, st, "ms", [128, 64]) for _ in range(2)]
        tmp = sb(g, st, "mtmp", [128, 512])
        sq = sb(g, st, "msq", [128, 512])
        ss = sb(g, st, "mss", [128, 8])
        rstd = sb(g, st, "mrs", [128, 8])
        for ti in range(g.NT):
            i = ti % 2
            rows = slice(ti * 128, (ti + 1) * 128)
            S.dma("sp", qin[i][:, :, :], g.QML[rows, :].rearrange("p (h d) -> p h d", h=4), writes=[qin[i].b])
            S.dma("sp", kvin[i][:, :, :], g.KVML[rows, :].rearrange("p (h d) -> p h d", h=4), writes=[kvin[i].b])
            S.dma("sp", krin[i][:, 0, :], g.Z[rows, O_KR:O_KR + 64], writes=[krin[i].b])
            rope = None
            if ti >= g.NTC:
                p0 = (ti - g.NTC) * 128
                S.dma("sp", cs[i][:, :], g.ropeM[0, p0:p0 + 128, :], writes=[cs[i].b])
                S.dma("sp", sn[i][:, :], g.ropeM[1, p0:p0 + 128, :], writes=[sn[i].b])
                rope = (cs[i], sn[i], 16)
            norm_rope(g, qin[i][:, :, 0:128], qin[i].b, 4, 128, gq_n[:, :].unsqueeze(1).broadcast_to([128, 4, 128]), gq_n.b,
                      xqn[i][:, :, :], xqn[i].b, tmp, sq, ss, rstd)
            norm_rope(g, qin[i][:, :, 128:192], qin[i].b, 4, 64, gq_r[:, :].unsqueeze(1).broadcast_to([128, 4, 64]), gq_r.b,
                      xqr[i][:, :, :], xqr[i].b, tmp, sq, ss, rstd, rope=rope)
            norm_rope(g, kvin[i][:, :, 0:128], kvin[i].b, 4, 128, gk_n[:, :].unsqueeze(1).broadcast_to([128, 4, 128]), gk_n.b,
                      xkn[i][:, :, :], xkn[i].b, tmp, sq, ss, rstd)
            norm_rope(g, krin[i][:, :, :], krin[i].b, 1, 64, gk_r[:, :].unsqueeze(1), gk_r.b,
                      xkr[i][:, :, :], xkr[i].b, tmp, sq, ss, rstd, rope=rope)
            transpose_slots(g, xqn[i], [0, 1, 2, 3], 128, tqn[i], 0)
            transpose_slots(g, xqr[i], [0, 1, 2, 3], 64, tqr[i], 0)
            transpose_slots(g, xkn[i], [0, 1, 2, 3], 128, tkn[i], 0)
            transpose_slots(g, xkr[i], [0], 64, tkr[i], 0)
            cols = slice(ti * 128, (ti + 1) * 128)
            S.dma("pool", g.MQN[:, :, cols].rearrange("s d t -> d s t"), tqn[i][:, :, :], reads=[tqn[i].b])
            S.dma("pool", g.MQR[:, :, cols].rearrange("s d t -> d s t"), tqr[i][:, :, :], reads=[tqr[i].b])
            S.dma("pool", g.MKN[:, :, cols].rearrange("s d t -> d s t"), tkn[i][:, :, :], reads=[tkn[i].b])
            S.dma("pool", g.MKR[:, :, cols].rearrange("s d t -> d s t"), tkr[i][:, :, :], reads=[tkr[i].b])
    S.barrier()
    heads = []
    for h in range(4):
        heads.append(dict(kparts=[(g.MKN[h], 128), (g.MKR[0], 64)], qparts=[[(g.MQN[h], 128), (g.MQR[h], 64)]],
                          v=g.KVML[:, h * 256 + 128:h * 256 + 256], ycols=[1536 + h * 128], blocks=dense_blocks(g)))
    attention(g, heads, 192 ** -0.5)


RQ_R, RQ_LW, RQ_K, RQ_V, RQ_KK, RQ_KKA = 0, 1, 2, 3, 4, 5
NRW = 2176


def stage_rwkv_prep(g, l):
    S = g.S
    NT, NTC = g.NT, g.NTC
    with ExitStack() as st:
        MU = [sb(g, st, "rmu", [128, NRW]) for _ in range(2)]
        w0 = [sb(g, st, "rw0", [128, 512]) for _ in range(2)]
        a0 = [sb(g, st, "ra0", [128, 512]) for _ in range(2)]
        w2 = [sb(g, st, "rw2", [96, 512]) for _ in range(2)]
        a2 = [sb(g, st, "ra2", [96, 512]) for _ in range(2)]
        g2 = sb(g, st, "rg2", [128, 2, 512])
        kk_ = sb(g, st, "rkk", [128, 512])
        ka = sb(g, st, "rka", [128, 512])
        omka = sb(g, st, "romka", [128, 512])
        for d in range(2):
            S.op("dve", lambda e: e.memset(MU[d][:, :], 0.0), writes=[MU[d].b])
            bcast_row(g, "sp", MU[d][:, 0:1792], MU[d].b, g.rwkv_mu[l, d, 0:1792])
            bcast_row(g, "sp", MU[d][:, 1792 + d * 96:1792 + (d + 1) * 96], MU[d].b, g.rwkv_mu[l, d, 1792:1888])
            bcast_row(g, "sp", MU[d][:, 1984 + d * 96:1984 + (d + 1) * 96], MU[d].b, g.rwkv_mu[l, d, 1888:1984])
            bcast_row(g, "sp", w0[d][:, :], w0[d].b, g.rwkv_w0[l, d, :])
            bcast_row(g, "sp", a0[d][:, :], a0[d].b, g.rwkv_a0[l, d, :])
            S.dma("sp", w2[d][:, :], g.rwkv_w2[l, d], writes=[w2[d].b])
            S.dma("sp", a2[d][:, :], g.rwkv_a2[l, d], writes=[a2[d].b])
        S.dma("sp", g2[:, :, :], g.rwkv_g2[l].rearrange("(kc p) n -> p kc n", p=128), writes=[g2.b])
        bcast_row(g, "sp", kk_[:, :], kk_.b, g.rwkv_k_k[l, :])
        bcast_row(g, "sp", ka[:, :], ka.b, g.rwkv_k_a[l, :])
        S.op("dve", lambda e: e.tensor_scalar(out=omka[:, :], in0=ka[:, :], scalar1=-1.0, scalar2=1.0, op0=ALU.mult, op1=ALU.add),
             reads=[ka.b], writes=[omka.b])
        zc = [sb(g, st, "rzc", [128, NRW]) for _ in range(2)]
        zs = [sb(g, st, "rzs", [128, NRW]) for _ in range(2)]
        u = sb(g, st, "ru", [128, NRW])
        tw = sb(g, st, "rtw", [128, 96])
        sg = sb(g, st, "rsg", [128, 256])
        tT = sb(g, st, "rtT", [128, 4, 128])
        aa = sb(g, st, "raa", [128, 512])
        t1 = sb(g, st, "rt1", [128, 512])
        t2 = sb(g, st, "rt2", [128, 512])
        ss = sb(g, st, "rss", [128, 8])
        outs = [[sb(g, st, "ro", [128, 512]) for _ in range(2)] for _ in range(5)]
        for ti in range(NT):
            z = zc[ti % 2]
            r0 = ti * 128
            S.dma("sp", z[:, :], g.Z[r0:r0 + 128, O_RKVG:O_RKVG + NRW], writes=[z.b])
            for d in range(2):
                zz = zs[d]
                if d == 0:
                    S.dma("sp", zz[1:128, :], g.Z[r0:r0 + 127, O_RKVG:O_RKVG + NRW], writes=[zz.b])
                    src = g.zrow[0:1, :] if ti in (0, NTC) else g.Z[r0 - 1:r0, O_RKVG:O_RKVG + NRW]
                    S.dma("sp", zz[0:1, :], src, writes=[zz.b])
                else:
                    S.dma("sp", zz[0:127, :], g.Z[r0 + 1:r0 + 128, O_RKVG:O_RKVG + NRW], writes=[zz.b])
                    src = g.zrow[0:1, :] if ti in (NTC - 1, NT - 1) else g.Z[r0 + 128:r0 + 129, O_RKVG:O_RKVG + NRW]
                    S.dma("sp", zz[127:128, :], src, writes=[zz.b])
                S.op("dve", lambda e: e.tensor_tensor(out=u[:, :], in0=zz[:, :], in1=z[:, :], op=ALU.subtract), reads=[zz.b, z.b], writes=[u.b])
                S.op("dve", lambda e: e.tensor_tensor(out=u[:, :], in0=u[:, :], in1=MU[d][:, :], op=ALU.mult), reads=[u.b, MU[d].b], writes=[u.b])
                S.op("dve", lambda e: e.tensor_tensor(out=u[:, :], in0=u[:, :], in1=z[:, :], op=ALU.add), reads=[u.b, z.b], writes=[u.b])
                ur, uk, uv, ugd = u[:, 0:512], u[:, 512:1024], u[:, 1024:1536], u[:, 1536:1792]
                uwd = u[:, 1792 + d * 96:1792 + (d + 1) * 96]
                uad = u[:, 1984 + d * 96:1984 + (d + 1) * 96]
                o_lw, o_k, o_kk, o_kka, o_g = [outs[i][(2 * ti + d) % 2] for i in range(5)]
                rows = slice(r0, r0 + 128)
                S.dma("pool", g.RW[d, RQ_R, rows, :], ur, reads=[u.b])
                S.dma("pool", g.RW[d, RQ_V, rows, :], uv, reads=[u.b])
                S.op("act", lambda e: e.activation(out=tw[:, :], in_=uwd, func=AF.Tanh), reads=[u.b], writes=[tw.b])
                S.op("act", lambda e: e.activation(out=sg[:, :], in_=ugd, func=AF.Sigmoid), reads=[u.b], writes=[sg.b])
                ps = next_psum(g)
                S.op("pe", lambda e: e.transpose(ps[0:96, 0:128], tw[:, :], g.ident[:]), reads=[tw.b, g.ident.b], writes=[ps.b])
                S.op("pe", lambda e: e.transpose(ps[0:96, 128:256], uad, g.ident[:]), reads=[u.b, g.ident.b], writes=[ps.b])
                copy_op(g, "act", tT[0:96, 0:2, :], ps[0:96, 0:256].rearrange("p (k t) -> p k t", k=2), [ps.b], [tT.b])
                ps = next_psum(g)
                S.op("pe", lambda e: e.transpose(ps[:, 0:128], sg[:, 0:128], g.ident[:]), reads=[sg.b, g.ident.b], writes=[ps.b])
                S.op("pe", lambda e: e.transpose(ps[:, 128:256], sg[:, 128:256], g.ident[:]), reads=[sg.b, g.ident.b], writes=[ps.b])
                copy_op(g, "dve", tT[:, 2:4, :], ps[:, 0:256].rearrange("p (k t) -> p k t", k=2), [ps.b], [tT.b])
                psw = next_psum(g)
                S.op("pe", lambda e: e.matmul(psw[:, :], lhsT=tT[0:96, 0, :], rhs=w2[d][:, :], start=True, stop=True), reads=[tT.b, w2[d].b], writes=[psw.b])
                S.op("dve", lambda e: e.tensor_tensor(out=t1[:, :], in0=psw[:, :], in1=w0[d][:, :], op=ALU.add), reads=[psw.b, w0[d].b], writes=[t1.b])
                S.op("act", lambda e: e.activation(out=t1[:, :], in_=t1[:, :], func=AF.Sigmoid), reads=[t1.b], writes=[t1.b])
                S.op("act", lambda e: e.mul(out=o_lw[:, :], in_=t1[:, :], mul=-0.6065306597126334), reads=[t1.b], writes=[o_lw.b])
                S.dma("pool", g.RW[d, RQ_LW, rows, :], o_lw[:, :], reads=[o_lw.b])
                psa = next_psum(g)
                S.op("pe", lambda e: e.matmul(psa[:, :], lhsT=tT[0:96, 1, :], rhs=a2[d][:, :], start=True, stop=True), reads=[tT.b, a2[d].b], writes=[psa.b])
                S.op("dve", lambda e: e.tensor_tensor(out=aa[:, :], in0=psa[:, :], in1=a0[d][:, :], op=ALU.add), reads=[psa.b, a0[d].b], writes=[aa.b])
                S.op("act", lambda e: e.activation(out=aa[:, :], in_=aa[:, :], func=AF.Sigmoid), reads=[aa.b], writes=[aa.b])
                psg = next_psum(g)
                for kc in range(2):
                    S.op("pe", lambda e: e.matmul(psg[:, :], lhsT=tT[:, 2 + kc, :], rhs=g2[:, kc, :], start=(kc == 0), stop=(kc == 1)),
                         reads=[tT.b, g2.b], writes=[psg.b])
                copy_op(g, "act", o_g[:, :], psg[:, :], [psg.b], [o_g.b])
                S.dma("pool", g.RG[d, rows, :], o_g[:, :], reads=[o_g.b])
                S.op("dve", lambda e: e.tensor_tensor(out=t1[:, :], in0=uk, in1=kk_[:, :], op=ALU.mult), reads=[u.b, kk_.b], writes=[t1.b])
                S.op("act", lambda e: e.activation(out=t2[:, :], in_=t1[:, :], func=AF.Square), reads=[t1.b], writes=[t2.b])
                S.op("dve", lambda e: e.tensor_reduce(out=ss[:, 0:8], in_=t2[:, :].rearrange("p (h k) -> p h k", h=8), axis=AX.X, op=ALU.add),
                     reads=[t2.b], writes=[ss.b])
                S.op("act", lambda e: e.activation(out=ss[:, 0:8], in_=ss[:, 0:8], func=AF.Sqrt), reads=[ss.b], writes=[ss.b])
                S.op("dve", lambda e: e.tensor_scalar(out=ss[:, 0:8], in0=ss[:, 0:8], scalar1=1e-12, scalar2=None, op0=ALU.max), reads=[ss.b], writes=[ss.b])
                S.op("dve", lambda e: e.reciprocal(out=ss[:, 0:8], in_=ss[:, 0:8]), reads=[ss.b], writes=[ss.b])
                S.op("dve", lambda e: e.tensor_tensor(out=o_kk[:, :].rearrange("p (h k) -> p h k", h=8), in0=t1[:, :].rearrange("p (h k) -> p h k", h=8),
                                                      in1=ss[:, 0:8].unsqueeze(2).broadcast_to([128, 8, 64]), op=ALU.mult),
                     reads=[t1.b, ss.b], writes=[o_kk.b])
                S.dma("pool", g.RW[d, RQ_KK, rows, :], o_kk[:, :], reads=[o_kk.b])
                S.op("dve", lambda e: e.tensor_tensor(out=o_kka[:, :], in0=o_kk[:, :], in1=aa[:, :], op=ALU.mult), reads=[o_kk.b, aa.b], writes=[o_kka.b])
                S.dma("pool", g.RW[d, RQ_KKA, rows, :], o_kka[:, :], reads=[o_kka.b])
                S.op("dve", lambda e: e.tensor_tensor(out=t2[:, :], in0=aa[:, :], in1=ka[:, :], op=ALU.mult), reads=[aa.b, ka.b], writes=[t2.b])
                S.op("dve", lambda e: e.tensor_tensor(out=t2[:, :], in0=t2[:, :], in1=omka[:, :], op=ALU.add), reads=[t2.b, omka.b], writes=[t2.b])
                S.op("dve", lambda e: e.tensor_tensor(out=o_k[:, :], in0=t2[:, :], in1=uk, op=ALU.mult), reads=[t2.b, u.b], writes=[o_k.b])
                S.dma("pool", g.RW[d, RQ_K, rows, :], o_k[:, :], reads=[o_k.b])
    S.barrier()


def stage_rwkv_scan(g, l):
    S = g.S
    C = 64
    nch = g.T // C
    nch_c = g.LC // C
    ident64 = g.ident[0:64, 0:64]

    def mm(ps_ap, psb, lhsT, rhs, reads, start=True, stop=True):
        S.op("pe", lambda e: e.matmul(ps_ap, lhsT=lhsT, rhs=rhs, start=start, stop=stop), reads=reads, writes=[psb])

    with ExitStack() as st:
        ST = [sb(g, st, "sST", [64, 8, 64]) for _ in range(2)]
        for d in range(2):
            S.op("dve", lambda e: e.memset(ST[d][:, :, :], 0.0), writes=[ST[d].b])
        qin = [[sb(g, st, "sq", [64, 512]) for _ in range(6)] for _ in range(2)]
        v2 = [sb(g, st, "sv2", [128, 512]) for _ in range(2)]
        names = ["Lsb", "eL", "enL", "eLx", "eEnd", "Rt", "Kt", "Kk", "Ak", "Kh", "Ah"]
        W = {n: sb(g, st, "s" + n, [64, 512]) for n in names}
        pCT = sb(g, st, "spCT", [64, 8])
        XT1 = sb(g, st, "sXT1", [64, 8, 128])
        XT2 = sb(g, st, "sXT2", [64, 8, 128])
        PRs = sb(g, st, "sPRs", [128, 8, 128])
        Nn = [sb(g, st, "sN", [64, 8, 64]) for _ in range(2)]
        NTt = [sb(g, st, "sNT", [64, 8, 64]) for _ in range(2)]
        Qq = [sb(g, st, "sQ", [64, 8, 64]) for _ in range(2)]
        W1T = sb(g, st, "sW1T", [64, 8, 64])
        MV = sb(g, st, "sMV", [64, 8, 64])
        W2 = sb(g, st, "sW2", [64, 8, 64])
        Y0 = sb(g, st, "sY0", [64, 8, 64])
        D0 = sb(g, st, "sD0", [64, 8, 64])
        U = sb(g, st, "sU", [64, 8, 64])
        Yo = [sb(g, st, "sYo", [64, 512]) for _ in range(2)]
        tmp = sb(g, st, "stmp", [64, 8, 64])
        it = 0
        for d in range(2):
            if d == 0:
                order = list(range(nch))
            else:
                order = list(range(nch_c - 1, -1, -1)) + list(range(nch - 1, nch_c - 1, -1))
            tri = g.tri[0:64, d, :]
            mk = g.mk[:, d, :]
            nmts = g.nmts[0:64, d, :]
            for c in order:
                it += 1
                q = qin[it % 2]
                vv = v2[it % 2]
                rows = slice(c * C, (c + 1) * C)
                for qi in range(6):
                    S.dma("sp", q[qi][:, :], g.RW[d, qi, rows, :], writes=[q[qi].b])
                S.dma("sp", vv[64:128, :], g.RW[d, RQ_V, rows, :], writes=[vv.b])
                r_, lw, k_, v_, kk, kka = q
                psL = next_psum(g)
                mm(psL[0:64, :], psL.b, tri, lw[:, :], [g.tri.b, lw.b])
                psE = next_psum(g)
                mm(psE[0:64, :], psE.b, g.ones64[0:64, :], lw[:, :], [g.ones64.b, lw.b])
                psP = next_psum(g)
                for h in range(8):
                    mm(psP[0:64, h:h + 1], psP.b, lw[:, h * 64:(h + 1) * 64], g.ones64[0:64, 0:1], [lw.b, g.ones64.b])
                S.op("act", lambda e: e.activation(out=pCT[:, :], in_=psP[0:64, 0:8], func=AF.Exp), reads=[psP.b], writes=[pCT.b])
                S.op("act", lambda e: e.activation(out=W["eL"][:, :], in_=psL[0:64, :], func=AF.Exp), reads=[psL.b], writes=[W["eL"].b])
                S.op("act", lambda e: e.activation(out=W["enL"][:, :], in_=psL[0:64, :], func=AF.Exp, scale=-1.0), reads=[psL.b], writes=[W["enL"].b])
                S.op("dve", lambda e: e.tensor_tensor(out=W["eLx"][:, :], in0=psL[0:64, :], in1=lw[:, :], op=ALU.subtract), reads=[psL.b, lw.b], writes=[W["eLx"].b])
                S.op("act", lambda e: e.activation(out=W["eLx"][:, :], in_=W["eLx"][:, :], func=AF.Exp), reads=[W["eLx"].b], writes=[W["eLx"].b])
                copy_op(g, "dve", W["Lsb"][:, :], psL[0:64, :], [psL.b], [W["Lsb"].b])
                S.op("dve", lambda e: e.tensor_tensor(out=W["eEnd"][:, :], in0=psE[0:64, :], in1=W["Lsb"][:, :], op=ALU.subtract),
                     reads=[psE.b, W["Lsb"].b], writes=[W["eEnd"].b])
                S.op("act", lambda e: e.activation(out=W["eEnd"][:, :], in_=W["eEnd"][:, :], func=AF.Exp), reads=[W["eEnd"].b], writes=[W["eEnd"].b])
                for (o, a, b) in [("Rt", r_, "eL"), ("Kt", kk, "eLx"), ("Kk", k_, "enL"), ("Ak", kka, "enL"), ("Kh", k_, "eEnd"), ("Ah", kka, "eEnd")]:
                    S.op("dve", lambda e: e.tensor_tensor(out=W[o][:, :], in0=a[:, :], in1=W[b][:, :], op=ALU.mult), reads=[a.b, W[b].b], writes=[W[o].b])
                for (XT, n0, n1) in [(XT1, "Ak", "Kk"), (XT2, "Kt", "Rt")]:
                    for hg in range(2):
                        ps = next_psum(g)
                        for hh in range(4):
                            h = hg * 4 + hh
                            for j, nm in enumerate([n0, n1]):
                                S.op("pe", lambda e: e.transpose(ps[0:64, hh * 128 + j * 64:hh * 128 + (j + 1) * 64], W[nm][:, h * 64:(h + 1) * 64], ident64),
                                     reads=[W[nm].b, g.ident.b], writes=[ps.b])
                        copy_op(g, evac_engine(g), XT[:, hg * 4:(hg + 1) * 4, :], ps[0:64, :].rearrange("p (h x) -> p h x", h=4), [ps.b], [XT.b])
                for hg in range(2):
                    ps = next_psum(g)
                    for hh in range(4):
                        h = hg * 4 + hh
                        mm(ps[:, hh * 128:(hh + 1) * 128], ps.b, XT1[:, h, :], XT2[:, h, :], [XT1.b, XT2.b])
                    S.op("dve", lambda e: e.tensor_tensor(out=PRs[:, hg * 4:(hg + 1) * 4, :], in0=ps[:, :].rearrange("p (h x) -> p h x", h=4),
                                                          in1=mk.unsqueeze(1).broadcast_to([128, 4, 128]), op=ALU.mult),
                         reads=[ps.b, g.mk.b], writes=[PRs.b])
                ps = next_psum(g)
                for h in range(8):
                    mm(ps[0:64, h * 64:(h + 1) * 64], ps.b, XT2[:, h, 0:64], XT1[:, h, 0:64], [XT1.b, XT2.b])
                N, NT_, Q = Nn[0], NTt[0], Qq[0]
                S.op("dve", lambda e: e.tensor_tensor(out=NT_[:, :, :], in0=ps[0:64, :].rearrange("p (h x) -> p h x", h=8),
                                                      in1=nmts.unsqueeze(1).broadcast_to([64, 8, 64]), op=ALU.mult),
                     reads=[ps.b, g.nmts.b], writes=[NT_.b])
                S.op("dve", lambda e: e.tensor_scalar(out=N[:, :, :], in0=PRs[0:64, :, 0:64], scalar1=-1.0, scalar2=None, op0=ALU.mult), reads=[PRs.b], writes=[N.b])
                S.op("dve", lambda e: e.tensor_tensor(out=Q[:, :, :], in0=N[:, :, :], in1=ident64.unsqueeze(1).broadcast_to([64, 8, 64]), op=ALU.add),
                     reads=[N.b, g.ident.b], writes=[Q.b])
                cur = 0
                for lev in range(5):
                    N, NT_, Q = Nn[cur], NTt[cur], Qq[cur]
                    N2, NT2, Q2 = Nn[1 - cur], NTt[1 - cur], Qq[1 - cur]
                    psn = next_psum(g)
                    pst = next_psum(g)
                    for h in range(8):
                        if lev < 4:
                            mm(psn[0:64, h * 64:(h + 1) * 64], psn.b, NT_[:, h, :], N[:, h, :], [NT_.b, N.b])
                        mm(pst[0:64, h * 64:(h + 1) * 64], pst.b, N[:, h, :], NT_[:, h, :], [NT_.b, N.b])
                    if lev < 4:
                        copy_op(g, "act", N2[:, :, :], psn[0:64, :].rearrange("p (h x) -> p h x", h=8), [psn.b], [N2.b])
                    copy_op(g, "dve", NT2[:, :, :], pst[0:64, :].rearrange("p (h x) -> p h x", h=8), [pst.b], [NT2.b])
                    psq = next_psum(g)
                    for h in range(8):
                        mm(psq[0:64, h * 64:(h + 1) * 64], psq.b, NT2[:, h, :], Q[:, h, :], [NT2.b, Q.b])
                    S.op("dve", lambda e: e.tensor_tensor(out=Q2[:, :, :], in0=psq[0:64, :].rearrange("p (h x) -> p h x", h=8), in1=Q[:, :, :], op=ALU.add),
                         reads=[psq.b, Q.b], writes=[Q2.b])
                    cur = 1 - cur
                Q = Qq[cur]
                ps1 = next_psum(g)
                ps2 = next_psum(g)
                ps3 = next_psum(g)
                ps4 = next_psum(g)
                for h in range(8):
                    hs = slice(h * 64, (h + 1) * 64)
                    mm(ps1[0:64, hs], ps1.b, W["Kt"][:, hs], Q[:, h, :], [W["Kt"].b, Q.b])
                    mm(ps2[0:64, hs], ps2.b, PRs[64:128, h, 0:64], vv[64:128, hs], [PRs.b, vv.b])
                    mm(ps3[0:64, hs], ps3.b, PRs[64:128, h, 64:128], vv[64:128, hs], [PRs.b, vv.b])
                    mm(ps4[0:64, hs], ps4.b, W["Kh"][:, hs], v_[:, hs], [W["Kh"].b, v_.b])
                copy_op(g, "act", W1T[:, :, :], ps1[0:64, :].rearrange("p (h x) -> p h x", h=8), [ps1.b], [W1T.b])
                copy_op(g, "dve", MV[:, :, :], ps2[0:64, :].rearrange("p (h x) -> p h x", h=8), [ps2.b], [MV.b])
                copy_op(g, "act", Y0[:, :, :], ps3[0:64, :].rearrange("p (h x) -> p h x", h=8), [ps3.b], [Y0.b])
                copy_op(g, "dve", D0[:, :, :], ps4[0:64, :].rearrange("p (h x) -> p h x", h=8), [ps4.b], [D0.b])
                ps5 = next_psum(g)
                for h in range(8):
                    mm(ps5[0:64, h * 64:(h + 1) * 64], ps5.b, Q[:, h, :], MV[:, h, :], [Q.b, MV.b])
                copy_op(g, "act", W2[:, :, :], ps5[0:64, :].rearrange("p (h x) -> p h x", h=8), [ps5.b], [W2.b])
                st_ = ST[d]
                psu = next_psum(g)
                for h in range(8):
                    mm(psu[0:64, h * 64:(h + 1) * 64], psu.b, W1T[:, h, :], st_[:, h, :], [W1T.b, st_.b])
                S.op("dve", lambda e: e.scalar_tensor_tensor(out=U[:, :, :], in0=psu[0:64, :].rearrange("p (h x) -> p h x", h=8), scalar=-1.0,
                                                             in1=W2[:, :, :], op0=ALU.mult, op1=ALU.subtract),
                     reads=[psu.b, W2.b], writes=[U.b])
                psy = next_psum(g)
                for h in range(8):
                    hs = slice(h * 64, (h + 1) * 64)
                    mm(psy[0:64, hs], psy.b, XT2[:, h, 64:128], st_[:, h, :], [XT2.b, st_.b], start=True, stop=False)
                    mm(psy[0:64, hs], psy.b, PRs[0:64, h, 64:128], U[:, h, :], [PRs.b, U.b], start=False, stop=True)
                yo = Yo[it % 2]
                S.op("dve", lambda e: e.tensor_tensor(out=yo[:, :], in0=psy[0:64, :], in1=Y0[:, :, :].rearrange("p h x -> p (h x)"), op=ALU.add),
                     reads=[psy.b, Y0.b], writes=[yo.b])
                S.dma("pool", g.YR[d, rows, :], yo[:, :], reads=[yo.b])
                psd = next_psum(g)
                for h in range(8):
                    hs = slice(h * 64, (h + 1) * 64)
                    mm(psd[0:64, hs], psd.b, W["Ah"][:, hs], U[:, h, :], [W["Ah"].b, U.b])
                S.op("dve", lambda e: e.tensor_tensor(out=tmp[:, :, :], in0=st_[:, :, :], in1=pCT[:, 0:8].unsqueeze(2).broadcast_to([64, 8, 64]), op=ALU.mult),
                     reads=[st_.b, pCT.b], writes=[tmp.b])
                S.op("dve", lambda e: e.tensor_tensor(out=tmp[:, :, :], in0=tmp[:, :, :], in1=D0[:, :, :], op=ALU.add), reads=[tmp.b, D0.b], writes=[tmp.b])
                S.op("dve", lambda e: e.tensor_tensor(out=st_[:, :, :], in0=psd[0:64, :].rearrange("p (h x) -> p h x", h=8), in1=tmp[:, :, :], op=ALU.add),
                     reads=[psd.b, tmp.b], writes=[st_.b])
    S.barrier()


def stage_rwkv_out(g, l):
    S = g.S
    with ExitStack() as st:
        lw_ = sb(g, st, "olw", [128, 512])
        lb_ = sb(g, st, "olb", [128, 512])
        rk_ = sb(g, st, "ork", [128, 512])
        bcast_row(g, "sp", lw_[:, :], lw_.b, g.rwkv_lnx_w[l, :])
        bcast_row(g, "sp", lb_[:, :], lb_.b, g.rwkv_lnx_b[l, :])
        bcast_row(g, "sp", rk_[:, :], rk_.b, g.rwkv_r_k[l].rearrange("h k -> (h k)"))
        ins = [[sb(g, st, "oin", [128, 512]) for _ in range(5)] for _ in range(2)]
        t1 = sb(g, st, "ot1", [128, 512])
        t2 = sb(g, st, "ot2", [128, 512])
        acc = [sb(g, st, "oacc", [128, 512]) for _ in range(2)]
        ss = sb(g, st, "oss", [128, 8])
        s2 = sb(g, st, "os2", [128, 8])
        h8 = lambda ap: ap.rearrange("p (h k) -> p h k", h=8)
        b8 = lambda t_: t_[:, 0:8].unsqueeze(2).broadcast_to([128, 8, 64])
        it = 0
        for ti in range(g.NT):
            rows = slice(ti * 128, (ti + 1) * 128)
            a_ = acc[ti % 2]
            for d in range(2):
                it += 1
                y, r_, k_, v_, g_ = ins[it % 2]
                S.dma("sp", y[:, :], g.YR[d, rows, :], writes=[y.b])
                S.dma("sp", r_[:, :], g.RW[d, RQ_R, rows, :], writes=[r_.b])
                S.dma("sp", k_[:, :], g.RW[d, RQ_K, rows, :], writes=[k_.b])
                S.dma("sp", v_[:, :], g.RW[d, RQ_V, rows, :], writes=[v_.b])
                S.dma("sp", g_[:, :], g.RG[d, rows, :], writes=[g_.b])
                S.op("dve", lambda e: e.tensor_reduce(out=ss[:, 0:8], in_=h8(y[:, :]), axis=AX.X, op=ALU.add), reads=[y.b], writes=[ss.b])
                S.op("dve", lambda e: e.tensor_scalar(out=ss[:, 0:8], in0=ss[:, 0:8], scalar1=-1.0 / 64, scalar2=None, op0=ALU.mult), reads=[ss.b], writes=[ss.b])
                S.op("dve", lambda e: e.tensor_tensor(out=h8(t1[:, :]), in0=h8(y[:, :]), in1=b8(ss), op=ALU.add), reads=[y.b, ss.b], writes=[t1.b])
                S.op("act", lambda e: e.activation(out=t2[:, :], in_=t1[:, :], func=AF.Square), reads=[t1.b], writes=[t2.b])
                S.op("dve", lambda e: e.tensor_reduce(out=s2[:, 0:8], in_=h8(t2[:, :]), axis=AX.X, op=ALU.add), reads=[t2.b], writes=[s2.b])
                S.op("dve", lambda e: e.tensor_scalar(out=s2[:, 0:8], in0=s2[:, 0:8], scalar1=1.0 / 64, scalar2=LNX_EPS, op0=ALU.mult, op1=ALU.add),
                     reads=[s2.b], writes=[s2.b])
                S.op("act", lambda e: e.activation(out=s2[:, 0:8], in_=s2[:, 0:8], func=AF.Sqrt), reads=[s2.b], writes=[s2.b])
                S.op("dve", lambda e: e.reciprocal(out=s2[:, 0:8], in_=s2[:, 0:8]), reads=[s2.b], writes=[s2.b])
                S.op("dve", lambda e: e.tensor_tensor(out=h8(t1[:, :]), in0=h8(t1[:, :]), in1=b8(s2), op=ALU.mult), reads=[t1.b, s2.b], writes=[t1.b])
                S.op("dve", lambda e: e.tensor_tensor(out=t1[:, :], in0=t1[:, :], in1=lw_[:, :], op=ALU.mult), reads=[t1.b, lw_.b], writes=[t1.b])
                S.op("dve", lambda e: e.tensor_tensor(out=t1[:, :], in0=t1[:, :], in1=lb_[:, :], op=ALU.add), reads=[t1.b, lb_.b], writes=[t1.b])
                S.op("dve", lambda e: e.tensor_tensor(out=t2[:, :], in0=r_[:, :], in1=k_[:, :], op=ALU.mult), reads=[r_.b, k_.b], writes=[t2.b])
                S.op("dve", lambda e: e.tensor_tensor(out=t2[:, :], in0=t2[:, :], in1=rk_[:, :], op=ALU.mult), reads=[t2.b, rk_.b], writes=[t2.b])
                S.op("dve", lambda e: e.tensor_reduce(out=ss[:, 0:8], in_=h8(t2[:, :]), axis=AX.X, op=ALU.add), reads=[t2.b], writes=[ss.b])
                S.op("dve", lambda e: e.tensor_tensor(out=h8(t2[:, :]), in0=h8(v_[:, :]), in1=b8(ss), op=ALU.mult), reads=[v_.b, ss.b], writes=[t2.b])
                S.op("dve", lambda e: e.tensor_tensor(out=t1[:, :], in0=t1[:, :], in1=t2[:, :], op=ALU.add), reads=[t1.b, t2.b], writes=[t1.b])
                if d == 0:
                    S.op("dve", lambda e: e.tensor_tensor(out=a_[:, :], in0=t1[:, :], in1=g_[:, :], op=ALU.mult), reads=[t1.b, g_.b], writes=[a_.b])
                else:
                    S.op("dve", lambda e: e.tensor_tensor(out=t1[:, :], in0=t1[:, :], in1=g_[:, :], op=ALU.mult), reads=[t1.b, g_.b], writes=[t1.b])
                    S.op("dve", lambda e: e.tensor_tensor(out=a_[:, :], in0=a_[:, :], in1=t1[:, :], op=ALU.add), reads=[a_.b, t1.b], writes=[a_.b])
            S.dma("pool", g.YMIX[rows, 1024:1536], a_[:, :], reads=[a_.b])
    S.barrier()


def stage_merge(g, l):
    S = g.S
    with ExitStack() as st:
        gt = [[sb(g, st, "mgt", [128, 512]) for _ in range(4)] for _ in range(2)]
        t1 = sb(g, st, "mt1", [128, 512])
        cnt = [0]

        def epi(pss, o, ti, n0, nb):
            gs = gt[cnt[0] % 2]
            cnt[0] += 1
            for i in range(4):
                c0 = O_GATE + i * D + n0
                S.dma("sp", gs[i][:, 0:nb], g.Z[ti * 128:(ti + 1) * 128, c0:c0 + nb], writes=[gs[i].b])
                S.op("act", lambda e: e.activation(out=gs[i][:, 0:nb], in_=gs[i][:, 0:nb], func=AF.Sigmoid), reads=[gs[i].b], writes=[gs[i].b])
            S.op("dve", lambda e: e.tensor_tensor(out=o[:, 0:nb], in0=pss[0][:, 0:nb], in1=gs[0][:, 0:nb], op=ALU.mult), reads=[pss[0].b, gs[0].b], writes=[o.b])
            for i in range(1, 4):
                S.op("dve", lambda e: e.tensor_tensor(out=t1[:, 0:nb], in0=pss[i][:, 0:nb], in1=gs[i][:, 0:nb], op=ALU.mult), reads=[pss[i].b, gs[i].b], writes=[t1.b])
                S.op("dve", lambda e: e.tensor_tensor(out=o[:, 0:nb], in0=o[:, 0:nb], in1=t1[:, 0:nb], op=ALU.add), reads=[o.b, t1.b], writes=[o.b])
            S.dma("pool", g.MERGED[ti * 128:(ti + 1) * 128, n0:n0 + nb], o[:, 0:nb], reads=[o.b])
        linear(g, g.YMIX, g.w_branch[l].rearrange("b k n -> (b k) n"), None, g.T, D, D, G=8, NB=512, epi=epi,
               segs=[(0, 4), (4, 8), (8, 12), (12, 16)])


def stage_conv(g, l):
    S = g.S
    NT, NTC = g.NT, g.NTC
    with ExitStack() as st:
        cw = [sb(g, st, "ccw", [128, 3, 512]) for _ in range(2)]
        cbias = [sb(g, st, "ccb", [128, 512]) for _ in range(2)]
        ins = [[sb(g, st, "cin", [128, 512]) for _ in range(4)] for _ in range(2)]
        acc = sb(g, st, "cacc", [128, 512])
        t1 = sb(g, st, "ct1", [128, 512])
        outs = [sb(g, st, "cout", [128, 512]) for _ in range(2)]
        it = 0
        for bi, n0 in enumerate(range(0, DFF, 512)):
            w_ = cw[bi % 2]
            b_ = cbias[bi % 2]
            for j in range(3):
                bcast_row(g, "sp", w_[:, j, :], w_.b, g.ffn_conv_w[l, j, n0:n0 + 512])
            bcast_row(g, "sp", b_[:, :], b_.b, g.ffn_conv_b[l, n0:n0 + 512])
            for ti in range(NT):
                it += 1
                ap_, ac_, an_, bb = ins[it % 2]
                r0 = ti * 128
                S.dma("sp", ac_[:, :], g.AB[r0:r0 + 128, n0:n0 + 512], writes=[ac_.b])
                S.dma("sp", bb[:, :], g.AB[r0:r0 + 128, DFF + n0:DFF + n0 + 512], writes=[bb.b])
                S.dma("sp", ap_[1:128, :], g.AB[r0:r0 + 127, n0:n0 + 512], writes=[ap_.b])
                src = g.zrow[0:1, 0:512] if ti in (0, NTC) else g.AB[r0 - 1:r0, n0:n0 + 512]
                S.dma("sp", ap_[0:1, :], src, writes=[ap_.b])
                S.dma("sp", an_[0:127, :], g.AB[r0 + 1:r0 + 128, n0:n0 + 512], writes=[an_.b])
                src = g.zrow[0:1, 0:512] if ti in (NTC - 1, NT - 1) else g.AB[r0 + 128:r0 + 129, n0:n0 + 512]
                S.dma("sp", an_[127:128, :], src, writes=[an_.b])
                S.op("dve", lambda e: e.tensor_tensor(out=acc[:, :], in0=ap_[:, :], in1=w_[:, 0, :], op=ALU.mult), reads=[ap_.b, w_.b], writes=[acc.b])
                S.op("dve", lambda e: e.tensor_tensor(out=t1[:, :], in0=ac_[:, :], in1=w_[:, 1, :], op=ALU.mult), reads=[ac_.b, w_.b], writes=[t1.b])
                S.op("dve", lambda e: e.tensor_tensor(out=acc[:, :], in0=acc[:, :], in1=t1[:, :], op=ALU.add), reads=[acc.b, t1.b], writes=[acc.b])
                S.op("dve", lambda e: e.tensor_tensor(out=t1[:, :], in0=an_[:, :], in1=w_[:, 2, :], op=ALU.mult), reads=[an_.b, w_.b], writes=[t1.b])
                S.op("dve", lambda e: e.tensor_tensor(out=acc[:, :], in0=acc[:, :], in1=t1[:, :], op=ALU.add), reads=[acc.b, t1.b], writes=[acc.b])
                S.op("dve", lambda e: e.tensor_tensor(out=acc[:, :], in0=acc[:, :], in1=b_[:, :], op=ALU.add), reads=[acc.b, b_.b], writes=[acc.b])
                S.op("act", lambda e: e.activation(out=t1[:, :], in_=acc[:, :], func=AF.Square), reads=[acc.b], writes=[t1.b])
                S.op("dve", lambda e: e.tensor_scalar(out=t1[:, :], in0=t1[:, :], scalar1=0.044715, scalar2=1.0, op0=ALU.mult, op1=ALU.add), reads=[t1.b], writes=[t1.b])
                S.op("dve", lambda e: e.tensor_tensor(out=t1[:, :], in0=t1[:, :], in1=acc[:, :], op=ALU.mult), reads=[t1.b, acc.b], writes=[t1.b])
                S.op("act", lambda e: e.activation(out=t1[:, :], in_=t1[:, :], func=AF.Sigmoid, scale=1.5957691216057308), reads=[t1.b], writes=[t1.b])
                S.op("dve", lambda e: e.tensor_tensor(out=t1[:, :], in0=t1[:, :], in1=acc[:, :], op=ALU.mult), reads=[t1.b, acc.b], writes=[t1.b])
                o = outs[it % 2]
                S.op("dve", lambda e: e.tensor_tensor(out=o[:, :], in0=t1[:, :], in1=bb[:, :], op=ALU.mult), reads=[t1.b, bb.b], writes=[o.b])
                S.dma("pool", g.GG[r0:r0 + 128, n0:n0 + 512], o[:, :], reads=[o.b])
    S.barrier()


def build(cfg):
    TL, LC, DEPTH = cfg["TL"], cfg["LC"], cfg["DEPTH"]
    T = TL + LC
    nc = bass.Bass("TRN2", target_bir_lowering=False)
    g = Ctx()
    g.nc = nc
    g.uid = 0
    g.ev = 0
    g.pi = 0
    g.T, g.TL, g.LC, g.NT, g.NTC = T, TL, LC, T // 128, LC // 128
    g.S = S = Sync(nc)

    def din(name, shape):
        return nc.dram_tensor(name, list(shape), F32, kind="ExternalInput").ap()

    def dscr(name, shape):
        return nc.dram_tensor(name, list(shape), F32, kind="Internal").ap()

    L = DEPTH
    g.x_in = din("x", [TL, D])
    g.ctx_in = din("ctx", [LC, D])
    g.cvec = din("cvec", [2, D])
    g.ada_w = din("ada_w", [L, D, 6 * D])
    g.ada_b = din("ada_b", [L, 6 * D])
    g.norm1_g = din("norm1_g", [L, D])
    g.norm2_g = din("norm2_g", [L, D])
    g.w_in = din("w_in", [L, D, IN_W])
    g.identd = din("ident", [128, 128])
    g.ga_q_norm = din("ga_q_norm", [L, 128])
    g.ga_k_norm = din("ga_k_norm", [L, 128])
    g.wa_q_norm = din("wa_q_norm", [L, 128])
    g.wa_k_norm = din("wa_k_norm", [L, 128])
    g.wa_sink = din("wa_sink", [L, 4])
    g.mla_cq_norm = din("mla_cq_norm", [L, 384])
    g.mla_ckv_norm = din("mla_ckv_norm", [L, 512])
    g.mla_w_uq = din("mla_w_uq", [L, 384, 768])
    g.mla_w_ukv = din("mla_w_ukv", [L, 512, 1024])
    g.mla_qn_norm = din("mla_qn_norm", [L, 128])
    g.mla_qr_norm = din("mla_qr_norm", [L, 64])
    g.mla_kn_norm = din("mla_kn_norm", [L, 128])
    g.mla_kr_norm = din("mla_kr_norm", [L, 64])
    g.rwkv_mu = din("rwkv_mu", [L, 2, 1984])
    g.rwkv_w0 = din("rwkv_w0", [L, 2, 512])
    g.rwkv_w2 = din("rwkv_w2", [L, 2, 96, 512])
    g.rwkv_a0 = din("rwkv_a0", [L, 2, 512])
    g.rwkv_a2 = din("rwkv_a2", [L, 2, 96, 512])
    g.rwkv_g2 = din("rwkv_g2", [L, 256, 512])
    g.rwkv_k_k = din("rwkv_k_k", [L, 512])
    g.rwkv_k_a = din("rwkv_k_a", [L, 512])
    g.rwkv_r_k = din("rwkv_r_k", [L, 8, 64])
    g.rwkv_lnx_w = din("rwkv_lnx_w", [L, 512])
    g.rwkv_lnx_b = din("rwkv_lnx_b", [L, 512])
    g.w_branch = din("w_branch", [L, 4, 512, D])
    g.w_out = din("w_out", [L, D, D])
    g.ffn_up = din("ffn_up", [L, D, 2 * DFF])
    g.ffn_conv_w = din("ffn_conv_w", [L, 3, DFF])
    g.ffn_conv_b = din("ffn_conv_b", [L, DFF])
    g.ffn_down = din("ffn_down", [L, DFF, D])
    g.trid = din("tri", [64, 2, 64])
    g.mkd = din("mk", [128, 2, 128])
    g.nmtsd = din("nmts", [64, 2, 64])
    g.zrow = din("zrow", [1, NRW])
    g.ropeA = din("ropeA", [2, TL, 128])
    g.ropeM = din("ropeM", [2, TL, 64])
    g.wmaskd = din("wmask", [128, 2, 128])
    g.y_out = nc.dram_tensor("y", [TL, D], F32, kind="ExternalOutput").ap()
    dbg = cfg.get("debug")
    g.XS = dscr("XS", [T, D])
    g.MODB = dscr("MODB", [2, 128, 6 * D])
    g.Z = dscr("Z", [T, IN_W])
    g.QKT = dscr("QKT", [12, 128, T])
    g.YMIX = dscr("YMIX", [T, D])
    g.RW = dscr("RW", [2, 6, T, 512])
    g.RG = dscr("RG", [2, T, 512])
    g.YR = dscr("YR", [2, T, 512])
    g.MERGED = dscr("MERGED", [T, D])
    g.AB = dscr("AB", [T, 2 * DFF])
    g.GG = dscr("GG", [T, DFF])
    g.QML = dscr("QML", [T, 768])
    g.KVML = dscr("KVML", [T, 1024])
    g.MQN = dscr("MQN", [4, 128, T])
    g.MQR = dscr("MQR", [4, 64, T])
    g.MKN = dscr("MKN", [4, 128, T])
    g.MKR = dscr("MKR", [1, 64, T])
    if dbg:
        g.dbg_z = nc.dram_tensor("dbg_z", [T, IN_W], F32, kind="ExternalOutput").ap()
        g.dbg_y = nc.dram_tensor("dbg_y", [T, D], F32, kind="ExternalOutput").ap()
        g.dbg_qkt = nc.dram_tensor("dbg_qkt", [12, 128, T], F32, kind="ExternalOutput").ap()
        g.dbg_xs = nc.dram_tensor("dbg_xs", [T, D], F32, kind="ExternalOutput").ap()

    with ExitStack() as es:
        g.psums = [Tl(es.enter_context(nc.psum_tensor("ps%d" % i, [128, 512], F32)), "ps%d" % i) for i in range(8)]
        g.ident = sb(g, es, "ident", [128, 128])
        S.dma("sp", g.ident[:, :], g.identd[:, :], writes=[g.ident.b])
        g.tri = sb(g, es, "tri", [64, 2, 64])
        g.mk = sb(g, es, "mk", [128, 2, 128])
        g.nmts = sb(g, es, "nmts", [64, 2, 64])
        g.ones64 = sb(g, es, "ones64", [64, 64])
        S.dma("sp", g.tri[:, :, :], g.trid[:, :, :], writes=[g.tri.b])
        S.dma("sp", g.mk[:, :, :], g.mkd[:, :, :], writes=[g.mk.b])
        S.dma("sp", g.nmts[:, :, :], g.nmtsd[:, :, :], writes=[g.nmts.b])
        S.op("dve", lambda e: e.memset(g.ones64[:, :], 1.0), writes=[g.ones64.b])
        g.wmask = sb(g, es, "wmask", [128, 2, 128])
        S.dma("sp", g.wmask[:, :, :], g.wmaskd[:, :, :], writes=[g.wmask.b])
        S.dma("sp", g.XS[0:LC, :], g.ctx_in[:, :])
        S.dma("sp", g.XS[LC:T, :], g.x_in[:, :])
        g.cb = [sb(g, es, "cb", [128, 16, 128]) for _ in range(2)]
        with ExitStack() as st:
            cT = [sb(g, st, "cT", [128, 16]) for _ in range(2)]
            ones = sb(g, st, "ones", [128, 128])
            S.op("dve", lambda e: e.memset(ones[:, :], 1.0), writes=[ones.b])
            for r in range(2):
                S.dma("sp", cT[r][:, :], g.cvec[r, :].rearrange("(kc p) -> p kc", p=128), writes=[cT[r].b], allow_slow_non_contiguous=True)
                S.op("act", lambda e: e.activation(out=cT[r][:, :], in_=cT[r][:, :], func=AF.Silu), reads=[cT[r].b], writes=[cT[r].b])
                for kc in range(16):
                    S.op("dve", lambda e: e.tensor_scalar(out=g.cb[r][:, kc, :], in0=ones[:, :], scalar1=cT[r][:, kc:kc + 1], scalar2=None, op0=ALU.mult),
                         reads=[ones.b, cT[r].b], writes=[g.cb[r].b])
            S.barrier()
        for l in range(L):
            stage_mod(g, l)
            with ExitStack() as st:
                pro = make_norm_pro(g, st, l, g.norm1_g[l, :], D, 0)
                linear(g, g.XS, g.w_in[l], g.Z, T, D, IN_W, G=6, NB=512, pro=pro)
            if cfg.get("stop") == "z":
                break
            stage_attn_prep(g, l)
            stage_attn_gawa(g, l)
            if cfg.get("stop") == "gawa":
                break
            if cfg.get("stop") != "rwkv":
                stage_mla(g, l)
            if cfg.get("stop") == "mla":
                break
            stage_rwkv_prep(g, l)
            stage_rwkv_scan(g, l)
            stage_rwkv_out(g, l)
            if cfg.get("stop") == "rwkv":
                break
            stage_merge(g, l)
            with ExitStack() as st:
                epi = make_resid_epi(g, st, 2 * D)
                linear(g, g.MERGED, g.w_out[l], None, T, D, D, G=8, NB=512, epi=epi)
            if cfg.get("stop") == "attn":
                break
            with ExitStack() as st:
                pro = make_norm_pro(g, st, l, g.norm2_g[l, :], 4 * D, 3 * D)
                linear(g, g.XS, g.ffn_up[l], g.AB, T, D, 2 * DFF, G=6, NB=512, pro=pro)
            stage_conv(g, l)
            with ExitStack() as st:
                epi = make_resid_epi(g, st, 5 * D)
                linear(g, g.GG, g.ffn_down[l], None, T, DFF, D, G=2, NB=256, epi=epi)
        if dbg:
            S.dma("sp", g.dbg_z[:, :], g.Z[:, :])
            S.dma("sp", g.dbg_y[:, :], g.YMIX[:, :])
            S.dma("sp", g.dbg_qkt[:, :, :], g.QKT[:, :, :])
            S.dma("sp", g.dbg_xs[:, :], g.XS[:, :])
        S.dma("sp", g.y_out[:, :], g.XS[LC:T, :])
        S.barrier()
    return nc, g


_CACHE = {}


def rope_table(n_tokens, rot_dim):
    rows = n_tokens // GRID_W
    row = np.repeat(np.arange(rows), GRID_W).astype(np.float32)
    col = np.tile(np.arange(GRID_W), rows).astype(np.float32)
    quarter = rot_dim // 4
    inv_freq = (10000.0 ** (-np.arange(quarter, dtype=np.float32) / quarter)).astype(np.float32)
    ang_r = row[:, None] * inv_freq
    ang_c = col[:, None] * inv_freq
    ang = np.concatenate([ang_r, ang_r, ang_c, ang_c], axis=-1).astype(np.float32)
    sign = np.concatenate([-np.ones(quarter), np.ones(quarter), -np.ones(quarter), np.ones(quarter)]).astype(np.float32)
    return np.stack([np.cos(ang), np.sin(ang) * sign]).astype(np.float32)


def const_tables(TL):
    idx = np.arange(128)
    m_lo = (idx[None, :] <= idx[:, None]).astype(np.float32)
    m_hi = (idx[:, None] <= idx[None, :]).astype(np.float32)
    i64 = np.arange(64)
    tri = np.stack([(i64[:, None] <= i64[None, :]), (i64[:, None] >= i64[None, :])]).astype(np.float32)
    strict = tri - np.eye(64, dtype=np.float32)[None]
    half = np.concatenate([strict, tri], axis=2)
    mk = np.concatenate([half, half], axis=1)
    nmts = -np.transpose(strict, (0, 2, 1))
    return {
        "tri": np.ascontiguousarray(np.transpose(tri, (1, 0, 2))),
        "mk": np.ascontiguousarray(np.transpose(mk, (1, 0, 2))),
        "nmts": np.ascontiguousarray(np.transpose(nmts, (1, 0, 2))),
        "zrow": np.zeros((1, NRW), np.float32),
        "ropeA": rope_table(TL, 128),
        "ropeM": rope_table(TL, 64),
        "wmask": np.ascontiguousarray(np.stack([m_lo, m_hi], axis=1)),
    }


def make_inputs_for_core(inputs, b, L):
    f = lambda a: np.ascontiguousarray(np.asarray(a, dtype=np.float32))
    m = {
        "x": f(inputs["x"][b]),
        "ctx": f(inputs["ctx"][b]),
        "cvec": f(np.stack([np.asarray(inputs["c"][b]), np.asarray(inputs["c_ctx"])])),
        "ident": np.eye(128, dtype=np.float32),
    }
    m.update(const_tables(np.asarray(inputs["x"]).shape[1]))
    for k in ["ada_w", "ada_b", "norm1_g", "norm2_g", "w_in", "ga_q_norm", "ga_k_norm", "wa_q_norm", "wa_k_norm", "wa_sink",
              "mla_cq_norm", "mla_ckv_norm", "mla_w_uq", "mla_w_ukv", "mla_qn_norm", "mla_qr_norm", "mla_kn_norm", "mla_kr_norm",
              "rwkv_mu", "rwkv_w0", "rwkv_w2", "rwkv_a0", "rwkv_a2", "rwkv_g2", "rwkv_k_k", "rwkv_k_a", "rwkv_r_k", "rwkv_lnx_w", "rwkv_lnx_b",
              "w_branch", "w_out", "ffn_up", "ffn_conv_w", "ffn_conv_b", "ffn_down"]:
        m[k] = f(inputs[k][:L])
    return m


def kernel(**inputs):
    x = np.asarray(inputs["x"])
    B, TL, _ = x.shape
    LC = np.asarray(inputs["ctx"]).shape[1]
    L = np.asarray(inputs["ada_w"]).shape[0]
    cfg = {"TL": TL, "LC": LC, "DEPTH": L}
    nc, g = build(cfg)
    n = 8
    in_maps = [make_inputs_for_core(inputs, c % B, L) for c in range(n)]
    res = run_bass_kernel_spmd(nc, in_maps, core_ids=list(range(n)))
    return np.stack([res.results[b]["y"] for b in range(B)]).astype(np.float32)
```

```python
from contextlib import ExitStack
import numpy as np
import concourse.bass as bass
import concourse.mybir as mybir
from concourse.bass_utils import run_bass_kernel_spmd

F32 = mybir.dt.float32
BF16 = mybir.dt.bfloat16
AF = mybir.ActivationFunctionType
ALU = mybir.AluOpType
AX = mybir.AxisListType

D = 2048
GRID_W = 64
EPS = 1e-6
IN_W = 13376
DFF = 5632
O_GAQ, O_GAK, O_GAV, O_WAQ, O_WAK, O_WAV = 0, 512, 768, 1024, 1536, 1792
O_RKVG, O_ZW, O_ZA, O_CQ, O_CKV, O_KR, O_GATE = 2048, 3840, 4032, 4224, 4608, 5120, 5184
LNX_EPS = 64e-5


class Buf:
    __slots__ = ("name", "w", "r")

    def __init__(self, name=""):
        self.name = name
        self.w = None
        self.r = {}


class Sync:
    MAXC = 30000

    def __init__(self, nc, n_dma_sems=48, same_engine_sync=True):
        self.nc = nc
        self.hw = {"pe": nc.tensor, "dve": nc.vector, "act": nc.scalar, "pool": nc.gpsimd, "sp": nc.sync}
        self.gen = {k: 0 for k in self.hw}
        self.sem = {k: nc.alloc_semaphore("sem_%s_0" % k) for k in self.hw}
        self.count = {k: 0 for k in self.hw}
        self.seen = {k: {} for k in self.hw}
        self.dma_sems = [nc.alloc_semaphore("dsem%d" % i) for i in range(n_dma_sems)]
        self.dma_uses = [0] * n_dma_sems
        self.dma_next = 0
        self.same_engine_sync = same_engine_sync
        self.latest = {}

    def _wait(self, eng, dep):
        key, sem, val = dep
        if self.seen[eng].get(key, 0) >= val:
            return
        self.hw[eng].wait_ge(sem, val)
        self.seen[eng][key] = val

    @staticmethod
    def _add(deps, d):
        if d is not None and deps.get(d[0], (0, 0, 0))[2] < d[2]:
            deps[d[0]] = d

    def _deps(self, reads, writes):
        deps = {}
        for b in reads:
            self._add(deps, b.w)
        for b in writes:
            self._add(deps, b.w)
            for d in b.r.values():
                self._add(deps, d)
        return deps

    def _mark(self, dep, reads, writes):
        for b in reads:
            b.r[dep[0]] = dep
        for b in writes:
            b.w = dep
            b.r = {}
        self.latest[dep[0]] = dep

    def op(self, eng, fn, reads=(), writes=()):
        for key, d in self._deps(reads, writes).items():
            if isinstance(key, tuple) and key[0] == eng and (eng == "pe" or not self.same_engine_sync):
                continue
            self._wait(eng, d)
        ins = fn(self.hw[eng])
        self.count[eng] += 1
        ins.then_inc(self.sem[eng], 1)
        self._mark(((eng, self.gen[eng]), self.sem[eng], self.count[eng]), reads, writes)
        if self.count[eng] >= self.MAXC:
            self.gen[eng] += 1
            self.sem[eng] = self.nc.alloc_semaphore("sem_%s_%d" % (eng, self.gen[eng]))
            self.count[eng] = 0
        return ins

    def dma(self, q, out, in_, reads=(), writes=(), **kw):
        i = self.dma_next
        self.dma_next = (self.dma_next + 1) % len(self.dma_sems)
        sem = self.dma_sems[i]
        key = "d%d" % i
        if self.dma_uses[i] > 0:
            self._wait(q, (key, sem, 16 * self.dma_uses[i]))
        for k, d in self._deps(reads, writes).items():
            self._wait(q, d)
        self.dma_uses[i] += 1
        ins = self.hw[q].dma_start(out=out, in_=in_, **kw)
        ins.then_inc(sem, 16)
        self._mark((key, sem, 16 * self.dma_uses[i]), reads, writes)
        return ins

    def barrier(self):
        for eng in self.hw:
            for dep in list(self.latest.values()):
                key = dep[0]
                if isinstance(key, tuple) and key[0] == "pe" and eng == "pe":
                    continue
                self._wait(eng, dep)


class Tl:
    def __init__(self, t, name):
        self.t = t
        self.b = Buf(name)

    def __getitem__(self, idx):
        return self.t[idx]


class Ctx:
    pass


def sb(g, st, name, shape, dtype=F32):
    g.uid += 1
    nm = "%s_%d" % (name, g.uid)
    return Tl(st.enter_context(g.nc.sbuf_tensor(nm, list(shape), dtype)), nm)


def evac_engine(g):
    g.ev += 1
    return "dve" if g.ev % 2 else "act"


def copy_op(g, eng, out, in_, reads, writes):
    if eng == "act":
        return g.S.op("act", lambda e: e.copy(out=out, in_=in_), reads=reads, writes=writes)
    return g.S.op(eng, lambda e: e.tensor_copy(out=out, in_=in_), reads=reads, writes=writes)


def next_psum(g):
    g.pi = (g.pi + 1) % len(g.psums)
    return g.psums[g.pi]


def linear(g, x, w, y, T, K, N, G=8, NB=512, pro=None, epi=None, segs=None, wlist=None):
    S, nc = g.S, g.nc
    KC = K // 128
    assert K % 128 == 0 and T % 128 == 0
    NT = T // 128
    XW = min(K, 2048)
    if segs is None:
        segs = [(0, KC)]
    with ExitStack() as st:
        xt = sb(g, st, "xt", [128, KC, G * 128], BF16)
        xin = [sb(g, st, "xin", [128, XW]) for _ in range(2)]
        wt = [sb(g, st, "wt", [128, KC, NB], BF16) for _ in range(3)]
        ot = [sb(g, st, "ot", [128, NB]) for _ in range(3)]
        nxin = nw = no = 0
        for g0 in range(0, NT, G):
            gn = min(G, NT - g0)
            for gi in range(gn):
                ti = g0 + gi
                for c0 in range(0, K, XW):
                    cw = min(XW, K - c0)
                    xi = xin[nxin % 2]
                    nxin += 1
                    S.dma("sp", xi[:, 0:cw], x[ti * 128:(ti + 1) * 128, c0:c0 + cw], writes=[xi.b])
                    if pro is not None:
                        pro(xi, ti, cw)
                    for k4 in range(0, cw // 128, 4):
                        ps = next_psum(g)
                        nk = min(4, cw // 128 - k4)
                        for j in range(nk):
                            kk = k4 + j
                            S.op("pe", lambda e: e.transpose(ps[:, j * 128:(j + 1) * 128], xi[:, kk * 128:(kk + 1) * 128], g.ident[:]),
                                 reads=[xi.b, g.ident.b], writes=[ps.b])
                        kc0 = c0 // 128 + k4
                        copy_op(g, evac_engine(g), xt[:, kc0:kc0 + nk, gi * 128:(gi + 1) * 128],
                                ps[:, 0:nk * 128].rearrange("p (k t) -> p k t", k=nk), [ps.b], [xt.b])
            for n0 in range(0, N, NB):
                nb = min(NB, N - n0)
                wi = wt[nw % 3]
                nw += 1
                S.dma("pool", wi[:, :, 0:nb], w.rearrange("(kc p) n -> p kc n", p=128)[:, :, n0:n0 + nb], writes=[wi.b])
                for gi in range(gn):
                    ti = g0 + gi
                    pss = []
                    for (k0, k1) in segs:
                        ps = next_psum(g)
                        pss.append(ps)
                        for kc in range(k0, k1):
                            S.op("pe", lambda e: e.matmul(ps[:, 0:nb], lhsT=xt[:, kc, gi * 128:(gi + 1) * 128], rhs=wi[:, kc, 0:nb],
                                                          start=(kc == k0), stop=(kc == k1 - 1)),
                                 reads=[xt.b, wi.b], writes=[ps.b])
                    o = ot[no % 3]
                    no += 1
                    if epi is None:
                        copy_op(g, evac_engine(g), o[:, 0:nb], pss[0][:, 0:nb], [pss[0].b], [o.b])
                        S.dma("sp", y[ti * 128:(ti + 1) * 128, n0:n0 + nb], o[:, 0:nb], reads=[o.b])
                    else:
                        epi(pss, o, ti, n0, nb)
    S.barrier()


def rms_rstd(g, st_tiles, x_ap, n, width, reads_b, sq, ss, rstd, eps=EPS):
    S = g.S
    S.op("act", lambda e: e.activation(out=sq[:, 0:n * width].rearrange("p (n w) -> p n w", n=n), in_=x_ap, func=AF.Square),
         reads=[reads_b], writes=[sq.b])
    S.op("dve", lambda e: e.tensor_reduce(out=ss[:, 0:n], in_=sq[:, 0:n * width].rearrange("p (n w) -> p n w", n=n), axis=AX.X, op=ALU.add),
         reads=[sq.b], writes=[ss.b])
    S.op("dve", lambda e: e.tensor_scalar(out=ss[:, 0:n], in0=ss[:, 0:n], scalar1=1.0 / width, scalar2=eps, op0=ALU.mult, op1=ALU.add),
         reads=[ss.b], writes=[ss.b])
    S.op("act", lambda e: e.activation(out=ss[:, 0:n], in_=ss[:, 0:n], func=AF.Sqrt), reads=[ss.b], writes=[ss.b])
    S.op("dve", lambda e: e.reciprocal(out=rstd[:, 0:n], in_=ss[:, 0:n]), reads=[ss.b], writes=[rstd.b])


def bcast_row(g, q, tile_ap, buf, dram_row_ap):
    g.S.dma(q, tile_ap, dram_row_ap.partition_broadcast(128), writes=[buf])


def stage_mod(g, l):
    S, nc = g.S, g.nc
    with ExitStack() as st:
        wt = [sb(g, st, "mw", [128, 16, 512]) for _ in range(2)]
        bt = [sb(g, st, "mb", [128, 512]) for _ in range(2)]
        ot = [sb(g, st, "mo", [128, 512]) for _ in range(3)]
        no = 0
        for bi, n0 in enumerate(range(0, 6 * D, 512)):
            wi = wt[bi % 2]
            bb = bt[bi % 2]
            S.dma("sp", wi[:, :, :], g.ada_w[l].rearrange("(kc p) n -> p kc n", p=128)[:, :, n0:n0 + 512], writes=[wi.b])
            bcast_row(g, "sp", bb[:, :], bb.b, g.ada_b[l, n0:n0 + 512])
            for r in range(2):
                ps = next_psum(g)
                for kc in range(16):
                    S.op("pe", lambda e: e.matmul(ps[:, :], lhsT=g.cb[r][:, kc, :], rhs=wi[:, kc, :], start=(kc == 0), stop=(kc == 15)),
                         reads=[g.cb[r].b, wi.b], writes=[ps.b])
                o = ot[no % 3]
                no += 1
                S.op("dve", lambda e: e.tensor_tensor(out=o[:, :], in0=ps[:, :], in1=bb[:, :], op=ALU.add), reads=[ps.b, bb.b], writes=[o.b])
                S.dma("pool", g.MODB[r, :, n0:n0 + 512], o[:, :], reads=[o.b])
    S.barrier()


def make_norm_pro(g, st, l, gain_dram, sc_off, sh_off):
    S = g.S
    A = [sb(g, st, "nA", [128, D]) for _ in range(2)]
    B = [sb(g, st, "nB", [128, D]) for _ in range(2)]
    sq = sb(g, st, "nsq", [128, D])
    ss = sb(g, st, "nss", [128, 1])
    rstd = sb(g, st, "nrs", [128, 1])
    with ExitStack() as st2:
        gt = sb(g, st2, "ng", [128, D])
        bcast_row(g, "sp", gt[:, :], gt.b, gain_dram)
        for r in range(2):
            S.dma("sp", A[r][:, :], g.MODB[r, :, sc_off:sc_off + D], writes=[A[r].b])
            S.dma("sp", B[r][:, :], g.MODB[r, :, sh_off:sh_off + D], writes=[B[r].b])
            S.op("dve", lambda e: e.scalar_tensor_tensor(out=A[r][:, :], in0=A[r][:, :], scalar=1.0, in1=gt[:, :], op0=ALU.add, op1=ALU.mult),
                 reads=[A[r].b, gt.b], writes=[A[r].b])
        S.barrier()

    def pro(xi, ti, cw):
        r = 1 if ti < g.NTC else 0
        rms_rstd(g, None, xi[:, 0:D].rearrange("p (n w) -> p n w", n=1), 1, D, xi.b, sq, ss, rstd)
        S.op("dve", lambda e: e.scalar_tensor_tensor(out=xi[:, 0:D], in0=xi[:, 0:D], scalar=rstd[:, 0:1], in1=A[r][:, :], op0=ALU.mult, op1=ALU.mult),
             reads=[xi.b, rstd.b, A[r].b], writes=[xi.b])
        S.op("dve", lambda e: e.tensor_tensor(out=xi[:, 0:D], in0=xi[:, 0:D], in1=B[r][:, :], op=ALU.add), reads=[xi.b, B[r].b], writes=[xi.b])
    return pro


def make_resid_epi(g, st, gate_off):
    S = g.S
    gts = [sb(g, st, "rg", [128, D]) for _ in range(2)]
    for r in range(2):
        S.dma("sp", gts[r][:, :], g.MODB[r, :, gate_off:gate_off + D], writes=[gts[r].b])
    xb = [sb(g, st, "rx", [128, 512]) for _ in range(3)]
    cnt = [0]

    def epi(pss, o, ti, n0, nb):
        r = 1 if ti < g.NTC else 0
        x = xb[cnt[0] % 3]
        cnt[0] += 1
        S.dma("sp", x[:, 0:nb], g.XS[ti * 128:(ti + 1) * 128, n0:n0 + nb], writes=[x.b])
        S.op("dve", lambda e: e.tensor_tensor(out=o[:, 0:nb], in0=pss[0][:, 0:nb], in1=gts[r][:, n0:n0 + nb], op=ALU.mult),
             reads=[pss[0].b, gts[r].b], writes=[o.b])
        S.op("dve", lambda e: e.tensor_tensor(out=o[:, 0:nb], in0=o[:, 0:nb], in1=x[:, 0:nb], op=ALU.add), reads=[o.b, x.b], writes=[o.b])
        S.dma("sp", g.XS[ti * 128:(ti + 1) * 128, n0:n0 + nb], o[:, 0:nb], reads=[o.b])
    return epi


def norm_rope(g, src, src_b, n, w, gain_ap, gain_b, out, out_b, tmp, sq, ss, rstd, rope=None):
    S = g.S
    rms_rstd(g, None, src, n, w, src_b, sq, ss, rstd)
    dst = out if rope is None else tmp[:, 0:n * w].rearrange("p (n w) -> p n w", n=n)
    dst_b = out_b if rope is None else tmp.b
    S.op("dve", lambda e: e.tensor_tensor(out=dst, in0=src, in1=rstd[:, 0:n].unsqueeze(2).broadcast_to([128, n, w]), op=ALU.mult),
         reads=[src_b, rstd.b], writes=[dst_b])
    S.op("dve", lambda e: e.tensor_tensor(out=dst, in0=dst, in1=gain_ap, op=ALU.mult), reads=[dst_b, gain_b], writes=[dst_b])
    if rope is None:
        return
    cos, ssin, blk = rope
    nh = w // (2 * blk)
    xv = dst.rearrange("p n (h b k) -> p n h b k", h=nh, b=2, k=blk)
    sv = ssin[:, 0:w].rearrange("p (h b k) -> p h b k", h=nh, b=2, k=blk)
    sw = sq[:, 0:n * w].rearrange("p (n h b k) -> p n h b k", n=n, h=nh, b=2, k=blk)
    for n_i in range(n):
        for b in range(2):
            S.op("dve", lambda e: e.tensor_tensor(out=sw[:, n_i, :, b, :], in0=xv[:, n_i, :, 1 - b, :], in1=sv[:, :, b, :], op=ALU.mult),
                 reads=[dst_b, ssin.b], writes=[sq.b])
    S.op("dve", lambda e: e.tensor_tensor(out=dst, in0=dst, in1=cos[:, 0:w].unsqueeze(1).broadcast_to([128, n, w]), op=ALU.mult),
         reads=[dst_b, cos.b], writes=[dst_b])
    S.op("dve", lambda e: e.tensor_tensor(out=out, in0=dst, in1=sq[:, 0:n * w].rearrange("p (n w) -> p n w", n=n), op=ALU.add),
         reads=[dst_b, sq.b], writes=[out_b])


def transpose_slots(g, src_tl, slots, w, dst_tl, dst_slot0):
    S = g.S
    for i0 in range(0, len(slots), 4):
        grp = slots[i0:i0 + 4]
        ps = next_psum(g)
        for j, s_ in enumerate(grp):
            S.op("pe", lambda e: e.transpose(ps[0:w, j * 128:(j + 1) * 128], src_tl[:, s_, 0:w], g.ident[:]),
                 reads=[src_tl.b, g.ident.b], writes=[ps.b])
        copy_op(g, evac_engine(g), dst_tl[0:w, dst_slot0 + i0:dst_slot0 + i0 + len(grp), :],
                ps[0:w, 0:len(grp) * 128].rearrange("p (k t) -> p k t", k=len(grp)), [ps.b], [dst_tl.b])


def stage_attn_prep(g, l):
    S = g.S
    with ExitStack() as st:
        gain = sb(g, st, "apg", [128, 2, 6, 128])
        for gi, (qn, kn) in enumerate([(g.ga_q_norm, g.ga_k_norm), (g.wa_q_norm, g.wa_k_norm)]):
            for s_ in range(6):
                bcast_row(g, "sp", gain[:, gi, s_, :], gain.b, (qn if s_ < 4 else kn)[l, :])
        zt = [sb(g, st, "apz", [128, 16, 128]) for _ in range(2)]
        xr = [sb(g, st, "apx", [128, 12, 128]) for _ in range(2)]
        tT = [sb(g, st, "apt", [128, 12, 128]) for _ in range(2)]
        cs = [sb(g, st, "apc", [128, 128]) for _ in range(2)]
        sn = [sb(g, st, "aps", [128, 128]) for _ in range(2)]
        tmp = sb(g, st, "aptmp", [128, 6 * 128])
        sq = sb(g, st, "apsq", [128, 6 * 128])
        ss = sb(g, st, "apss", [128, 8])
        rstd = sb(g, st, "aprs", [128, 8])
        for ti in range(g.NT):
            z = zt[ti % 2]
            x = xr[ti % 2]
            t_ = tT[ti % 2]
            S.dma("sp", z[:, :, :], g.Z[ti * 128:(ti + 1) * 128, 0:2048].rearrange("p (s d) -> p s d", s=16), writes=[z.b])
            rope = None
            if ti >= g.NTC:
                c_, s_t = cs[ti % 2], sn[ti % 2]
                p0 = (ti - g.NTC) * 128
                S.dma("sp", c_[:, :], g.ropeA[0, p0:p0 + 128, :], writes=[c_.b])
                S.dma("sp", s_t[:, :], g.ropeA[1, p0:p0 + 128, :], writes=[s_t.b])
                rope = (c_, s_t, 32)
            for gi, base in enumerate([0, 8]):
                norm_rope(g, z[:, base:base + 6, :], z.b, 6, 128, gain[:, gi, :, :], gain.b,
                          x[:, gi * 6:(gi + 1) * 6, :], x.b, tmp, sq, ss, rstd, rope=rope)
            transpose_slots(g, x, list(range(12)), 128, t_, 0)
            S.dma("pool", g.QKT[:, :, ti * 128:(ti + 1) * 128].rearrange("s d t -> d s t"), t_[:, :, :], reads=[t_.b])
    S.barrier()


def attention(g, heads, scale, sink_tl=None):
    S = g.S
    T, NT = g.T, g.NT
    po = g.psums[0:4]
    sps = g.psums[4:8]
    with ExitStack() as st:
        kts = [sb(g, st, "akt", [128, T], BF16) for _ in range(2)]
        vt = sb(g, st, "avt", [128, NT, 129], BF16)
        qts = [[sb(g, st, "aqt", [128, 512], BF16) for _ in range(2)] for _ in range(3)]
        pts = [sb(g, st, "apt", [128, 512], BF16) for _ in range(3)]
        yts = [sb(g, st, "ayt", [128, 128]) for _ in range(3)]
        den = sb(g, st, "aden", [128, 4])
        S.op("dve", lambda e: e.memset(vt[:, :, 128:129], 1.0), writes=[vt.b])
        nq = npt = ny = nsp = 0
        for hd in heads:
            for pi_, (kd_ap, kd) in enumerate(hd["kparts"]):
                S.dma("pool", kts[pi_][0:kd, :], kd_ap, writes=[kts[pi_].b])
            S.dma("pool", vt[:, :, 0:128], hd["v"].rearrange("(n p) d -> p n d", p=128), writes=[vt.b])
            for qh, qparts in enumerate(hd["qparts"]):
                for (q0, qlen, keys) in hd["blocks"]:
                    nqb = qlen // 128
                    qt = qts[nq % 3]
                    nq += 1
                    for pi_, (qd_ap, kd) in enumerate(qparts):
                        S.dma("pool", qt[pi_][0:kd, 0:qlen], qd_ap[:, q0:q0 + qlen], writes=[qt[pi_].b])
                    for idx, (kt, mask) in enumerate(keys):
                        ps = sps[nsp % 4]
                        nsp += 1
                        npart = len(qparts)
                        for pi_, (qd_ap, kd) in enumerate(qparts):
                            S.op("pe", lambda e: e.matmul(ps[:, 0:qlen], lhsT=kts[pi_][0:kd, kt * 128:(kt + 1) * 128], rhs=qt[pi_][0:kd, 0:qlen],
                                                          start=(pi_ == 0), stop=(pi_ == npart - 1)),
                                 reads=[kts[pi_].b, qt[pi_].b], writes=[ps.b])
                        pt = pts[npt % 3]
                        npt += 1
                        S.op("act", lambda e: e.activation(out=pt[:, 0:qlen], in_=ps[:, 0:qlen], func=AF.Exp, scale=scale),
                             reads=[ps.b], writes=[pt.b])
                        if mask is not None:
                            S.op("dve", lambda e: e.tensor_tensor(out=pt[:, 0:qlen], in0=pt[:, 0:qlen], in1=g.wmask[:, mask, :], op=ALU.mult),
                                 reads=[pt.b, g.wmask.b], writes=[pt.b])
                        for qb in range(nqb):
                            S.op("pe", lambda e: e.matmul(po[qb][:, 0:129], lhsT=pt[:, qb * 128:(qb + 1) * 128], rhs=vt[:, kt, :],
                                                          start=(idx == 0), stop=(idx == len(keys) - 1)),
                                 reads=[pt.b, vt.b], writes=[po[qb].b])
                    for qb in range(nqb):
                        y = yts[ny % 3]
                        ny += 1
                        if hd.get("sink_idx") is not None:
                            si = hd["sink_idx"][qh]
                            S.op("dve", lambda e: e.tensor_tensor(out=den[:, 0:1], in0=po[qb][:, 128:129], in1=sink_tl[:, si:si + 1], op=ALU.add),
                                 reads=[po[qb].b, sink_tl.b], writes=[den.b])
                            S.op("dve", lambda e: e.reciprocal(out=den[:, 0:1], in_=den[:, 0:1]), reads=[den.b], writes=[den.b])
                        else:
                            S.op("dve", lambda e: e.reciprocal(out=den[:, 0:1], in_=po[qb][:, 128:129]), reads=[po[qb].b], writes=[den.b])
                        S.op("dve", lambda e: e.tensor_scalar(out=y[:, :], in0=po[qb][:, 0:128], scalar1=den[:, 0:1], scalar2=None, op0=ALU.mult),
                             reads=[po[qb].b, den.b], writes=[y.b])
                        r0 = q0 + qb * 128
                        yc = hd["ycols"][qh]
                        S.dma("sp", g.YMIX[r0:r0 + 128, yc:yc + 128], y[:, :], reads=[y.b])
    S.barrier()


def dense_blocks(g):
    blocks = [(0, g.LC, [(kt, None) for kt in range(g.NTC)])]
    for q0 in range(g.LC, g.T, 512):
        blocks.append((q0, min(512, g.T - q0), [(kt, None) for kt in range(g.NT)]))
    return blocks


def window_blocks(g):
    blocks = [(0, g.LC, [(kt, None) for kt in range(g.NTC)])]
    nblk = g.TL // 128
    for n in range(nblk):
        keys = [(kt, None) for kt in range(g.NTC)]
        if n > 0:
            keys.append((g.NTC + n - 1, 0))
        keys.append((g.NTC + n, None))
        if n < nblk - 1:
            keys.append((g.NTC + n + 1, 1))
        blocks.append((g.LC + n * 128, 128, keys))
    return blocks


def stage_attn_gawa(g, l):
    S = g.S
    heads = []
    for hk in range(2):
        heads.append(dict(kparts=[(g.QKT[4 + hk], 128)], qparts=[[(g.QKT[2 * hk + j], 128)] for j in range(2)],
                          v=g.Z[:, O_GAV + hk * 128:O_GAV + (hk + 1) * 128], ycols=[(2 * hk + j) * 128 for j in range(2)],
                          blocks=dense_blocks(g)))
    attention(g, heads, 128 ** -0.5)
    with ExitStack() as st:
        sink = sb(g, st, "sink", [128, 4])
        bcast_row(g, "sp", sink[:, :], sink.b, g.wa_sink[l, :])
        S.op("act", lambda e: e.activation(out=sink[:, :], in_=sink[:, :], func=AF.Exp), reads=[sink.b], writes=[sink.b])
        heads = []
        for hk in range(2):
            heads.append(dict(kparts=[(g.QKT[10 + hk], 128)], qparts=[[(g.QKT[6 + 2 * hk + j], 128)] for j in range(2)],
                              v=g.Z[:, O_WAV + hk * 128:O_WAV + (hk + 1) * 128], ycols=[512 + (2 * hk + j) * 128 for j in range(2)],
                              blocks=window_blocks(g), sink_idx=[2 * hk, 2 * hk + 1]))
        attention(g, heads, 128 ** -0.5, sink_tl=sink)


def make_rms_pro(g, st, gain_dram, K):
    S = g.S
    gt = sb(g, st, "pg", [128, K])
    sq = sb(g, st, "psq", [128, K])
    ss = sb(g, st, "pss", [128, 1])
    rstd = sb(g, st, "prs", [128, 1])
    bcast_row(g, "sp", gt[:, :], gt.b, gain_dram)

    def pro(xi, ti, cw):
        rms_rstd(g, None, xi[:, 0:K].rearrange("p (n w) -> p n w", n=1), 1, K, xi.b, sq, ss, rstd)
        S.op("dve", lambda e: e.scalar_tensor_tensor(out=xi[:, 0:K], in0=xi[:, 0:K], scalar=rstd[:, 0:1], in1=gt[:, :], op0=ALU.mult, op1=ALU.mult),
             reads=[xi.b, rstd.b, gt.b], writes=[xi.b])
    return pro


def stage_mla(g, l):
    S = g.S
    T = g.T
    with ExitStack() as st:
        pro = make_rms_pro(g, st, g.mla_cq_norm[l, :], 384)
        linear(g, g.Z[:, O_CQ:O_CQ + 384], g.mla_w_uq[l], g.QML, T, 384, 768, G=17, NB=512, pro=pro)
    with ExitStack() as st:
        pro = make_rms_pro(g, st, g.mla_ckv_norm[l, :], 512)
        linear(g, g.Z[:, O_CKV:O_CKV + 512], g.mla_w_ukv[l], g.KVML, T, 512, 1024, G=17, NB=512, pro=pro)
    with ExitStack() as st:
        gq_n = sb(g, st, "mgqn", [128, 128])
        gq_r = sb(g, st, "mgqr", [128, 64])
        gk_n = sb(g, st, "mgkn", [128, 128])
        gk_r = sb(g, st, "mgkr", [128, 64])
        bcast_row(g, "sp", gq_n[:, :], gq_n.b, g.mla_qn_norm[l, :])
        bcast_row(g, "sp", gq_r[:, :], gq_r.b, g.mla_qr_norm[l, :])
        bcast_row(g, "sp", gk_n[:, :], gk_n.b, g.mla_kn_norm[l, :])
        bcast_row(g, "sp", gk_r[:, :], gk_r.b, g.mla_kr_norm[l, :])
        qin = [sb(g, st, "mq", [128, 4, 192]) for _ in range(2)]
        kvin = [sb(g, st, "mkv", [128, 4, 256]) for _ in range(2)]
        krin = [sb(g, st, "mkr", [128, 1, 64]) for _ in range(2)]
        xqn = [sb(g, st, "xqn", [128, 4, 128]) for _ in range(2)]
        xqr = [sb(g, st, "xqr", [128, 4, 64]) for _ in range(2)]
        xkn = [sb(g, st, "xkn", [128, 4, 128]) for _ in range(2)]
        xkr = [sb(g, st, "xkr", [128, 1, 64]) for _ in range(2)]
        tqn = [sb(g, st, "tqn", [128, 4, 128]) for _ in range(2)]
        tqr = [sb(g, st, "tqr", [64, 4, 128]) for _ in range(2)]
        tkn = [sb(g, st, "tkn", [128, 4, 128]) for _ in range(2)]
        tkr = [sb(g, st, "tkr", [64, 1, 128]) for _ in range(2)]
        cs = [sb(g, st, "mc", [128, 64]) for _ in range(2)]
        sn = [sb(g, st, "ms", [128, 64]) for _ in range(2)]
        tmp = sb(g, st, "mtmp", [128, 512])
        sq = sb(g, st, "msq", [128, 512])
        ss = sb(g, st, "mss", [128, 8])
        rstd = sb(g, st, "mrs", [128, 8])
        for ti in range(g.NT):
            i = ti % 2
            rows = slice(ti * 128, (ti + 1) * 128)
            S.dma("sp", qin[i][:, :, :], g.QML[rows, :].rearrange("p (h d) -> p h d", h=4), writes=[qin[i].b])
            S.dma("sp", kvin[i][:, :, :], g.KVML[rows, :].rearrange("p (h d) -> p h d", h=4), writes=[kvin[i].b])
            S.dma("sp", krin[i][:, 0, :], g.Z[rows, O_KR:O_KR + 64], writes=[krin[i].b])
            rope = None
            if ti >= g.NTC:
                p0 = (ti - g.NTC) * 128
                S.dma("sp", cs[i][:, :], g.ropeM[0, p0:p0 + 128, :], writes=[cs[i].b])
                S.dma("sp", sn[i][:, :], g.ropeM[1, p0:p0 + 128, :], writes=[sn[i].b])
                rope = (cs[i], sn[i], 16)
            norm_rope(g, qin[i][:, :, 0:128], qin[i].b, 4, 128, gq_n[:, :].unsqueeze(1).broadcast_to([128, 4, 128]), gq_n.b,
                      xqn[i][:, :, :], xqn[i].b, tmp, sq, ss, rstd)
            norm_rope(g, qin[i][:, :, 128:192], qin[i].b, 4, 64, gq_r[:, :].unsqueeze(1).broadcast_to([128, 4, 64]), gq_r.b,
                      xqr[i][:, :, :], xqr[i].b, tmp, sq, ss, rstd, rope=rope)
            norm_rope(g, kvin[i][:, :, 0:128], kvin[i].b, 4, 128, gk_n[:, :].unsqueeze(1).broadcast_to([128, 4, 128]), gk_n.b,
                      xkn[i][:, :, :], xkn[i].b, tmp, sq, ss, rstd)
            norm_rope(g, krin[i][:, :, :], krin[i].b, 1, 64, gk_r[:, :].unsqueeze(1), gk_r.b,
                      xkr[i][:, :, :], xkr[i].b, tmp, sq, ss, rstd, rope=rope)
            transpose_slots(g, xqn[i], [0, 1, 2, 3], 128, tqn[i], 0)
            transpose_slots(g, xqr[i], [0, 1, 2, 3], 64, tqr[i], 0)
            transpose_slots(g, xkn[i], [0, 1, 2, 3], 128, tkn[i], 0)
            transpose_slots(g, xkr[i], [0], 64, tkr[i], 0)
            cols = slice(ti * 128, (ti + 1) * 128)
            S.dma("pool", g.MQN[:, :, cols].rearrange("s d t -> d s t"), tqn[i][:, :, :], reads=[tqn[i].b])
            S.dma("pool", g.MQR[:, :, cols].rearrange("s d t -> d s t"), tqr[i][:, :, :], reads=[tqr[i].b])
            S.dma("pool", g.MKN[:, :, cols].rearrange("s d t -> d s t"), tkn[i][:, :, :], reads=[tkn[i].b])
            S.dma("pool", g.MKR[:, :, cols].rearrange("s d t -> d s t"), tkr[i][:, :, :], reads=[tkr[i].b])
    S.barrier()
    heads = []
    for h in range(4):
        heads.append(dict(kparts=[(g.MKN[h], 128), (g.MKR[0], 64)], qparts=[[(g.MQN[h], 128), (g.MQR[h], 64)]],
                          v=g.KVML[:, h * 256 + 128:h * 256 + 256], ycols=[1536 + h * 128], blocks=dense_blocks(g)))
    attention(g, heads, 192 ** -0.5)


RQ_R, RQ_LW, RQ_K, RQ_V, RQ_KK, RQ_KKA = 0, 1, 2, 3, 4, 5
NRW = 2176


def stage_rwkv_prep(g, l):
    S = g.S
    NT, NTC = g.NT, g.NTC
    with ExitStack() as st:
        MU = [sb(g, st, "rmu", [128, NRW]) for _ in range(2)]
        w0 = [sb(g, st, "rw0", [128, 512]) for _ in range(2)]
        a0 = [sb(g, st, "ra0", [128, 512]) for _ in range(2)]
        w2 = [sb(g, st, "rw2", [96, 512]) for _ in range(2)]
        a2 = [sb(g, st, "ra2", [96, 512]) for _ in range(2)]
        g2 = sb(g, st, "rg2", [128, 2, 512])
        kk_ = sb(g, st, "rkk", [128, 512])
        ka = sb(g, st, "rka", [128, 512])
        omka = sb(g, st, "romka", [128, 512])
        for d in range(2):
            S.op("dve", lambda e: e.memset(MU[d][:, :], 0.0), writes=[MU[d].b])
            bcast_row(g, "sp", MU[d][:, 0:1792], MU[d].b, g.rwkv_mu[l, d, 0:1792])
            bcast_row(g, "sp", MU[d][:, 1792 + d * 96:1792 + (d + 1) * 96], MU[d].b, g.rwkv_mu[l, d, 1792:1888])
            bcast_row(g, "sp", MU[d][:, 1984 + d * 96:1984 + (d + 1) * 96], MU[d].b, g.rwkv_mu[l, d, 1888:1984])
            bcast_row(g, "sp", w0[d][:, :], w0[d].b, g.rwkv_w0[l, d, :])
            bcast_row(g, "sp", a0[d][:, :], a0[d].b, g.rwkv_a0[l, d, :])
            S.dma("sp", w2[d][:, :], g.rwkv_w2[l, d], writes=[w2[d].b])
            S.dma("sp", a2[d][:, :], g.rwkv_a2[l, d], writes=[a2[d].b])
        S.dma("sp", g2[:, :, :], g.rwkv_g2[l].rearrange("(kc p) n -> p kc n", p=128), writes=[g2.b])
        bcast_row(g, "sp", kk_[:, :], kk_.b, g.rwkv_k_k[l, :])
        bcast_row(g, "sp", ka[:, :], ka.b, g.rwkv_k_a[l, :])
        S.op("dve", lambda e: e.tensor_scalar(out=omka[:, :], in0=ka[:, :], scalar1=-1.0, scalar2=1.0, op0=ALU.mult, op1=ALU.add),
             reads=[ka.b], writes=[omka.b])
        zc = [sb(g, st, "rzc", [128, NRW]) for _ in range(2)]
        zs = [sb(g, st, "rzs", [128, NRW]) for _ in range(2)]
        u = sb(g, st, "ru", [128, NRW])
        tw = sb(g, st, "rtw", [128, 96])
        sg = sb(g, st, "rsg", [128, 256])
        tT = sb(g, st, "rtT", [128, 4, 128])
        aa = sb(g, st, "raa", [128, 512])
        t1 = sb(g, st, "rt1", [128, 512])
        t2 = sb(g, st, "rt2", [128, 512])
        ss = sb(g, st, "rss", [128, 8])
        outs = [[sb(g, st, "ro", [128, 512]) for _ in range(2)] for _ in range(5)]
        for ti in range(NT):
            z = zc[ti % 2]
            r0 = ti * 128
            S.dma("sp", z[:, :], g.Z[r0:r0 + 128, O_RKVG:O_RKVG + NRW], writes=[z.b])
            for d in range(2):
                zz = zs[d]
                if d == 0:
                    S.dma("sp", zz[1:128, :], g.Z[r0:r0 + 127, O_RKVG:O_RKVG + NRW], writes=[zz.b])
                    src = g.zrow[0:1, :] if ti in (0, NTC) else g.Z[r0 - 1:r0, O_RKVG:O_RKVG + NRW]
                    S.dma("sp", zz[0:1, :], src, writes=[zz.b])
                else:
                    S.dma("sp", zz[0:127, :], g.Z[r0 + 1:r0 + 128, O_RKVG:O_RKVG + NRW], writes=[zz.b])
                    src = g.zrow[0:1, :] if ti in (NTC - 1, NT - 1) else g.Z[r0 + 128:r0 + 129, O_RKVG:O_RKVG + NRW]
                    S.dma("sp", zz[127:128, :], src, writes=[zz.b])
                S.op("dve", lambda e: e.tensor_tensor(out=u[:, :], in0=zz[:, :], in1=z[:, :], op=ALU.subtract), reads=[zz.b, z.b], writes=[u.b])
                S.op("dve", lambda e: e.tensor_tensor(out=u[:, :], in0=u[:, :], in1=MU[d][:, :], op=ALU.mult), reads=[u.b, MU[d].b], writes=[u.b])
                S.op("dve", lambda e: e.tensor_tensor(out=u[:, :], in0=u[:, :], in1=z[:, :], op=ALU.add), reads=[u.b, z.b], writes=[u.b])
                ur, uk, uv, ugd = u[:, 0:512], u[:, 512:1024], u[:, 1024:1536], u[:, 1536:1792]
                uwd = u[:, 1792 + d * 96:1792 + (d + 1) * 96]
                uad = u[:, 1984 + d * 96:1984 + (d + 1) * 96]
                o_lw, o_k, o_kk, o_kka, o_g = [outs[i][(2 * ti + d) % 2] for i in range(5)]
                rows = slice(r0, r0 + 128)
                S.dma("pool", g.RW[d, RQ_R, rows, :], ur, reads=[u.b])
                S.dma("pool", g.RW[d, RQ_V, rows, :], uv, reads=[u.b])
                S.op("act", lambda e: e.activation(out=tw[:, :], in_=uwd, func=AF.Tanh), reads=[u.b], writes=[tw.b])
                S.op("act", lambda e: e.activation(out=sg[:, :], in_=ugd, func=AF.Sigmoid), reads=[u.b], writes=[sg.b])
                ps = next_psum(g)
                S.op("pe", lambda e: e.transpose(ps[0:96, 0:128], tw[:, :], g.ident[:]), reads=[tw.b, g.ident.b], writes=[ps.b])
                S.op("pe", lambda e: e.transpose(ps[0:96, 128:256], uad, g.ident[:]), reads=[u.b, g.ident.b], writes=[ps.b])
                copy_op(g, "act", tT[0:96, 0:2, :], ps[0:96, 0:256].rearrange("p (k t) -> p k t", k=2), [ps.b], [tT.b])
                ps = next_psum(g)
                S.op("pe", lambda e: e.transpose(ps[:, 0:128], sg[:, 0:128], g.ident[:]), reads=[sg.b, g.ident.b], writes=[ps.b])
                S.op("pe", lambda e: e.transpose(ps[:, 128:256], sg[:, 128:256], g.ident[:]), reads=[sg.b, g.ident.b], writes=[ps.b])
                copy_op(g, "dve", tT[:, 2:4, :], ps[:, 0:256].rearrange("p (k t) -> p k t", k=2), [ps.b], [tT.b])
                psw = next_psum(g)
                S.op("pe", lambda e: e.matmul(psw[:, :], lhsT=tT[0:96, 0, :], rhs=w2[d][:, :], start=True, stop=True), reads=[tT.b, w2[d].b], writes=[psw.b])
                S.op("dve", lambda e: e.tensor_tensor(out=t1[:, :], in0=psw[:, :], in1=w0[d][:, :], op=ALU.add), reads=[psw.b, w0[d].b], writes=[t1.b])
                S.op("act", lambda e: e.activation(out=t1[:, :], in_=t1[:, :], func=AF.Sigmoid), reads=[t1.b], writes=[t1.b])
                S.op("act", lambda e: e.mul(out=o_lw[:, :], in_=t1[:, :], mul=-0.6065306597126334), reads=[t1.b], writes=[o_lw.b])
                S.dma("pool", g.RW[d, RQ_LW, rows, :], o_lw[:, :], reads=[o_lw.b])
                psa = next_psum(g)
                S.op("pe", lambda e: e.matmul(psa[:, :], lhsT=tT[0:96, 1, :], rhs=a2[d][:, :], start=True, stop=True), reads=[tT.b, a2[d].b], writes=[psa.b])
                S.op("dve", lambda e: e.tensor_tensor(out=aa[:, :], in0=psa[:, :], in1=a0[d][:, :], op=ALU.add), reads=[psa.b, a0[d].b], writes=[aa.b])
                S.op("act", lambda e: e.activation(out=aa[:, :], in_=aa[:, :], func=AF.Sigmoid), reads=[aa.b], writes=[aa.b])
                psg = next_psum(g)
                for kc in range(2):
                    S.op("pe", lambda e: e.matmul(psg[:, :], lhsT=tT[:, 2 + kc, :], rhs=g2[:, kc, :], start=(kc == 0), stop=(kc == 1)),
                         reads=[tT.b, g2.b], writes=[psg.b])
                copy_op(g, "act", o_g[:, :], psg[:, :], [psg.b], [o_g.b])
                S.dma("pool", g.RG[d, rows, :], o_g[:, :], reads=[o_g.b])
                S.op("dve", lambda e: e.tensor_tensor(out=t1[:, :], in0=uk, in1=kk_[:, :], op=ALU.mult), reads=[u.b, kk_.b], writes=[t1.b])
                S.op("act", lambda e: e.activation(out=t2[:, :], in_=t1[:, :], func=AF.Square), reads=[t1.b], writes=[t2.b])
                S.op("dve", lambda e: e.tensor_reduce(out=ss[:, 0:8], in_=t2[:, :].rearrange("p (h k) -> p h k", h=8), axis=AX.X, op=ALU.add),
                     reads=[t2.b], writes=[ss.b])
                S.op("act", lambda e: e.activation(out=ss[:, 0:8], in_=ss[:, 0:8], func=AF.Sqrt), reads=[ss.b], writes=[ss.b])
                S.op("dve", lambda e: e.tensor_scalar(out=ss[:, 0:8], in0=ss[:, 0:8], scalar1=1e-12, scalar2=None, op0=ALU.max), reads=[ss.b], writes=[ss.b])
                S.op("dve", lambda e: e.reciprocal(out=ss[:, 0:8], in_=ss[:, 0:8]), reads=[ss.b], writes=[ss.b])
                S.op("dve", lambda e: e.tensor_tensor(out=o_kk[:, :].rearrange("p (h k) -> p h k", h=8), in0=t1[:, :].rearrange("p (h k) -> p h k", h=8),
                                                      in1=ss[:, 0:8].unsqueeze(2).broadcast_to([128, 8, 64]), op=ALU.mult),
                     reads=[t1.b, ss.b], writes=[o_kk.b])
                S.dma("pool", g.RW[d, RQ_KK, rows, :], o_kk[:, :], reads=[o_kk.b])
                S.op("dve", lambda e: e.tensor_tensor(out=o_kka[:, :], in0=o_kk[:, :], in1=aa[:, :], op=ALU.mult), reads=[o_kk.b, aa.b], writes=[o_kka.b])
                S.dma("pool", g.RW[d, RQ_KKA, rows, :], o_kka[:, :], reads=[o_kka.b])
                S.op("dve", lambda e: e.tensor_tensor(out=t2[:, :], in0=aa[:, :], in1=ka[:, :], op=ALU.mult), reads=[aa.b, ka.b], writes=[t2.b])
                S.op("dve", lambda e: e.tensor_tensor(out=t2[:, :], in0=t2[:, :], in1=omka[:, :], op=ALU.add), reads=[t2.b, omka.b], writes=[t2.b])
                S.op("dve", lambda e: e.tensor_tensor(out=o_k[:, :], in0=t2[:, :], in1=uk, op=ALU.mult), reads=[t2.b, u.b], writes=[o_k.b])
                S.dma("pool", g.RW[d, RQ_K, rows, :], o_k[:, :], reads=[o_k.b])
    S.barrier()


def stage_rwkv_scan(g, l):
    S = g.S
    C = 64
    nch = g.T // C
    nch_c = g.LC // C
    ident64 = g.ident[0:64, 0:64]

    def mm(ps_ap, psb, lhsT, rhs, reads, start=True, stop=True):
        S.op("pe", lambda e: e.matmul(ps_ap, lhsT=lhsT, rhs=rhs, start=start, stop=stop), reads=reads, writes=[psb])

    with ExitStack() as st:
        ST = [sb(g, st, "sST", [64, 8, 64]) for _ in range(2)]
        for d in range(2):
            S.op("dve", lambda e: e.memset(ST[d][:, :, :], 0.0), writes=[ST[d].b])
        qin = [[sb(g, st, "sq", [64, 512]) for _ in range(6)] for _ in range(2)]
        v2 = [sb(g, st, "sv2", [128, 512]) for _ in range(2)]
        names = ["Lsb", "eL", "enL", "eLx", "eEnd", "Rt", "Kt", "Kk", "Ak", "Kh", "Ah"]
        W = {n: sb(g, st, "s" + n, [64, 512]) for n in names}
        pCT = sb(g, st, "spCT", [64, 8])
        XT1 = sb(g, st, "sXT1", [64, 8, 128])
        XT2 = sb(g, st, "sXT2", [64, 8, 128])
        PRs = sb(g, st, "sPRs", [128, 8, 128])
        Nn = [sb(g, st, "sN", [64, 8, 64]) for _ in range(2)]
        NTt = [sb(g, st, "sNT", [64, 8, 64]) for _ in range(2)]
        Qq = [sb(g, st, "sQ", [64, 8, 64]) for _ in range(2)]
        W1T = sb(g, st, "sW1T", [64, 8, 64])
        MV = sb(g, st, "sMV", [64, 8, 64])
        W2 = sb(g, st, "sW2", [64, 8, 64])
        Y0 = sb(g, st, "sY0", [64, 8, 64])
        D0 = sb(g, st, "sD0", [64, 8, 64])
        U = sb(g, st, "sU", [64, 8, 64])
        Yo = [sb(g, st, "sYo", [64, 512]) for _ in range(2)]
        tmp = sb(g, st, "stmp", [64, 8, 64])
        it = 0
        for d in range(2):
            if d == 0:
                order = list(range(nch))
            else:
                order = list(range(nch_c - 1, -1, -1)) + list(range(nch - 1, nch_c - 1, -1))
            tri = g.tri[0:64, d, :]
            mk = g.mk[:, d, :]
            nmts = g.nmts[0:64, d, :]
            for c in order:
                it += 1
                q = qin[it % 2]
                vv = v2[it % 2]
                rows = slice(c * C, (c + 1) * C)
                for qi in range(6):
                    S.dma("sp", q[qi][:, :], g.RW[d, qi, rows, :], writes=[q[qi].b])
                S.dma("sp", vv[64:128, :], g.RW[d, RQ_V, rows, :], writes=[vv.b])
                r_, lw, k_, v_, kk, kka = q
                psL = next_psum(g)
                mm(psL[0:64, :], psL.b, tri, lw[:, :], [g.tri.b, lw.b])
                psE = next_psum(g)
                mm(psE[0:64, :], psE.b, g.ones64[0:64, :], lw[:, :], [g.ones64.b, lw.b])
                psP = next_psum(g)
                for h in range(8):
                    mm(psP[0:64, h:h + 1], psP.b, lw[:, h * 64:(h + 1) * 64], g.ones64[0:64, 0:1], [lw.b, g.ones64.b])
                S.op("act", lambda e: e.activation(out=pCT[:, :], in_=psP[0:64, 0:8], func=AF.Exp), reads=[psP.b], writes=[pCT.b])
                S.op("act", lambda e: e.activation(out=W["eL"][:, :], in_=psL[0:64, :], func=AF.Exp), reads=[psL.b], writes=[W["eL"].b])
                S.op("act", lambda e: e.activation(out=W["enL"][:, :], in_=psL[0:64, :], func=AF.Exp, scale=-1.0), reads=[psL.b], writes=[W["enL"].b])
                S.op("dve", lambda e: e.tensor_tensor(out=W["eLx"][:, :], in0=psL[0:64, :], in1=lw[:, :], op=ALU.subtract), reads=[psL.b, lw.b], writes=[W["eLx"].b])
                S.op("act", lambda e: e.activation(out=W["eLx"][:, :], in_=W["eLx"][:, :], func=AF.Exp), reads=[W["eLx"].b], writes=[W["eLx"].b])
                copy_op(g, "dve", W["Lsb"][:, :], psL[0:64, :], [psL.b], [W["Lsb"].b])
                S.op("dve", lambda e: e.tensor_tensor(out=W["eEnd"][:, :], in0=psE[0:64, :], in1=W["Lsb"][:, :], op=ALU.subtract),
                     reads=[psE.b, W["Lsb"].b], writes=[W["eEnd"].b])
                S.op("act", lambda e: e.activation(out=W["eEnd"][:, :], in_=W["eEnd"][:, :], func=AF.Exp), reads=[W["eEnd"].b], writes=[W["eEnd"].b])
                for (o, a, b) in [("Rt", r_, "eL"), ("Kt", kk, "eLx"), ("Kk", k_, "enL"), ("Ak", kka, "enL"), ("Kh", k_, "eEnd"), ("Ah", kka, "eEnd")]:
                    S.op("dve", lambda e: e.tensor_tensor(out=W[o][:, :], in0=a[:, :], in1=W[b][:, :], op=ALU.mult), reads=[a.b, W[b].b], writes=[W[o].b])
                for (XT, n0, n1) in [(XT1, "Ak", "Kk"), (XT2, "Kt", "Rt")]:
                    for hg in range(2):
                        ps = next_psum(g)
                        for hh in range(4):
                            h = hg * 4 + hh
                            for j, nm in enumerate([n0, n1]):
                                S.op("pe", lambda e: e.transpose(ps[0:64, hh * 128 + j * 64:hh * 128 + (j + 1) * 64], W[nm][:, h * 64:(h + 1) * 64], ident64),
                                     reads=[W[nm].b, g.ident.b], writes=[ps.b])
                        copy_op(g, evac_engine(g), XT[:, hg * 4:(hg + 1) * 4, :], ps[0:64, :].rearrange("p (h x) -> p h x", h=4), [ps.b], [XT.b])
                for hg in range(2):
                    ps = next_psum(g)
                    for hh in range(4):
                        h = hg * 4 + hh
                        mm(ps[:, hh * 128:(hh + 1) * 128], ps.b, XT1[:, h, :], XT2[:, h, :], [XT1.b, XT2.b])
                    S.op("dve", lambda e: e.tensor_tensor(out=PRs[:, hg * 4:(hg + 1) * 4, :], in0=ps[:, :].rearrange("p (h x) -> p h x", h=4),
                                                          in1=mk.unsqueeze(1).broadcast_to([128, 4, 128]), op=ALU.mult),
                         reads=[ps.b, g.mk.b], writes=[PRs.b])
                ps = next_psum(g)
                for h in range(8):
                    mm(ps[0:64, h * 64:(h + 1) * 64], ps.b, XT2[:, h, 0:64], XT1[:, h, 0:64], [XT1.b, XT2.b])
                N, NT_, Q = Nn[0], NTt[0], Qq[0]
                S.op("dve", lambda e: e.tensor_tensor(out=NT_[:, :, :], in0=ps[0:64, :].rearrange("p (h x) -> p h x", h=8),
                                                      in1=nmts.unsqueeze(1).broadcast_to([64, 8, 64]), op=ALU.mult),
                     reads=[ps.b, g.nmts.b], writes=[NT_.b])
                S.op("dve", lambda e: e.tensor_scalar(out=N[:, :, :], in0=PRs[0:64, :, 0:64], scalar1=-1.0, scalar2=None, op0=ALU.mult), reads=[PRs.b], writes=[N.b])
                S.op("dve", lambda e: e.tensor_tensor(out=Q[:, :, :], in0=N[:, :, :], in1=ident64.unsqueeze(1).broadcast_to([64, 8, 64]), op=ALU.add),
                     reads=[N.b, g.ident.b], writes=[Q.b])
                cur = 0
                for lev in range(5):
                    N, NT_, Q = Nn[cur], NTt[cur], Qq[cur]
                    N2, NT2, Q2 = Nn[1 - cur], NTt[1 - cur], Qq[1 - cur]
                    psn = next_psum(g)
                    pst = next_psum(g)
                    for h in range(8):
                        if lev < 4:
                            mm(psn[0:64, h * 64:(h + 1) * 64], psn.b, NT_[:, h, :], N[:, h, :], [NT_.b, N.b])
                        mm(pst[0:64, h * 64:(h + 1) * 64], pst.b, N[:, h, :], NT_[:, h, :], [NT_.b, N.b])
                    if lev < 4:
                        copy_op(g, "act", N2[:, :, :], psn[0:64, :].rearrange("p (h x) -> p h x", h=8), [psn.b], [N2.b])
                    copy_op(g, "dve", NT2[:, :, :], pst[0:64, :].rearrange("p (h x) -> p h x", h=8), [pst.b], [NT2.b])
                    psq = next_psum(g)
                    for h in range(8):
                        mm(psq[0:64, h * 64:(h + 1) * 64], psq.b, NT2[:, h, :], Q[:, h, :], [NT2.b, Q.b])
                    S.op("dve", lambda e: e.tensor_tensor(out=Q2[:, :, :], in0=psq[0:64, :].rearrange("p (h x) -> p h x", h=8), in1=Q[:, :, :], op=ALU.add),
                         reads=[psq.b, Q.b], writes=[Q2.b])
                    cur = 1 - cur
                Q = Qq[cur]
                ps1 = next_psum(g)
                ps2 = next_psum(g)
                ps3 = next_psum(g)
                ps4 = next_psum(g)
                for h in range(8):
                    hs = slice(h * 64, (h + 1) * 64)
                    mm(ps1[0:64, hs], ps1.b, W["Kt"][:, hs], Q[:, h, :], [W["Kt"].b, Q.b])
                    mm(ps2[0:64, hs], ps2.b, PRs[64:128, h, 0:64], vv[64:128, hs], [PRs.b, vv.b])
                    mm(ps3[0:64, hs], ps3.b, PRs[64:128, h, 64:128], vv[64:128, hs], [PRs.b, vv.b])
                    mm(ps4[0:64, hs], ps4.b, W["Kh"][:, hs], v_[:, hs], [W["Kh"].b, v_.b])
                copy_op(g, "act", W1T[:, :, :], ps1[0:64, :].rearrange("p (h x) -> p h x", h=8), [ps1.b], [W1T.b])
                copy_op(g, "dve", MV[:, :, :], ps2[0:64, :].rearrange("p (h x) -> p h x", h=8), [ps2.b], [MV.b])
                copy_op(g, "act", Y0[:, :, :], ps3[0:64, :].rearrange("p (h x) -> p h x", h=8), [ps3.b], [Y0.b])
                copy_op(g, "dve", D0[:, :, :], ps4[0:64, :].rearrange("p (h x) -> p h x", h=8), [ps4.b], [D0.b])
                ps5 = next_psum(g)
                for h in range(8):
                    mm(ps5[0:64, h * 64:(h + 1) * 64], ps5.b, Q[:, h, :], MV[:, h, :], [Q.b, MV.b])
                copy_op(g, "act", W2[:, :, :], ps5[0:64, :].rearrange("p (h x) -> p h x", h=8), [ps5.b], [W2.b])
                st_ = ST[d]
                psu = next_psum(g)
                for h in range(8):
                    mm(psu[0:64, h * 64:(h + 1) * 64], psu.b, W1T[:, h, :], st_[:, h, :], [W1T.b, st_.b])
                S.op("dve", lambda e: e.scalar_tensor_tensor(out=U[:, :, :], in0=psu[0:64, :].rearrange("p (h x) -> p h x", h=8), scalar=-1.0,
                                                             in1=W2[:, :, :], op0=ALU.mult, op1=ALU.subtract),
                     reads=[psu.b, W2.b], writes=[U.b])
                psy = next_psum(g)
                for h in range(8):
                    hs = slice(h * 64, (h + 1) * 64)
                    mm(psy[0:64, hs], psy.b, XT2[:, h, 64:128], st_[:, h, :], [XT2.b, st_.b], start=True, stop=False)
                    mm(psy[0:64, hs], psy.b, PRs[0:64, h, 64:128], U[:, h, :], [PRs.b, U.b], start=False, stop=True)
                yo = Yo[it % 2]
                S.op("dve", lambda e: e.tensor_tensor(out=yo[:, :], in0=psy[0:64, :], in1=Y0[:, :, :].rearrange("p h x -> p (h x)"), op=ALU.add),
                     reads=[psy.b, Y0.b], writes=[yo.b])
                S.dma("pool", g.YR[d, rows, :], yo[:, :], reads=[yo.b])
                psd = next_psum(g)
                for h in range(8):
                    hs = slice(h * 64, (h + 1) * 64)
                    mm(psd[0:64, hs], psd.b, W["Ah"][:, hs], U[:, h, :], [W["Ah"].b, U.b])
                S.op("dve", lambda e: e.tensor_tensor(out=tmp[:, :, :], in0=st_[:, :, :], in1=pCT[:, 0:8].unsqueeze(2).broadcast_to([64, 8, 64]), op=ALU.mult),
                     reads=[st_.b, pCT.b], writes=[tmp.b])
                S.op("dve", lambda e: e.tensor_tensor(out=tmp[:, :, :], in0=tmp[:, :, :], in1=D0[:, :, :], op=ALU.add), reads=[tmp.b, D0.b], writes=[tmp.b])
                S.op("dve", lambda e: e.tensor_tensor(out=st_[:, :, :], in0=psd[0:64, :].rearrange("p (h x) -> p h x", h=8), in1=tmp[:, :, :], op=ALU.add),
                     reads=[psd.b, tmp.b], writes=[st_.b])
    S.barrier()


def stage_rwkv_out(g, l):
    S = g.S
    with ExitStack() as st:
        lw_ = sb(g, st, "olw", [128, 512])
        lb_ = sb(g, st, "olb", [128, 512])
        rk_ = sb(g, st, "ork", [128, 512])
        bcast_row(g, "sp", lw_[:, :], lw_.b, g.rwkv_lnx_w[l, :])
        bcast_row(g, "sp", lb_[:, :], lb_.b, g.rwkv_lnx_b[l, :])
        bcast_row(g, "sp", rk_[:, :], rk_.b, g.rwkv_r_k[l].rearrange("h k -> (h k)"))
        ins = [[sb(g, st, "oin", [128, 512]) for _ in range(5)] for _ in range(2)]
        t1 = sb(g, st, "ot1", [128, 512])
        t2 = sb(g, st, "ot2", [128, 512])
        acc = [sb(g, st, "oacc", [128, 512]) for _ in range(2)]
        ss = sb(g, st, "oss", [128, 8])
        s2 = sb(g, st, "os2", [128, 8])
        h8 = lambda ap: ap.rearrange("p (h k) -> p h k", h=8)
        b8 = lambda t_: t_[:, 0:8].unsqueeze(2).broadcast_to([128, 8, 64])
        it = 0
        for ti in range(g.NT):
            rows = slice(ti * 128, (ti + 1) * 128)
            a_ = acc[ti % 2]
            for d in range(2):
                it += 1
                y, r_, k_, v_, g_ = ins[it % 2]
                S.dma("sp", y[:, :], g.YR[d, rows, :], writes=[y.b])
                S.dma("sp", r_[:, :], g.RW[d, RQ_R, rows, :], writes=[r_.b])
                S.dma("sp", k_[:, :], g.RW[d, RQ_K, rows, :], writes=[k_.b])
                S.dma("sp", v_[:, :], g.RW[d, RQ_V, rows, :], writes=[v_.b])
                S.dma("sp", g_[:, :], g.RG[d, rows, :], writes=[g_.b])
                S.op("dve", lambda e: e.tensor_reduce(out=ss[:, 0:8], in_=h8(y[:, :]), axis=AX.X, op=ALU.add), reads=[y.b], writes=[ss.b])
                S.op("dve", lambda e: e.tensor_scalar(out=ss[:, 0:8], in0=ss[:, 0:8], scalar1=-1.0 / 64, scalar2=None, op0=ALU.mult), reads=[ss.b], writes=[ss.b])
                S.op("dve", lambda e: e.tensor_tensor(out=h8(t1[:, :]), in0=h8(y[:, :]), in1=b8(ss), op=ALU.add), reads=[y.b, ss.b], writes=[t1.b])
                S.op("act", lambda e: e.activation(out=t2[:, :], in_=t1[:, :], func=AF.Square), reads=[t1.b], writes=[t2.b])
                S.op("dve", lambda e: e.tensor_reduce(out=s2[:, 0:8], in_=h8(t2[:, :]), axis=AX.X, op=ALU.add), reads=[t2.b], writes=[s2.b])
                S.op("dve", lambda e: e.tensor_scalar(out=s2[:, 0:8], in0=s2[:, 0:8], scalar1=1.0 / 64, scalar2=LNX_EPS, op0=ALU.mult, op1=ALU.add),
                     reads=[s2.b], writes=[s2.b])
                S.op("act", lambda e: e.activation(out=s2[:, 0:8], in_=s2[:, 0:8], func=AF.Sqrt), reads=[s2.b], writes=[s2.b])
                S.op("dve", lambda e: e.reciprocal(out=s2[:, 0:8], in_=s2[:, 0:8]), reads=[s2.b], writes=[s2.b])
                S.op("dve", lambda e: e.tensor_tensor(out=h8(t1[:, :]), in0=h8(t1[:, :]), in1=b8(s2), op=ALU.mult), reads=[t1.b, s2.b], writes=[t1.b])
                S.op("dve", lambda e: e.tensor_tensor(out=t1[:, :], in0=t1[:, :], in1=lw_[:, :], op=ALU.mult), reads=[t1.b, lw_.b], writes=[t1.b])
                S.op("dve", lambda e: e.tensor_tensor(out=t1[:, :], in0=t1[:, :], in1=lb_[:, :], op=ALU.add), reads=[t1.b, lb_.b], writes=[t1.b])
                S.op("dve", lambda e: e.tensor_tensor(out=t2[:, :], in0=r_[:, :], in1=k_[:, :], op=ALU.mult), reads=[r_.b, k_.b], writes=[t2.b])
                S.op("dve", lambda e: e.tensor_tensor(out=t2[:, :], in0=t2[:, :], in1=rk_[:, :], op=ALU.mult), reads=[t2.b, rk_.b], writes=[t2.b])
                S.op("dve", lambda e: e.tensor_reduce(out=ss[:, 0:8], in_=h8(t2[:, :]), axis=AX.X, op=ALU.add), reads=[t2.b], writes=[ss.b])
                S.op("dve", lambda e: e.tensor_tensor(out=h8(t2[:, :]), in0=h8(v_[:, :]), in1=b8(ss), op=ALU.mult), reads=[v_.b, ss.b], writes=[t2.b])
                S.op("dve", lambda e: e.tensor_tensor(out=t1[:, :], in0=t1[:, :], in1=t2[:, :], op=ALU.add), reads=[t1.b, t2.b], writes=[t1.b])
                if d == 0:
                    S.op("dve", lambda e: e.tensor_tensor(out=a_[:, :], in0=t1[:, :], in1=g_[:, :], op=ALU.mult), reads=[t1.b, g_.b], writes=[a_.b])
                else:
                    S.op("dve", lambda e: e.tensor_tensor(out=t1[:, :], in0=t1[:, :], in1=g_[:, :], op=ALU.mult), reads=[t1.b, g_.b], writes=[t1.b])
                    S.op("dve", lambda e: e.tensor_tensor(out=a_[:, :], in0=a_[:, :], in1=t1[:, :], op=ALU.add), reads=[a_.b, t1.b], writes=[a_.b])
            S.dma("pool", g.YMIX[rows, 1024:1536], a_[:, :], reads=[a_.b])
    S.barrier()


def stage_merge(g, l):
    S = g.S
    with ExitStack() as st:
        gt = [[sb(g, st, "mgt", [128, 512]) for _ in range(4)] for _ in range(2)]
        t1 = sb(g, st, "mt1", [128, 512])
        cnt = [0]

        def epi(pss, o, ti, n0, nb):
            gs = gt[cnt[0] % 2]
            cnt[0] += 1
            for i in range(4):
                c0 = O_GATE + i * D + n0
                S.dma("sp", gs[i][:, 0:nb], g.Z[ti * 128:(ti + 1) * 128, c0:c0 + nb], writes=[gs[i].b])
                S.op("act", lambda e: e.activation(out=gs[i][:, 0:nb], in_=gs[i][:, 0:nb], func=AF.Sigmoid), reads=[gs[i].b], writes=[gs[i].b])
            S.op("dve", lambda e: e.tensor_tensor(out=o[:, 0:nb], in0=pss[0][:, 0:nb], in1=gs[0][:, 0:nb], op=ALU.mult), reads=[pss[0].b, gs[0].b], writes=[o.b])
            for i in range(1, 4):
                S.op("dve", lambda e: e.tensor_tensor(out=t1[:, 0:nb], in0=pss[i][:, 0:nb], in1=gs[i][:, 0:nb], op=ALU.mult), reads=[pss[i].b, gs[i].b], writes=[t1.b])
                S.op("dve", lambda e: e.tensor_tensor(out=o[:, 0:nb], in0=o[:, 0:nb], in1=t1[:, 0:nb], op=ALU.add), reads=[o.b, t1.b], writes=[o.b])
            S.dma("sp", g.MERGED[ti * 128:(ti + 1) * 128, n0:n0 + nb], o[:, 0:nb], reads=[o.b])
        linear(g, g.YMIX, g.w_branch[l].rearrange("b k n -> (b k) n"), None, g.T, D, D, G=12, NB=512, epi=epi,
               segs=[(0, 4), (4, 8), (8, 12), (12, 16)])


def stage_conv(g, l):
    S = g.S
    NT, NTC = g.NT, g.NTC
    with ExitStack() as st:
        cw = [sb(g, st, "ccw", [128, 3, 512]) for _ in range(2)]
        cbias = [sb(g, st, "ccb", [128, 512]) for _ in range(2)]
        ins = [[sb(g, st, "cin", [128, 512]) for _ in range(4)] for _ in range(2)]
        acc = sb(g, st, "cacc", [128, 512])
        t1 = sb(g, st, "ct1", [128, 512])
        outs = [sb(g, st, "cout", [128, 512]) for _ in range(2)]
        it = 0
        for bi, n0 in enumerate(range(0, DFF, 512)):
            w_ = cw[bi % 2]
            b_ = cbias[bi % 2]
            for j in range(3):
                bcast_row(g, "sp", w_[:, j, :], w_.b, g.ffn_conv_w[l, j, n0:n0 + 512])
            bcast_row(g, "sp", b_[:, :], b_.b, g.ffn_conv_b[l, n0:n0 + 512])
            for ti in range(NT):
                it += 1
                ap_, ac_, an_, bb = ins[it % 2]
                r0 = ti * 128
                S.dma("sp", ac_[:, :], g.AB[r0:r0 + 128, n0:n0 + 512], writes=[ac_.b])
                S.dma("sp", bb[:, :], g.AB[r0:r0 + 128, DFF + n0:DFF + n0 + 512], writes=[bb.b])
                S.dma("sp", ap_[1:128, :], g.AB[r0:r0 + 127, n0:n0 + 512], writes=[ap_.b])
                src = g.zrow[0:1, 0:512] if ti in (0, NTC) else g.AB[r0 - 1:r0, n0:n0 + 512]
                S.dma("sp", ap_[0:1, :], src, writes=[ap_.b])
                S.dma("sp", an_[0:127, :], g.AB[r0 + 1:r0 + 128, n0:n0 + 512], writes=[an_.b])
                src = g.zrow[0:1, 0:512] if ti in (NTC - 1, NT - 1) else g.AB[r0 + 128:r0 + 129, n0:n0 + 512]
                S.dma("sp", an_[127:128, :], src, writes=[an_.b])
                S.op("dve", lambda e: e.tensor_tensor(out=acc[:, :], in0=ap_[:, :], in1=w_[:, 0, :], op=ALU.mult), reads=[ap_.b, w_.b], writes=[acc.b])
                S.op("dve", lambda e: e.tensor_tensor(out=t1[:, :], in0=ac_[:, :], in1=w_[:, 1, :], op=ALU.mult), reads=[ac_.b, w_.b], writes=[t1.b])
                S.op("dve", lambda e: e.tensor_tensor(out=acc[:, :], in0=acc[:, :], in1=t1[:, :], op=ALU.add), reads=[acc.b, t1.b], writes=[acc.b])
                S.op("dve", lambda e: e.tensor_tensor(out=t1[:, :], in0=an_[:, :], in1=w_[:, 2, :], op=ALU.mult), reads=[an_.b, w_.b], writes=[t1.b])
                S.op("dve", lambda e: e.tensor_tensor(out=acc[:, :], in0=acc[:, :], in1=t1[:, :], op=ALU.add), reads=[acc.b, t1.b], writes=[acc.b])
                S.op("dve", lambda e: e.tensor_tensor(out=acc[:, :], in0=acc[:, :], in1=b_[:, :], op=ALU.add), reads=[acc.b, b_.b], writes=[acc.b])
                S.op("act", lambda e: e.activation(out=t1[:, :], in_=acc[:, :], func=AF.Square), reads=[acc.b], writes=[t1.b])
                S.op("dve", lambda e: e.tensor_scalar(out=t1[:, :], in0=t1[:, :], scalar1=0.044715, scalar2=1.0, op0=ALU.mult, op1=ALU.add), reads=[t1.b], writes=[t1.b])
                S.op("dve", lambda e: e.tensor_tensor(out=t1[:, :], in0=t1[:, :], in1=acc[:, :], op=ALU.mult), reads=[t1.b, acc.b], writes=[t1.b])
                S.op("act", lambda e: e.activation(out=t1[:, :], in_=t1[:, :], func=AF.Sigmoid, scale=1.5957691216057308), reads=[t1.b], writes=[t1.b])
                S.op("dve", lambda e: e.tensor_tensor(out=t1[:, :], in0=t1[:, :], in1=acc[:, :], op=ALU.mult), reads=[t1.b, acc.b], writes=[t1.b])
                o = outs[it % 2]
                S.op("dve", lambda e: e.tensor_tensor(out=o[:, :], in0=t1[:, :], in1=bb[:, :], op=ALU.mult), reads=[t1.b, bb.b], writes=[o.b])
                S.dma("pool", g.GG[r0:r0 + 128, n0:n0 + 512], o[:, :], reads=[o.b])
    S.barrier()


def build(cfg):
    TL, LC, DEPTH = cfg["TL"], cfg["LC"], cfg["DEPTH"]
    T = TL + LC
    nc = bass.Bass("TRN2", target_bir_lowering=False)
    g = Ctx()
    g.nc = nc
    g.uid = 0
    g.ev = 0
    g.pi = 0
    g.T, g.TL, g.LC, g.NT, g.NTC = T, TL, LC, T // 128, LC // 128
    g.S = S = Sync(nc)

    def din(name, shape):
        return nc.dram_tensor(name, list(shape), F32, kind="ExternalInput").ap()

    def dscr(name, shape):
        return nc.dram_tensor(name, list(shape), F32, kind="Internal").ap()

    L = DEPTH
    g.x_in = din("x", [TL, D])
    g.ctx_in = din("ctx", [LC, D])
    g.cvec = din("cvec", [2, D])
    g.ada_w = din("ada_w", [L, D, 6 * D])
    g.ada_b = din("ada_b", [L, 6 * D])
    g.norm1_g = din("norm1_g", [L, D])
    g.norm2_g = din("norm2_g", [L, D])
    g.w_in = din("w_in", [L, D, IN_W])
    g.identd = din("ident", [128, 128])
    g.ga_q_norm = din("ga_q_norm", [L, 128])
    g.ga_k_norm = din("ga_k_norm", [L, 128])
    g.wa_q_norm = din("wa_q_norm", [L, 128])
    g.wa_k_norm = din("wa_k_norm", [L, 128])
    g.wa_sink = din("wa_sink", [L, 4])
    g.mla_cq_norm = din("mla_cq_norm", [L, 384])
    g.mla_ckv_norm = din("mla_ckv_norm", [L, 512])
    g.mla_w_uq = din("mla_w_uq", [L, 384, 768])
    g.mla_w_ukv = din("mla_w_ukv", [L, 512, 1024])
    g.mla_qn_norm = din("mla_qn_norm", [L, 128])
    g.mla_qr_norm = din("mla_qr_norm", [L, 64])
    g.mla_kn_norm = din("mla_kn_norm", [L, 128])
    g.mla_kr_norm = din("mla_kr_norm", [L, 64])
    g.rwkv_mu = din("rwkv_mu", [L, 2, 1984])
    g.rwkv_w0 = din("rwkv_w0", [L, 2, 512])
    g.rwkv_w2 = din("rwkv_w2", [L, 2, 96, 512])
    g.rwkv_a0 = din("rwkv_a0", [L, 2, 512])
    g.rwkv_a2 = din("rwkv_a2", [L, 2, 96, 512])
    g.rwkv_g2 = din("rwkv_g2", [L, 256, 512])
    g.rwkv_k_k = din("rwkv_k_k", [L, 512])
    g.rwkv_k_a = din("rwkv_k_a", [L, 512])
    g.rwkv_r_k = din("rwkv_r_k", [L, 8, 64])
    g.rwkv_lnx_w = din("rwkv_lnx_w", [L, 512])
    g.rwkv_lnx_b = din("rwkv_lnx_b", [L, 512])
    g.w_branch = din("w_branch", [L, 4, 512, D])
    g.w_out = din("w_out", [L, D, D])
    g.ffn_up = din("ffn_up", [L, D, 2 * DFF])
    g.ffn_conv_w = din("ffn_conv_w", [L, 3, DFF])
    g.ffn_conv_b = din("ffn_conv_b", [L, DFF])
    g.ffn_down = din("ffn_down", [L, DFF, D])
    g.trid = din("tri", [64, 2, 64])
    g.mkd = din("mk", [128, 2, 128])
    g.nmtsd = din("nmts", [64, 2, 64])
    g.zrow = din("zrow", [1, NRW])
    g.ropeA = din("ropeA", [2, TL, 128])
    g.ropeM = din("ropeM", [2, TL, 64])
    g.wmaskd = din("wmask", [128, 2, 128])
    g.y_out = nc.dram_tensor("y", [TL, D], F32, kind="ExternalOutput").ap()
    dbg = cfg.get("debug")
    g.XS = dscr("XS", [T, D])
    g.MODB = dscr("MODB", [2, 128, 6 * D])
    g.Z = dscr("Z", [T, IN_W])
    g.QKT = dscr("QKT", [12, 128, T])
    g.YMIX = dscr("YMIX", [T, D])
    g.RW = dscr("RW", [2, 6, T, 512])
    g.RG = dscr("RG", [2, T, 512])
    g.YR = dscr("YR", [2, T, 512])
    g.MERGED = dscr("MERGED", [T, D])
    g.AB = dscr("AB", [T, 2 * DFF])
    g.GG = dscr("GG", [T, DFF])
    g.QML = dscr("QML", [T, 768])
    g.KVML = dscr("KVML", [T, 1024])
    g.MQN = dscr("MQN", [4, 128, T])
    g.MQR = dscr("MQR", [4, 64, T])
    g.MKN = dscr("MKN", [4, 128, T])
    g.MKR = dscr("MKR", [1, 64, T])
    if dbg:
        g.dbg_z = nc.dram_tensor("dbg_z", [T, IN_W], F32, kind="ExternalOutput").ap()
        g.dbg_y = nc.dram_tensor("dbg_y", [T, D], F32, kind="ExternalOutput").ap()
        g.dbg_qkt = nc.dram_tensor("dbg_qkt", [12, 128, T], F32, kind="ExternalOutput").ap()
        g.dbg_xs = nc.dram_tensor("dbg_xs", [T, D], F32, kind="ExternalOutput").ap()

    with ExitStack() as es:
        g.psums = [Tl(es.enter_context(nc.psum_tensor("ps%d" % i, [128, 512], F32)), "ps%d" % i) for i in range(8)]
        g.ident = sb(g, es, "ident", [128, 128])
        S.dma("sp", g.ident[:, :], g.identd[:, :], writes=[g.ident.b])
        g.tri = sb(g, es, "tri", [64, 2, 64])
        g.mk = sb(g, es, "mk", [128, 2, 128])
        g.nmts = sb(g, es, "nmts", [64, 2, 64])
        g.ones64 = sb(g, es, "ones64", [64, 64])
        S.dma("sp", g.tri[:, :, :], g.trid[:, :, :], writes=[g.tri.b])
        S.dma("sp", g.mk[:, :, :], g.mkd[:, :, :], writes=[g.mk.b])
        S.dma("sp", g.nmts[:, :, :], g.nmtsd[:, :, :], writes=[g.nmts.b])
        S.op("dve", lambda e: e.memset(g.ones64[:, :], 1.0), writes=[g.ones64.b])
        g.wmask = sb(g, es, "wmask", [128, 2, 128])
        S.dma("sp", g.wmask[:, :, :], g.wmaskd[:, :, :], writes=[g.wmask.b])
        S.dma("sp", g.XS[0:LC, :], g.ctx_in[:, :])
        S.dma("sp", g.XS[LC:T, :], g.x_in[:, :])
        g.cb = [sb(g, es, "cb", [128, 16, 128]) for _ in range(2)]
        with ExitStack() as st:
            cT = [sb(g, st, "cT", [128, 16]) for _ in range(2)]
            ones = sb(g, st, "ones", [128, 128])
            S.op("dve", lambda e: e.memset(ones[:, :], 1.0), writes=[ones.b])
            for r in range(2):
                S.dma("sp", cT[r][:, :], g.cvec[r, :].rearrange("(kc p) -> p kc", p=128), writes=[cT[r].b], allow_slow_non_contiguous=True)
                S.op("act", lambda e: e.activation(out=cT[r][:, :], in_=cT[r][:, :], func=AF.Silu), reads=[cT[r].b], writes=[cT[r].b])
                for kc in range(16):
                    S.op("dve", lambda e: e.tensor_scalar(out=g.cb[r][:, kc, :], in0=ones[:, :], scalar1=cT[r][:, kc:kc + 1], scalar2=None, op0=ALU.mult),
                         reads=[ones.b, cT[r].b], writes=[g.cb[r].b])
            S.barrier()
        for l in range(L):
            stage_mod(g, l)
            with ExitStack() as st:
                pro = make_norm_pro(g, st, l, g.norm1_g[l, :], D, 0)
                linear(g, g.XS, g.w_in[l], g.Z, T, D, IN_W, G=12, NB=512, pro=pro)
            if cfg.get("stop") == "z":
                break
            stage_attn_prep(g, l)
            stage_attn_gawa(g, l)
            if cfg.get("stop") == "gawa":
                break
            if cfg.get("stop") != "rwkv":
                stage_mla(g, l)
            if cfg.get("stop") == "mla":
                break
            stage_rwkv_prep(g, l)
            stage_rwkv_scan(g, l)
            stage_rwkv_out(g, l)
            if cfg.get("stop") == "rwkv":
                break
            stage_merge(g, l)
            with ExitStack() as st:
                epi = make_resid_epi(g, st, 2 * D)
                linear(g, g.MERGED, g.w_out[l], None, T, D, D, G=12, NB=512, epi=epi)
            if cfg.get("stop") == "attn":
                break
            with ExitStack() as st:
                pro = make_norm_pro(g, st, l, g.norm2_g[l, :], 4 * D, 3 * D)
                linear(g, g.XS, g.ffn_up[l], g.AB, T, D, 2 * DFF, G=12, NB=512, pro=pro)
            stage_conv(g, l)
            with ExitStack() as st:
                epi = make_resid_epi(g, st, 5 * D)
                linear(g, g.GG, g.ffn_down[l], None, T, DFF, D, G=6, NB=256, epi=epi)
        if dbg:
            S.dma("sp", g.dbg_z[:, :], g.Z[:, :])
            S.dma("sp", g.dbg_y[:, :], g.YMIX[:, :])
            S.dma("sp", g.dbg_qkt[:, :, :], g.QKT[:, :, :])
            S.dma("sp", g.dbg_xs[:, :], g.XS[:, :])
        S.dma("sp", g.y_out[:, :], g.XS[LC:T, :])
        S.barrier()
    return nc, g


_CACHE = {}


def rope_table(n_tokens, rot_dim):
    rows = n_tokens // GRID_W
    row = np.repeat(np.arange(rows), GRID_W).astype(np.float32)
    col = np.tile(np.arange(GRID_W), rows).astype(np.float32)
    quarter = rot_dim // 4
    inv_freq = (10000.0 ** (-np.arange(quarter, dtype=np.float32) / quarter)).astype(np.float32)
    ang_r = row[:, None] * inv_freq
    ang_c = col[:, None] * inv_freq
    ang = np.concatenate([ang_r, ang_r, ang_c, ang_c], axis=-1).astype(np.float32)
    sign = np.concatenate([-np.ones(quarter), np.ones(quarter), -np.ones(quarter), np.ones(quarter)]).astype(np.float32)
    return np.stack([np.cos(ang), np.sin(ang) * sign]).astype(np.float32)


def const_tables(TL):
    idx = np.arange(128)
    m_lo = (idx[None, :] <= idx[:, None]).astype(np.float32)
    m_hi = (idx[:, None] <= idx[None, :]).astype(np.float32)
    i64 = np.arange(64)
    tri = np.stack([(i64[:, None] <= i64[None, :]), (i64[:, None] >= i64[None, :])]).astype(np.float32)
    strict = tri - np.eye(64, dtype=np.float32)[None]
    half = np.concatenate([strict, tri], axis=2)
    mk = np.concatenate([half, half], axis=1)
    nmts = -np.transpose(strict, (0, 2, 1))
    return {
        "tri": np.ascontiguousarray(np.transpose(tri, (1, 0, 2))),
        "mk": np.ascontiguousarray(np.transpose(mk, (1, 0, 2))),
        "nmts": np.ascontiguousarray(np.transpose(nmts, (1, 0, 2))),
        "zrow": np.zeros((1, NRW), np.float32),
        "ropeA": rope_table(TL, 128),
        "ropeM": rope_table(TL, 64),
        "wmask": np.ascontiguousarray(np.stack([m_lo, m_hi], axis=1)),
    }


def make_inputs_for_core(inputs, b, L):
    f = lambda a: np.ascontiguousarray(np.asarray(a, dtype=np.float32))
    m = {
        "x": f(inputs["x"][b]),
        "ctx": f(inputs["ctx"][b]),
        "cvec": f(np.stack([np.asarray(inputs["c"][b]), np.asarray(inputs["c_ctx"])])),
        "ident": np.eye(128, dtype=np.float32),
    }
    m.update(const_tables(np.asarray(inputs["x"]).shape[1]))
    for k in ["ada_w", "ada_b", "norm1_g", "norm2_g", "w_in", "ga_q_norm", "ga_k_norm", "wa_q_norm", "wa_k_norm", "wa_sink",
              "mla_cq_norm", "mla_ckv_norm", "mla_w_uq", "mla_w_ukv", "mla_qn_norm", "mla_qr_norm", "mla_kn_norm", "mla_kr_norm",
              "rwkv_mu", "rwkv_w0", "rwkv_w2", "rwkv_a0", "rwkv_a2", "rwkv_g2", "rwkv_k_k", "rwkv_k_a", "rwkv_r_k", "rwkv_lnx_w", "rwkv_lnx_b",
              "w_branch", "w_out", "ffn_up", "ffn_conv_w", "ffn_conv_b", "ffn_down"]:
        m[k] = f(inputs[k][:L])
    return m


def kernel(**inputs):
    x = np.asarray(inputs["x"])
    B, TL, _ = x.shape
    LC = np.asarray(inputs["ctx"]).shape[1]
    L = np.asarray(inputs["ada_w"]).shape[0]
    cfg = {"TL": TL, "LC": LC, "DEPTH": L}
    nc, g = build(cfg)
    n = 8
    in_maps = [make_inputs_for_core(inputs, c % B, L) for c in range(n)]
    res = run_bass_kernel_spmd(nc, in_maps, core_ids=list(range(n)))
    return np.stack([res.results[b]["y"] for b in range(B)]).astype(np.float32)
```

```python
from contextlib import ExitStack
import threading
import numpy as np
import concourse.bass as bass
import concourse.mybir as mybir
from concourse.bass_utils import run_bass_kernel_spmd

F32 = mybir.dt.float32
BF16 = mybir.dt.bfloat16
AF = mybir.ActivationFunctionType
ALU = mybir.AluOpType
AX = mybir.AxisListType

D = 2048
GRID_W = 64
EPS = 1e-6
IN_W = 13376
DFF = 5632
O_GAQ, O_GAK, O_GAV, O_WAQ, O_WAK, O_WAV = 0, 512, 768, 1024, 1536, 1792
O_RKVG, O_ZW, O_ZA, O_CQ, O_CKV, O_KR, O_GATE = 2048, 3840, 4032, 4224, 4608, 5120, 5184
LNX_EPS = 64e-5


class Buf:
    __slots__ = ("name", "w", "r")

    def __init__(self, name=""):
        self.name = name
        self.w = None
        self.r = {}


class Sync:
    MAXC = 30000

    def __init__(self, nc, n_dma_sems=48, same_engine_sync=True):
        self.nc = nc
        self.hw = {"pe": nc.tensor, "dve": nc.vector, "act": nc.scalar, "pool": nc.gpsimd, "sp": nc.sync}
        self.gen = {k: 0 for k in self.hw}
        self.sem = {k: nc.alloc_semaphore("sem_%s_0" % k) for k in self.hw}
        self.count = {k: 0 for k in self.hw}
        self.seen = {k: {} for k in self.hw}
        self.dma_sems = [nc.alloc_semaphore("dsem%d" % i) for i in range(n_dma_sems)]
        self.dma_uses = [0] * n_dma_sems
        self.dma_next = 0
        self.same_engine_sync = same_engine_sync
        self.latest = {}
        self._lane = None

    def run_lanes(self, fns):
        n = len(fns)
        if n == 1:
            fns[0]()
            return
        cv = threading.Condition()
        st = {"turn": 0, "alive": [True] * n, "err": None}
        ids = {}

        def nxt(i):
            for k in range(1, n + 1):
                j = (i + k) % n
                if st["alive"][j]:
                    return j
            return -1

        def worker(i):
            ids[threading.get_ident()] = i
            with cv:
                while st["turn"] != i:
                    cv.wait()
            try:
                fns[i]()
            except BaseException as e:
                st["err"] = e
            with cv:
                st["alive"][i] = False
                st["turn"] = nxt(i)
                cv.notify_all()

        self._lane = (cv, st, ids, nxt)
        self.lane_pi = {}
        threads = [threading.Thread(target=worker, args=(i,)) for i in range(n)]
        for t in threads:
            t.start()
        for t in threads:
            t.join()
        self._lane = None
        if st["err"] is not None:
            raise st["err"]

    def lane_index(self):
        if self._lane is None:
            return None
        i = self._lane[2].get(threading.get_ident())
        if i is None:
            return None
        return i, len(self._lane[1]["alive"])

    def lane_yield(self):
        if self._lane is None:
            return
        cv, st, ids, nxt = self._lane
        i = ids.get(threading.get_ident())
        if i is None:
            return
        with cv:
            j = nxt(i)
            if j == i or j < 0:
                return
            st["turn"] = j
            cv.notify_all()
            while st["turn"] != i:
                cv.wait()

    def _wait(self, eng, dep):
        key, sem, val = dep
        if self.seen[eng].get(key, 0) >= val:
            return
        self.hw[eng].wait_ge(sem, val)
        self.seen[eng][key] = val

    @staticmethod
    def _add(deps, d):
        if d is not None and deps.get(d[0], (0, 0, 0))[2] < d[2]:
            deps[d[0]] = d

    def _deps(self, reads, writes):
        deps = {}
        for b in reads:
            self._add(deps, b.w)
        for b in writes:
            self._add(deps, b.w)
            for d in b.r.values():
                self._add(deps, d)
        return deps

    def _mark(self, dep, reads, writes):
        for b in reads:
            b.r[dep[0]] = dep
        for b in writes:
            b.w = dep
            b.r = {}
        self.latest[dep[0]] = dep

    def op(self, eng, fn, reads=(), writes=()):
        for key, d in self._deps(reads, writes).items():
            if isinstance(key, tuple) and key[0] == eng and (eng == "pe" or not self.same_engine_sync):
                continue
            self._wait(eng, d)
        ins = fn(self.hw[eng])
        self.count[eng] += 1
        ins.then_inc(self.sem[eng], 1)
        self._mark(((eng, self.gen[eng]), self.sem[eng], self.count[eng]), reads, writes)
        if self.count[eng] >= self.MAXC:
            self.gen[eng] += 1
            self.sem[eng] = self.nc.alloc_semaphore("sem_%s_%d" % (eng, self.gen[eng]))
            self.count[eng] = 0
        self.lane_yield()
        return ins

    def dma(self, q, out, in_, reads=(), writes=(), **kw):
        i = self.dma_next
        self.dma_next = (self.dma_next + 1) % len(self.dma_sems)
        sem = self.dma_sems[i]
        key = "d%d" % i
        if self.dma_uses[i] > 0:
            self._wait(q, (key, sem, 16 * self.dma_uses[i]))
        for k, d in self._deps(reads, writes).items():
            self._wait(q, d)
        self.dma_uses[i] += 1
        ins = self.hw[q].dma_start(out=out, in_=in_, **kw)
        ins.then_inc(sem, 16)
        self._mark((key, sem, 16 * self.dma_uses[i]), reads, writes)
        self.lane_yield()
        return ins

    def barrier(self):
        for eng in self.hw:
            for dep in list(self.latest.values()):
                key = dep[0]
                if isinstance(key, tuple) and key[0] == "pe" and eng == "pe":
                    continue
                self._wait(eng, dep)


class Tl:
    def __init__(self, t, name):
        self.t = t
        self.b = Buf(name)

    def __getitem__(self, idx):
        return self.t[idx]


class Ctx:
    pass


def sb(g, st, name, shape, dtype=F32):
    g.uid += 1
    nm = "%s_%d" % (name, g.uid)
    return Tl(st.enter_context(g.nc.sbuf_tensor(nm, list(shape), dtype)), nm)


def evac_engine(g):
    g.ev += 1
    return "dve" if g.ev % 2 else "act"


def copy_op(g, eng, out, in_, reads, writes):
    if eng == "act":
        return g.S.op("act", lambda e: e.copy(out=out, in_=in_), reads=reads, writes=writes)
    return g.S.op(eng, lambda e: e.tensor_copy(out=out, in_=in_), reads=reads, writes=writes)


def next_psum(g):
    lane = g.S.lane_index()
    if lane is None:
        g.pi = (g.pi + 1) % len(g.psums)
        return g.psums[g.pi]
    i, n = lane
    lo, hi = i * 8 // n, (i + 1) * 8 // n
    k = g.S.lane_pi.get(i, 0)
    g.S.lane_pi[i] = k + 1
    return g.psums[lo + k % (hi - lo)]


def linear(g, x, w, y, T, K, N, G=8, NB=512, pro=None, epi=None, segs=None, wlist=None):
    S, nc = g.S, g.nc
    KC = K // 128
    assert K % 128 == 0 and T % 128 == 0
    NT = T // 128
    XW = min(K, 2048)
    if segs is None:
        segs = [(0, KC)]
    with ExitStack() as st:
        xt = sb(g, st, "xt", [128, KC, G * 128], BF16)
        xin = [sb(g, st, "xin", [128, XW]) for _ in range(2)]
        wt = [sb(g, st, "wt", [128, KC, NB], BF16) for _ in range(3)]
        ot = [sb(g, st, "ot", [128, NB]) for _ in range(3)]
        nxin = nw = no = 0
        for g0 in range(0, NT, G):
            gn = min(G, NT - g0)
            for gi in range(gn):
                ti = g0 + gi
                for c0 in range(0, K, XW):
                    cw = min(XW, K - c0)
                    xi = xin[nxin % 2]
                    nxin += 1
                    S.dma("sp", xi[:, 0:cw], x[ti * 128:(ti + 1) * 128, c0:c0 + cw], writes=[xi.b])
                    if pro is not None:
                        pro(xi, ti, cw)
                    for k4 in range(0, cw // 128, 4):
                        ps = next_psum(g)
                        nk = min(4, cw // 128 - k4)
                        for j in range(nk):
                            kk = k4 + j
                            S.op("pe", lambda e: e.transpose(ps[:, j * 128:(j + 1) * 128], xi[:, kk * 128:(kk + 1) * 128], g.ident[:]),
                                 reads=[xi.b, g.ident.b], writes=[ps.b])
                        kc0 = c0 // 128 + k4
                        copy_op(g, evac_engine(g), xt[:, kc0:kc0 + nk, gi * 128:(gi + 1) * 128],
                                ps[:, 0:nk * 128].rearrange("p (k t) -> p k t", k=nk), [ps.b], [xt.b])
            for n0 in range(0, N, NB):
                nb = min(NB, N - n0)
                wi = wt[nw % 3]
                nw += 1
                S.dma("pool", wi[:, :, 0:nb], w.rearrange("(kc p) n -> p kc n", p=128)[:, :, n0:n0 + nb], writes=[wi.b])
                for gi in range(gn):
                    ti = g0 + gi
                    pss = []
                    for (k0, k1) in segs:
                        ps = next_psum(g)
                        pss.append(ps)
                        for kc in range(k0, k1):
                            S.op("pe", lambda e: e.matmul(ps[:, 0:nb], lhsT=xt[:, kc, gi * 128:(gi + 1) * 128], rhs=wi[:, kc, 0:nb],
                                                          start=(kc == k0), stop=(kc == k1 - 1)),
                                 reads=[xt.b, wi.b], writes=[ps.b])
                    o = ot[no % 3]
                    no += 1
                    if epi is None:
                        copy_op(g, evac_engine(g), o[:, 0:nb], pss[0][:, 0:nb], [pss[0].b], [o.b])
                        S.dma("sp", y[ti * 128:(ti + 1) * 128, n0:n0 + nb], o[:, 0:nb], reads=[o.b])
                    else:
                        epi(pss, o, ti, n0, nb)
    S.barrier()


def rms_rstd(g, st_tiles, x_ap, n, width, reads_b, sq, ss, rstd, eps=EPS):
    S = g.S
    S.op("act", lambda e: e.activation(out=sq[:, 0:n * width].rearrange("p (n w) -> p n w", n=n), in_=x_ap, func=AF.Square),
         reads=[reads_b], writes=[sq.b])
    S.op("dve", lambda e: e.tensor_reduce(out=ss[:, 0:n], in_=sq[:, 0:n * width].rearrange("p (n w) -> p n w", n=n), axis=AX.X, op=ALU.add),
         reads=[sq.b], writes=[ss.b])
    S.op("dve", lambda e: e.tensor_scalar(out=ss[:, 0:n], in0=ss[:, 0:n], scalar1=1.0 / width, scalar2=eps, op0=ALU.mult, op1=ALU.add),
         reads=[ss.b], writes=[ss.b])
    S.op("act", lambda e: e.activation(out=ss[:, 0:n], in_=ss[:, 0:n], func=AF.Sqrt), reads=[ss.b], writes=[ss.b])
    S.op("dve", lambda e: e.reciprocal(out=rstd[:, 0:n], in_=ss[:, 0:n]), reads=[ss.b], writes=[rstd.b])


def bcast_row(g, q, tile_ap, buf, dram_row_ap):
    g.S.dma(q, tile_ap, dram_row_ap.partition_broadcast(128), writes=[buf])


def stage_mod(g, l):
    S, nc = g.S, g.nc
    with ExitStack() as st:
        wt = [sb(g, st, "mw", [128, 16, 512]) for _ in range(2)]
        bt = [sb(g, st, "mb", [128, 512]) for _ in range(2)]
        ot = [sb(g, st, "mo", [128, 512]) for _ in range(3)]
        no = 0
        for bi, n0 in enumerate(range(0, 6 * D, 512)):
            wi = wt[bi % 2]
            bb = bt[bi % 2]
            S.dma("sp", wi[:, :, :], g.ada_w[l].rearrange("(kc p) n -> p kc n", p=128)[:, :, n0:n0 + 512], writes=[wi.b])
            bcast_row(g, "sp", bb[:, :], bb.b, g.ada_b[l, n0:n0 + 512])
            for r in range(2):
                ps = next_psum(g)
                for kc in range(16):
                    S.op("pe", lambda e: e.matmul(ps[:, :], lhsT=g.cb[r][:, kc, :], rhs=wi[:, kc, :], start=(kc == 0), stop=(kc == 15)),
                         reads=[g.cb[r].b, wi.b], writes=[ps.b])
                o = ot[no % 3]
                no += 1
                S.op("dve", lambda e: e.tensor_tensor(out=o[:, :], in0=ps[:, :], in1=bb[:, :], op=ALU.add), reads=[ps.b, bb.b], writes=[o.b])
                S.dma("pool", g.MODB[r, :, n0:n0 + 512], o[:, :], reads=[o.b])
    S.barrier()


def make_norm_pro(g, st, l, gain_dram, sc_off, sh_off):
    S = g.S
    A = [sb(g, st, "nA", [128, D]) for _ in range(2)]
    B = [sb(g, st, "nB", [128, D]) for _ in range(2)]
    sq = sb(g, st, "nsq", [128, D])
    ss = sb(g, st, "nss", [128, 1])
    rstd = sb(g, st, "nrs", [128, 1])
    with ExitStack() as st2:
        gt = sb(g, st2, "ng", [128, D])
        bcast_row(g, "sp", gt[:, :], gt.b, gain_dram)
        for r in range(2):
            S.dma("sp", A[r][:, :], g.MODB[r, :, sc_off:sc_off + D], writes=[A[r].b])
            S.dma("sp", B[r][:, :], g.MODB[r, :, sh_off:sh_off + D], writes=[B[r].b])
            S.op("dve", lambda e: e.scalar_tensor_tensor(out=A[r][:, :], in0=A[r][:, :], scalar=1.0, in1=gt[:, :], op0=ALU.add, op1=ALU.mult),
                 reads=[A[r].b, gt.b], writes=[A[r].b])
        S.barrier()

    def pro(xi, ti, cw):
        r = 1 if ti < g.NTC else 0
        rms_rstd(g, None, xi[:, 0:D].rearrange("p (n w) -> p n w", n=1), 1, D, xi.b, sq, ss, rstd)
        S.op("dve", lambda e: e.scalar_tensor_tensor(out=xi[:, 0:D], in0=xi[:, 0:D], scalar=rstd[:, 0:1], in1=A[r][:, :], op0=ALU.mult, op1=ALU.mult),
             reads=[xi.b, rstd.b, A[r].b], writes=[xi.b])
        S.op("dve", lambda e: e.tensor_tensor(out=xi[:, 0:D], in0=xi[:, 0:D], in1=B[r][:, :], op=ALU.add), reads=[xi.b, B[r].b], writes=[xi.b])
    return pro


def make_resid_epi(g, st, gate_off):
    S = g.S
    gts = [sb(g, st, "rg", [128, D]) for _ in range(2)]
    for r in range(2):
        S.dma("sp", gts[r][:, :], g.MODB[r, :, gate_off:gate_off + D], writes=[gts[r].b])
    xb = [sb(g, st, "rx", [128, 512]) for _ in range(3)]
    cnt = [0]

    def epi(pss, o, ti, n0, nb):
        r = 1 if ti < g.NTC else 0
        x = xb[cnt[0] % 3]
        cnt[0] += 1
        S.dma("sp", x[:, 0:nb], g.XS[ti * 128:(ti + 1) * 128, n0:n0 + nb], writes=[x.b])
        S.op("dve", lambda e: e.tensor_tensor(out=o[:, 0:nb], in0=pss[0][:, 0:nb], in1=gts[r][:, n0:n0 + nb], op=ALU.mult),
             reads=[pss[0].b, gts[r].b], writes=[o.b])
        S.op("dve", lambda e: e.tensor_tensor(out=o[:, 0:nb], in0=o[:, 0:nb], in1=x[:, 0:nb], op=ALU.add), reads=[o.b, x.b], writes=[o.b])
        S.dma("sp", g.XS[ti * 128:(ti + 1) * 128, n0:n0 + nb], o[:, 0:nb], reads=[o.b])
    return epi


def norm_rope(g, src, src_b, n, w, gain_ap, gain_b, out, out_b, tmp, sq, ss, rstd, rope=None):
    S = g.S
    rms_rstd(g, None, src, n, w, src_b, sq, ss, rstd)
    dst = out if rope is None else tmp[:, 0:n * w].rearrange("p (n w) -> p n w", n=n)
    dst_b = out_b if rope is None else tmp.b
    S.op("dve", lambda e: e.tensor_tensor(out=dst, in0=src, in1=rstd[:, 0:n].unsqueeze(2).broadcast_to([128, n, w]), op=ALU.mult),
         reads=[src_b, rstd.b], writes=[dst_b])
    S.op("dve", lambda e: e.tensor_tensor(out=dst, in0=dst, in1=gain_ap, op=ALU.mult), reads=[dst_b, gain_b], writes=[dst_b])
    if rope is None:
        return
    cos, ssin, blk = rope
    nh = w // (2 * blk)
    xv = dst.rearrange("p n (h b k) -> p n h b k", h=nh, b=2, k=blk)
    sv = ssin[:, 0:w].rearrange("p (h b k) -> p h b k", h=nh, b=2, k=blk)
    sw = sq[:, 0:n * w].rearrange("p (n h b k) -> p n h b k", n=n, h=nh, b=2, k=blk)
    for n_i in range(n):
        for b in range(2):
            S.op("dve", lambda e: e.tensor_tensor(out=sw[:, n_i, :, b, :], in0=xv[:, n_i, :, 1 - b, :], in1=sv[:, :, b, :], op=ALU.mult),
                 reads=[dst_b, ssin.b], writes=[sq.b])
    S.op("dve", lambda e: e.tensor_tensor(out=dst, in0=dst, in1=cos[:, 0:w].unsqueeze(1).broadcast_to([128, n, w]), op=ALU.mult),
         reads=[dst_b, cos.b], writes=[dst_b])
    S.op("dve", lambda e: e.tensor_tensor(out=out, in0=dst, in1=sq[:, 0:n * w].rearrange("p (n w) -> p n w", n=n), op=ALU.add),
         reads=[dst_b, sq.b], writes=[out_b])


def transpose_slots(g, src_tl, slots, w, dst_tl, dst_slot0):
    S = g.S
    for i0 in range(0, len(slots), 4):
        grp = slots[i0:i0 + 4]
        ps = next_psum(g)
        for j, s_ in enumerate(grp):
            S.op("pe", lambda e: e.transpose(ps[0:w, j * 128:(j + 1) * 128], src_tl[:, s_, 0:w], g.ident[:]),
                 reads=[src_tl.b, g.ident.b], writes=[ps.b])
        copy_op(g, evac_engine(g), dst_tl[0:w, dst_slot0 + i0:dst_slot0 + i0 + len(grp), :],
                ps[0:w, 0:len(grp) * 128].rearrange("p (k t) -> p k t", k=len(grp)), [ps.b], [dst_tl.b])


def stage_attn_prep(g, l):
    S = g.S
    with ExitStack() as st:
        gain = sb(g, st, "apg", [128, 2, 6, 128])
        for gi, (qn, kn) in enumerate([(g.ga_q_norm, g.ga_k_norm), (g.wa_q_norm, g.wa_k_norm)]):
            for s_ in range(6):
                bcast_row(g, "sp", gain[:, gi, s_, :], gain.b, (qn if s_ < 4 else kn)[l, :])
        NL = 3
        LT = [dict(z=sb(g, st, "apz", [128, 16, 128]), x=sb(g, st, "apx", [128, 12, 128]), t_=sb(g, st, "apt", [128, 12, 128]),
                   cs=sb(g, st, "apc", [128, 128]), sn=sb(g, st, "aps", [128, 128]),
                   tmp=sb(g, st, "aptmp", [128, 6 * 128]), sq=sb(g, st, "apsq", [128, 6 * 128]),
                   ss=sb(g, st, "apss", [128, 8]), rstd=sb(g, st, "aprs", [128, 8])) for _ in range(NL)]

        def lane(li):
          lt = LT[li]
          tmp, sq, ss, rstd = lt["tmp"], lt["sq"], lt["ss"], lt["rstd"]
          cs = [lt["cs"], lt["cs"]]
          sn = [lt["sn"], lt["sn"]]
          for ti in range(li, g.NT, NL):
            z = lt["z"]
            x = lt["x"]
            t_ = lt["t_"]
            S.dma("sp", z[:, :, :], g.Z[ti * 128:(ti + 1) * 128, 0:2048].rearrange("p (s d) -> p s d", s=16), writes=[z.b])
            rope = None
            if ti >= g.NTC:
                c_, s_t = cs[ti % 2], sn[ti % 2]
                p0 = (ti - g.NTC) * 128
                S.dma("sp", c_[:, :], g.ropeA[0, p0:p0 + 128, :], writes=[c_.b])
                S.dma("sp", s_t[:, :], g.ropeA[1, p0:p0 + 128, :], writes=[s_t.b])
                rope = (c_, s_t, 32)
            for gi, base in enumerate([0, 8]):
                norm_rope(g, z[:, base:base + 6, :], z.b, 6, 128, gain[:, gi, :, :], gain.b,
                          x[:, gi * 6:(gi + 1) * 6, :], x.b, tmp, sq, ss, rstd, rope=rope)
            transpose_slots(g, x, list(range(12)), 128, t_, 0)
            S.dma("pool", g.QKT[:, :, ti * 128:(ti + 1) * 128].rearrange("s d t -> d s t"), t_[:, :, :], reads=[t_.b])
        S.run_lanes([(lambda li=li: lane(li)) for li in range(NL)])
    S.barrier()


def attention(g, heads, scale, sink_tl=None):
    S = g.S
    T, NT = g.T, g.NT
    po = g.psums[0:4]
    sps = g.psums[4:8]
    with ExitStack() as st:
        kts = [sb(g, st, "akt", [128, T], BF16) for _ in range(2)]
        vt = sb(g, st, "avt", [128, NT, 129], BF16)
        qts = [[sb(g, st, "aqt", [128, 512], BF16) for _ in range(2)] for _ in range(3)]
        pts = [sb(g, st, "apt", [128, 512], BF16) for _ in range(3)]
        yts = [sb(g, st, "ayt", [128, 128]) for _ in range(3)]
        den = sb(g, st, "aden", [128, 4])
        S.op("dve", lambda e: e.memset(vt[:, :, 128:129], 1.0), writes=[vt.b])
        nq = npt = ny = nsp = 0
        for hd in heads:
            for pi_, (kd_ap, kd) in enumerate(hd["kparts"]):
                S.dma("pool", kts[pi_][0:kd, :], kd_ap, writes=[kts[pi_].b])
            S.dma("pool", vt[:, :, 0:128], hd["v"].rearrange("(n p) d -> p n d", p=128), writes=[vt.b])
            for qh, qparts in enumerate(hd["qparts"]):
                for (q0, qlen, keys) in hd["blocks"]:
                    nqb = qlen // 128
                    qt = qts[nq % 3]
                    nq += 1
                    for pi_, (qd_ap, kd) in enumerate(qparts):
                        S.dma("pool", qt[pi_][0:kd, 0:qlen], qd_ap[:, q0:q0 + qlen], writes=[qt[pi_].b])
                    npart = len(qparts)

                    def scores(kt_):
                        nonlocal nsp
                        ps_ = sps[nsp % 4]
                        nsp += 1
                        for pi_, (qd_ap, kd) in enumerate(qparts):
                            S.op("pe", lambda e: e.matmul(ps_[:, 0:qlen], lhsT=kts[pi_][0:kd, kt_ * 128:(kt_ + 1) * 128], rhs=qt[pi_][0:kd, 0:qlen],
                                                          start=(pi_ == 0), stop=(pi_ == npart - 1)),
                                 reads=[kts[pi_].b, qt[pi_].b], writes=[ps_.b])
                        return ps_
                    ps_next = scores(keys[0][0])
                    for idx, (kt, mask) in enumerate(keys):
                        ps = ps_next
                        if idx + 1 < len(keys):
                            ps_next = scores(keys[idx + 1][0])
                        pt = pts[npt % 3]
                        npt += 1
                        S.op("act", lambda e: e.activation(out=pt[:, 0:qlen], in_=ps[:, 0:qlen], func=AF.Exp, scale=scale),
                             reads=[ps.b], writes=[pt.b])
                        if mask is not None:
                            S.op("dve", lambda e: e.tensor_tensor(out=pt[:, 0:qlen], in0=pt[:, 0:qlen], in1=g.wmask[:, mask, :], op=ALU.mult),
                                 reads=[pt.b, g.wmask.b], writes=[pt.b])
                        for qb in range(nqb):
                            S.op("pe", lambda e: e.matmul(po[qb][:, 0:129], lhsT=pt[:, qb * 128:(qb + 1) * 128], rhs=vt[:, kt, :],
                                                          start=(idx == 0), stop=(idx == len(keys) - 1)),
                                 reads=[pt.b, vt.b], writes=[po[qb].b])
                    for qb in range(nqb):
                        y = yts[ny % 3]
                        ny += 1
                        if hd.get("sink_idx") is not None:
                            si = hd["sink_idx"][qh]
                            S.op("dve", lambda e: e.tensor_tensor(out=den[:, 0:1], in0=po[qb][:, 128:129], in1=sink_tl[:, si:si + 1], op=ALU.add),
                                 reads=[po[qb].b, sink_tl.b], writes=[den.b])
                            S.op("dve", lambda e: e.reciprocal(out=den[:, 0:1], in_=den[:, 0:1]), reads=[den.b], writes=[den.b])
                        else:
                            S.op("dve", lambda e: e.reciprocal(out=den[:, 0:1], in_=po[qb][:, 128:129]), reads=[po[qb].b], writes=[den.b])
                        S.op("dve", lambda e: e.tensor_scalar(out=y[:, :], in0=po[qb][:, 0:128], scalar1=den[:, 0:1], scalar2=None, op0=ALU.mult),
                             reads=[po[qb].b, den.b], writes=[y.b])
                        r0 = q0 + qb * 128
                        yc = hd["ycols"][qh]
                        S.dma("sp", g.YMIX[r0:r0 + 128, yc:yc + 128], y[:, :], reads=[y.b])
    S.barrier()


def dense_blocks(g):
    blocks = [(0, g.LC, [(kt, None) for kt in range(g.NTC)])]
    for q0 in range(g.LC, g.T, 512):
        blocks.append((q0, min(512, g.T - q0), [(kt, None) for kt in range(g.NT)]))
    return blocks


def window_blocks(g):
    blocks = [(0, g.LC, [(kt, None) for kt in range(g.NTC)])]
    nblk = g.TL // 128
    for n in range(nblk):
        keys = [(kt, None) for kt in range(g.NTC)]
        if n > 0:
            keys.append((g.NTC + n - 1, 0))
        keys.append((g.NTC + n, None))
        if n < nblk - 1:
            keys.append((g.NTC + n + 1, 1))
        blocks.append((g.LC + n * 128, 128, keys))
    return blocks


def stage_attn_gawa(g, l):
    S = g.S
    heads = []
    for hk in range(2):
        heads.append(dict(kparts=[(g.QKT[4 + hk], 128)], qparts=[[(g.QKT[2 * hk + j], 128)] for j in range(2)],
                          v=g.Z[:, O_GAV + hk * 128:O_GAV + (hk + 1) * 128], ycols=[(2 * hk + j) * 128 for j in range(2)],
                          blocks=dense_blocks(g)))
    attention(g, heads, 128 ** -0.5)
    with ExitStack() as st:
        sink = sb(g, st, "sink", [128, 4])
        bcast_row(g, "sp", sink[:, :], sink.b, g.wa_sink[l, :])
        S.op("act", lambda e: e.activation(out=sink[:, :], in_=sink[:, :], func=AF.Exp), reads=[sink.b], writes=[sink.b])
        heads = []
        for hk in range(2):
            heads.append(dict(kparts=[(g.QKT[10 + hk], 128)], qparts=[[(g.QKT[6 + 2 * hk + j], 128)] for j in range(2)],
                              v=g.Z[:, O_WAV + hk * 128:O_WAV + (hk + 1) * 128], ycols=[512 + (2 * hk + j) * 128 for j in range(2)],
                              blocks=window_blocks(g), sink_idx=[2 * hk, 2 * hk + 1]))
        attention(g, heads, 128 ** -0.5, sink_tl=sink)


def make_rms_pro(g, st, gain_dram, K):
    S = g.S
    gt = sb(g, st, "pg", [128, K])
    sq = sb(g, st, "psq", [128, K])
    ss = sb(g, st, "pss", [128, 1])
    rstd = sb(g, st, "prs", [128, 1])
    bcast_row(g, "sp", gt[:, :], gt.b, gain_dram)

    def pro(xi, ti, cw):
        rms_rstd(g, None, xi[:, 0:K].rearrange("p (n w) -> p n w", n=1), 1, K, xi.b, sq, ss, rstd)
        S.op("dve", lambda e: e.scalar_tensor_tensor(out=xi[:, 0:K], in0=xi[:, 0:K], scalar=rstd[:, 0:1], in1=gt[:, :], op0=ALU.mult, op1=ALU.mult),
             reads=[xi.b, rstd.b, gt.b], writes=[xi.b])
    return pro


def stage_mla(g, l):
    S = g.S
    T = g.T
    with ExitStack() as st:
        pro = make_rms_pro(g, st, g.mla_cq_norm[l, :], 384)
        linear(g, g.Z[:, O_CQ:O_CQ + 384], g.mla_w_uq[l], g.QML, T, 384, 768, G=17, NB=512, pro=pro)
    with ExitStack() as st:
        pro = make_rms_pro(g, st, g.mla_ckv_norm[l, :], 512)
        linear(g, g.Z[:, O_CKV:O_CKV + 512], g.mla_w_ukv[l], g.KVML, T, 512, 1024, G=17, NB=512, pro=pro)
    with ExitStack() as st:
        gq_n = sb(g, st, "mgqn", [128, 128])
        gq_r = sb(g, st, "mgqr", [128, 64])
        gk_n = sb(g, st, "mgkn", [128, 128])
        gk_r = sb(g, st, "mgkr", [128, 64])
        bcast_row(g, "sp", gq_n[:, :], gq_n.b, g.mla_qn_norm[l, :])
        bcast_row(g, "sp", gq_r[:, :], gq_r.b, g.mla_qr_norm[l, :])
        bcast_row(g, "sp", gk_n[:, :], gk_n.b, g.mla_kn_norm[l, :])
        bcast_row(g, "sp", gk_r[:, :], gk_r.b, g.mla_kr_norm[l, :])
        NL = 3
        qin = [sb(g, st, "mq", [128, 4, 192]) for _ in range(NL)]
        kvin = [sb(g, st, "mkv", [128, 4, 256]) for _ in range(NL)]
        krin = [sb(g, st, "mkr", [128, 1, 64]) for _ in range(NL)]
        xqn = [sb(g, st, "xqn", [128, 4, 128]) for _ in range(NL)]
        xqr = [sb(g, st, "xqr", [128, 4, 64]) for _ in range(NL)]
        xkn = [sb(g, st, "xkn", [128, 4, 128]) for _ in range(NL)]
        xkr = [sb(g, st, "xkr", [128, 1, 64]) for _ in range(NL)]
        tqn = [sb(g, st, "tqn", [128, 4, 128]) for _ in range(NL)]
        tqr = [sb(g, st, "tqr", [64, 4, 128]) for _ in range(NL)]
        tkn = [sb(g, st, "tkn", [128, 4, 128]) for _ in range(NL)]
        tkr = [sb(g, st, "tkr", [64, 1, 128]) for _ in range(NL)]
        cs = [sb(g, st, "mc", [128, 64]) for _ in range(NL)]
        sn = [sb(g, st, "ms", [128, 64]) for _ in range(NL)]
        tmps = [sb(g, st, "mtmp", [128, 512]) for _ in range(NL)]
        sqs = [sb(g, st, "msq", [128, 512]) for _ in range(NL)]
        sss = [sb(g, st, "mss", [128, 8]) for _ in range(NL)]
        rstds = [sb(g, st, "mrs", [128, 8]) for _ in range(NL)]

        def lane(i):
          tmp, sq, ss, rstd = tmps[i], sqs[i], sss[i], rstds[i]
          for ti in range(i, g.NT, NL):
            rows = slice(ti * 128, (ti + 1) * 128)
            S.dma("sp", qin[i][:, :, :], g.QML[rows, :].rearrange("p (h d) -> p h d", h=4), writes=[qin[i].b])
            S.dma("sp", kvin[i][:, :, :], g.KVML[rows, :].rearrange("p (h d) -> p h d", h=4), writes=[kvin[i].b])
            S.dma("sp", krin[i][:, 0, :], g.Z[rows, O_KR:O_KR + 64], writes=[krin[i].b])
            rope = None
            if ti >= g.NTC:
                p0 = (ti - g.NTC) * 128
                S.dma("sp", cs[i][:, :], g.ropeM[0, p0:p0 + 128, :], writes=[cs[i].b])
                S.dma("sp", sn[i][:, :], g.ropeM[1, p0:p0 + 128, :], writes=[sn[i].b])
                rope = (cs[i], sn[i], 16)
            norm_rope(g, qin[i][:, :, 0:128], qin[i].b, 4, 128, gq_n[:, :].unsqueeze(1).broadcast_to([128, 4, 128]), gq_n.b,
                      xqn[i][:, :, :], xqn[i].b, tmp, sq, ss, rstd)
            norm_rope(g, qin[i][:, :, 128:192], qin[i].b, 4, 64, gq_r[:, :].unsqueeze(1).broadcast_to([128, 4, 64]), gq_r.b,
                      xqr[i][:, :, :], xqr[i].b, tmp, sq, ss, rstd, rope=rope)
            norm_rope(g, kvin[i][:, :, 0:128], kvin[i].b, 4, 128, gk_n[:, :].unsqueeze(1).broadcast_to([128, 4, 128]), gk_n.b,
                      xkn[i][:, :, :], xkn[i].b, tmp, sq, ss, rstd)
            norm_rope(g, krin[i][:, :, :], krin[i].b, 1, 64, gk_r[:, :].unsqueeze(1), gk_r.b,
                      xkr[i][:, :, :], xkr[i].b, tmp, sq, ss, rstd, rope=rope)
            transpose_slots(g, xqn[i], [0, 1, 2, 3], 128, tqn[i], 0)
            transpose_slots(g, xqr[i], [0, 1, 2, 3], 64, tqr[i], 0)
            transpose_slots(g, xkn[i], [0, 1, 2, 3], 128, tkn[i], 0)
            transpose_slots(g, xkr[i], [0], 64, tkr[i], 0)
            cols = slice(ti * 128, (ti + 1) * 128)
            S.dma("pool", g.MQN[:, :, cols].rearrange("s d t -> d s t"), tqn[i][:, :, :], reads=[tqn[i].b])
            S.dma("pool", g.MQR[:, :, cols].rearrange("s d t -> d s t"), tqr[i][:, :, :], reads=[tqr[i].b])
            S.dma("pool", g.MKN[:, :, cols].rearrange("s d t -> d s t"), tkn[i][:, :, :], reads=[tkn[i].b])
            S.dma("pool", g.MKR[:, :, cols].rearrange("s d t -> d s t"), tkr[i][:, :, :], reads=[tkr[i].b])
        S.run_lanes([(lambda i=i: lane(i)) for i in range(NL)])
    S.barrier()
    heads = []
    for h in range(4):
        heads.append(dict(kparts=[(g.MKN[h], 128), (g.MKR[0], 64)], qparts=[[(g.MQN[h], 128), (g.MQR[h], 64)]],
                          v=g.KVML[:, h * 256 + 128:h * 256 + 256], ycols=[1536 + h * 128], blocks=dense_blocks(g)))
    attention(g, heads, 192 ** -0.5)


RQ_R, RQ_LW, RQ_K, RQ_V, RQ_KK, RQ_KKA = 0, 1, 2, 3, 4, 5
NRW = 2176


def stage_rwkv_prep(g, l):
    S = g.S
    NT, NTC = g.NT, g.NTC
    with ExitStack() as st:
        MU = [sb(g, st, "rmu", [128, NRW]) for _ in range(2)]
        w0 = [sb(g, st, "rw0", [128, 512]) for _ in range(2)]
        a0 = [sb(g, st, "ra0", [128, 512]) for _ in range(2)]
        w2 = [sb(g, st, "rw2", [96, 512]) for _ in range(2)]
        a2 = [sb(g, st, "ra2", [96, 512]) for _ in range(2)]
        g2 = sb(g, st, "rg2", [128, 2, 512])
        kk_ = sb(g, st, "rkk", [128, 512])
        ka = sb(g, st, "rka", [128, 512])
        omka = sb(g, st, "romka", [128, 512])
        for d in range(2):
            S.op("dve", lambda e: e.memset(MU[d][:, :], 0.0), writes=[MU[d].b])
            bcast_row(g, "sp", MU[d][:, 0:1792], MU[d].b, g.rwkv_mu[l, d, 0:1792])
            bcast_row(g, "sp", MU[d][:, 1792 + d * 96:1792 + (d + 1) * 96], MU[d].b, g.rwkv_mu[l, d, 1792:1888])
            bcast_row(g, "sp", MU[d][:, 1984 + d * 96:1984 + (d + 1) * 96], MU[d].b, g.rwkv_mu[l, d, 1888:1984])
            bcast_row(g, "sp", w0[d][:, :], w0[d].b, g.rwkv_w0[l, d, :])
            bcast_row(g, "sp", a0[d][:, :], a0[d].b, g.rwkv_a0[l, d, :])
            S.dma("sp", w2[d][:, :], g.rwkv_w2[l, d], writes=[w2[d].b])
            S.dma("sp", a2[d][:, :], g.rwkv_a2[l, d], writes=[a2[d].b])
        S.dma("sp", g2[:, :, :], g.rwkv_g2[l].rearrange("(kc p) n -> p kc n", p=128), writes=[g2.b])
        bcast_row(g, "sp", kk_[:, :], kk_.b, g.rwkv_k_k[l, :])
        bcast_row(g, "sp", ka[:, :], ka.b, g.rwkv_k_a[l, :])
        S.op("dve", lambda e: e.tensor_scalar(out=omka[:, :], in0=ka[:, :], scalar1=-1.0, scalar2=1.0, op0=ALU.mult, op1=ALU.add),
             reads=[ka.b], writes=[omka.b])
        LT = []
        for d in range(2):
            LT.append(dict(
                zc=[sb(g, st, "rzc", [128, NRW]) for _ in range(2)],
                zz=sb(g, st, "rzs", [128, NRW]),
                u=sb(g, st, "ru", [128, NRW]),
                tw=sb(g, st, "rtw", [128, 96]),
                sg=sb(g, st, "rsg", [128, 256]),
                tT=sb(g, st, "rtT", [128, 4, 128]),
                aa=sb(g, st, "raa", [128, 512]),
                t1=sb(g, st, "rt1", [128, 512]),
                t2=sb(g, st, "rt2", [128, 512]),
                ss=sb(g, st, "rss", [128, 8]),
                outs=[[sb(g, st, "ro", [128, 512]) for _ in range(2)] for _ in range(5)]))

        def lane(d):
            lt = LT[d]
            zz, u, tw, sg, tT, aa, t1, t2, ss, outs = (lt[k] for k in ["zz", "u", "tw", "sg", "tT", "aa", "t1", "t2", "ss", "outs"])
            for ti in range(NT):
                z = lt["zc"][ti % 2]
                r0 = ti * 128
                S.dma("sp", z[:, :], g.Z[r0:r0 + 128, O_RKVG:O_RKVG + NRW], writes=[z.b])
                if True:
                    if d == 0:
                        S.dma("sp", zz[1:128, :], g.Z[r0:r0 + 127, O_RKVG:O_RKVG + NRW], writes=[zz.b])
                        src = g.zrow[0:1, :] if ti in (0, NTC) else g.Z[r0 - 1:r0, O_RKVG:O_RKVG + NRW]
                        S.dma("sp", zz[0:1, :], src, writes=[zz.b])
                    else:
                        S.dma("sp", zz[0:127, :], g.Z[r0 + 1:r0 + 128, O_RKVG:O_RKVG + NRW], writes=[zz.b])
                        src = g.zrow[0:1, :] if ti in (NTC - 1, NT - 1) else g.Z[r0 + 128:r0 + 129, O_RKVG:O_RKVG + NRW]
                        S.dma("sp", zz[127:128, :], src, writes=[zz.b])
                    S.op("dve", lambda e: e.tensor_tensor(out=u[:, :], in0=zz[:, :], in1=z[:, :], op=ALU.subtract), reads=[zz.b, z.b], writes=[u.b])
                    S.op("dve", lambda e: e.tensor_tensor(out=u[:, :], in0=u[:, :], in1=MU[d][:, :], op=ALU.mult), reads=[u.b, MU[d].b], writes=[u.b])
                    S.op("dve", lambda e: e.tensor_tensor(out=u[:, :], in0=u[:, :], in1=z[:, :], op=ALU.add), reads=[u.b, z.b], writes=[u.b])
                    ur, uk, uv, ugd = u[:, 0:512], u[:, 512:1024], u[:, 1024:1536], u[:, 1536:1792]
                    uwd = u[:, 1792 + d * 96:1792 + (d + 1) * 96]
                    uad = u[:, 1984 + d * 96:1984 + (d + 1) * 96]
                    o_lw, o_k, o_kk, o_kka, o_g = [outs[i][ti % 2] for i in range(5)]
                    rows = slice(r0, r0 + 128)
                    S.dma("pool", g.RW[d, RQ_R, rows, :], ur, reads=[u.b])
                    S.dma("pool", g.RW[d, RQ_V, rows, :], uv, reads=[u.b])
                    S.op("act", lambda e: e.activation(out=tw[:, :], in_=uwd, func=AF.Tanh), reads=[u.b], writes=[tw.b])
                    S.op("act", lambda e: e.activation(out=sg[:, :], in_=ugd, func=AF.Sigmoid), reads=[u.b], writes=[sg.b])
                    ps = next_psum(g)
                    S.op("pe", lambda e: e.transpose(ps[0:96, 0:128], tw[:, :], g.ident[:]), reads=[tw.b, g.ident.b], writes=[ps.b])
                    S.op("pe", lambda e: e.transpose(ps[0:96, 128:256], uad, g.ident[:]), reads=[u.b, g.ident.b], writes=[ps.b])
                    copy_op(g, "act", tT[0:96, 0:2, :], ps[0:96, 0:256].rearrange("p (k t) -> p k t", k=2), [ps.b], [tT.b])
                    ps = next_psum(g)
                    S.op("pe", lambda e: e.transpose(ps[:, 0:128], sg[:, 0:128], g.ident[:]), reads=[sg.b, g.ident.b], writes=[ps.b])
                    S.op("pe", lambda e: e.transpose(ps[:, 128:256], sg[:, 128:256], g.ident[:]), reads=[sg.b, g.ident.b], writes=[ps.b])
                    copy_op(g, "dve", tT[:, 2:4, :], ps[:, 0:256].rearrange("p (k t) -> p k t", k=2), [ps.b], [tT.b])
                    psw = next_psum(g)
                    S.op("pe", lambda e: e.matmul(psw[:, :], lhsT=tT[0:96, 0, :], rhs=w2[d][:, :], start=True, stop=True), reads=[tT.b, w2[d].b], writes=[psw.b])
                    S.op("dve", lambda e: e.tensor_tensor(out=t1[:, :], in0=psw[:, :], in1=w0[d][:, :], op=ALU.add), reads=[psw.b, w0[d].b], writes=[t1.b])
                    S.op("act", lambda e: e.activation(out=t1[:, :], in_=t1[:, :], func=AF.Sigmoid), reads=[t1.b], writes=[t1.b])
                    S.op("act", lambda e: e.mul(out=o_lw[:, :], in_=t1[:, :], mul=-0.6065306597126334), reads=[t1.b], writes=[o_lw.b])
                    S.dma("pool", g.RW[d, RQ_LW, rows, :], o_lw[:, :], reads=[o_lw.b])
                    psa = next_psum(g)
                    S.op("pe", lambda e: e.matmul(psa[:, :], lhsT=tT[0:96, 1, :], rhs=a2[d][:, :], start=True, stop=True), reads=[tT.b, a2[d].b], writes=[psa.b])
                    S.op("dve", lambda e: e.tensor_tensor(out=aa[:, :], in0=psa[:, :], in1=a0[d][:, :], op=ALU.add), reads=[psa.b, a0[d].b], writes=[aa.b])
                    S.op("act", lambda e: e.activation(out=aa[:, :], in_=aa[:, :], func=AF.Sigmoid), reads=[aa.b], writes=[aa.b])
                    psg = next_psum(g)
                    for kc in range(2):
                        S.op("pe", lambda e: e.matmul(psg[:, :], lhsT=tT[:, 2 + kc, :], rhs=g2[:, kc, :], start=(kc == 0), stop=(kc == 1)),
                             reads=[tT.b, g2.b], writes=[psg.b])
                    copy_op(g, "act", o_g[:, :], psg[:, :], [psg.b], [o_g.b])
                    S.dma("pool", g.RG[d, rows, :], o_g[:, :], reads=[o_g.b])
                    S.op("dve", lambda e: e.tensor_tensor(out=t1[:, :], in0=uk, in1=kk_[:, :], op=ALU.mult), reads=[u.b, kk_.b], writes=[t1.b])
                    S.op("act", lambda e: e.activation(out=t2[:, :], in_=t1[:, :], func=AF.Square), reads=[t1.b], writes=[t2.b])
                    S.op("dve", lambda e: e.tensor_reduce(out=ss[:, 0:8], in_=t2[:, :].rearrange("p (h k) -> p h k", h=8), axis=AX.X, op=ALU.add),
                         reads=[t2.b], writes=[ss.b])
                    S.op("act", lambda e: e.activation(out=ss[:, 0:8], in_=ss[:, 0:8], func=AF.Sqrt), reads=[ss.b], writes=[ss.b])
                    S.op("dve", lambda e: e.tensor_scalar(out=ss[:, 0:8], in0=ss[:, 0:8], scalar1=1e-12, scalar2=None, op0=ALU.max), reads=[ss.b], writes=[ss.b])
                    S.op("dve", lambda e: e.reciprocal(out=ss[:, 0:8], in_=ss[:, 0:8]), reads=[ss.b], writes=[ss.b])
                    S.op("dve", lambda e: e.tensor_tensor(out=o_kk[:, :].rearrange("p (h k) -> p h k", h=8), in0=t1[:, :].rearrange("p (h k) -> p h k", h=8),
                                                          in1=ss[:, 0:8].unsqueeze(2).broadcast_to([128, 8, 64]), op=ALU.mult),
                         reads=[t1.b, ss.b], writes=[o_kk.b])
                    S.dma("pool", g.RW[d, RQ_KK, rows, :], o_kk[:, :], reads=[o_kk.b])
                    S.op("dve", lambda e: e.tensor_tensor(out=o_kka[:, :], in0=o_kk[:, :], in1=aa[:, :], op=ALU.mult), reads=[o_kk.b, aa.b], writes=[o_kka.b])
                    S.dma("pool", g.RW[d, RQ_KKA, rows, :], o_kka[:, :], reads=[o_kka.b])
                    S.op("dve", lambda e: e.tensor_tensor(out=t2[:, :], in0=aa[:, :], in1=ka[:, :], op=ALU.mult), reads=[aa.b, ka.b], writes=[t2.b])
                    S.op("dve", lambda e: e.tensor_tensor(out=t2[:, :], in0=t2[:, :], in1=omka[:, :], op=ALU.add), reads=[t2.b, omka.b], writes=[t2.b])
                    S.op("dve", lambda e: e.tensor_tensor(out=o_k[:, :], in0=t2[:, :], in1=uk, op=ALU.mult), reads=[t2.b, u.b], writes=[o_k.b])
                    S.dma("pool", g.RW[d, RQ_K, rows, :], o_k[:, :], reads=[o_k.b])
        S.run_lanes([(lambda d=d: lane(d)) for d in range(2)])
    S.barrier()


def stage_rwkv_scan(g, l):
    S = g.S
    C = 64
    nch = g.T // C
    nch_c = g.LC // C
    ident64 = g.ident[0:64, 0:64]

    def mm(ps_ap, psb, lhsT, rhs, reads, start=True, stop=True):
        S.op("pe", lambda e: e.matmul(ps_ap, lhsT=lhsT, rhs=rhs, start=start, stop=stop), reads=reads, writes=[psb])

    with ExitStack() as st:
        names = ["Lsb", "eL", "enL", "eLx", "eEnd", "Rt", "Kt", "Kk", "Ak", "Kh", "Ah"]
        LT = []
        for d in range(2):
            LT.append(dict(
                ST=sb(g, st, "sST", [64, 8, 64]),
                qin=[sb(g, st, "sq", [64, 512]) for _ in range(6)],
                v2=sb(g, st, "sv2", [128, 512]),
                W={n: sb(g, st, "s" + n, [64, 512]) for n in names},
                pCT=sb(g, st, "spCT", [64, 8]),
                XT1=sb(g, st, "sXT1", [64, 8, 128]), XT2=sb(g, st, "sXT2", [64, 8, 128]),
                PRs=sb(g, st, "sPRs", [128, 8, 128]),
                Nn=[sb(g, st, "sN", [64, 8, 64]) for _ in range(2)],
                NTt=[sb(g, st, "sNT", [64, 8, 64]) for _ in range(2)],
                Qq=[sb(g, st, "sQ", [64, 8, 64]) for _ in range(2)],
                W1T=sb(g, st, "sW1T", [64, 8, 64]), MV=sb(g, st, "sMV", [64, 8, 64]), W2=sb(g, st, "sW2", [64, 8, 64]),
                Y0=sb(g, st, "sY0", [64, 8, 64]), D0=sb(g, st, "sD0", [64, 8, 64]), U=sb(g, st, "sU", [64, 8, 64]),
                Yo=[sb(g, st, "sYo", [64, 512]) for _ in range(2)], tmp=sb(g, st, "stmp", [64, 8, 64])))
            S.op("dve", lambda e: e.memset(LT[d]["ST"][:, :, :], 0.0), writes=[LT[d]["ST"].b])

        def lane(d):
            lt = LT[d]
            W, pCT, XT1, XT2, PRs, Nn, NTt, Qq = (lt[k] for k in ["W", "pCT", "XT1", "XT2", "PRs", "Nn", "NTt", "Qq"])
            W1T, MV, W2, Y0, D0, U, Yo, tmp = (lt[k] for k in ["W1T", "MV", "W2", "Y0", "D0", "U", "Yo", "tmp"])
            ST = {d: lt["ST"]}
            it = 0
            if d == 0:
                order = list(range(nch))
            else:
                order = list(range(nch_c - 1, -1, -1)) + list(range(nch - 1, nch_c - 1, -1))
            tri = g.tri[0:64, d, :]
            mk = g.mk[:, d, :]
            nmts = g.nmts[0:64, d, :]
            for c in order:
                it += 1
                q = lt["qin"]
                vv = lt["v2"]
                rows = slice(c * C, (c + 1) * C)
                for qi in range(6):
                    S.dma("sp", q[qi][:, :], g.RW[d, qi, rows, :], writes=[q[qi].b])
                S.dma("sp", vv[64:128, :], g.RW[d, RQ_V, rows, :], writes=[vv.b])
                r_, lw, k_, v_, kk, kka = q
                psL = next_psum(g)
                mm(psL[0:64, :], psL.b, tri, lw[:, :], [g.tri.b, lw.b])
                psE = next_psum(g)
                mm(psE[0:64, :], psE.b, g.ones64[0:64, :], lw[:, :], [g.ones64.b, lw.b])
                psP = next_psum(g)
                for h in range(8):
                    mm(psP[0:64, h:h + 1], psP.b, lw[:, h * 64:(h + 1) * 64], g.ones64[0:64, 0:1], [lw.b, g.ones64.b])
                S.op("act", lambda e: e.activation(out=pCT[:, :], in_=psP[0:64, 0:8], func=AF.Exp), reads=[psP.b], writes=[pCT.b])
                S.op("act", lambda e: e.activation(out=W["eL"][:, :], in_=psL[0:64, :], func=AF.Exp), reads=[psL.b], writes=[W["eL"].b])
                S.op("act", lambda e: e.activation(out=W["enL"][:, :], in_=psL[0:64, :], func=AF.Exp, scale=-1.0), reads=[psL.b], writes=[W["enL"].b])
                S.op("dve", lambda e: e.tensor_tensor(out=W["eLx"][:, :], in0=psL[0:64, :], in1=lw[:, :], op=ALU.subtract), reads=[psL.b, lw.b], writes=[W["eLx"].b])
                S.op("act", lambda e: e.activation(out=W["eLx"][:, :], in_=W["eLx"][:, :], func=AF.Exp), reads=[W["eLx"].b], writes=[W["eLx"].b])
                copy_op(g, "dve", W["Lsb"][:, :], psL[0:64, :], [psL.b], [W["Lsb"].b])
                S.op("dve", lambda e: e.tensor_tensor(out=W["eEnd"][:, :], in0=psE[0:64, :], in1=W["Lsb"][:, :], op=ALU.subtract),
                     reads=[psE.b, W["Lsb"].b], writes=[W["eEnd"].b])
                S.op("act", lambda e: e.activation(out=W["eEnd"][:, :], in_=W["eEnd"][:, :], func=AF.Exp), reads=[W["eEnd"].b], writes=[W["eEnd"].b])
                for (o, a, b) in [("Rt", r_, "eL"), ("Kt", kk, "eLx"), ("Kk", k_, "enL"), ("Ak", kka, "enL"), ("Kh", k_, "eEnd"), ("Ah", kka, "eEnd")]:
                    S.op("dve", lambda e: e.tensor_tensor(out=W[o][:, :], in0=a[:, :], in1=W[b][:, :], op=ALU.mult), reads=[a.b, W[b].b], writes=[W[o].b])
                for (XT, n0, n1) in [(XT1, "Ak", "Kk"), (XT2, "Kt", "Rt")]:
                    for hg in range(2):
                        ps = next_psum(g)
                        for hh in range(4):
                            h = hg * 4 + hh
                            for j, nm in enumerate([n0, n1]):
                                S.op("pe", lambda e: e.transpose(ps[0:64, hh * 128 + j * 64:hh * 128 + (j + 1) * 64], W[nm][:, h * 64:(h + 1) * 64], ident64),
                                     reads=[W[nm].b, g.ident.b], writes=[ps.b])
                        copy_op(g, evac_engine(g), XT[:, hg * 4:(hg + 1) * 4, :], ps[0:64, :].rearrange("p (h x) -> p h x", h=4), [ps.b], [XT.b])
                for hg in range(2):
                    ps = next_psum(g)
                    for hh in range(4):
                        h = hg * 4 + hh
                        mm(ps[:, hh * 128:(hh + 1) * 128], ps.b, XT1[:, h, :], XT2[:, h, :], [XT1.b, XT2.b])
                    S.op("dve", lambda e: e.tensor_tensor(out=PRs[:, hg * 4:(hg + 1) * 4, :], in0=ps[:, :].rearrange("p (h x) -> p h x", h=4),
                                                          in1=mk.unsqueeze(1).broadcast_to([128, 4, 128]), op=ALU.mult),
                         reads=[ps.b, g.mk.b], writes=[PRs.b])
                ps = next_psum(g)
                for h in range(8):
                    mm(ps[0:64, h * 64:(h + 1) * 64], ps.b, XT2[:, h, 0:64], XT1[:, h, 0:64], [XT1.b, XT2.b])
                N, NT_, Q = Nn[0], NTt[0], Qq[0]
                S.op("dve", lambda e: e.tensor_tensor(out=NT_[:, :, :], in0=ps[0:64, :].rearrange("p (h x) -> p h x", h=8),
                                                      in1=nmts.unsqueeze(1).broadcast_to([64, 8, 64]), op=ALU.mult),
                     reads=[ps.b, g.nmts.b], writes=[NT_.b])
                S.op("dve", lambda e: e.tensor_scalar(out=N[:, :, :], in0=PRs[0:64, :, 0:64], scalar1=-1.0, scalar2=None, op0=ALU.mult), reads=[PRs.b], writes=[N.b])
                S.op("dve", lambda e: e.tensor_tensor(out=Q[:, :, :], in0=N[:, :, :], in1=ident64.unsqueeze(1).broadcast_to([64, 8, 64]), op=ALU.add),
                     reads=[N.b, g.ident.b], writes=[Q.b])
                cur = 0
                for lev in range(5):
                    N, NT_, Q = Nn[cur], NTt[cur], Qq[cur]
                    N2, NT2, Q2 = Nn[1 - cur], NTt[1 - cur], Qq[1 - cur]
                    psn = next_psum(g)
                    pst = next_psum(g)
                    for h in range(8):
                        if lev < 4:
                            mm(psn[0:64, h * 64:(h + 1) * 64], psn.b, NT_[:, h, :], N[:, h, :], [NT_.b, N.b])
                        mm(pst[0:64, h * 64:(h + 1) * 64], pst.b, N[:, h, :], NT_[:, h, :], [NT_.b, N.b])
                    if lev < 4:
                        copy_op(g, "act", N2[:, :, :], psn[0:64, :].rearrange("p (h x) -> p h x", h=8), [psn.b], [N2.b])
                    copy_op(g, "dve", NT2[:, :, :], pst[0:64, :].rearrange("p (h x) -> p h x", h=8), [pst.b], [NT2.b])
                    psq = next_psum(g)
                    for h in range(8):
                        mm(psq[0:64, h * 64:(h + 1) * 64], psq.b, NT2[:, h, :], Q[:, h, :], [NT2.b, Q.b])
                    S.op("dve", lambda e: e.tensor_tensor(out=Q2[:, :, :], in0=psq[0:64, :].rearrange("p (h x) -> p h x", h=8), in1=Q[:, :, :], op=ALU.add),
                         reads=[psq.b, Q.b], writes=[Q2.b])
                    cur = 1 - cur
                Q = Qq[cur]
                ps1 = next_psum(g)
                ps2 = next_psum(g)
                ps3 = next_psum(g)
                ps4 = next_psum(g)
                for h in range(8):
                    hs = slice(h * 64, (h + 1) * 64)
                    mm(ps1[0:64, hs], ps1.b, W["Kt"][:, hs], Q[:, h, :], [W["Kt"].b, Q.b])
                    mm(ps2[0:64, hs], ps2.b, PRs[64:128, h, 0:64], vv[64:128, hs], [PRs.b, vv.b])
                    mm(ps3[0:64, hs], ps3.b, PRs[64:128, h, 64:128], vv[64:128, hs], [PRs.b, vv.b])
                    mm(ps4[0:64, hs], ps4.b, W["Kh"][:, hs], v_[:, hs], [W["Kh"].b, v_.b])
                copy_op(g, "act", W1T[:, :, :], ps1[0:64, :].rearrange("p (h x) -> p h x", h=8), [ps1.b], [W1T.b])
                copy_op(g, "dve", MV[:, :, :], ps2[0:64, :].rearrange("p (h x) -> p h x", h=8), [ps2.b], [MV.b])
                copy_op(g, "act", Y0[:, :, :], ps3[0:64, :].rearrange("p (h x) -> p h x", h=8), [ps3.b], [Y0.b])
                copy_op(g, "dve", D0[:, :, :], ps4[0:64, :].rearrange("p (h x) -> p h x", h=8), [ps4.b], [D0.b])
                ps5 = next_psum(g)
                for h in range(8):
                    mm(ps5[0:64, h * 64:(h + 1) * 64], ps5.b, Q[:, h, :], MV[:, h, :], [Q.b, MV.b])
                copy_op(g, "act", W2[:, :, :], ps5[0:64, :].rearrange("p (h x) -> p h x", h=8), [ps5.b], [W2.b])
                st_ = ST[d]
                psu = next_psum(g)
                for h in range(8):
                    mm(psu[0:64, h * 64:(h + 1) * 64], psu.b, W1T[:, h, :], st_[:, h, :], [W1T.b, st_.b])
                S.op("dve", lambda e: e.scalar_tensor_tensor(out=U[:, :, :], in0=psu[0:64, :].rearrange("p (h x) -> p h x", h=8), scalar=-1.0,
                                                             in1=W2[:, :, :], op0=ALU.mult, op1=ALU.subtract),
                     reads=[psu.b, W2.b], writes=[U.b])
                psy = next_psum(g)
                for h in range(8):
                    hs = slice(h * 64, (h + 1) * 64)
                    mm(psy[0:64, hs], psy.b, XT2[:, h, 64:128], st_[:, h, :], [XT2.b, st_.b], start=True, stop=False)
                    mm(psy[0:64, hs], psy.b, PRs[0:64, h, 64:128], U[:, h, :], [PRs.b, U.b], start=False, stop=True)
                yo = Yo[it % 2]
                S.op("dve", lambda e: e.tensor_tensor(out=yo[:, :], in0=psy[0:64, :], in1=Y0[:, :, :].rearrange("p h x -> p (h x)"), op=ALU.add),
                     reads=[psy.b, Y0.b], writes=[yo.b])
                S.dma("pool", g.YR[d, rows, :], yo[:, :], reads=[yo.b])
                psd = next_psum(g)
                for h in range(8):
                    hs = slice(h * 64, (h + 1) * 64)
                    mm(psd[0:64, hs], psd.b, W["Ah"][:, hs], U[:, h, :], [W["Ah"].b, U.b])
                S.op("dve", lambda e: e.tensor_tensor(out=tmp[:, :, :], in0=st_[:, :, :], in1=pCT[:, 0:8].unsqueeze(2).broadcast_to([64, 8, 64]), op=ALU.mult),
                     reads=[st_.b, pCT.b], writes=[tmp.b])
                S.op("dve", lambda e: e.tensor_tensor(out=tmp[:, :, :], in0=tmp[:, :, :], in1=D0[:, :, :], op=ALU.add), reads=[tmp.b, D0.b], writes=[tmp.b])
                S.op("dve", lambda e: e.tensor_tensor(out=st_[:, :, :], in0=psd[0:64, :].rearrange("p (h x) -> p h x", h=8), in1=tmp[:, :, :], op=ALU.add),
                     reads=[psd.b, tmp.b], writes=[st_.b])
        S.run_lanes([(lambda d=d: lane(d)) for d in range(2)])
    S.barrier()


def stage_rwkv_out(g, l):
    S = g.S
    with ExitStack() as st:
        lw_ = sb(g, st, "olw", [128, 512])
        lb_ = sb(g, st, "olb", [128, 512])
        rk_ = sb(g, st, "ork", [128, 512])
        bcast_row(g, "sp", lw_[:, :], lw_.b, g.rwkv_lnx_w[l, :])
        bcast_row(g, "sp", lb_[:, :], lb_.b, g.rwkv_lnx_b[l, :])
        bcast_row(g, "sp", rk_[:, :], rk_.b, g.rwkv_r_k[l].rearrange("h k -> (h k)"))
        NL = 4
        LT = [dict(ins=[[sb(g, st, "oin", [128, 512]) for _ in range(5)] for _ in range(2)],
                   t1=sb(g, st, "ot1", [128, 512]), t2=sb(g, st, "ot2", [128, 512]),
                   acc=[sb(g, st, "oacc", [128, 512]) for _ in range(2)],
                   ss=sb(g, st, "oss", [128, 8]), s2=sb(g, st, "os2", [128, 8])) for _ in range(NL)]
        h8 = lambda ap: ap.rearrange("p (h k) -> p h k", h=8)
        b8 = lambda t_: t_[:, 0:8].unsqueeze(2).broadcast_to([128, 8, 64])

        def lane(li):
          lt = LT[li]
          ins, t1, t2, acc, ss, s2 = (lt[k] for k in ["ins", "t1", "t2", "acc", "ss", "s2"])
          it = 0
          for kk_i, ti in enumerate(range(li, g.NT, NL)):
            rows = slice(ti * 128, (ti + 1) * 128)
            a_ = acc[kk_i % 2]
            for d in range(2):
                it += 1
                y, r_, k_, v_, g_ = ins[it % 2]
                S.dma("sp", y[:, :], g.YR[d, rows, :], writes=[y.b])
                S.dma("sp", r_[:, :], g.RW[d, RQ_R, rows, :], writes=[r_.b])
                S.dma("sp", k_[:, :], g.RW[d, RQ_K, rows, :], writes=[k_.b])
                S.dma("sp", v_[:, :], g.RW[d, RQ_V, rows, :], writes=[v_.b])
                S.dma("sp", g_[:, :], g.RG[d, rows, :], writes=[g_.b])
                S.op("dve", lambda e: e.tensor_reduce(out=ss[:, 0:8], in_=h8(y[:, :]), axis=AX.X, op=ALU.add), reads=[y.b], writes=[ss.b])
                S.op("dve", lambda e: e.tensor_scalar(out=ss[:, 0:8], in0=ss[:, 0:8], scalar1=-1.0 / 64, scalar2=None, op0=ALU.mult), reads=[ss.b], writes=[ss.b])
                S.op("dve", lambda e: e.tensor_tensor(out=h8(t1[:, :]), in0=h8(y[:, :]), in1=b8(ss), op=ALU.add), reads=[y.b, ss.b], writes=[t1.b])
                S.op("act", lambda e: e.activation(out=t2[:, :], in_=t1[:, :], func=AF.Square), reads=[t1.b], writes=[t2.b])
                S.op("dve", lambda e: e.tensor_reduce(out=s2[:, 0:8], in_=h8(t2[:, :]), axis=AX.X, op=ALU.add), reads=[t2.b], writes=[s2.b])
                S.op("dve", lambda e: e.tensor_scalar(out=s2[:, 0:8], in0=s2[:, 0:8], scalar1=1.0 / 64, scalar2=LNX_EPS, op0=ALU.mult, op1=ALU.add),
                     reads=[s2.b], writes=[s2.b])
                S.op("act", lambda e: e.activation(out=s2[:, 0:8], in_=s2[:, 0:8], func=AF.Sqrt), reads=[s2.b], writes=[s2.b])
                S.op("dve", lambda e: e.reciprocal(out=s2[:, 0:8], in_=s2[:, 0:8]), reads=[s2.b], writes=[s2.b])
                S.op("dve", lambda e: e.tensor_tensor(out=h8(t1[:, :]), in0=h8(t1[:, :]), in1=b8(s2), op=ALU.mult), reads=[t1.b, s2.b], writes=[t1.b])
                S.op("dve", lambda e: e.tensor_tensor(out=t1[:, :], in0=t1[:, :], in1=lw_[:, :], op=ALU.mult), reads=[t1.b, lw_.b], writes=[t1.b])
                S.op("dve", lambda e: e.tensor_tensor(out=t1[:, :], in0=t1[:, :], in1=lb_[:, :], op=ALU.add), reads=[t1.b, lb_.b], writes=[t1.b])
                S.op("dve", lambda e: e.tensor_tensor(out=t2[:, :], in0=r_[:, :], in1=k_[:, :], op=ALU.mult), reads=[r_.b, k_.b], writes=[t2.b])
                S.op("dve", lambda e: e.tensor_tensor(out=t2[:, :], in0=t2[:, :], in1=rk_[:, :], op=ALU.mult), reads=[t2.b, rk_.b], writes=[t2.b])
                S.op("dve", lambda e: e.tensor_reduce(out=ss[:, 0:8], in_=h8(t2[:, :]), axis=AX.X, op=ALU.add), reads=[t2.b], writes=[ss.b])
                S.op("dve", lambda e: e.tensor_tensor(out=h8(t2[:, :]), in0=h8(v_[:, :]), in1=b8(ss), op=ALU.mult), reads=[v_.b, ss.b], writes=[t2.b])
                S.op("dve", lambda e: e.tensor_tensor(out=t1[:, :], in0=t1[:, :], in1=t2[:, :], op=ALU.add), reads=[t1.b, t2.b], writes=[t1.b])
                if d == 0:
                    S.op("dve", lambda e: e.tensor_tensor(out=a_[:, :], in0=t1[:, :], in1=g_[:, :], op=ALU.mult), reads=[t1.b, g_.b], writes=[a_.b])
                else:
                    S.op("dve", lambda e: e.tensor_tensor(out=t1[:, :], in0=t1[:, :], in1=g_[:, :], op=ALU.mult), reads=[t1.b, g_.b], writes=[t1.b])
                    S.op("dve", lambda e: e.tensor_tensor(out=a_[:, :], in0=a_[:, :], in1=t1[:, :], op=ALU.add), reads=[a_.b, t1.b], writes=[a_.b])
            S.dma("pool", g.YMIX[rows, 1024:1536], a_[:, :], reads=[a_.b])
        S.run_lanes([(lambda li=li: lane(li)) for li in range(NL)])
    S.barrier()


def stage_merge(g, l):
    S = g.S
    with ExitStack() as st:
        gt = [[sb(g, st, "mgt", [128, 512]) for _ in range(4)] for _ in range(2)]
        t1 = sb(g, st, "mt1", [128, 512])
        cnt = [0]

        def epi(pss, o, ti, n0, nb):
            gs = gt[cnt[0] % 2]
            cnt[0] += 1
            for i in range(4):
                c0 = O_GATE + i * D + n0
                S.dma("sp", gs[i][:, 0:nb], g.Z[ti * 128:(ti + 1) * 128, c0:c0 + nb], writes=[gs[i].b])
                S.op("act", lambda e: e.activation(out=gs[i][:, 0:nb], in_=gs[i][:, 0:nb], func=AF.Sigmoid), reads=[gs[i].b], writes=[gs[i].b])
            S.op("dve", lambda e: e.tensor_tensor(out=o[:, 0:nb], in0=pss[0][:, 0:nb], in1=gs[0][:, 0:nb], op=ALU.mult), reads=[pss[0].b, gs[0].b], writes=[o.b])
            for i in range(1, 4):
                S.op("dve", lambda e: e.tensor_tensor(out=t1[:, 0:nb], in0=pss[i][:, 0:nb], in1=gs[i][:, 0:nb], op=ALU.mult), reads=[pss[i].b, gs[i].b], writes=[t1.b])
                S.op("dve", lambda e: e.tensor_tensor(out=o[:, 0:nb], in0=o[:, 0:nb], in1=t1[:, 0:nb], op=ALU.add), reads=[o.b, t1.b], writes=[o.b])
            S.dma("sp", g.MERGED[ti * 128:(ti + 1) * 128, n0:n0 + nb], o[:, 0:nb], reads=[o.b])
        linear(g, g.YMIX, g.w_branch[l].rearrange("b k n -> (b k) n"), None, g.T, D, D, G=12, NB=512, epi=epi,
               segs=[(0, 4), (4, 8), (8, 12), (12, 16)])


def stage_conv(g, l):
    S = g.S
    NT, NTC = g.NT, g.NTC
    NL = 4
    with ExitStack() as st:
        cw = [sb(g, st, "ccw", [128, 3, 512]) for _ in range(2)]
        cbias = [sb(g, st, "ccb", [128, 512]) for _ in range(2)]
        L_ins = [[[sb(g, st, "cin", [128, 512]) for _ in range(4)] for _ in range(2)] for _ in range(NL)]
        L_acc = [sb(g, st, "cacc", [128, 512]) for _ in range(NL)]
        L_t1 = [sb(g, st, "ct1", [128, 512]) for _ in range(NL)]
        L_out = [[sb(g, st, "cout", [128, 512]) for _ in range(2)] for _ in range(NL)]
        for bi, n0 in enumerate(range(0, DFF, 512)):
            w_ = cw[bi % 2]
            b_ = cbias[bi % 2]
            for j in range(3):
                bcast_row(g, "sp", w_[:, j, :], w_.b, g.ffn_conv_w[l, j, n0:n0 + 512])
            bcast_row(g, "sp", b_[:, :], b_.b, g.ffn_conv_b[l, n0:n0 + 512])

            def lane(li, w_=w_, b_=b_, n0=n0):
                acc, t1 = L_acc[li], L_t1[li]
                for k, ti in enumerate(range(li, NT, NL)):
                    ap_, ac_, an_, bb = L_ins[li][k % 2]
                    r0 = ti * 128
                    S.dma("sp", ac_[:, :], g.AB[r0:r0 + 128, n0:n0 + 512], writes=[ac_.b])
                    S.dma("sp", bb[:, :], g.AB[r0:r0 + 128, DFF + n0:DFF + n0 + 512], writes=[bb.b])
                    S.dma("sp", ap_[1:128, :], g.AB[r0:r0 + 127, n0:n0 + 512], writes=[ap_.b])
                    src = g.zrow[0:1, 0:512] if ti in (0, NTC) else g.AB[r0 - 1:r0, n0:n0 + 512]
                    S.dma("sp", ap_[0:1, :], src, writes=[ap_.b])
                    S.dma("sp", an_[0:127, :], g.AB[r0 + 1:r0 + 128, n0:n0 + 512], writes=[an_.b])
                    src = g.zrow[0:1, 0:512] if ti in (NTC - 1, NT - 1) else g.AB[r0 + 128:r0 + 129, n0:n0 + 512]
                    S.dma("sp", an_[127:128, :], src, writes=[an_.b])
                    S.op("dve", lambda e: e.tensor_tensor(out=acc[:, :], in0=ap_[:, :], in1=w_[:, 0, :], op=ALU.mult), reads=[ap_.b, w_.b], writes=[acc.b])
                    S.op("dve", lambda e: e.tensor_tensor(out=t1[:, :], in0=ac_[:, :], in1=w_[:, 1, :], op=ALU.mult), reads=[ac_.b, w_.b], writes=[t1.b])
                    S.op("dve", lambda e: e.tensor_tensor(out=acc[:, :], in0=acc[:, :], in1=t1[:, :], op=ALU.add), reads=[acc.b, t1.b], writes=[acc.b])
                    S.op("dve", lambda e: e.tensor_tensor(out=t1[:, :], in0=an_[:, :], in1=w_[:, 2, :], op=ALU.mult), reads=[an_.b, w_.b], writes=[t1.b])
                    S.op("dve", lambda e: e.tensor_tensor(out=acc[:, :], in0=acc[:, :], in1=t1[:, :], op=ALU.add), reads=[acc.b, t1.b], writes=[acc.b])
                    S.op("dve", lambda e: e.tensor_tensor(out=acc[:, :], in0=acc[:, :], in1=b_[:, :], op=ALU.add), reads=[acc.b, b_.b], writes=[acc.b])
                    S.op("dve", lambda e: e.tensor_tensor(out=t1[:, :], in0=acc[:, :], in1=acc[:, :], op=ALU.mult), reads=[acc.b], writes=[t1.b])
                    S.op("dve", lambda e: e.tensor_scalar(out=t1[:, :], in0=t1[:, :], scalar1=0.044715, scalar2=1.0, op0=ALU.mult, op1=ALU.add), reads=[t1.b], writes=[t1.b])
                    S.op("dve", lambda e: e.tensor_tensor(out=t1[:, :], in0=t1[:, :], in1=acc[:, :], op=ALU.mult), reads=[t1.b, acc.b], writes=[t1.b])
                    S.op("act", lambda e: e.activation(out=t1[:, :], in_=t1[:, :], func=AF.Sigmoid, scale=1.5957691216057308), reads=[t1.b], writes=[t1.b])
                    S.op("dve", lambda e: e.tensor_tensor(out=t1[:, :], in0=t1[:, :], in1=acc[:, :], op=ALU.mult), reads=[t1.b, acc.b], writes=[t1.b])
                    o = L_out[li][k % 2]
                    S.op("dve", lambda e: e.tensor_tensor(out=o[:, :], in0=t1[:, :], in1=bb[:, :], op=ALU.mult), reads=[t1.b, bb.b], writes=[o.b])
                    S.dma("pool", g.GG[r0:r0 + 128, n0:n0 + 512], o[:, :], reads=[o.b])
            S.run_lanes([(lambda li=li: lane(li)) for li in range(NL)])
    S.barrier()


def build(cfg):
    TL, LC, DEPTH = cfg["TL"], cfg["LC"], cfg["DEPTH"]
    T = TL + LC
    nc = bass.Bass("TRN2", target_bir_lowering=False)
    g = Ctx()
    g.nc = nc
    g.uid = 0
    g.ev = 0
    g.pi = 0
    g.T, g.TL, g.LC, g.NT, g.NTC = T, TL, LC, T // 128, LC // 128
    g.S = S = Sync(nc)

    def din(name, shape):
        return nc.dram_tensor(name, list(shape), F32, kind="ExternalInput").ap()

    def dscr(name, shape):
        return nc.dram_tensor(name, list(shape), F32, kind="Internal").ap()

    L = DEPTH
    g.x_in = din("x", [TL, D])
    g.ctx_in = din("ctx", [LC, D])
    g.cvec = din("cvec", [2, D])
    g.ada_w = din("ada_w", [L, D, 6 * D])
    g.ada_b = din("ada_b", [L, 6 * D])
    g.norm1_g = din("norm1_g", [L, D])
    g.norm2_g = din("norm2_g", [L, D])
    g.w_in = din("w_in", [L, D, IN_W])
    g.identd = din("ident", [128, 128])
    g.ga_q_norm = din("ga_q_norm", [L, 128])
    g.ga_k_norm = din("ga_k_norm", [L, 128])
    g.wa_q_norm = din("wa_q_norm", [L, 128])
    g.wa_k_norm = din("wa_k_norm", [L, 128])
    g.wa_sink = din("wa_sink", [L, 4])
    g.mla_cq_norm = din("mla_cq_norm", [L, 384])
    g.mla_ckv_norm = din("mla_ckv_norm", [L, 512])
    g.mla_w_uq = din("mla_w_uq", [L, 384, 768])
    g.mla_w_ukv = din("mla_w_ukv", [L, 512, 1024])
    g.mla_qn_norm = din("mla_qn_norm", [L, 128])
    g.mla_qr_norm = din("mla_qr_norm", [L, 64])
    g.mla_kn_norm = din("mla_kn_norm", [L, 128])
    g.mla_kr_norm = din("mla_kr_norm", [L, 64])
    g.rwkv_mu = din("rwkv_mu", [L, 2, 1984])
    g.rwkv_w0 = din("rwkv_w0", [L, 2, 512])
    g.rwkv_w2 = din("rwkv_w2", [L, 2, 96, 512])
    g.rwkv_a0 = din("rwkv_a0", [L, 2, 512])
    g.rwkv_a2 = din("rwkv_a2", [L, 2, 96, 512])
    g.rwkv_g2 = din("rwkv_g2", [L, 256, 512])
    g.rwkv_k_k = din("rwkv_k_k", [L, 512])
    g.rwkv_k_a = din("rwkv_k_a", [L, 512])
    g.rwkv_r_k = din("rwkv_r_k", [L, 8, 64])
    g.rwkv_lnx_w = din("rwkv_lnx_w", [L, 512])
    g.rwkv_lnx_b = din("rwkv_lnx_b", [L, 512])
    g.w_branch = din("w_branch", [L, 4, 512, D])
    g.w_out = din("w_out", [L, D, D])
    g.ffn_up = din("ffn_up", [L, D, 2 * DFF])
    g.ffn_conv_w = din("ffn_conv_w", [L, 3, DFF])
    g.ffn_conv_b = din("ffn_conv_b", [L, DFF])
    g.ffn_down = din("ffn_down", [L, DFF, D])
    g.trid = din("tri", [64, 2, 64])
    g.mkd = din("mk", [128, 2, 128])
    g.nmtsd = din("nmts", [64, 2, 64])
    g.zrow = din("zrow", [1, NRW])
    g.ropeA = din("ropeA", [2, TL, 128])
    g.ropeM = din("ropeM", [2, TL, 64])
    g.wmaskd = din("wmask", [128, 2, 128])
    g.y_out = nc.dram_tensor("y", [TL, D], F32, kind="ExternalOutput").ap()
    dbg = cfg.get("debug")
    g.XS = dscr("XS", [T, D])
    g.MODB = dscr("MODB", [2, 128, 6 * D])
    g.Z = dscr("Z", [T, IN_W])
    g.QKT = dscr("QKT", [12, 128, T])
    g.YMIX = dscr("YMIX", [T, D])
    g.RW = dscr("RW", [2, 6, T, 512])
    g.RG = dscr("RG", [2, T, 512])
    g.YR = dscr("YR", [2, T, 512])
    g.MERGED = dscr("MERGED", [T, D])
    g.AB = dscr("AB", [T, 2 * DFF])
    g.GG = dscr("GG", [T, DFF])
    g.QML = dscr("QML", [T, 768])
    g.KVML = dscr("KVML", [T, 1024])
    g.MQN = dscr("MQN", [4, 128, T])
    g.MQR = dscr("MQR", [4, 64, T])
    g.MKN = dscr("MKN", [4, 128, T])
    g.MKR = dscr("MKR", [1, 64, T])
    if dbg:
        g.dbg_z = nc.dram_tensor("dbg_z", [T, IN_W], F32, kind="ExternalOutput").ap()
        g.dbg_y = nc.dram_tensor("dbg_y", [T, D], F32, kind="ExternalOutput").ap()
        g.dbg_qkt = nc.dram_tensor("dbg_qkt", [12, 128, T], F32, kind="ExternalOutput").ap()
        g.dbg_xs = nc.dram_tensor("dbg_xs", [T, D], F32, kind="ExternalOutput").ap()

    with ExitStack() as es:
        g.psums = [Tl(es.enter_context(nc.psum_tensor("ps%d" % i, [128, 512], F32)), "ps%d" % i) for i in range(8)]
        g.ident = sb(g, es, "ident", [128, 128])
        S.dma("sp", g.ident[:, :], g.identd[:, :], writes=[g.ident.b])
        g.tri = sb(g, es, "tri", [64, 2, 64])
        g.mk = sb(g, es, "mk", [128, 2, 128])
        g.nmts = sb(g, es, "nmts", [64, 2, 64])
        g.ones64 = sb(g, es, "ones64", [64, 64])
        S.dma("sp", g.tri[:, :, :], g.trid[:, :, :], writes=[g.tri.b])
        S.dma("sp", g.mk[:, :, :], g.mkd[:, :, :], writes=[g.mk.b])
        S.dma("sp", g.nmts[:, :, :], g.nmtsd[:, :, :], writes=[g.nmts.b])
        S.op("dve", lambda e: e.memset(g.ones64[:, :], 1.0), writes=[g.ones64.b])
        g.wmask = sb(g, es, "wmask", [128, 2, 128])
        S.dma("sp", g.wmask[:, :, :], g.wmaskd[:, :, :], writes=[g.wmask.b])
        S.dma("sp", g.XS[0:LC, :], g.ctx_in[:, :])
        S.dma("sp", g.XS[LC:T, :], g.x_in[:, :])
        g.cb = [sb(g, es, "cb", [128, 16, 128]) for _ in range(2)]
        with ExitStack() as st:
            cT = [sb(g, st, "cT", [128, 16]) for _ in range(2)]
            ones = sb(g, st, "ones", [128, 128])
            S.op("dve", lambda e: e.memset(ones[:, :], 1.0), writes=[ones.b])
            for r in range(2):
                S.dma("sp", cT[r][:, :], g.cvec[r, :].rearrange("(kc p) -> p kc", p=128), writes=[cT[r].b], allow_slow_non_contiguous=True)
                S.op("act", lambda e: e.activation(out=cT[r][:, :], in_=cT[r][:, :], func=AF.Silu), reads=[cT[r].b], writes=[cT[r].b])
                for kc in range(16):
                    S.op("dve", lambda e: e.tensor_scalar(out=g.cb[r][:, kc, :], in0=ones[:, :], scalar1=cT[r][:, kc:kc + 1], scalar2=None, op0=ALU.mult),
                         reads=[ones.b, cT[r].b], writes=[g.cb[r].b])
            S.barrier()
        for l in range(L):
            stage_mod(g, l)
            with ExitStack() as st:
                pro = make_norm_pro(g, st, l, g.norm1_g[l, :], D, 0)
                linear(g, g.XS, g.w_in[l], g.Z, T, D, IN_W, G=12, NB=512, pro=pro)
            if cfg.get("stop") == "z":
                break
            stage_attn_prep(g, l)
            stage_attn_gawa(g, l)
            if cfg.get("stop") == "gawa":
                break
            if cfg.get("stop") != "rwkv":
                stage_mla(g, l)
            if cfg.get("stop") == "mla":
                break
            stage_rwkv_prep(g, l)
            stage_rwkv_scan(g, l)
            stage_rwkv_out(g, l)
            if cfg.get("stop") == "rwkv":
                break
            stage_merge(g, l)
            with ExitStack() as st:
                epi = make_resid_epi(g, st, 2 * D)
                linear(g, g.MERGED, g.w_out[l], None, T, D, D, G=12, NB=512, epi=epi)
            if cfg.get("stop") == "attn":
                break
            with ExitStack() as st:
                pro = make_norm_pro(g, st, l, g.norm2_g[l, :], 4 * D, 3 * D)
                linear(g, g.XS, g.ffn_up[l], g.AB, T, D, 2 * DFF, G=12, NB=512, pro=pro)
            stage_conv(g, l)
            with ExitStack() as st:
                epi = make_resid_epi(g, st, 5 * D)
                linear(g, g.GG, g.ffn_down[l], None, T, DFF, D, G=6, NB=256, epi=epi)
        if dbg:
            S.dma("sp", g.dbg_z[:, :], g.Z[:, :])
            S.dma("sp", g.dbg_y[:, :], g.YMIX[:, :])
            S.dma("sp", g.dbg_qkt[:, :, :], g.QKT[:, :, :])
            S.dma("sp", g.dbg_xs[:, :], g.XS[:, :])
        S.dma("sp", g.y_out[:, :], g.XS[LC:T, :])
        S.barrier()
    return nc, g


_CACHE = {}


def rope_table(n_tokens, rot_dim):
    rows = n_tokens // GRID_W
    row = np.repeat(np.arange(rows), GRID_W).astype(np.float32)
    col = np.tile(np.arange(GRID_W), rows).astype(np.float32)
    quarter = rot_dim // 4
    inv_freq = (10000.0 ** (-np.arange(quarter, dtype=np.float32) / quarter)).astype(np.float32)
    ang_r = row[:, None] * inv_freq
    ang_c = col[:, None] * inv_freq
    ang = np.concatenate([ang_r, ang_r, ang_c, ang_c], axis=-1).astype(np.float32)
    sign = np.concatenate([-np.ones(quarter), np.ones(quarter), -np.ones(quarter), np.ones(quarter)]).astype(np.float32)
    return np.stack([np.cos(ang), np.sin(ang) * sign]).astype(np.float32)


def const_tables(TL):
    idx = np.arange(128)
    m_lo = (idx[None, :] <= idx[:, None]).astype(np.float32)
    m_hi = (idx[:, None] <= idx[None, :]).astype(np.float32)
    i64 = np.arange(64)
    tri = np.stack([(i64[:, None] <= i64[None, :]), (i64[:, None] >= i64[None, :])]).astype(np.float32)
    strict = tri - np.eye(64, dtype=np.float32)[None]
    half = np.concatenate([strict, tri], axis=2)
    mk = np.concatenate([half, half], axis=1)
    nmts = -np.transpose(strict, (0, 2, 1))
    return {
        "tri": np.ascontiguousarray(np.transpose(tri, (1, 0, 2))),
        "mk": np.ascontiguousarray(np.transpose(mk, (1, 0, 2))),
        "nmts": np.ascontiguousarray(np.transpose(nmts, (1, 0, 2))),
        "zrow": np.zeros((1, NRW), np.float32),
        "ropeA": rope_table(TL, 128),
        "ropeM": rope_table(TL, 64),
        "wmask": np.ascontiguousarray(np.stack([m_lo, m_hi], axis=1)),
    }


def make_inputs_for_core(inputs, b, L):
    f = lambda a: np.ascontiguousarray(np.asarray(a, dtype=np.float32))
    m = {
        "x": f(inputs["x"][b]),
        "ctx": f(inputs["ctx"][b]),
        "cvec": f(np.stack([np.asarray(inputs["c"][b]), np.asarray(inputs["c_ctx"])])),
        "ident": np.eye(128, dtype=np.float32),
    }
    m.update(const_tables(np.asarray(inputs["x"]).shape[1]))
    for k in ["ada_w", "ada_b", "norm1_g", "norm2_g", "w_in", "ga_q_norm", "ga_k_norm", "wa_q_norm", "wa_k_norm", "wa_sink",
              "mla_cq_norm", "mla_ckv_norm", "mla_w_uq", "mla_w_ukv", "mla_qn_norm", "mla_qr_norm", "mla_kn_norm", "mla_kr_norm",
              "rwkv_mu", "rwkv_w0", "rwkv_w2", "rwkv_a0", "rwkv_a2", "rwkv_g2", "rwkv_k_k", "rwkv_k_a", "rwkv_r_k", "rwkv_lnx_w", "rwkv_lnx_b",
              "w_branch", "w_out", "ffn_up", "ffn_conv_w", "ffn_conv_b", "ffn_down"]:
        m[k] = f(inputs[k][:L])
    return m


def kernel(**inputs):
    x = np.asarray(inputs["x"])
    B, TL, _ = x.shape
    LC = np.asarray(inputs["ctx"]).shape[1]
    L = np.asarray(inputs["ada_w"]).shape[0]
    cfg = {"TL": TL, "LC": LC, "DEPTH": L}
    nc, g = build(cfg)
    n = 8
    in_maps = [make_inputs_for_core(inputs, c % B, L) for c in range(n)]
    res = run_bass_kernel_spmd(nc, in_maps, core_ids=list(range(n)))
    return np.stack([res.results[b]["y"] for b in range(B)]).astype(np.float32)
```

```python
from contextlib import ExitStack
import threading
import numpy as np
import concourse.bass as bass
import concourse.mybir as mybir
from concourse.bass_utils import run_bass_kernel_spmd

F32 = mybir.dt.float32
BF16 = mybir.dt.bfloat16
AF = mybir.ActivationFunctionType
ALU = mybir.AluOpType
AX = mybir.AxisListType

D = 2048
GRID_W = 64
EPS = 1e-6
IN_W = 13376
DFF = 5632
O_GAQ, O_GAK, O_GAV, O_WAQ, O_WAK, O_WAV = 0, 512, 768, 1024, 1536, 1792
O_RKVG, O_ZW, O_ZA, O_CQ, O_CKV, O_KR, O_GATE = 2048, 3840, 4032, 4224, 4608, 5120, 5184
LNX_EPS = 64e-5


class Buf:
    __slots__ = ("name", "w", "r")

    def __init__(self, name=""):
        self.name = name
        self.w = None
        self.r = {}


class Sync:
    MAXC = 30000

    def __init__(self, nc, n_dma_sems=48, same_engine_sync=True):
        self.nc = nc
        self.hw = {"pe": nc.tensor, "dve": nc.vector, "act": nc.scalar, "pool": nc.gpsimd, "sp": nc.sync}
        self.gen = {k: 0 for k in self.hw}
        self.sem = {k: nc.alloc_semaphore("sem_%s_0" % k) for k in self.hw}
        self.count = {k: 0 for k in self.hw}
        self.seen = {k: {} for k in self.hw}
        self.dma_sems = [nc.alloc_semaphore("dsem%d" % i) for i in range(n_dma_sems)]
        self.dma_uses = [0] * n_dma_sems
        self.dma_next = 0
        self.same_engine_sync = same_engine_sync
        self.latest = {}
        self._lane = None

    def run_lanes(self, fns):
        n = len(fns)
        if n == 1:
            fns[0]()
            return
        cv = threading.Condition()
        st = {"turn": 0, "alive": [True] * n, "err": None}
        ids = {}

        def nxt(i):
            for k in range(1, n + 1):
                j = (i + k) % n
                if st["alive"][j]:
                    return j
            return -1

        def worker(i):
            ids[threading.get_ident()] = i
            with cv:
                while st["turn"] != i:
                    cv.wait()
            try:
                fns[i]()
            except BaseException as e:
                st["err"] = e
            with cv:
                st["alive"][i] = False
                st["turn"] = nxt(i)
                cv.notify_all()

        self._lane = (cv, st, ids, nxt)
        self.lane_pi = {}
        threads = [threading.Thread(target=worker, args=(i,)) for i in range(n)]
        for t in threads:
            t.start()
        for t in threads:
            t.join()
        self._lane = None
        if st["err"] is not None:
            raise st["err"]

    def lane_index(self):
        if self._lane is None:
            return None
        i = self._lane[2].get(threading.get_ident())
        if i is None:
            return None
        return i, len(self._lane[1]["alive"])

    def lane_yield(self):
        if self._lane is None:
            return
        cv, st, ids, nxt = self._lane
        i = ids.get(threading.get_ident())
        if i is None:
            return
        with cv:
            j = nxt(i)
            if j == i or j < 0:
                return
            st["turn"] = j
            cv.notify_all()
            while st["turn"] != i:
                cv.wait()

    def _wait(self, eng, dep):
        key, sem, val = dep
        if self.seen[eng].get(key, 0) >= val:
            return
        self.hw[eng].wait_ge(sem, val)
        self.seen[eng][key] = val

    @staticmethod
    def _add(deps, d):
        if d is not None and deps.get(d[0], (0, 0, 0))[2] < d[2]:
            deps[d[0]] = d

    def _deps(self, reads, writes):
        deps = {}
        for b in reads:
            self._add(deps, b.w)
        for b in writes:
            self._add(deps, b.w)
            for d in b.r.values():
                self._add(deps, d)
        return deps

    def _mark(self, dep, reads, writes):
        for b in reads:
            b.r[dep[0]] = dep
        for b in writes:
            b.w = dep
            b.r = {}
        self.latest[dep[0]] = dep

    def op(self, eng, fn, reads=(), writes=()):
        for key, d in self._deps(reads, writes).items():
            if isinstance(key, tuple) and key[0] == eng and (eng == "pe" or not self.same_engine_sync):
                continue
            self._wait(eng, d)
        ins = fn(self.hw[eng])
        self.count[eng] += 1
        ins.then_inc(self.sem[eng], 1)
        self._mark(((eng, self.gen[eng]), self.sem[eng], self.count[eng]), reads, writes)
        if self.count[eng] >= self.MAXC:
            self.gen[eng] += 1
            self.sem[eng] = self.nc.alloc_semaphore("sem_%s_%d" % (eng, self.gen[eng]))
            self.count[eng] = 0
        self.lane_yield()
        return ins

    def dma(self, q, out, in_, reads=(), writes=(), **kw):
        i = self.dma_next
        self.dma_next = (self.dma_next + 1) % len(self.dma_sems)
        sem = self.dma_sems[i]
        key = "d%d" % i
        if self.dma_uses[i] > 0:
            self._wait(q, (key, sem, 16 * self.dma_uses[i]))
        for k, d in self._deps(reads, writes).items():
            self._wait(q, d)
        self.dma_uses[i] += 1
        ins = self.hw[q].dma_start(out=out, in_=in_, **kw)
        ins.then_inc(sem, 16)
        self._mark((key, sem, 16 * self.dma_uses[i]), reads, writes)
        self.lane_yield()
        return ins

    def barrier(self):
        for eng in self.hw:
            for dep in list(self.latest.values()):
                key = dep[0]
                if isinstance(key, tuple) and key[0] == "pe" and eng == "pe":
                    continue
                self._wait(eng, dep)


class Tl:
    def __init__(self, t, name):
        self.t = t
        self.b = Buf(name)

    def __getitem__(self, idx):
        return self.t[idx]


class Ctx:
    pass


def sb(g, st, name, shape, dtype=F32):
    g.uid += 1
    nm = "%s_%d" % (name, g.uid)
    return Tl(st.enter_context(g.nc.sbuf_tensor(nm, list(shape), dtype)), nm)


def evac_engine(g):
    g.ev += 1
    return "dve" if g.ev % 2 else "act"


def copy_op(g, eng, out, in_, reads, writes):
    if eng == "act":
        return g.S.op("act", lambda e: e.copy(out=out, in_=in_), reads=reads, writes=writes)
    return g.S.op(eng, lambda e: e.tensor_copy(out=out, in_=in_), reads=reads, writes=writes)


def next_psum(g):
    lane = g.S.lane_index()
    if lane is None:
        g.pi = (g.pi + 1) % len(g.psums)
        return g.psums[g.pi]
    i, n = lane
    lo, hi = i * 8 // n, (i + 1) * 8 // n
    k = g.S.lane_pi.get(i, 0)
    g.S.lane_pi[i] = k + 1
    return g.psums[lo + k % (hi - lo)]


def linear(g, x, w, y, T, K, N, G=8, NB=512, pro=None, epi=None, segs=None, wlist=None):
    S, nc = g.S, g.nc
    KC = K // 128
    assert K % 128 == 0 and T % 128 == 0
    NT = T // 128
    XW = min(K, 2048)
    if segs is None:
        segs = [(0, KC)]
    with ExitStack() as st:
        xt = sb(g, st, "xt", [128, KC, G * 128], BF16)
        xin = [sb(g, st, "xin", [128, XW]) for _ in range(2)]
        wt = [sb(g, st, "wt", [128, KC, NB], BF16) for _ in range(3)]
        ot = [sb(g, st, "ot", [128, NB]) for _ in range(3)]
        nxin = nw = no = 0
        for g0 in range(0, NT, G):
            gn = min(G, NT - g0)
            for gi in range(gn):
                ti = g0 + gi
                for c0 in range(0, K, XW):
                    cw = min(XW, K - c0)
                    xi = xin[nxin % 2]
                    nxin += 1
                    S.dma("sp", xi[:, 0:cw], x[ti * 128:(ti + 1) * 128, c0:c0 + cw], writes=[xi.b])
                    if pro is not None:
                        pro(xi, ti, cw)
                    for k4 in range(0, cw // 128, 4):
                        ps = next_psum(g)
                        nk = min(4, cw // 128 - k4)
                        for j in range(nk):
                            kk = k4 + j
                            S.op("pe", lambda e: e.transpose(ps[:, j * 128:(j + 1) * 128], xi[:, kk * 128:(kk + 1) * 128], g.ident[:]),
                                 reads=[xi.b, g.ident.b], writes=[ps.b])
                        kc0 = c0 // 128 + k4
                        copy_op(g, evac_engine(g), xt[:, kc0:kc0 + nk, gi * 128:(gi + 1) * 128],
                                ps[:, 0:nk * 128].rearrange("p (k t) -> p k t", k=nk), [ps.b], [xt.b])
            for n0 in range(0, N, NB):
                nb = min(NB, N - n0)
                wi = wt[nw % 3]
                nw += 1
                S.dma("pool", wi[:, :, 0:nb], w.rearrange("(kc p) n -> p kc n", p=128)[:, :, n0:n0 + nb], writes=[wi.b])
                for gi in range(gn):
                    ti = g0 + gi
                    pss = []
                    for (k0, k1) in segs:
                        ps = next_psum(g)
                        pss.append(ps)
                        for kc in range(k0, k1):
                            S.op("pe", lambda e: e.matmul(ps[:, 0:nb], lhsT=xt[:, kc, gi * 128:(gi + 1) * 128], rhs=wi[:, kc, 0:nb],
                                                          start=(kc == k0), stop=(kc == k1 - 1)),
                                 reads=[xt.b, wi.b], writes=[ps.b])
                    o = ot[no % 3]
                    no += 1
                    if epi is None:
                        copy_op(g, evac_engine(g), o[:, 0:nb], pss[0][:, 0:nb], [pss[0].b], [o.b])
                        S.dma("sp", y[ti * 128:(ti + 1) * 128, n0:n0 + nb], o[:, 0:nb], reads=[o.b])
                    else:
                        epi(pss, o, ti, n0, nb)
    S.barrier()


def rms_rstd(g, st_tiles, x_ap, n, width, reads_b, sq, ss, rstd, eps=EPS):
    S = g.S
    S.op("act", lambda e: e.activation(out=sq[:, 0:n * width].rearrange("p (n w) -> p n w", n=n), in_=x_ap, func=AF.Square),
         reads=[reads_b], writes=[sq.b])
    S.op("dve", lambda e: e.tensor_reduce(out=ss[:, 0:n], in_=sq[:, 0:n * width].rearrange("p (n w) -> p n w", n=n), axis=AX.X, op=ALU.add),
         reads=[sq.b], writes=[ss.b])
    S.op("dve", lambda e: e.tensor_scalar(out=ss[:, 0:n], in0=ss[:, 0:n], scalar1=1.0 / width, scalar2=eps, op0=ALU.mult, op1=ALU.add),
         reads=[ss.b], writes=[ss.b])
    S.op("act", lambda e: e.activation(out=ss[:, 0:n], in_=ss[:, 0:n], func=AF.Sqrt), reads=[ss.b], writes=[ss.b])
    S.op("dve", lambda e: e.reciprocal(out=rstd[:, 0:n], in_=ss[:, 0:n]), reads=[ss.b], writes=[rstd.b])


def bcast_row(g, q, tile_ap, buf, dram_row_ap):
    g.S.dma(q, tile_ap, dram_row_ap.partition_broadcast(128), writes=[buf])


def stage_mod(g, l):
    S, nc = g.S, g.nc
    with ExitStack() as st:
        wt = [sb(g, st, "mw", [128, 16, 512]) for _ in range(2)]
        bt = [sb(g, st, "mb", [128, 512]) for _ in range(2)]
        ot = [sb(g, st, "mo", [128, 512]) for _ in range(3)]
        no = 0
        for bi, n0 in enumerate(range(0, 6 * D, 512)):
            wi = wt[bi % 2]
            bb = bt[bi % 2]
            S.dma("sp", wi[:, :, :], g.ada_w[l].rearrange("(kc p) n -> p kc n", p=128)[:, :, n0:n0 + 512], writes=[wi.b])
            bcast_row(g, "sp", bb[:, :], bb.b, g.ada_b[l, n0:n0 + 512])
            for r in range(2):
                ps = next_psum(g)
                for kc in range(16):
                    S.op("pe", lambda e: e.matmul(ps[:, :], lhsT=g.cb[r][:, kc, :], rhs=wi[:, kc, :], start=(kc == 0), stop=(kc == 15)),
                         reads=[g.cb[r].b, wi.b], writes=[ps.b])
                o = ot[no % 3]
                no += 1
                S.op("dve", lambda e: e.tensor_tensor(out=o[:, :], in0=ps[:, :], in1=bb[:, :], op=ALU.add), reads=[ps.b, bb.b], writes=[o.b])
                S.dma("pool", g.MODB[r, :, n0:n0 + 512], o[:, :], reads=[o.b])
    S.barrier()


def make_norm_pro(g, st, l, gain_dram, sc_off, sh_off):
    S = g.S
    A = [sb(g, st, "nA", [128, D]) for _ in range(2)]
    B = [sb(g, st, "nB", [128, D]) for _ in range(2)]
    sq = sb(g, st, "nsq", [128, D])
    ss = sb(g, st, "nss", [128, 1])
    rstd = sb(g, st, "nrs", [128, 1])
    with ExitStack() as st2:
        gt = sb(g, st2, "ng", [128, D])
        bcast_row(g, "sp", gt[:, :], gt.b, gain_dram)
        for r in range(2):
            S.dma("sp", A[r][:, :], g.MODB[r, :, sc_off:sc_off + D], writes=[A[r].b])
            S.dma("sp", B[r][:, :], g.MODB[r, :, sh_off:sh_off + D], writes=[B[r].b])
            S.op("dve", lambda e: e.scalar_tensor_tensor(out=A[r][:, :], in0=A[r][:, :], scalar=1.0, in1=gt[:, :], op0=ALU.add, op1=ALU.mult),
                 reads=[A[r].b, gt.b], writes=[A[r].b])
        S.barrier()

    def pro(xi, ti, cw):
        r = 1 if ti < g.NTC else 0
        rms_rstd(g, None, xi[:, 0:D].rearrange("p (n w) -> p n w", n=1), 1, D, xi.b, sq, ss, rstd)
        S.op("dve", lambda e: e.scalar_tensor_tensor(out=xi[:, 0:D], in0=xi[:, 0:D], scalar=rstd[:, 0:1], in1=A[r][:, :], op0=ALU.mult, op1=ALU.mult),
             reads=[xi.b, rstd.b, A[r].b], writes=[xi.b])
        S.op("dve", lambda e: e.tensor_tensor(out=xi[:, 0:D], in0=xi[:, 0:D], in1=B[r][:, :], op=ALU.add), reads=[xi.b, B[r].b], writes=[xi.b])
    return pro


def make_resid_epi(g, st, gate_off):
    S = g.S
    gts = [sb(g, st, "rg", [128, D]) for _ in range(2)]
    for r in range(2):
        S.dma("sp", gts[r][:, :], g.MODB[r, :, gate_off:gate_off + D], writes=[gts[r].b])
    xb = [sb(g, st, "rx", [128, 512]) for _ in range(3)]
    cnt = [0]

    def epi(pss, o, ti, n0, nb):
        r = 1 if ti < g.NTC else 0
        x = xb[cnt[0] % 3]
        cnt[0] += 1
        S.dma("sp", x[:, 0:nb], g.XS[ti * 128:(ti + 1) * 128, n0:n0 + nb], writes=[x.b])
        S.op("dve", lambda e: e.tensor_tensor(out=o[:, 0:nb], in0=pss[0][:, 0:nb], in1=gts[r][:, n0:n0 + nb], op=ALU.mult),
             reads=[pss[0].b, gts[r].b], writes=[o.b])
        S.op("dve", lambda e: e.tensor_tensor(out=o[:, 0:nb], in0=o[:, 0:nb], in1=x[:, 0:nb], op=ALU.add), reads=[o.b, x.b], writes=[o.b])
        S.dma("sp", g.XS[ti * 128:(ti + 1) * 128, n0:n0 + nb], o[:, 0:nb], reads=[o.b])
    return epi


def norm_rope(g, src, src_b, n, w, gain_ap, gain_b, out, out_b, tmp, sq, ss, rstd, rope=None):
    S = g.S
    rms_rstd(g, None, src, n, w, src_b, sq, ss, rstd)
    dst = out if rope is None else tmp[:, 0:n * w].rearrange("p (n w) -> p n w", n=n)
    dst_b = out_b if rope is None else tmp.b
    S.op("dve", lambda e: e.tensor_tensor(out=dst, in0=src, in1=rstd[:, 0:n].unsqueeze(2).broadcast_to([128, n, w]), op=ALU.mult),
         reads=[src_b, rstd.b], writes=[dst_b])
    S.op("dve", lambda e: e.tensor_tensor(out=dst, in0=dst, in1=gain_ap, op=ALU.mult), reads=[dst_b, gain_b], writes=[dst_b])
    if rope is None:
        return
    cos, ssin, blk = rope
    nh = w // (2 * blk)
    xv = dst.rearrange("p n (h b k) -> p n h b k", h=nh, b=2, k=blk)
    sv = ssin[:, 0:w].rearrange("p (h b k) -> p h b k", h=nh, b=2, k=blk)
    sw = sq[:, 0:n * w].rearrange("p (n h b k) -> p n h b k", n=n, h=nh, b=2, k=blk)
    for n_i in range(n):
        for b in range(2):
            S.op("dve", lambda e: e.tensor_tensor(out=sw[:, n_i, :, b, :], in0=xv[:, n_i, :, 1 - b, :], in1=sv[:, :, b, :], op=ALU.mult),
                 reads=[dst_b, ssin.b], writes=[sq.b])
    S.op("dve", lambda e: e.tensor_tensor(out=dst, in0=dst, in1=cos[:, 0:w].unsqueeze(1).broadcast_to([128, n, w]), op=ALU.mult),
         reads=[dst_b, cos.b], writes=[dst_b])
    S.op("dve", lambda e: e.tensor_tensor(out=out, in0=dst, in1=sq[:, 0:n * w].rearrange("p (n w) -> p n w", n=n), op=ALU.add),
         reads=[dst_b, sq.b], writes=[out_b])


def transpose_slots(g, src_tl, slots, w, dst_tl, dst_slot0):
    S = g.S
    for i0 in range(0, len(slots), 4):
        grp = slots[i0:i0 + 4]
        ps = next_psum(g)
        for j, s_ in enumerate(grp):
            S.op("pe", lambda e: e.transpose(ps[0:w, j * 128:(j + 1) * 128], src_tl[:, s_, 0:w], g.ident[:]),
                 reads=[src_tl.b, g.ident.b], writes=[ps.b])
        copy_op(g, evac_engine(g), dst_tl[0:w, dst_slot0 + i0:dst_slot0 + i0 + len(grp), :],
                ps[0:w, 0:len(grp) * 128].rearrange("p (k t) -> p k t", k=len(grp)), [ps.b], [dst_tl.b])


def stage_attn_prep(g, l):
    S = g.S
    with ExitStack() as st:
        gain = sb(g, st, "apg", [128, 2, 6, 128])
        for gi, (qn, kn) in enumerate([(g.ga_q_norm, g.ga_k_norm), (g.wa_q_norm, g.wa_k_norm)]):
            for s_ in range(6):
                bcast_row(g, "sp", gain[:, gi, s_, :], gain.b, (qn if s_ < 4 else kn)[l, :])
        NL = 3
        LT = [dict(z=sb(g, st, "apz", [128, 16, 128]), x=sb(g, st, "apx", [128, 12, 128]), t_=sb(g, st, "apt", [128, 12, 128]),
                   cs=sb(g, st, "apc", [128, 128]), sn=sb(g, st, "aps", [128, 128]),
                   tmp=sb(g, st, "aptmp", [128, 6 * 128]), sq=sb(g, st, "apsq", [128, 6 * 128]),
                   ss=sb(g, st, "apss", [128, 8]), rstd=sb(g, st, "aprs", [128, 8])) for _ in range(NL)]

        def lane(li):
          lt = LT[li]
          tmp, sq, ss, rstd = lt["tmp"], lt["sq"], lt["ss"], lt["rstd"]
          cs = [lt["cs"], lt["cs"]]
          sn = [lt["sn"], lt["sn"]]
          for ti in range(li, g.NT, NL):
            z = lt["z"]
            x = lt["x"]
            t_ = lt["t_"]
            S.dma("sp", z[:, :, :], g.Z[ti * 128:(ti + 1) * 128, 0:2048].rearrange("p (s d) -> p s d", s=16), writes=[z.b])
            rope = None
            if ti >= g.NTC:
                c_, s_t = cs[ti % 2], sn[ti % 2]
                p0 = (ti - g.NTC) * 128
                S.dma("sp", c_[:, :], g.ropeA[0, p0:p0 + 128, :], writes=[c_.b])
                S.dma("sp", s_t[:, :], g.ropeA[1, p0:p0 + 128, :], writes=[s_t.b])
                rope = (c_, s_t, 32)
            for gi, base in enumerate([0, 8]):
                norm_rope(g, z[:, base:base + 6, :], z.b, 6, 128, gain[:, gi, :, :], gain.b,
                          x[:, gi * 6:(gi + 1) * 6, :], x.b, tmp, sq, ss, rstd, rope=rope)
            transpose_slots(g, x, list(range(12)), 128, t_, 0)
            S.dma("pool", g.QKT[:, :, ti * 128:(ti + 1) * 128].rearrange("s d t -> d s t"), t_[:, :, :], reads=[t_.b])
        S.run_lanes([(lambda li=li: lane(li)) for li in range(NL)])
    S.barrier()


def attention(g, heads, scale, sink_tl=None):
    S = g.S
    T, NT = g.T, g.NT
    po = g.psums[0:4]
    sps = g.psums[4:8]
    with ExitStack() as st:
        kts = [sb(g, st, "akt", [128, T], BF16) for _ in range(2)]
        vt = sb(g, st, "avt", [128, NT, 129], BF16)
        qts = [[sb(g, st, "aqt", [128, 512], BF16) for _ in range(2)] for _ in range(3)]
        pts = [sb(g, st, "apt", [128, 512], BF16) for _ in range(3)]
        yts = [sb(g, st, "ayt", [128, 128]) for _ in range(3)]
        den = sb(g, st, "aden", [128, 4])
        S.op("dve", lambda e: e.memset(vt[:, :, 128:129], 1.0), writes=[vt.b])
        nq = npt = ny = nsp = 0
        for hd in heads:
            for pi_, (kd_ap, kd) in enumerate(hd["kparts"]):
                S.dma("pool", kts[pi_][0:kd, :], kd_ap, writes=[kts[pi_].b])
            S.dma("pool", vt[:, :, 0:128], hd["v"].rearrange("(n p) d -> p n d", p=128), writes=[vt.b])
            for qh, qparts in enumerate(hd["qparts"]):
                for (q0, qlen, keys) in hd["blocks"]:
                    nqb = qlen // 128
                    qt = qts[nq % 3]
                    nq += 1
                    for pi_, (qd_ap, kd) in enumerate(qparts):
                        S.dma("pool", qt[pi_][0:kd, 0:qlen], qd_ap[:, q0:q0 + qlen], writes=[qt[pi_].b])
                    npart = len(qparts)

                    def scores(kt_):
                        nonlocal nsp
                        ps_ = sps[nsp % 4]
                        nsp += 1
                        for pi_, (qd_ap, kd) in enumerate(qparts):
                            S.op("pe", lambda e: e.matmul(ps_[:, 0:qlen], lhsT=kts[pi_][0:kd, kt_ * 128:(kt_ + 1) * 128], rhs=qt[pi_][0:kd, 0:qlen],
                                                          start=(pi_ == 0), stop=(pi_ == npart - 1)),
                                 reads=[kts[pi_].b, qt[pi_].b], writes=[ps_.b])
                        return ps_
                    ps_next = scores(keys[0][0])
                    for idx, (kt, mask) in enumerate(keys):
                        ps = ps_next
                        if idx + 1 < len(keys):
                            ps_next = scores(keys[idx + 1][0])
                        pt = pts[npt % 3]
                        npt += 1
                        S.op("act", lambda e: e.activation(out=pt[:, 0:qlen], in_=ps[:, 0:qlen], func=AF.Exp, scale=scale),
                             reads=[ps.b], writes=[pt.b])
                        if mask is not None:
                            S.op("dve", lambda e: e.tensor_tensor(out=pt[:, 0:qlen], in0=pt[:, 0:qlen], in1=g.wmask[:, mask, :], op=ALU.mult),
                                 reads=[pt.b, g.wmask.b], writes=[pt.b])
                        for qb in range(nqb):
                            S.op("pe", lambda e: e.matmul(po[qb][:, 0:129], lhsT=pt[:, qb * 128:(qb + 1) * 128], rhs=vt[:, kt, :],
                                                          start=(idx == 0), stop=(idx == len(keys) - 1)),
                                 reads=[pt.b, vt.b], writes=[po[qb].b])
                    for qb in range(nqb):
                        y = yts[ny % 3]
                        ny += 1
                        if hd.get("sink_idx") is not None:
                            si = hd["sink_idx"][qh]
                            S.op("dve", lambda e: e.tensor_tensor(out=den[:, 0:1], in0=po[qb][:, 128:129], in1=sink_tl[:, si:si + 1], op=ALU.add),
                                 reads=[po[qb].b, sink_tl.b], writes=[den.b])
                            S.op("dve", lambda e: e.reciprocal(out=den[:, 0:1], in_=den[:, 0:1]), reads=[den.b], writes=[den.b])
                        else:
                            S.op("dve", lambda e: e.reciprocal(out=den[:, 0:1], in_=po[qb][:, 128:129]), reads=[po[qb].b], writes=[den.b])
                        S.op("dve", lambda e: e.tensor_scalar(out=y[:, :], in0=po[qb][:, 0:128], scalar1=den[:, 0:1], scalar2=None, op0=ALU.mult),
                             reads=[po[qb].b, den.b], writes=[y.b])
                        r0 = q0 + qb * 128
                        yc = hd["ycols"][qh]
                        S.dma("sp", g.YMIX[r0:r0 + 128, yc:yc + 128], y[:, :], reads=[y.b])
    S.barrier()


def dense_blocks(g):
    blocks = [(0, g.LC, [(kt, None) for kt in range(g.NTC)])]
    for q0 in range(g.LC, g.T, 512):
        blocks.append((q0, min(512, g.T - q0), [(kt, None) for kt in range(g.NT)]))
    return blocks


def window_blocks(g):
    blocks = [(0, g.LC, [(kt, None) for kt in range(g.NTC)])]
    nblk = g.TL // 128
    for n in range(nblk):
        keys = [(kt, None) for kt in range(g.NTC)]
        if n > 0:
            keys.append((g.NTC + n - 1, 0))
        keys.append((g.NTC + n, None))
        if n < nblk - 1:
            keys.append((g.NTC + n + 1, 1))
        blocks.append((g.LC + n * 128, 128, keys))
    return blocks


def stage_attn_gawa(g, l):
    S = g.S
    heads = []
    for hk in range(2):
        heads.append(dict(kparts=[(g.QKT[4 + hk], 128)], qparts=[[(g.QKT[2 * hk + j], 128)] for j in range(2)],
                          v=g.Z[:, O_GAV + hk * 128:O_GAV + (hk + 1) * 128], ycols=[(2 * hk + j) * 128 for j in range(2)],
                          blocks=dense_blocks(g)))
    attention(g, heads, 128 ** -0.5)
    with ExitStack() as st:
        sink = sb(g, st, "sink", [128, 4])
        bcast_row(g, "sp", sink[:, :], sink.b, g.wa_sink[l, :])
        S.op("act", lambda e: e.activation(out=sink[:, :], in_=sink[:, :], func=AF.Exp), reads=[sink.b], writes=[sink.b])
        heads = []
        for hk in range(2):
            heads.append(dict(kparts=[(g.QKT[10 + hk], 128)], qparts=[[(g.QKT[6 + 2 * hk + j], 128)] for j in range(2)],
                              v=g.Z[:, O_WAV + hk * 128:O_WAV + (hk + 1) * 128], ycols=[512 + (2 * hk + j) * 128 for j in range(2)],
                              blocks=window_blocks(g), sink_idx=[2 * hk, 2 * hk + 1]))
        attention(g, heads, 128 ** -0.5, sink_tl=sink)


def make_rms_pro(g, st, gain_dram, K):
    S = g.S
    gt = sb(g, st, "pg", [128, K])
    sq = sb(g, st, "psq", [128, K])
    ss = sb(g, st, "pss", [128, 1])
    rstd = sb(g, st, "prs", [128, 1])
    bcast_row(g, "sp", gt[:, :], gt.b, gain_dram)

    def pro(xi, ti, cw):
        rms_rstd(g, None, xi[:, 0:K].rearrange("p (n w) -> p n w", n=1), 1, K, xi.b, sq, ss, rstd)
        S.op("dve", lambda e: e.scalar_tensor_tensor(out=xi[:, 0:K], in0=xi[:, 0:K], scalar=rstd[:, 0:1], in1=gt[:, :], op0=ALU.mult, op1=ALU.mult),
             reads=[xi.b, rstd.b, gt.b], writes=[xi.b])
    return pro


def stage_mla(g, l):
    S = g.S
    T = g.T
    with ExitStack() as st:
        pro = make_rms_pro(g, st, g.mla_cq_norm[l, :], 384)
        linear(g, g.Z[:, O_CQ:O_CQ + 384], g.mla_w_uq[l], g.QML, T, 384, 768, G=17, NB=512, pro=pro)
    with ExitStack() as st:
        pro = make_rms_pro(g, st, g.mla_ckv_norm[l, :], 512)
        linear(g, g.Z[:, O_CKV:O_CKV + 512], g.mla_w_ukv[l], g.KVML, T, 512, 1024, G=17, NB=512, pro=pro)
    with ExitStack() as st:
        gq_n = sb(g, st, "mgqn", [128, 128])
        gq_r = sb(g, st, "mgqr", [128, 64])
        gk_n = sb(g, st, "mgkn", [128, 128])
        gk_r = sb(g, st, "mgkr", [128, 64])
        bcast_row(g, "sp", gq_n[:, :], gq_n.b, g.mla_qn_norm[l, :])
        bcast_row(g, "sp", gq_r[:, :], gq_r.b, g.mla_qr_norm[l, :])
        bcast_row(g, "sp", gk_n[:, :], gk_n.b, g.mla_kn_norm[l, :])
        bcast_row(g, "sp", gk_r[:, :], gk_r.b, g.mla_kr_norm[l, :])
        NL = 3
        qin = [sb(g, st, "mq", [128, 4, 192]) for _ in range(NL)]
        kvin = [sb(g, st, "mkv", [128, 4, 256]) for _ in range(NL)]
        krin = [sb(g, st, "mkr", [128, 1, 64]) for _ in range(NL)]
        xqn = [sb(g, st, "xqn", [128, 4, 128]) for _ in range(NL)]
        xqr = [sb(g, st, "xqr", [128, 4, 64]) for _ in range(NL)]
        xkn = [sb(g, st, "xkn", [128, 4, 128]) for _ in range(NL)]
        xkr = [sb(g, st, "xkr", [128, 1, 64]) for _ in range(NL)]
        tqn = [sb(g, st, "tqn", [128, 4, 128]) for _ in range(NL)]
        tqr = [sb(g, st, "tqr", [64, 4, 128]) for _ in range(NL)]
        tkn = [sb(g, st, "tkn", [128, 4, 128]) for _ in range(NL)]
        tkr = [sb(g, st, "tkr", [64, 1, 128]) for _ in range(NL)]
        cs = [sb(g, st, "mc", [128, 64]) for _ in range(NL)]
        sn = [sb(g, st, "ms", [128, 64]) for _ in range(NL)]
        tmps = [sb(g, st, "mtmp", [128, 512]) for _ in range(NL)]
        sqs = [sb(g, st, "msq", [128, 512]) for _ in range(NL)]
        sss = [sb(g, st, "mss", [128, 8]) for _ in range(NL)]
        rstds = [sb(g, st, "mrs", [128, 8]) for _ in range(NL)]

        def lane(i):
          tmp, sq, ss, rstd = tmps[i], sqs[i], sss[i], rstds[i]
          for ti in range(i, g.NT, NL):
            rows = slice(ti * 128, (ti + 1) * 128)
            S.dma("sp", qin[i][:, :, :], g.QML[rows, :].rearrange("p (h d) -> p h d", h=4), writes=[qin[i].b])
            S.dma("sp", kvin[i][:, :, :], g.KVML[rows, :].rearrange("p (h d) -> p h d", h=4), writes=[kvin[i].b])
            S.dma("sp", krin[i][:, 0, :], g.Z[rows, O_KR:O_KR + 64], writes=[krin[i].b])
            rope = None
            if ti >= g.NTC:
                p0 = (ti - g.NTC) * 128
                S.dma("sp", cs[i][:, :], g.ropeM[0, p0:p0 + 128, :], writes=[cs[i].b])
                S.dma("sp", sn[i][:, :], g.ropeM[1, p0:p0 + 128, :], writes=[sn[i].b])
                rope = (cs[i], sn[i], 16)
            norm_rope(g, qin[i][:, :, 0:128], qin[i].b, 4, 128, gq_n[:, :].unsqueeze(1).broadcast_to([128, 4, 128]), gq_n.b,
                      xqn[i][:, :, :], xqn[i].b, tmp, sq, ss, rstd)
            norm_rope(g, qin[i][:, :, 128:192], qin[i].b, 4, 64, gq_r[:, :].unsqueeze(1).broadcast_to([128, 4, 64]), gq_r.b,
                      xqr[i][:, :, :], xqr[i].b, tmp, sq, ss, rstd, rope=rope)
            norm_rope(g, kvin[i][:, :, 0:128], kvin[i].b, 4, 128, gk_n[:, :].unsqueeze(1).broadcast_to([128, 4, 128]), gk_n.b,
                      xkn[i][:, :, :], xkn[i].b, tmp, sq, ss, rstd)
            norm_rope(g, krin[i][:, :, :], krin[i].b, 1, 64, gk_r[:, :].unsqueeze(1), gk_r.b,
                      xkr[i][:, :, :], xkr[i].b, tmp, sq, ss, rstd, rope=rope)
            transpose_slots(g, xqn[i], [0, 1, 2, 3], 128, tqn[i], 0)
            transpose_slots(g, xqr[i], [0, 1, 2, 3], 64, tqr[i], 0)
            transpose_slots(g, xkn[i], [0, 1, 2, 3], 128, tkn[i], 0)
            transpose_slots(g, xkr[i], [0], 64, tkr[i], 0)
            cols = slice(ti * 128, (ti + 1) * 128)
            S.dma("pool", g.MQN[:, :, cols].rearrange("s d t -> d s t"), tqn[i][:, :, :], reads=[tqn[i].b])
            S.dma("pool", g.MQR[:, :, cols].rearrange("s d t -> d s t"), tqr[i][:, :, :], reads=[tqr[i].b])
            S.dma("pool", g.MKN[:, :, cols].rearrange("s d t -> d s t"), tkn[i][:, :, :], reads=[tkn[i].b])
            S.dma("pool", g.MKR[:, :, cols].rearrange("s d t -> d s t"), tkr[i][:, :, :], reads=[tkr[i].b])
        S.run_lanes([(lambda i=i: lane(i)) for i in range(NL)])
    S.barrier()
    heads = []
    for h in range(4):
        heads.append(dict(kparts=[(g.MKN[h], 128), (g.MKR[0], 64)], qparts=[[(g.MQN[h], 128), (g.MQR[h], 64)]],
                          v=g.KVML[:, h * 256 + 128:h * 256 + 256], ycols=[1536 + h * 128], blocks=dense_blocks(g)))
    attention(g, heads, 192 ** -0.5)


RQ_R, RQ_V, RQ_LW, RQ_K, RQ_KK, RQ_KKA = 0, 1, 2, 3, 4, 5
RQ_SCAN = [RQ_R, RQ_LW, RQ_K, RQ_V, RQ_KK, RQ_KKA]
NRW = 2176


def stage_rwkv_prep(g, l):
    S = g.S
    NT, NTC = g.NT, g.NTC
    with ExitStack() as st:
        MU = [sb(g, st, "rmu", [128, NRW]) for _ in range(2)]
        w0 = [sb(g, st, "rw0", [128, 512]) for _ in range(2)]
        a0 = [sb(g, st, "ra0", [128, 512]) for _ in range(2)]
        w2 = [sb(g, st, "rw2", [96, 512]) for _ in range(2)]
        a2 = [sb(g, st, "ra2", [96, 512]) for _ in range(2)]
        g2 = sb(g, st, "rg2", [128, 2, 512])
        kk_ = sb(g, st, "rkk", [128, 512])
        ka = sb(g, st, "rka", [128, 512])
        omka = sb(g, st, "romka", [128, 512])
        for d in range(2):
            S.op("dve", lambda e: e.memset(MU[d][:, :], 0.0), writes=[MU[d].b])
            bcast_row(g, "sp", MU[d][:, 0:1792], MU[d].b, g.rwkv_mu[l, d, 0:1792])
            bcast_row(g, "sp", MU[d][:, 1792 + d * 96:1792 + (d + 1) * 96], MU[d].b, g.rwkv_mu[l, d, 1792:1888])
            bcast_row(g, "sp", MU[d][:, 1984 + d * 96:1984 + (d + 1) * 96], MU[d].b, g.rwkv_mu[l, d, 1888:1984])
            bcast_row(g, "sp", w0[d][:, :], w0[d].b, g.rwkv_w0[l, d, :])
            bcast_row(g, "sp", a0[d][:, :], a0[d].b, g.rwkv_a0[l, d, :])
            S.dma("sp", w2[d][:, :], g.rwkv_w2[l, d], writes=[w2[d].b])
            S.dma("sp", a2[d][:, :], g.rwkv_a2[l, d], writes=[a2[d].b])
        S.dma("sp", g2[:, :, :], g.rwkv_g2[l].rearrange("(kc p) n -> p kc n", p=128), writes=[g2.b])
        bcast_row(g, "sp", kk_[:, :], kk_.b, g.rwkv_k_k[l, :])
        bcast_row(g, "sp", ka[:, :], ka.b, g.rwkv_k_a[l, :])
        S.op("dve", lambda e: e.tensor_scalar(out=omka[:, :], in0=ka[:, :], scalar1=-1.0, scalar2=1.0, op0=ALU.mult, op1=ALU.add),
             reads=[ka.b], writes=[omka.b])
        LT = []
        for d in range(2):
            LT.append(dict(
                zc=[sb(g, st, "rzc", [128, NRW]) for _ in range(2)],
                zz=sb(g, st, "rzs", [128, NRW]),
                u=sb(g, st, "ru", [128, NRW]),
                tw=sb(g, st, "rtw", [128, 96]),
                sg=sb(g, st, "rsg", [128, 256]),
                tT=sb(g, st, "rtT", [128, 4, 128]),
                aa=sb(g, st, "raa", [128, 512]),
                t1=sb(g, st, "rt1", [128, 512]),
                t2=sb(g, st, "rt2", [128, 512]),
                ss=sb(g, st, "rss", [128, 8]),
                ob=[sb(g, st, "rob", [128, 4, 512]) for _ in range(2)],
                og=[sb(g, st, "rog", [128, 512]) for _ in range(2)]))

        def lane(d):
            lt = LT[d]
            zz, u, tw, sg, tT, aa, t1, t2, ss = (lt[k] for k in ["zz", "u", "tw", "sg", "tT", "aa", "t1", "t2", "ss"])
            for ti in range(NT):
                z = lt["zc"][ti % 2]
                r0 = ti * 128
                S.dma("sp", z[:, :], g.Z[r0:r0 + 128, O_RKVG:O_RKVG + NRW], writes=[z.b])
                if True:
                    if d == 0:
                        if ti in (0, NTC):
                            S.dma("sp", zz[1:128, :], g.Z[r0:r0 + 127, O_RKVG:O_RKVG + NRW], writes=[zz.b])
                            S.dma("sp", zz[0:1, :], g.zrow[0:1, :], writes=[zz.b])
                        else:
                            S.dma("sp", zz[:, :], g.Z[r0 - 1:r0 + 127, O_RKVG:O_RKVG + NRW], writes=[zz.b])
                    else:
                        if ti in (NTC - 1, NT - 1):
                            S.dma("sp", zz[0:127, :], g.Z[r0 + 1:r0 + 128, O_RKVG:O_RKVG + NRW], writes=[zz.b])
                            S.dma("sp", zz[127:128, :], g.zrow[0:1, :], writes=[zz.b])
                        else:
                            S.dma("sp", zz[:, :], g.Z[r0 + 1:r0 + 129, O_RKVG:O_RKVG + NRW], writes=[zz.b])
                    S.op("dve", lambda e: e.tensor_tensor(out=u[:, :], in0=zz[:, :], in1=z[:, :], op=ALU.subtract), reads=[zz.b, z.b], writes=[u.b])
                    S.op("dve", lambda e: e.tensor_tensor(out=u[:, :], in0=u[:, :], in1=MU[d][:, :], op=ALU.mult), reads=[u.b, MU[d].b], writes=[u.b])
                    S.op("dve", lambda e: e.tensor_tensor(out=u[:, :], in0=u[:, :], in1=z[:, :], op=ALU.add), reads=[u.b, z.b], writes=[u.b])
                    ur, uk, uv, ugd = u[:, 0:512], u[:, 512:1024], u[:, 1024:1536], u[:, 1536:1792]
                    uwd = u[:, 1792 + d * 96:1792 + (d + 1) * 96]
                    uad = u[:, 1984 + d * 96:1984 + (d + 1) * 96]
                    ob = lt["ob"][ti % 2]
                    o_g = lt["og"][ti % 2]

                    class _V:
                        def __init__(self, j):
                            self.j = j
                            self.b = ob.b

                        def __getitem__(self, idx):
                            return ob[:, self.j, :]
                    o_lw, o_k, o_kk, o_kka = _V(0), _V(1), _V(2), _V(3)
                    rows = slice(r0, r0 + 128)
                    S.dma("pool", g.RW[d, 0:2, rows, :].rearrange("q t c -> t q c"),
                          u[:, 0:2048].rearrange("p (a c) -> p a c", a=2)[:, :, 0:512], reads=[u.b])
                    S.op("act", lambda e: e.activation(out=tw[:, :], in_=uwd, func=AF.Tanh), reads=[u.b], writes=[tw.b])
                    S.op("act", lambda e: e.activation(out=sg[:, :], in_=ugd, func=AF.Sigmoid), reads=[u.b], writes=[sg.b])
                    ps = next_psum(g)
                    S.op("pe", lambda e: e.transpose(ps[0:96, 0:128], tw[:, :], g.ident[:]), reads=[tw.b, g.ident.b], writes=[ps.b])
                    S.op("pe", lambda e: e.transpose(ps[0:96, 128:256], uad, g.ident[:]), reads=[u.b, g.ident.b], writes=[ps.b])
                    copy_op(g, "act", tT[0:96, 0:2, :], ps[0:96, 0:256].rearrange("p (k t) -> p k t", k=2), [ps.b], [tT.b])
                    ps = next_psum(g)
                    S.op("pe", lambda e: e.transpose(ps[:, 0:128], sg[:, 0:128], g.ident[:]), reads=[sg.b, g.ident.b], writes=[ps.b])
                    S.op("pe", lambda e: e.transpose(ps[:, 128:256], sg[:, 128:256], g.ident[:]), reads=[sg.b, g.ident.b], writes=[ps.b])
                    copy_op(g, "dve", tT[:, 2:4, :], ps[:, 0:256].rearrange("p (k t) -> p k t", k=2), [ps.b], [tT.b])
                    psw = next_psum(g)
                    S.op("pe", lambda e: e.matmul(psw[:, :], lhsT=tT[0:96, 0, :], rhs=w2[d][:, :], start=True, stop=True), reads=[tT.b, w2[d].b], writes=[psw.b])
                    S.op("dve", lambda e: e.tensor_tensor(out=t1[:, :], in0=psw[:, :], in1=w0[d][:, :], op=ALU.add), reads=[psw.b, w0[d].b], writes=[t1.b])
                    S.op("act", lambda e: e.activation(out=t1[:, :], in_=t1[:, :], func=AF.Sigmoid), reads=[t1.b], writes=[t1.b])
                    S.op("act", lambda e: e.mul(out=o_lw[:, :], in_=t1[:, :], mul=-0.6065306597126334), reads=[t1.b], writes=[o_lw.b])
                    psa = next_psum(g)
                    S.op("pe", lambda e: e.matmul(psa[:, :], lhsT=tT[0:96, 1, :], rhs=a2[d][:, :], start=True, stop=True), reads=[tT.b, a2[d].b], writes=[psa.b])
                    S.op("dve", lambda e: e.tensor_tensor(out=aa[:, :], in0=psa[:, :], in1=a0[d][:, :], op=ALU.add), reads=[psa.b, a0[d].b], writes=[aa.b])
                    S.op("act", lambda e: e.activation(out=aa[:, :], in_=aa[:, :], func=AF.Sigmoid), reads=[aa.b], writes=[aa.b])
                    psg = next_psum(g)
                    for kc in range(2):
                        S.op("pe", lambda e: e.matmul(psg[:, :], lhsT=tT[:, 2 + kc, :], rhs=g2[:, kc, :], start=(kc == 0), stop=(kc == 1)),
                             reads=[tT.b, g2.b], writes=[psg.b])
                    copy_op(g, "act", o_g[:, :], psg[:, :], [psg.b], [o_g.b])
                    S.dma("pool", g.RG[d, rows, :], o_g[:, :], reads=[o_g.b])
                    S.op("dve", lambda e: e.tensor_tensor(out=t1[:, :], in0=uk, in1=kk_[:, :], op=ALU.mult), reads=[u.b, kk_.b], writes=[t1.b])
                    S.op("act", lambda e: e.activation(out=t2[:, :], in_=t1[:, :], func=AF.Square), reads=[t1.b], writes=[t2.b])
                    S.op("dve", lambda e: e.tensor_reduce(out=ss[:, 0:8], in_=t2[:, :].rearrange("p (h k) -> p h k", h=8), axis=AX.X, op=ALU.add),
                         reads=[t2.b], writes=[ss.b])
                    S.op("act", lambda e: e.activation(out=ss[:, 0:8], in_=ss[:, 0:8], func=AF.Sqrt), reads=[ss.b], writes=[ss.b])
                    S.op("dve", lambda e: e.tensor_scalar(out=ss[:, 0:8], in0=ss[:, 0:8], scalar1=1e-12, scalar2=None, op0=ALU.max), reads=[ss.b], writes=[ss.b])
                    S.op("dve", lambda e: e.reciprocal(out=ss[:, 0:8], in_=ss[:, 0:8]), reads=[ss.b], writes=[ss.b])
                    S.op("dve", lambda e: e.tensor_tensor(out=o_kk[:, :].rearrange("p (h k) -> p h k", h=8), in0=t1[:, :].rearrange("p (h k) -> p h k", h=8),
                                                          in1=ss[:, 0:8].unsqueeze(2).broadcast_to([128, 8, 64]), op=ALU.mult),
                         reads=[t1.b, ss.b], writes=[o_kk.b])
                    S.op("dve", lambda e: e.tensor_tensor(out=o_kka[:, :], in0=o_kk[:, :], in1=aa[:, :], op=ALU.mult), reads=[o_kk.b, aa.b], writes=[o_kka.b])
                    S.op("dve", lambda e: e.tensor_tensor(out=t2[:, :], in0=aa[:, :], in1=ka[:, :], op=ALU.mult), reads=[aa.b, ka.b], writes=[t2.b])
                    S.op("dve", lambda e: e.tensor_tensor(out=t2[:, :], in0=t2[:, :], in1=omka[:, :], op=ALU.add), reads=[t2.b, omka.b], writes=[t2.b])
                    S.op("dve", lambda e: e.tensor_tensor(out=o_k[:, :], in0=t2[:, :], in1=uk, op=ALU.mult), reads=[t2.b, u.b], writes=[o_k.b])
                    S.dma("pool", g.RW[d, 2:6, rows, :].rearrange("q t c -> t q c"), ob[:, :, :], reads=[ob.b])
        S.run_lanes([(lambda d=d: lane(d)) for d in range(2)])
    S.barrier()


def stage_rwkv_scan(g, l):
    S = g.S
    C = 64
    nch = g.T // C
    nch_c = g.LC // C
    ident64 = g.ident[0:64, 0:64]

    def mm(ps_ap, psb, lhsT, rhs, reads, start=True, stop=True):
        S.op("pe", lambda e: e.matmul(ps_ap, lhsT=lhsT, rhs=rhs, start=start, stop=stop), reads=reads, writes=[psb])

    with ExitStack() as st:
        names = ["Lsb", "eL", "enL", "eLx", "eEnd", "Rt", "Kt", "Kk", "Ak", "Kh", "Ah"]
        LT = []
        for d in range(2):
            LT.append(dict(
                ST=sb(g, st, "sST", [64, 8, 64]),
                qin=[sb(g, st, "sq", [64, 512]) for _ in range(6)],
                v2=sb(g, st, "sv2", [128, 512]),
                W={n: sb(g, st, "s" + n, [64, 512]) for n in names},
                pCT=sb(g, st, "spCT", [64, 8]),
                XT1=sb(g, st, "sXT1", [64, 8, 128]), XT2=sb(g, st, "sXT2", [64, 8, 128]),
                PRs=sb(g, st, "sPRs", [128, 8, 128]),
                Nn=[sb(g, st, "sN", [64, 8, 64]) for _ in range(2)],
                NTt=[sb(g, st, "sNT", [64, 8, 64]) for _ in range(2)],
                Qq=[sb(g, st, "sQ", [64, 8, 64]) for _ in range(2)],
                W1T=sb(g, st, "sW1T", [64, 8, 64]), MV=sb(g, st, "sMV", [64, 8, 64]), W2=sb(g, st, "sW2", [64, 8, 64]),
                Y0=sb(g, st, "sY0", [64, 8, 64]), D0=sb(g, st, "sD0", [64, 8, 64]), U=sb(g, st, "sU", [64, 8, 64]),
                Yo=[sb(g, st, "sYo", [64, 512]) for _ in range(2)], tmp=sb(g, st, "stmp", [64, 8, 64])))
            S.op("dve", lambda e: e.memset(LT[d]["ST"][:, :, :], 0.0), writes=[LT[d]["ST"].b])

        def lane(d):
            lt = LT[d]
            W, pCT, XT1, XT2, PRs, Nn, NTt, Qq = (lt[k] for k in ["W", "pCT", "XT1", "XT2", "PRs", "Nn", "NTt", "Qq"])
            W1T, MV, W2, Y0, D0, U, Yo, tmp = (lt[k] for k in ["W1T", "MV", "W2", "Y0", "D0", "U", "Yo", "tmp"])
            ST = {d: lt["ST"]}
            it = 0
            if d == 0:
                order = list(range(nch))
            else:
                order = list(range(nch_c - 1, -1, -1)) + list(range(nch - 1, nch_c - 1, -1))
            tri = g.tri[0:64, d, :]
            mk = g.mk[:, d, :]
            nmts = g.nmts[0:64, d, :]
            for c in order:
                it += 1
                q = lt["qin"]
                vv = lt["v2"]
                rows = slice(c * C, (c + 1) * C)
                for qi in range(6):
                    S.dma("sp", q[qi][:, :], g.RW[d, RQ_SCAN[qi], rows, :], writes=[q[qi].b])
                S.dma("sp", vv[64:128, :], g.RW[d, RQ_V, rows, :], writes=[vv.b])
                r_, lw, k_, v_, kk, kka = q
                psL = next_psum(g)
                mm(psL[0:64, :], psL.b, tri, lw[:, :], [g.tri.b, lw.b])
                psE = next_psum(g)
                mm(psE[0:64, :], psE.b, g.ones64[0:64, :], lw[:, :], [g.ones64.b, lw.b])
                psP = next_psum(g)
                for h in range(8):
                    mm(psP[0:64, h:h + 1], psP.b, lw[:, h * 64:(h + 1) * 64], g.ones64[0:64, 0:1], [lw.b, g.ones64.b])
                S.op("act", lambda e: e.activation(out=pCT[:, :], in_=psP[0:64, 0:8], func=AF.Exp), reads=[psP.b], writes=[pCT.b])
                S.op("act", lambda e: e.activation(out=W["eL"][:, :], in_=psL[0:64, :], func=AF.Exp), reads=[psL.b], writes=[W["eL"].b])
                S.op("act", lambda e: e.activation(out=W["enL"][:, :], in_=psL[0:64, :], func=AF.Exp, scale=-1.0), reads=[psL.b], writes=[W["enL"].b])
                S.op("dve", lambda e: e.tensor_tensor(out=W["eLx"][:, :], in0=psL[0:64, :], in1=lw[:, :], op=ALU.subtract), reads=[psL.b, lw.b], writes=[W["eLx"].b])
                S.op("act", lambda e: e.activation(out=W["eLx"][:, :], in_=W["eLx"][:, :], func=AF.Exp), reads=[W["eLx"].b], writes=[W["eLx"].b])
                copy_op(g, "dve", W["Lsb"][:, :], psL[0:64, :], [psL.b], [W["Lsb"].b])
                S.op("dve", lambda e: e.tensor_tensor(out=W["eEnd"][:, :], in0=psE[0:64, :], in1=W["Lsb"][:, :], op=ALU.subtract),
                     reads=[psE.b, W["Lsb"].b], writes=[W["eEnd"].b])
                S.op("act", lambda e: e.activation(out=W["eEnd"][:, :], in_=W["eEnd"][:, :], func=AF.Exp), reads=[W["eEnd"].b], writes=[W["eEnd"].b])
                for (o, a, b) in [("Rt", r_, "eL"), ("Kt", kk, "eLx"), ("Kk", k_, "enL"), ("Ak", kka, "enL"), ("Kh", k_, "eEnd"), ("Ah", kka, "eEnd")]:
                    S.op("dve", lambda e: e.tensor_tensor(out=W[o][:, :], in0=a[:, :], in1=W[b][:, :], op=ALU.mult), reads=[a.b, W[b].b], writes=[W[o].b])
                for (XT, n0, n1) in [(XT1, "Ak", "Kk"), (XT2, "Kt", "Rt")]:
                    for hg in range(2):
                        ps = next_psum(g)
                        for hh in range(4):
                            h = hg * 4 + hh
                            for j, nm in enumerate([n0, n1]):
                                S.op("pe", lambda e: e.transpose(ps[0:64, hh * 128 + j * 64:hh * 128 + (j + 1) * 64], W[nm][:, h * 64:(h + 1) * 64], ident64),
                                     reads=[W[nm].b, g.ident.b], writes=[ps.b])
                        copy_op(g, evac_engine(g), XT[:, hg * 4:(hg + 1) * 4, :], ps[0:64, :].rearrange("p (h x) -> p h x", h=4), [ps.b], [XT.b])
                for hg in range(2):
                    ps = next_psum(g)
                    for hh in range(4):
                        h = hg * 4 + hh
                        mm(ps[:, hh * 128:(hh + 1) * 128], ps.b, XT1[:, h, :], XT2[:, h, :], [XT1.b, XT2.b])
                    S.op("dve", lambda e: e.tensor_tensor(out=PRs[:, hg * 4:(hg + 1) * 4, :], in0=ps[:, :].rearrange("p (h x) -> p h x", h=4),
                                                          in1=mk.unsqueeze(1).broadcast_to([128, 4, 128]), op=ALU.mult),
                         reads=[ps.b, g.mk.b], writes=[PRs.b])
                ps = next_psum(g)
                for h in range(8):
                    mm(ps[0:64, h * 64:(h + 1) * 64], ps.b, XT2[:, h, 0:64], XT1[:, h, 0:64], [XT1.b, XT2.b])
                N, NT_, Q = Nn[0], NTt[0], Qq[0]
                S.op("dve", lambda e: e.tensor_tensor(out=NT_[:, :, :], in0=ps[0:64, :].rearrange("p (h x) -> p h x", h=8),
                                                      in1=nmts.unsqueeze(1).broadcast_to([64, 8, 64]), op=ALU.mult),
                     reads=[ps.b, g.nmts.b], writes=[NT_.b])
                S.op("dve", lambda e: e.tensor_scalar(out=N[:, :, :], in0=PRs[0:64, :, 0:64], scalar1=-1.0, scalar2=None, op0=ALU.mult), reads=[PRs.b], writes=[N.b])
                S.op("dve", lambda e: e.tensor_tensor(out=Q[:, :, :], in0=N[:, :, :], in1=ident64.unsqueeze(1).broadcast_to([64, 8, 64]), op=ALU.add),
                     reads=[N.b, g.ident.b], writes=[Q.b])
                cur = 0
                for lev in range(5):
                    N, NT_, Q = Nn[cur], NTt[cur], Qq[cur]
                    N2, NT2, Q2 = Nn[1 - cur], NTt[1 - cur], Qq[1 - cur]
                    psn = next_psum(g)
                    pst = next_psum(g)
                    for h in range(8):
                        if lev < 4:
                            mm(psn[0:64, h * 64:(h + 1) * 64], psn.b, NT_[:, h, :], N[:, h, :], [NT_.b, N.b])
                        mm(pst[0:64, h * 64:(h + 1) * 64], pst.b, N[:, h, :], NT_[:, h, :], [NT_.b, N.b])
                    if lev < 4:
                        copy_op(g, "act", N2[:, :, :], psn[0:64, :].rearrange("p (h x) -> p h x", h=8), [psn.b], [N2.b])
                    copy_op(g, "dve", NT2[:, :, :], pst[0:64, :].rearrange("p (h x) -> p h x", h=8), [pst.b], [NT2.b])
                    psq = next_psum(g)
                    for h in range(8):
                        mm(psq[0:64, h * 64:(h + 1) * 64], psq.b, NT2[:, h, :], Q[:, h, :], [NT2.b, Q.b])
                    S.op("dve", lambda e: e.tensor_tensor(out=Q2[:, :, :], in0=psq[0:64, :].rearrange("p (h x) -> p h x", h=8), in1=Q[:, :, :], op=ALU.add),
                         reads=[psq.b, Q.b], writes=[Q2.b])
                    cur = 1 - cur
                Q = Qq[cur]
                ps1 = next_psum(g)
                ps2 = next_psum(g)
                ps3 = next_psum(g)
                ps4 = next_psum(g)
                for h in range(8):
                    hs = slice(h * 64, (h + 1) * 64)
                    mm(ps1[0:64, hs], ps1.b, W["Kt"][:, hs], Q[:, h, :], [W["Kt"].b, Q.b])
                    mm(ps2[0:64, hs], ps2.b, PRs[64:128, h, 0:64], vv[64:128, hs], [PRs.b, vv.b])
                    mm(ps3[0:64, hs], ps3.b, PRs[64:128, h, 64:128], vv[64:128, hs], [PRs.b, vv.b])
                    mm(ps4[0:64, hs], ps4.b, W["Kh"][:, hs], v_[:, hs], [W["Kh"].b, v_.b])
                copy_op(g, "act", W1T[:, :, :], ps1[0:64, :].rearrange("p (h x) -> p h x", h=8), [ps1.b], [W1T.b])
                copy_op(g, "dve", MV[:, :, :], ps2[0:64, :].rearrange("p (h x) -> p h x", h=8), [ps2.b], [MV.b])
                copy_op(g, "act", Y0[:, :, :], ps3[0:64, :].rearrange("p (h x) -> p h x", h=8), [ps3.b], [Y0.b])
                copy_op(g, "dve", D0[:, :, :], ps4[0:64, :].rearrange("p (h x) -> p h x", h=8), [ps4.b], [D0.b])
                ps5 = next_psum(g)
                for h in range(8):
                    mm(ps5[0:64, h * 64:(h + 1) * 64], ps5.b, Q[:, h, :], MV[:, h, :], [Q.b, MV.b])
                copy_op(g, "act", W2[:, :, :], ps5[0:64, :].rearrange("p (h x) -> p h x", h=8), [ps5.b], [W2.b])
                st_ = ST[d]
                psu = next_psum(g)
                for h in range(8):
                    mm(psu[0:64, h * 64:(h + 1) * 64], psu.b, W1T[:, h, :], st_[:, h, :], [W1T.b, st_.b])
                S.op("dve", lambda e: e.scalar_tensor_tensor(out=U[:, :, :], in0=psu[0:64, :].rearrange("p (h x) -> p h x", h=8), scalar=-1.0,
                                                             in1=W2[:, :, :], op0=ALU.mult, op1=ALU.subtract),
                     reads=[psu.b, W2.b], writes=[U.b])
                psy = next_psum(g)
                for h in range(8):
                    hs = slice(h * 64, (h + 1) * 64)
                    mm(psy[0:64, hs], psy.b, XT2[:, h, 64:128], st_[:, h, :], [XT2.b, st_.b], start=True, stop=False)
                    mm(psy[0:64, hs], psy.b, PRs[0:64, h, 64:128], U[:, h, :], [PRs.b, U.b], start=False, stop=True)
                yo = Yo[it % 2]
                S.op("dve", lambda e: e.tensor_tensor(out=yo[:, :], in0=psy[0:64, :], in1=Y0[:, :, :].rearrange("p h x -> p (h x)"), op=ALU.add),
                     reads=[psy.b, Y0.b], writes=[yo.b])
                S.dma("pool", g.YR[d, rows, :], yo[:, :], reads=[yo.b])
                psd = next_psum(g)
                for h in range(8):
                    hs = slice(h * 64, (h + 1) * 64)
                    mm(psd[0:64, hs], psd.b, W["Ah"][:, hs], U[:, h, :], [W["Ah"].b, U.b])
                S.op("dve", lambda e: e.tensor_tensor(out=tmp[:, :, :], in0=st_[:, :, :], in1=pCT[:, 0:8].unsqueeze(2).broadcast_to([64, 8, 64]), op=ALU.mult),
                     reads=[st_.b, pCT.b], writes=[tmp.b])
                S.op("dve", lambda e: e.tensor_tensor(out=tmp[:, :, :], in0=tmp[:, :, :], in1=D0[:, :, :], op=ALU.add), reads=[tmp.b, D0.b], writes=[tmp.b])
                S.op("dve", lambda e: e.tensor_tensor(out=st_[:, :, :], in0=psd[0:64, :].rearrange("p (h x) -> p h x", h=8), in1=tmp[:, :, :], op=ALU.add),
                     reads=[psd.b, tmp.b], writes=[st_.b])
        S.run_lanes([(lambda d=d: lane(d)) for d in range(2)])
    S.barrier()


def stage_rwkv_out(g, l):
    S = g.S
    with ExitStack() as st:
        lw_ = sb(g, st, "olw", [128, 512])
        lb_ = sb(g, st, "olb", [128, 512])
        rk_ = sb(g, st, "ork", [128, 512])
        bcast_row(g, "sp", lw_[:, :], lw_.b, g.rwkv_lnx_w[l, :])
        bcast_row(g, "sp", lb_[:, :], lb_.b, g.rwkv_lnx_b[l, :])
        bcast_row(g, "sp", rk_[:, :], rk_.b, g.rwkv_r_k[l].rearrange("h k -> (h k)"))
        NL = 4
        LT = [dict(ins=[[sb(g, st, "oin", [128, 512]) for _ in range(5)] for _ in range(2)],
                   t1=sb(g, st, "ot1", [128, 512]), t2=sb(g, st, "ot2", [128, 512]),
                   acc=[sb(g, st, "oacc", [128, 512]) for _ in range(2)],
                   ss=sb(g, st, "oss", [128, 8]), s2=sb(g, st, "os2", [128, 8])) for _ in range(NL)]
        h8 = lambda ap: ap.rearrange("p (h k) -> p h k", h=8)
        b8 = lambda t_: t_[:, 0:8].unsqueeze(2).broadcast_to([128, 8, 64])

        def lane(li):
          lt = LT[li]
          ins, t1, t2, acc, ss, s2 = (lt[k] for k in ["ins", "t1", "t2", "acc", "ss", "s2"])
          it = 0
          for kk_i, ti in enumerate(range(li, g.NT, NL)):
            rows = slice(ti * 128, (ti + 1) * 128)
            a_ = acc[kk_i % 2]
            for d in range(2):
                it += 1
                y, r_, k_, v_, g_ = ins[it % 2]
                S.dma("sp", y[:, :], g.YR[d, rows, :], writes=[y.b])
                S.dma("sp", r_[:, :], g.RW[d, RQ_R, rows, :], writes=[r_.b])
                S.dma("sp", k_[:, :], g.RW[d, RQ_K, rows, :], writes=[k_.b])
                S.dma("sp", v_[:, :], g.RW[d, RQ_V, rows, :], writes=[v_.b])
                S.dma("sp", g_[:, :], g.RG[d, rows, :], writes=[g_.b])
                S.op("dve", lambda e: e.tensor_reduce(out=ss[:, 0:8], in_=h8(y[:, :]), axis=AX.X, op=ALU.add), reads=[y.b], writes=[ss.b])
                S.op("dve", lambda e: e.tensor_scalar(out=ss[:, 0:8], in0=ss[:, 0:8], scalar1=-1.0 / 64, scalar2=None, op0=ALU.mult), reads=[ss.b], writes=[ss.b])
                S.op("dve", lambda e: e.tensor_tensor(out=h8(t1[:, :]), in0=h8(y[:, :]), in1=b8(ss), op=ALU.add), reads=[y.b, ss.b], writes=[t1.b])
                S.op("act", lambda e: e.activation(out=t2[:, :], in_=t1[:, :], func=AF.Square), reads=[t1.b], writes=[t2.b])
                S.op("dve", lambda e: e.tensor_reduce(out=s2[:, 0:8], in_=h8(t2[:, :]), axis=AX.X, op=ALU.add), reads=[t2.b], writes=[s2.b])
                S.op("dve", lambda e: e.tensor_scalar(out=s2[:, 0:8], in0=s2[:, 0:8], scalar1=1.0 / 64, scalar2=LNX_EPS, op0=ALU.mult, op1=ALU.add),
                     reads=[s2.b], writes=[s2.b])
                S.op("act", lambda e: e.activation(out=s2[:, 0:8], in_=s2[:, 0:8], func=AF.Sqrt), reads=[s2.b], writes=[s2.b])
                S.op("dve", lambda e: e.reciprocal(out=s2[:, 0:8], in_=s2[:, 0:8]), reads=[s2.b], writes=[s2.b])
                S.op("dve", lambda e: e.tensor_tensor(out=h8(t1[:, :]), in0=h8(t1[:, :]), in1=b8(s2), op=ALU.mult), reads=[t1.b, s2.b], writes=[t1.b])
                S.op("dve", lambda e: e.tensor_tensor(out=t1[:, :], in0=t1[:, :], in1=lw_[:, :], op=ALU.mult), reads=[t1.b, lw_.b], writes=[t1.b])
                S.op("dve", lambda e: e.tensor_tensor(out=t1[:, :], in0=t1[:, :], in1=lb_[:, :], op=ALU.add), reads=[t1.b, lb_.b], writes=[t1.b])
                S.op("dve", lambda e: e.tensor_tensor(out=t2[:, :], in0=r_[:, :], in1=k_[:, :], op=ALU.mult), reads=[r_.b, k_.b], writes=[t2.b])
                S.op("dve", lambda e: e.tensor_tensor(out=t2[:, :], in0=t2[:, :], in1=rk_[:, :], op=ALU.mult), reads=[t2.b, rk_.b], writes=[t2.b])
                S.op("dve", lambda e: e.tensor_reduce(out=ss[:, 0:8], in_=h8(t2[:, :]), axis=AX.X, op=ALU.add), reads=[t2.b], writes=[ss.b])
                S.op("dve", lambda e: e.tensor_tensor(out=h8(t2[:, :]), in0=h8(v_[:, :]), in1=b8(ss), op=ALU.mult), reads=[v_.b, ss.b], writes=[t2.b])
                S.op("dve", lambda e: e.tensor_tensor(out=t1[:, :], in0=t1[:, :], in1=t2[:, :], op=ALU.add), reads=[t1.b, t2.b], writes=[t1.b])
                if d == 0:
                    S.op("dve", lambda e: e.tensor_tensor(out=a_[:, :], in0=t1[:, :], in1=g_[:, :], op=ALU.mult), reads=[t1.b, g_.b], writes=[a_.b])
                else:
                    S.op("dve", lambda e: e.tensor_tensor(out=t1[:, :], in0=t1[:, :], in1=g_[:, :], op=ALU.mult), reads=[t1.b, g_.b], writes=[t1.b])
                    S.op("dve", lambda e: e.tensor_tensor(out=a_[:, :], in0=a_[:, :], in1=t1[:, :], op=ALU.add), reads=[a_.b, t1.b], writes=[a_.b])
            S.dma("pool", g.YMIX[rows, 1024:1536], a_[:, :], reads=[a_.b])
        S.run_lanes([(lambda li=li: lane(li)) for li in range(NL)])
    S.barrier()


def stage_merge(g, l):
    S = g.S
    with ExitStack() as st:
        gt = [[sb(g, st, "mgt", [128, 512]) for _ in range(4)] for _ in range(2)]
        t1 = sb(g, st, "mt1", [128, 512])
        cnt = [0]

        def epi(pss, o, ti, n0, nb):
            gs = gt[cnt[0] % 2]
            cnt[0] += 1
            for i in range(4):
                c0 = O_GATE + i * D + n0
                S.dma("sp", gs[i][:, 0:nb], g.Z[ti * 128:(ti + 1) * 128, c0:c0 + nb], writes=[gs[i].b])
                S.op("act", lambda e: e.activation(out=gs[i][:, 0:nb], in_=gs[i][:, 0:nb], func=AF.Sigmoid), reads=[gs[i].b], writes=[gs[i].b])
            S.op("dve", lambda e: e.tensor_tensor(out=o[:, 0:nb], in0=pss[0][:, 0:nb], in1=gs[0][:, 0:nb], op=ALU.mult), reads=[pss[0].b, gs[0].b], writes=[o.b])
            for i in range(1, 4):
                S.op("dve", lambda e: e.tensor_tensor(out=t1[:, 0:nb], in0=pss[i][:, 0:nb], in1=gs[i][:, 0:nb], op=ALU.mult), reads=[pss[i].b, gs[i].b], writes=[t1.b])
                S.op("dve", lambda e: e.tensor_tensor(out=o[:, 0:nb], in0=o[:, 0:nb], in1=t1[:, 0:nb], op=ALU.add), reads=[o.b, t1.b], writes=[o.b])
            S.dma("sp", g.MERGED[ti * 128:(ti + 1) * 128, n0:n0 + nb], o[:, 0:nb], reads=[o.b])
        linear(g, g.YMIX, g.w_branch[l].rearrange("b k n -> (b k) n"), None, g.T, D, D, G=12, NB=512, epi=epi,
               segs=[(0, 4), (4, 8), (8, 12), (12, 16)])


def stage_conv(g, l):
    S = g.S
    NT, NTC = g.NT, g.NTC
    NL = 4
    blocks = list(range(0, DFF, 512))
    with ExitStack() as st:
        LT = [dict(cw=sb(g, st, "ccw", [128, 3, 512]), cb=sb(g, st, "ccb", [128, 512]),
                   aw=[sb(g, st, "caw", [128, 512]) for _ in range(4)],
                   bw=[sb(g, st, "cbw", [128, 512]) for _ in range(2)],
                   acc=sb(g, st, "cacc", [128, 512]), t1=sb(g, st, "ct1", [128, 512]),
                   out=[sb(g, st, "cout", [128, 512]) for _ in range(2)]) for _ in range(NL)]

        def lane(li):
            lt = LT[li]
            w_, b_, aw, bw, acc, t1 = lt["cw"], lt["cb"], lt["aw"], lt["bw"], lt["acc"], lt["t1"]
            lq = "sp" if li % 2 == 0 else "act"
            for n0 in blocks[li::NL]:
                for j in range(3):
                    bcast_row(g, lq, w_[:, j, :], w_.b, g.ffn_conv_w[l, j, n0:n0 + 512])
                bcast_row(g, lq, b_[:, :], b_.b, g.ffn_conv_b[l, n0:n0 + 512])
                for t0 in range(min(2, NT)):
                    S.dma(lq, aw[t0 % 4][:, :], g.AB[t0 * 128:(t0 + 1) * 128, n0:n0 + 512], writes=[aw[t0 % 4].b])
                for ti in range(NT):
                    r0 = ti * 128
                    if ti + 2 < NT:
                        t2_ = ti + 2
                        S.dma(lq, aw[t2_ % 4][:, :], g.AB[t2_ * 128:(t2_ + 1) * 128, n0:n0 + 512], writes=[aw[t2_ % 4].b])
                    bb = bw[ti % 2]
                    S.dma(lq, bb[:, :], g.AB[r0:r0 + 128, DFF + n0:DFF + n0 + 512], writes=[bb.b])
                    ac_ = aw[ti % 4]
                    has_prev = ti not in (0, NTC)
                    has_next = ti not in (NTC - 1, NT - 1)
                    psP = next_psum(g)
                    S.op("pe", lambda e: e.matmul(psP[:, :], lhsT=g.shm[:, 0, :], rhs=ac_[:, :], start=True, stop=not has_prev),
                         reads=[g.shm.b, ac_.b], writes=[psP.b])
                    if has_prev:
                        ap_ = aw[(ti - 1) % 4]
                        S.op("pe", lambda e: e.matmul(psP[:, :], lhsT=g.shm[:, 1, :], rhs=ap_[:, :], start=False, stop=True),
                             reads=[g.shm.b, ap_.b], writes=[psP.b])
                    psN = next_psum(g)
                    S.op("pe", lambda e: e.matmul(psN[:, :], lhsT=g.shm[:, 2, :], rhs=ac_[:, :], start=True, stop=not has_next),
                         reads=[g.shm.b, ac_.b], writes=[psN.b])
                    if has_next:
                        an_ = aw[(ti + 1) % 4]
                        S.op("pe", lambda e: e.matmul(psN[:, :], lhsT=g.shm[:, 3, :], rhs=an_[:, :], start=False, stop=True),
                             reads=[g.shm.b, an_.b], writes=[psN.b])
                    S.op("dve", lambda e: e.tensor_tensor(out=acc[:, :], in0=psP[:, :], in1=w_[:, 0, :], op=ALU.mult), reads=[psP.b, w_.b], writes=[acc.b])
                    S.op("dve", lambda e: e.tensor_tensor(out=t1[:, :], in0=ac_[:, :], in1=w_[:, 1, :], op=ALU.mult), reads=[ac_.b, w_.b], writes=[t1.b])
                    S.op("dve", lambda e: e.tensor_tensor(out=acc[:, :], in0=acc[:, :], in1=t1[:, :], op=ALU.add), reads=[acc.b, t1.b], writes=[acc.b])
                    S.op("dve", lambda e: e.tensor_tensor(out=t1[:, :], in0=psN[:, :], in1=w_[:, 2, :], op=ALU.mult), reads=[psN.b, w_.b], writes=[t1.b])
                    S.op("dve", lambda e: e.tensor_tensor(out=acc[:, :], in0=acc[:, :], in1=t1[:, :], op=ALU.add), reads=[acc.b, t1.b], writes=[acc.b])
                    S.op("dve", lambda e: e.tensor_tensor(out=acc[:, :], in0=acc[:, :], in1=b_[:, :], op=ALU.add), reads=[acc.b, b_.b], writes=[acc.b])
                    S.op("dve", lambda e: e.tensor_tensor(out=t1[:, :], in0=acc[:, :], in1=acc[:, :], op=ALU.mult), reads=[acc.b], writes=[t1.b])
                    S.op("dve", lambda e: e.tensor_scalar(out=t1[:, :], in0=t1[:, :], scalar1=0.044715, scalar2=1.0, op0=ALU.mult, op1=ALU.add), reads=[t1.b], writes=[t1.b])
                    S.op("dve", lambda e: e.tensor_tensor(out=t1[:, :], in0=t1[:, :], in1=acc[:, :], op=ALU.mult), reads=[t1.b, acc.b], writes=[t1.b])
                    S.op("act", lambda e: e.activation(out=t1[:, :], in_=t1[:, :], func=AF.Sigmoid, scale=1.5957691216057308), reads=[t1.b], writes=[t1.b])
                    S.op("dve", lambda e: e.tensor_tensor(out=t1[:, :], in0=t1[:, :], in1=acc[:, :], op=ALU.mult), reads=[t1.b, acc.b], writes=[t1.b])
                    o = lt["out"][ti % 2]
                    S.op("dve", lambda e: e.tensor_tensor(out=o[:, :], in0=t1[:, :], in1=bb[:, :], op=ALU.mult), reads=[t1.b, bb.b], writes=[o.b])
                    S.dma("pool", g.GG[r0:r0 + 128, n0:n0 + 512], o[:, :], reads=[o.b])
        S.run_lanes([(lambda li=li: lane(li)) for li in range(NL)])
    S.barrier()


def build(cfg):
    TL, LC, DEPTH = cfg["TL"], cfg["LC"], cfg["DEPTH"]
    T = TL + LC
    nc = bass.Bass("TRN2", target_bir_lowering=False)
    g = Ctx()
    g.nc = nc
    g.uid = 0
    g.ev = 0
    g.pi = 0
    g.T, g.TL, g.LC, g.NT, g.NTC = T, TL, LC, T // 128, LC // 128
    g.S = S = Sync(nc)

    def din(name, shape):
        return nc.dram_tensor(name, list(shape), F32, kind="ExternalInput").ap()

    def dscr(name, shape):
        return nc.dram_tensor(name, list(shape), F32, kind="Internal").ap()

    L = DEPTH
    g.x_in = din("x", [TL, D])
    g.ctx_in = din("ctx", [LC, D])
    g.cvec = din("cvec", [2, D])
    g.ada_w = din("ada_w", [L, D, 6 * D])
    g.ada_b = din("ada_b", [L, 6 * D])
    g.norm1_g = din("norm1_g", [L, D])
    g.norm2_g = din("norm2_g", [L, D])
    g.w_in = din("w_in", [L, D, IN_W])
    g.identd = din("ident", [128, 128])
    g.ga_q_norm = din("ga_q_norm", [L, 128])
    g.ga_k_norm = din("ga_k_norm", [L, 128])
    g.wa_q_norm = din("wa_q_norm", [L, 128])
    g.wa_k_norm = din("wa_k_norm", [L, 128])
    g.wa_sink = din("wa_sink", [L, 4])
    g.mla_cq_norm = din("mla_cq_norm", [L, 384])
    g.mla_ckv_norm = din("mla_ckv_norm", [L, 512])
    g.mla_w_uq = din("mla_w_uq", [L, 384, 768])
    g.mla_w_ukv = din("mla_w_ukv", [L, 512, 1024])
    g.mla_qn_norm = din("mla_qn_norm", [L, 128])
    g.mla_qr_norm = din("mla_qr_norm", [L, 64])
    g.mla_kn_norm = din("mla_kn_norm", [L, 128])
    g.mla_kr_norm = din("mla_kr_norm", [L, 64])
    g.rwkv_mu = din("rwkv_mu", [L, 2, 1984])
    g.rwkv_w0 = din("rwkv_w0", [L, 2, 512])
    g.rwkv_w2 = din("rwkv_w2", [L, 2, 96, 512])
    g.rwkv_a0 = din("rwkv_a0", [L, 2, 512])
    g.rwkv_a2 = din("rwkv_a2", [L, 2, 96, 512])
    g.rwkv_g2 = din("rwkv_g2", [L, 256, 512])
    g.rwkv_k_k = din("rwkv_k_k", [L, 512])
    g.rwkv_k_a = din("rwkv_k_a", [L, 512])
    g.rwkv_r_k = din("rwkv_r_k", [L, 8, 64])
    g.rwkv_lnx_w = din("rwkv_lnx_w", [L, 512])
    g.rwkv_lnx_b = din("rwkv_lnx_b", [L, 512])
    g.w_branch = din("w_branch", [L, 4, 512, D])
    g.w_out = din("w_out", [L, D, D])
    g.ffn_up = din("ffn_up", [L, D, 2 * DFF])
    g.ffn_conv_w = din("ffn_conv_w", [L, 3, DFF])
    g.ffn_conv_b = din("ffn_conv_b", [L, DFF])
    g.ffn_down = din("ffn_down", [L, DFF, D])
    g.trid = din("tri", [64, 2, 64])
    g.mkd = din("mk", [128, 2, 128])
    g.nmtsd = din("nmts", [64, 2, 64])
    g.zrow = din("zrow", [1, NRW])
    g.shmd = din("shm", [128, 4, 128])
    g.ropeA = din("ropeA", [2, TL, 128])
    g.ropeM = din("ropeM", [2, TL, 64])
    g.wmaskd = din("wmask", [128, 2, 128])
    g.y_out = nc.dram_tensor("y", [TL, D], F32, kind="ExternalOutput").ap()
    dbg = cfg.get("debug")
    g.XS = dscr("XS", [T, D])
    g.MODB = dscr("MODB", [2, 128, 6 * D])
    g.Z = dscr("Z", [T, IN_W])
    g.QKT = dscr("QKT", [12, 128, T])
    g.YMIX = dscr("YMIX", [T, D])
    g.RW = dscr("RW", [2, 6, T, 512])
    g.RG = dscr("RG", [2, T, 512])
    g.YR = dscr("YR", [2, T, 512])
    g.MERGED = dscr("MERGED", [T, D])
    g.AB = dscr("AB", [T, 2 * DFF])
    g.GG = dscr("GG", [T, DFF])
    g.QML = dscr("QML", [T, 768])
    g.KVML = dscr("KVML", [T, 1024])
    g.MQN = dscr("MQN", [4, 128, T])
    g.MQR = dscr("MQR", [4, 64, T])
    g.MKN = dscr("MKN", [4, 128, T])
    g.MKR = dscr("MKR", [1, 64, T])
    if dbg:
        g.dbg_z = nc.dram_tensor("dbg_z", [T, IN_W], F32, kind="ExternalOutput").ap()
        g.dbg_y = nc.dram_tensor("dbg_y", [T, D], F32, kind="ExternalOutput").ap()
        g.dbg_qkt = nc.dram_tensor("dbg_qkt", [12, 128, T], F32, kind="ExternalOutput").ap()
        g.dbg_xs = nc.dram_tensor("dbg_xs", [T, D], F32, kind="ExternalOutput").ap()

    with ExitStack() as es:
        g.psums = [Tl(es.enter_context(nc.psum_tensor("ps%d" % i, [128, 512], F32)), "ps%d" % i) for i in range(8)]
        g.ident = sb(g, es, "ident", [128, 128])
        S.dma("sp", g.ident[:, :], g.identd[:, :], writes=[g.ident.b])
        g.tri = sb(g, es, "tri", [64, 2, 64])
        g.mk = sb(g, es, "mk", [128, 2, 128])
        g.nmts = sb(g, es, "nmts", [64, 2, 64])
        g.ones64 = sb(g, es, "ones64", [64, 64])
        S.dma("sp", g.tri[:, :, :], g.trid[:, :, :], writes=[g.tri.b])
        S.dma("sp", g.mk[:, :, :], g.mkd[:, :, :], writes=[g.mk.b])
        S.dma("sp", g.nmts[:, :, :], g.nmtsd[:, :, :], writes=[g.nmts.b])
        S.op("dve", lambda e: e.memset(g.ones64[:, :], 1.0), writes=[g.ones64.b])
        g.shm = sb(g, es, "shm", [128, 4, 128])
        S.dma("sp", g.shm[:, :, :], g.shmd[:, :, :], writes=[g.shm.b])
        g.wmask = sb(g, es, "wmask", [128, 2, 128])
        S.dma("sp", g.wmask[:, :, :], g.wmaskd[:, :, :], writes=[g.wmask.b])
        S.dma("sp", g.XS[0:LC, :], g.ctx_in[:, :])
        S.dma("sp", g.XS[LC:T, :], g.x_in[:, :])
        g.cb = [sb(g, es, "cb", [128, 16, 128]) for _ in range(2)]
        with ExitStack() as st:
            cT = [sb(g, st, "cT", [128, 16]) for _ in range(2)]
            ones = sb(g, st, "ones", [128, 128])
            S.op("dve", lambda e: e.memset(ones[:, :], 1.0), writes=[ones.b])
            for r in range(2):
                S.dma("sp", cT[r][:, :], g.cvec[r, :].rearrange("(kc p) -> p kc", p=128), writes=[cT[r].b], allow_slow_non_contiguous=True)
                S.op("act", lambda e: e.activation(out=cT[r][:, :], in_=cT[r][:, :], func=AF.Silu), reads=[cT[r].b], writes=[cT[r].b])
                for kc in range(16):
                    S.op("dve", lambda e: e.tensor_scalar(out=g.cb[r][:, kc, :], in0=ones[:, :], scalar1=cT[r][:, kc:kc + 1], scalar2=None, op0=ALU.mult),
                         reads=[ones.b, cT[r].b], writes=[g.cb[r].b])
            S.barrier()
        for l in range(L):
            stage_mod(g, l)
            with ExitStack() as st:
                pro = make_norm_pro(g, st, l, g.norm1_g[l, :], D, 0)
                linear(g, g.XS, g.w_in[l], g.Z, T, D, IN_W, G=12, NB=512, pro=pro)
            if cfg.get("stop") == "z":
                break
            stage_attn_prep(g, l)
            stage_attn_gawa(g, l)
            if cfg.get("stop") == "gawa":
                break
            if cfg.get("stop") != "rwkv":
                stage_mla(g, l)
            if cfg.get("stop") == "mla":
                break
            stage_rwkv_prep(g, l)
            stage_rwkv_scan(g, l)
            stage_rwkv_out(g, l)
            if cfg.get("stop") == "rwkv":
                break
            stage_merge(g, l)
            with ExitStack() as st:
                epi = make_resid_epi(g, st, 2 * D)
                linear(g, g.MERGED, g.w_out[l], None, T, D, D, G=12, NB=512, epi=epi)
            if cfg.get("stop") == "attn":
                break
            with ExitStack() as st:
                pro = make_norm_pro(g, st, l, g.norm2_g[l, :], 4 * D, 3 * D)
                linear(g, g.XS, g.ffn_up[l], g.AB, T, D, 2 * DFF, G=12, NB=512, pro=pro)
            stage_conv(g, l)
            with ExitStack() as st:
                epi = make_resid_epi(g, st, 5 * D)
                linear(g, g.GG, g.ffn_down[l], None, T, DFF, D, G=6, NB=256, epi=epi)
        if dbg:
            S.dma("sp", g.dbg_z[:, :], g.Z[:, :])
            S.dma("sp", g.dbg_y[:, :], g.YMIX[:, :])
            S.dma("sp", g.dbg_qkt[:, :, :], g.QKT[:, :, :])
            S.dma("sp", g.dbg_xs[:, :], g.XS[:, :])
        S.dma("sp", g.y_out[:, :], g.XS[LC:T, :])
        S.barrier()
    return nc, g


_CACHE = {}


def rope_table(n_tokens, rot_dim):
    rows = n_tokens // GRID_W
    row = np.repeat(np.arange(rows), GRID_W).astype(np.float32)
    col = np.tile(np.arange(GRID_W), rows).astype(np.float32)
    quarter = rot_dim // 4
    inv_freq = (10000.0 ** (-np.arange(quarter, dtype=np.float32) / quarter)).astype(np.float32)
    ang_r = row[:, None] * inv_freq
    ang_c = col[:, None] * inv_freq
    ang = np.concatenate([ang_r, ang_r, ang_c, ang_c], axis=-1).astype(np.float32)
    sign = np.concatenate([-np.ones(quarter), np.ones(quarter), -np.ones(quarter), np.ones(quarter)]).astype(np.float32)
    return np.stack([np.cos(ang), np.sin(ang) * sign]).astype(np.float32)


def shift_mats():
    m = np.zeros((128, 4, 128), np.float32)
    for k in range(127):
        m[k, 0, k + 1] = 1.0
        m[k + 1, 2, k] = 1.0
    m[127, 1, 0] = 1.0
    m[0, 3, 127] = 1.0
    return m


def const_tables(TL):
    idx = np.arange(128)
    m_lo = (idx[None, :] <= idx[:, None]).astype(np.float32)
    m_hi = (idx[:, None] <= idx[None, :]).astype(np.float32)
    i64 = np.arange(64)
    tri = np.stack([(i64[:, None] <= i64[None, :]), (i64[:, None] >= i64[None, :])]).astype(np.float32)
    strict = tri - np.eye(64, dtype=np.float32)[None]
    half = np.concatenate([strict, tri], axis=2)
    mk = np.concatenate([half, half], axis=1)
    nmts = -np.transpose(strict, (0, 2, 1))
    return {
        "tri": np.ascontiguousarray(np.transpose(tri, (1, 0, 2))),
        "mk": np.ascontiguousarray(np.transpose(mk, (1, 0, 2))),
        "nmts": np.ascontiguousarray(np.transpose(nmts, (1, 0, 2))),
        "zrow": np.zeros((1, NRW), np.float32),
        "shm": shift_mats(),
        "ropeA": rope_table(TL, 128),
        "ropeM": rope_table(TL, 64),
        "wmask": np.ascontiguousarray(np.stack([m_lo, m_hi], axis=1)),
    }


def make_inputs_for_core(inputs, b, L):
    f = lambda a: np.ascontiguousarray(np.asarray(a, dtype=np.float32))
    m = {
        "x": f(inputs["x"][b]),
        "ctx": f(inputs["ctx"][b]),
        "cvec": f(np.stack([np.asarray(inputs["c"][b]), np.asarray(inputs["c_ctx"])])),
        "ident": np.eye(128, dtype=np.float32),
    }
    m.update(const_tables(np.asarray(inputs["x"]).shape[1]))
    for k in ["ada_w", "ada_b", "norm1_g", "norm2_g", "w_in", "ga_q_norm", "ga_k_norm", "wa_q_norm", "wa_k_norm", "wa_sink",
              "mla_cq_norm", "mla_ckv_norm", "mla_w_uq", "mla_w_ukv", "mla_qn_norm", "mla_qr_norm", "mla_kn_norm", "mla_kr_norm",
              "rwkv_mu", "rwkv_w0", "rwkv_w2", "rwkv_a0", "rwkv_a2", "rwkv_g2", "rwkv_k_k", "rwkv_k_a", "rwkv_r_k", "rwkv_lnx_w", "rwkv_lnx_b",
              "w_branch", "w_out", "ffn_up", "ffn_conv_w", "ffn_conv_b", "ffn_down"]:
        m[k] = f(inputs[k][:L])
    return m


def kernel(**inputs):
    x = np.asarray(inputs["x"])
    B, TL, _ = x.shape
    LC = np.asarray(inputs["ctx"]).shape[1]
    L = np.asarray(inputs["ada_w"]).shape[0]
    cfg = {"TL": TL, "LC": LC, "DEPTH": L}
    nc, g = build(cfg)
    n = 8
    in_maps = [make_inputs_for_core(inputs, c % B, L) for c in range(n)]
    res = run_bass_kernel_spmd(nc, in_maps, core_ids=list(range(n)))
    return np.stack([res.results[b]["y"] for b in range(B)]).astype(np.float32)
```

```python
from contextlib import ExitStack
import threading
import numpy as np
import concourse.bass as bass
import concourse.mybir as mybir
from concourse.bass_utils import run_bass_kernel_spmd

F32 = mybir.dt.float32
BF16 = mybir.dt.bfloat16
AF = mybir.ActivationFunctionType
ALU = mybir.AluOpType
AX = mybir.AxisListType

D = 2048
GRID_W = 64
EPS = 1e-6
IN_W = 13376
DFF = 5632
O_GAQ, O_GAK, O_GAV, O_WAQ, O_WAK, O_WAV = 0, 512, 768, 1024, 1536, 1792
O_RKVG, O_ZW, O_ZA, O_CQ, O_CKV, O_KR, O_GATE = 2048, 3840, 4032, 4224, 4608, 5120, 5184
LNX_EPS = 64e-5


class Buf:
    __slots__ = ("name", "w", "r")

    def __init__(self, name=""):
        self.name = name
        self.w = None
        self.r = {}


class Sync:
    MAXC = 30000

    def __init__(self, nc, n_dma_sems=48, same_engine_sync=True):
        self.nc = nc
        self.hw = {"pe": nc.tensor, "dve": nc.vector, "act": nc.scalar, "pool": nc.gpsimd, "sp": nc.sync}
        self.gen = {k: 0 for k in self.hw}
        self.sem = {k: nc.alloc_semaphore("sem_%s_0" % k) for k in self.hw}
        self.count = {k: 0 for k in self.hw}
        self.seen = {k: {} for k in self.hw}
        self.dma_sems = [nc.alloc_semaphore("dsem%d" % i) for i in range(n_dma_sems)]
        self.dma_uses = [0] * n_dma_sems
        self.dma_next = 0
        self.same_engine_sync = same_engine_sync
        self.latest = {}
        self._lane = None

    def run_lanes(self, fns):
        n = len(fns)
        if n == 1:
            fns[0]()
            return
        cv = threading.Condition()
        st = {"turn": 0, "alive": [True] * n, "err": None}
        ids = {}

        def nxt(i):
            for k in range(1, n + 1):
                j = (i + k) % n
                if st["alive"][j]:
                    return j
            return -1

        def worker(i):
            ids[threading.get_ident()] = i
            with cv:
                while st["turn"] != i:
                    cv.wait()
            try:
                fns[i]()
            except BaseException as e:
                st["err"] = e
            with cv:
                st["alive"][i] = False
                st["turn"] = nxt(i)
                cv.notify_all()

        self._lane = (cv, st, ids, nxt)
        self.lane_pi = {}
        threads = [threading.Thread(target=worker, args=(i,)) for i in range(n)]
        for t in threads:
            t.start()
        for t in threads:
            t.join()
        self._lane = None
        if st["err"] is not None:
            raise st["err"]

    def lane_index(self):
        if self._lane is None:
            return None
        i = self._lane[2].get(threading.get_ident())
        if i is None:
            return None
        return i, len(self._lane[1]["alive"])

    def lane_yield(self):
        if self._lane is None:
            return
        cv, st, ids, nxt = self._lane
        i = ids.get(threading.get_ident())
        if i is None:
            return
        with cv:
            j = nxt(i)
            if j == i or j < 0:
                return
            st["turn"] = j
            cv.notify_all()
            while st["turn"] != i:
                cv.wait()

    def _wait(self, eng, dep):
        key, sem, val = dep
        if self.seen[eng].get(key, 0) >= val:
            return
        self.hw[eng].wait_ge(sem, val)
        self.seen[eng][key] = val

    @staticmethod
    def _add(deps, d):
        if d is not None and deps.get(d[0], (0, 0, 0))[2] < d[2]:
            deps[d[0]] = d

    def _deps(self, reads, writes):
        deps = {}
        for b in reads:
            self._add(deps, b.w)
        for b in writes:
            self._add(deps, b.w)
            for d in b.r.values():
                self._add(deps, d)
        return deps

    def _mark(self, dep, reads, writes):
        for b in reads:
            b.r[dep[0]] = dep
        for b in writes:
            b.w = dep
            b.r = {}
        self.latest[dep[0]] = dep

    def op(self, eng, fn, reads=(), writes=()):
        for key, d in self._deps(reads, writes).items():
            if isinstance(key, tuple) and key[0] == eng and (eng == "pe" or not self.same_engine_sync):
                continue
            self._wait(eng, d)
        ins = fn(self.hw[eng])
        self.count[eng] += 1
        ins.then_inc(self.sem[eng], 1)
        self._mark(((eng, self.gen[eng]), self.sem[eng], self.count[eng]), reads, writes)
        if self.count[eng] >= self.MAXC:
            self.gen[eng] += 1
            self.sem[eng] = self.nc.alloc_semaphore("sem_%s_%d" % (eng, self.gen[eng]))
            self.count[eng] = 0
        self.lane_yield()
        return ins

    def dma(self, q, out, in_, reads=(), writes=(), **kw):
        i = self.dma_next
        self.dma_next = (self.dma_next + 1) % len(self.dma_sems)
        sem = self.dma_sems[i]
        key = "d%d" % i
        if self.dma_uses[i] > 0:
            self._wait(q, (key, sem, 16 * self.dma_uses[i]))
        for k, d in self._deps(reads, writes).items():
            self._wait(q, d)
        self.dma_uses[i] += 1
        ins = self.hw[q].dma_start(out=out, in_=in_, **kw)
        ins.then_inc(sem, 16)
        self._mark((key, sem, 16 * self.dma_uses[i]), reads, writes)
        self.lane_yield()
        return ins

    def barrier(self):
        for eng in self.hw:
            for dep in list(self.latest.values()):
                key = dep[0]
                if isinstance(key, tuple) and key[0] == "pe" and eng == "pe":
                    continue
                self._wait(eng, dep)


class Tl:
    def __init__(self, t, name):
        self.t = t
        self.b = Buf(name)

    def __getitem__(self, idx):
        return self.t[idx]


class Ctx:
    pass


def sb(g, st, name, shape, dtype=F32):
    g.uid += 1
    nm = "%s_%d" % (name, g.uid)
    return Tl(st.enter_context(g.nc.sbuf_tensor(nm, list(shape), dtype)), nm)


def evac_engine(g):
    g.ev += 1
    return "dve" if g.ev % 2 else "act"


def copy_op(g, eng, out, in_, reads, writes):
    if eng == "act":
        return g.S.op("act", lambda e: e.copy(out=out, in_=in_), reads=reads, writes=writes)
    return g.S.op(eng, lambda e: e.tensor_copy(out=out, in_=in_), reads=reads, writes=writes)


def next_psum(g):
    lane = g.S.lane_index()
    if lane is None:
        g.pi = (g.pi + 1) % len(g.psums)
        return g.psums[g.pi]
    i, n = lane
    lo, hi = i * 8 // n, (i + 1) * 8 // n
    k = g.S.lane_pi.get(i, 0)
    g.S.lane_pi[i] = k + 1
    return g.psums[lo + k % (hi - lo)]


def linear(g, x, w, y, T, K, N, G=8, NB=512, pro=None, epi=None, segs=None, wlist=None):
    S, nc = g.S, g.nc
    KC = K // 128
    assert K % 128 == 0 and T % 128 == 0
    NT = T // 128
    XW = min(K, 2048)
    if segs is None:
        segs = [(0, KC)]
    with ExitStack() as st:
        xt = sb(g, st, "xt", [128, KC, G * 128], BF16)
        xin = [sb(g, st, "xin", [128, XW]) for _ in range(2)]
        wt = [sb(g, st, "wt", [128, KC, NB], BF16) for _ in range(3)]
        ot = [sb(g, st, "ot", [128, NB]) for _ in range(3)]
        nxin = nw = no = 0
        for g0 in range(0, NT, G):
            gn = min(G, NT - g0)
            for gi in range(gn):
                ti = g0 + gi
                for c0 in range(0, K, XW):
                    cw = min(XW, K - c0)
                    xi = xin[nxin % 2]
                    nxin += 1
                    S.dma("sp", xi[:, 0:cw], x[ti * 128:(ti + 1) * 128, c0:c0 + cw], writes=[xi.b])
                    if pro is not None:
                        pro(xi, ti, cw)
                    for k4 in range(0, cw // 128, 4):
                        ps = next_psum(g)
                        nk = min(4, cw // 128 - k4)
                        for j in range(nk):
                            kk = k4 + j
                            S.op("pe", lambda e: e.transpose(ps[:, j * 128:(j + 1) * 128], xi[:, kk * 128:(kk + 1) * 128], g.ident[:]),
                                 reads=[xi.b, g.ident.b], writes=[ps.b])
                        kc0 = c0 // 128 + k4
                        copy_op(g, evac_engine(g), xt[:, kc0:kc0 + nk, gi * 128:(gi + 1) * 128],
                                ps[:, 0:nk * 128].rearrange("p (k t) -> p k t", k=nk), [ps.b], [xt.b])
            for n0 in range(0, N, NB):
                nb = min(NB, N - n0)
                wi = wt[nw % 3]
                nw += 1
                S.dma("pool", wi[:, :, 0:nb], w.rearrange("(kc p) n -> p kc n", p=128)[:, :, n0:n0 + nb], writes=[wi.b])
                for gi in range(gn):
                    ti = g0 + gi
                    pss = []
                    for (k0, k1) in segs:
                        ps = next_psum(g)
                        pss.append(ps)
                        for kc in range(k0, k1):
                            S.op("pe", lambda e: e.matmul(ps[:, 0:nb], lhsT=xt[:, kc, gi * 128:(gi + 1) * 128], rhs=wi[:, kc, 0:nb],
                                                          start=(kc == k0), stop=(kc == k1 - 1)),
                                 reads=[xt.b, wi.b], writes=[ps.b])
                    o = ot[no % 3]
                    no += 1
                    if epi is None:
                        copy_op(g, evac_engine(g), o[:, 0:nb], pss[0][:, 0:nb], [pss[0].b], [o.b])
                        S.dma("sp", y[ti * 128:(ti + 1) * 128, n0:n0 + nb], o[:, 0:nb], reads=[o.b])
                    else:
                        epi(pss, o, ti, n0, nb)
    S.barrier()


def rms_rstd(g, st_tiles, x_ap, n, width, reads_b, sq, ss, rstd, eps=EPS):
    S = g.S
    S.op("act", lambda e: e.activation(out=sq[:, 0:n * width].rearrange("p (n w) -> p n w", n=n), in_=x_ap, func=AF.Square),
         reads=[reads_b], writes=[sq.b])
    S.op("dve", lambda e: e.tensor_reduce(out=ss[:, 0:n], in_=sq[:, 0:n * width].rearrange("p (n w) -> p n w", n=n), axis=AX.X, op=ALU.add),
         reads=[sq.b], writes=[ss.b])
    S.op("dve", lambda e: e.tensor_scalar(out=ss[:, 0:n], in0=ss[:, 0:n], scalar1=1.0 / width, scalar2=eps, op0=ALU.mult, op1=ALU.add),
         reads=[ss.b], writes=[ss.b])
    S.op("act", lambda e: e.activation(out=ss[:, 0:n], in_=ss[:, 0:n], func=AF.Sqrt), reads=[ss.b], writes=[ss.b])
    S.op("dve", lambda e: e.reciprocal(out=rstd[:, 0:n], in_=ss[:, 0:n]), reads=[ss.b], writes=[rstd.b])


def bcast_row(g, q, tile_ap, buf, dram_row_ap):
    g.S.dma(q, tile_ap, dram_row_ap.partition_broadcast(128), writes=[buf])


def stage_mod(g, l):
    S, nc = g.S, g.nc
    with ExitStack() as st:
        wt = [sb(g, st, "mw", [128, 16, 512]) for _ in range(2)]
        bt = [sb(g, st, "mb", [128, 512]) for _ in range(2)]
        ot = [sb(g, st, "mo", [128, 512]) for _ in range(3)]
        no = 0
        for bi, n0 in enumerate(range(0, 6 * D, 512)):
            wi = wt[bi % 2]
            bb = bt[bi % 2]
            S.dma("sp", wi[:, :, :], g.ada_w[l].rearrange("(kc p) n -> p kc n", p=128)[:, :, n0:n0 + 512], writes=[wi.b])
            bcast_row(g, "sp", bb[:, :], bb.b, g.ada_b[l, n0:n0 + 512])
            for r in range(2):
                ps = next_psum(g)
                for kc in range(16):
                    S.op("pe", lambda e: e.matmul(ps[:, :], lhsT=g.cb[r][:, kc, :], rhs=wi[:, kc, :], start=(kc == 0), stop=(kc == 15)),
                         reads=[g.cb[r].b, wi.b], writes=[ps.b])
                o = ot[no % 3]
                no += 1
                S.op("dve", lambda e: e.tensor_tensor(out=o[:, :], in0=ps[:, :], in1=bb[:, :], op=ALU.add), reads=[ps.b, bb.b], writes=[o.b])
                S.dma("pool", g.MODB[r, :, n0:n0 + 512], o[:, :], reads=[o.b])
    S.barrier()


def make_norm_pro(g, st, l, gain_dram, sc_off, sh_off):
    S = g.S
    A = [sb(g, st, "nA", [128, D]) for _ in range(2)]
    B = [sb(g, st, "nB", [128, D]) for _ in range(2)]
    sq = sb(g, st, "nsq", [128, D])
    ss = sb(g, st, "nss", [128, 1])
    rstd = sb(g, st, "nrs", [128, 1])
    with ExitStack() as st2:
        gt = sb(g, st2, "ng", [128, D])
        bcast_row(g, "sp", gt[:, :], gt.b, gain_dram)
        for r in range(2):
            S.dma("sp", A[r][:, :], g.MODB[r, :, sc_off:sc_off + D], writes=[A[r].b])
            S.dma("sp", B[r][:, :], g.MODB[r, :, sh_off:sh_off + D], writes=[B[r].b])
            S.op("dve", lambda e: e.scalar_tensor_tensor(out=A[r][:, :], in0=A[r][:, :], scalar=1.0, in1=gt[:, :], op0=ALU.add, op1=ALU.mult),
                 reads=[A[r].b, gt.b], writes=[A[r].b])
        S.barrier()

    def pro(xi, ti, cw):
        r = 1 if ti < g.NTC else 0
        rms_rstd(g, None, xi[:, 0:D].rearrange("p (n w) -> p n w", n=1), 1, D, xi.b, sq, ss, rstd)
        S.op("dve", lambda e: e.scalar_tensor_tensor(out=xi[:, 0:D], in0=xi[:, 0:D], scalar=rstd[:, 0:1], in1=A[r][:, :], op0=ALU.mult, op1=ALU.mult),
             reads=[xi.b, rstd.b, A[r].b], writes=[xi.b])
        S.op("dve", lambda e: e.tensor_tensor(out=xi[:, 0:D], in0=xi[:, 0:D], in1=B[r][:, :], op=ALU.add), reads=[xi.b, B[r].b], writes=[xi.b])
    return pro


def make_resid_epi(g, st, gate_off):
    S = g.S
    gts = [sb(g, st, "rg", [128, D]) for _ in range(2)]
    for r in range(2):
        S.dma("sp", gts[r][:, :], g.MODB[r, :, gate_off:gate_off + D], writes=[gts[r].b])
    xb = [sb(g, st, "rx", [128, 512]) for _ in range(3)]
    cnt = [0]

    def epi(pss, o, ti, n0, nb):
        r = 1 if ti < g.NTC else 0
        x = xb[cnt[0] % 3]
        cnt[0] += 1
        S.dma("sp", x[:, 0:nb], g.XS[ti * 128:(ti + 1) * 128, n0:n0 + nb], writes=[x.b])
        S.op("dve", lambda e: e.tensor_tensor(out=o[:, 0:nb], in0=pss[0][:, 0:nb], in1=gts[r][:, n0:n0 + nb], op=ALU.mult),
             reads=[pss[0].b, gts[r].b], writes=[o.b])
        S.op("dve", lambda e: e.tensor_tensor(out=o[:, 0:nb], in0=o[:, 0:nb], in1=x[:, 0:nb], op=ALU.add), reads=[o.b, x.b], writes=[o.b])
        S.dma("sp", g.XS[ti * 128:(ti + 1) * 128, n0:n0 + nb], o[:, 0:nb], reads=[o.b])
    return epi


def norm_rope(g, src, src_b, n, w, gain_ap, gain_b, out, out_b, tmp, sq, ss, rstd, rope=None):
    S = g.S
    rms_rstd(g, None, src, n, w, src_b, sq, ss, rstd)
    dst = out if rope is None else tmp[:, 0:n * w].rearrange("p (n w) -> p n w", n=n)
    dst_b = out_b if rope is None else tmp.b
    S.op("dve", lambda e: e.tensor_tensor(out=dst, in0=src, in1=rstd[:, 0:n].unsqueeze(2).broadcast_to([128, n, w]), op=ALU.mult),
         reads=[src_b, rstd.b], writes=[dst_b])
    S.op("dve", lambda e: e.tensor_tensor(out=dst, in0=dst, in1=gain_ap, op=ALU.mult), reads=[dst_b, gain_b], writes=[dst_b])
    if rope is None:
        return
    cos, ssin, blk = rope
    nh = w // (2 * blk)
    xv = dst.rearrange("p n (h b k) -> p n h b k", h=nh, b=2, k=blk)
    sv = ssin[:, 0:w].rearrange("p (h b k) -> p h b k", h=nh, b=2, k=blk)
    sw = sq[:, 0:n * w].rearrange("p (n h b k) -> p n h b k", n=n, h=nh, b=2, k=blk)
    for n_i in range(n):
        for b in range(2):
            S.op("dve", lambda e: e.tensor_tensor(out=sw[:, n_i, :, b, :], in0=xv[:, n_i, :, 1 - b, :], in1=sv[:, :, b, :], op=ALU.mult),
                 reads=[dst_b, ssin.b], writes=[sq.b])
    S.op("dve", lambda e: e.tensor_tensor(out=dst, in0=dst, in1=cos[:, 0:w].unsqueeze(1).broadcast_to([128, n, w]), op=ALU.mult),
         reads=[dst_b, cos.b], writes=[dst_b])
    S.op("dve", lambda e: e.tensor_tensor(out=out, in0=dst, in1=sq[:, 0:n * w].rearrange("p (n w) -> p n w", n=n), op=ALU.add),
         reads=[dst_b, sq.b], writes=[out_b])


def transpose_slots(g, src_tl, slots, w, dst_tl, dst_slot0):
    S = g.S
    for i0 in range(0, len(slots), 4):
        grp = slots[i0:i0 + 4]
        ps = next_psum(g)
        for j, s_ in enumerate(grp):
            S.op("pe", lambda e: e.transpose(ps[0:w, j * 128:(j + 1) * 128], src_tl[:, s_, 0:w], g.ident[:]),
                 reads=[src_tl.b, g.ident.b], writes=[ps.b])
        copy_op(g, evac_engine(g), dst_tl[0:w, dst_slot0 + i0:dst_slot0 + i0 + len(grp), :],
                ps[0:w, 0:len(grp) * 128].rearrange("p (k t) -> p k t", k=len(grp)), [ps.b], [dst_tl.b])


def stage_attn_prep(g, l):
    S = g.S
    with ExitStack() as st:
        gain = sb(g, st, "apg", [128, 2, 6, 128])
        for gi, (qn, kn) in enumerate([(g.ga_q_norm, g.ga_k_norm), (g.wa_q_norm, g.wa_k_norm)]):
            for s_ in range(6):
                bcast_row(g, "sp", gain[:, gi, s_, :], gain.b, (qn if s_ < 4 else kn)[l, :])
        NL = 3
        LT = [dict(z=sb(g, st, "apz", [128, 16, 128]), x=sb(g, st, "apx", [128, 12, 128]), t_=sb(g, st, "apt", [128, 12, 128]),
                   cs=sb(g, st, "apc", [128, 128]), sn=sb(g, st, "aps", [128, 128]),
                   tmp=sb(g, st, "aptmp", [128, 6 * 128]), sq=sb(g, st, "apsq", [128, 6 * 128]),
                   ss=sb(g, st, "apss", [128, 8]), rstd=sb(g, st, "aprs", [128, 8])) for _ in range(NL)]

        def lane(li):
          lt = LT[li]
          tmp, sq, ss, rstd = lt["tmp"], lt["sq"], lt["ss"], lt["rstd"]
          cs = [lt["cs"], lt["cs"]]
          sn = [lt["sn"], lt["sn"]]
          for ti in range(li, g.NT, NL):
            z = lt["z"]
            x = lt["x"]
            t_ = lt["t_"]
            S.dma("sp", z[:, :, :], g.Z[ti * 128:(ti + 1) * 128, 0:2048].rearrange("p (s d) -> p s d", s=16), writes=[z.b])
            rope = None
            if ti >= g.NTC:
                c_, s_t = cs[ti % 2], sn[ti % 2]
                p0 = (ti - g.NTC) * 128
                S.dma("sp", c_[:, :], g.ropeA[0, p0:p0 + 128, :], writes=[c_.b])
                S.dma("sp", s_t[:, :], g.ropeA[1, p0:p0 + 128, :], writes=[s_t.b])
                rope = (c_, s_t, 32)
            for gi, base in enumerate([0, 8]):
                norm_rope(g, z[:, base:base + 6, :], z.b, 6, 128, gain[:, gi, :, :], gain.b,
                          x[:, gi * 6:(gi + 1) * 6, :], x.b, tmp, sq, ss, rstd, rope=rope)
            transpose_slots(g, x, list(range(12)), 128, t_, 0)
            S.dma("pool", g.QKT[:, :, ti * 128:(ti + 1) * 128].rearrange("s d t -> d s t"), t_[:, :, :], reads=[t_.b])
        S.run_lanes([(lambda li=li: lane(li)) for li in range(NL)])
    S.barrier()


def attention(g, heads, scale, sink_tl=None):
    S = g.S
    T, NT = g.T, g.NT
    po = g.psums[0:4]
    sps = g.psums[4:8]
    with ExitStack() as st:
        kts = [sb(g, st, "akt", [128, T], BF16) for _ in range(2)]
        vt = sb(g, st, "avt", [128, NT, 129], BF16)
        qts = [[sb(g, st, "aqt", [128, 512], BF16) for _ in range(2)] for _ in range(3)]
        pts = [sb(g, st, "apt", [128, 512], BF16) for _ in range(4)]
        yts = [sb(g, st, "ayt", [128, 128]) for _ in range(3)]
        den = sb(g, st, "aden", [128, 4])
        S.op("dve", lambda e: e.memset(vt[:, :, 128:129], 1.0), writes=[vt.b])
        nq = npt = ny = nsp = 0
        for hd in heads:
            for pi_, (kd_ap, kd) in enumerate(hd["kparts"]):
                S.dma("pool", kts[pi_][0:kd, :], kd_ap, writes=[kts[pi_].b])
            S.dma("pool", vt[:, :, 0:128], hd["v"].rearrange("(n p) d -> p n d", p=128), writes=[vt.b])
            for qh, qparts in enumerate(hd["qparts"]):
                for (q0, qlen, keys) in hd["blocks"]:
                    nqb = qlen // 128
                    qt = qts[nq % 3]
                    nq += 1
                    for pi_, (qd_ap, kd) in enumerate(qparts):
                        S.dma("pool", qt[pi_][0:kd, 0:qlen], qd_ap[:, q0:q0 + qlen], writes=[qt[pi_].b])
                    npart = len(qparts)

                    def scores(kt_):
                        nonlocal nsp
                        ps_ = sps[nsp % 4]
                        nsp += 1
                        for pi_, (qd_ap, kd) in enumerate(qparts):
                            S.op("pe", lambda e: e.matmul(ps_[:, 0:qlen], lhsT=kts[pi_][0:kd, kt_ * 128:(kt_ + 1) * 128], rhs=qt[pi_][0:kd, 0:qlen],
                                                          start=(pi_ == 0), stop=(pi_ == npart - 1)),
                                 reads=[kts[pi_].b, qt[pi_].b], writes=[ps_.b])
                        return ps_
                    ahead = [scores(keys[j][0]) for j in range(min(2, len(keys)))]
                    for idx, (kt, mask) in enumerate(keys):
                        ps = ahead.pop(0)
                        if idx + 2 < len(keys):
                            ahead.append(scores(keys[idx + 2][0]))
                        pt = pts[npt % 4]
                        npt += 1
                        S.op("act", lambda e: e.activation(out=pt[:, 0:qlen], in_=ps[:, 0:qlen], func=AF.Exp, scale=scale),
                             reads=[ps.b], writes=[pt.b])
                        if mask is not None:
                            S.op("dve", lambda e: e.tensor_tensor(out=pt[:, 0:qlen], in0=pt[:, 0:qlen], in1=g.wmask[:, mask, :], op=ALU.mult),
                                 reads=[pt.b, g.wmask.b], writes=[pt.b])
                        for qb in range(nqb):
                            S.op("pe", lambda e: e.matmul(po[qb][:, 0:129], lhsT=pt[:, qb * 128:(qb + 1) * 128], rhs=vt[:, kt, :],
                                                          start=(idx == 0), stop=(idx == len(keys) - 1)),
                                 reads=[pt.b, vt.b], writes=[po[qb].b])
                    for qb in range(nqb):
                        y = yts[ny % 3]
                        ny += 1
                        if hd.get("sink_idx") is not None:
                            si = hd["sink_idx"][qh]
                            S.op("dve", lambda e: e.tensor_tensor(out=den[:, 0:1], in0=po[qb][:, 128:129], in1=sink_tl[:, si:si + 1], op=ALU.add),
                                 reads=[po[qb].b, sink_tl.b], writes=[den.b])
                            S.op("dve", lambda e: e.reciprocal(out=den[:, 0:1], in_=den[:, 0:1]), reads=[den.b], writes=[den.b])
                        else:
                            S.op("dve", lambda e: e.reciprocal(out=den[:, 0:1], in_=po[qb][:, 128:129]), reads=[po[qb].b], writes=[den.b])
                        S.op("dve", lambda e: e.tensor_scalar(out=y[:, :], in0=po[qb][:, 0:128], scalar1=den[:, 0:1], scalar2=None, op0=ALU.mult),
                             reads=[po[qb].b, den.b], writes=[y.b])
                        r0 = q0 + qb * 128
                        yc = hd["ycols"][qh]
                        S.dma("sp", g.YMIX[r0:r0 + 128, yc:yc + 128], y[:, :], reads=[y.b])
    S.barrier()


def dense_blocks(g):
    blocks = [(0, g.LC, [(kt, None) for kt in range(g.NTC)])]
    for q0 in range(g.LC, g.T, 512):
        blocks.append((q0, min(512, g.T - q0), [(kt, None) for kt in range(g.NT)]))
    return blocks


def window_blocks(g):
    blocks = [(0, g.LC, [(kt, None) for kt in range(g.NTC)])]
    nblk = g.TL // 128
    for n in range(nblk):
        keys = [(kt, None) for kt in range(g.NTC)]
        if n > 0:
            keys.append((g.NTC + n - 1, 0))
        keys.append((g.NTC + n, None))
        if n < nblk - 1:
            keys.append((g.NTC + n + 1, 1))
        blocks.append((g.LC + n * 128, 128, keys))
    return blocks


def stage_attn_gawa(g, l):
    S = g.S
    heads = []
    for hk in range(2):
        heads.append(dict(kparts=[(g.QKT[4 + hk], 128)], qparts=[[(g.QKT[2 * hk + j], 128)] for j in range(2)],
                          v=g.Z[:, O_GAV + hk * 128:O_GAV + (hk + 1) * 128], ycols=[(2 * hk + j) * 128 for j in range(2)],
                          blocks=dense_blocks(g)))
    attention(g, heads, 128 ** -0.5)
    with ExitStack() as st:
        sink = sb(g, st, "sink", [128, 4])
        bcast_row(g, "sp", sink[:, :], sink.b, g.wa_sink[l, :])
        S.op("act", lambda e: e.activation(out=sink[:, :], in_=sink[:, :], func=AF.Exp), reads=[sink.b], writes=[sink.b])
        heads = []
        for hk in range(2):
            heads.append(dict(kparts=[(g.QKT[10 + hk], 128)], qparts=[[(g.QKT[6 + 2 * hk + j], 128)] for j in range(2)],
                              v=g.Z[:, O_WAV + hk * 128:O_WAV + (hk + 1) * 128], ycols=[512 + (2 * hk + j) * 128 for j in range(2)],
                              blocks=window_blocks(g), sink_idx=[2 * hk, 2 * hk + 1]))
        attention(g, heads, 128 ** -0.5, sink_tl=sink)


def make_rms_pro(g, st, gain_dram, K):
    S = g.S
    gt = sb(g, st, "pg", [128, K])
    sq = sb(g, st, "psq", [128, K])
    ss = sb(g, st, "pss", [128, 1])
    rstd = sb(g, st, "prs", [128, 1])
    bcast_row(g, "sp", gt[:, :], gt.b, gain_dram)

    def pro(xi, ti, cw):
        rms_rstd(g, None, xi[:, 0:K].rearrange("p (n w) -> p n w", n=1), 1, K, xi.b, sq, ss, rstd)
        S.op("dve", lambda e: e.scalar_tensor_tensor(out=xi[:, 0:K], in0=xi[:, 0:K], scalar=rstd[:, 0:1], in1=gt[:, :], op0=ALU.mult, op1=ALU.mult),
             reads=[xi.b, rstd.b, gt.b], writes=[xi.b])
    return pro


def stage_mla(g, l):
    S = g.S
    T = g.T
    with ExitStack() as st:
        pro = make_rms_pro(g, st, g.mla_cq_norm[l, :], 384)
        linear(g, g.Z[:, O_CQ:O_CQ + 384], g.mla_w_uq[l], g.QML, T, 384, 768, G=17, NB=512, pro=pro)
    with ExitStack() as st:
        pro = make_rms_pro(g, st, g.mla_ckv_norm[l, :], 512)
        linear(g, g.Z[:, O_CKV:O_CKV + 512], g.mla_w_ukv[l], g.KVML, T, 512, 1024, G=17, NB=512, pro=pro)
    with ExitStack() as st:
        gq_n = sb(g, st, "mgqn", [128, 128])
        gq_r = sb(g, st, "mgqr", [128, 64])
        gk_n = sb(g, st, "mgkn", [128, 128])
        gk_r = sb(g, st, "mgkr", [128, 64])
        bcast_row(g, "sp", gq_n[:, :], gq_n.b, g.mla_qn_norm[l, :])
        bcast_row(g, "sp", gq_r[:, :], gq_r.b, g.mla_qr_norm[l, :])
        bcast_row(g, "sp", gk_n[:, :], gk_n.b, g.mla_kn_norm[l, :])
        bcast_row(g, "sp", gk_r[:, :], gk_r.b, g.mla_kr_norm[l, :])
        NL = 3
        qin = [sb(g, st, "mq", [128, 4, 192]) for _ in range(NL)]
        kvin = [sb(g, st, "mkv", [128, 4, 256]) for _ in range(NL)]
        krin = [sb(g, st, "mkr", [128, 1, 64]) for _ in range(NL)]
        xqn = [sb(g, st, "xqn", [128, 4, 128]) for _ in range(NL)]
        xqr = [sb(g, st, "xqr", [128, 4, 64]) for _ in range(NL)]
        xkn = [sb(g, st, "xkn", [128, 4, 128]) for _ in range(NL)]
        xkr = [sb(g, st, "xkr", [128, 1, 64]) for _ in range(NL)]
        tqn = [sb(g, st, "tqn", [128, 4, 128]) for _ in range(NL)]
        tqr = [sb(g, st, "tqr", [64, 4, 128]) for _ in range(NL)]
        tkn = [sb(g, st, "tkn", [128, 4, 128]) for _ in range(NL)]
        tkr = [sb(g, st, "tkr", [64, 1, 128]) for _ in range(NL)]
        cs = [sb(g, st, "mc", [128, 64]) for _ in range(NL)]
        sn = [sb(g, st, "ms", [128, 64]) for _ in range(NL)]
        tmps = [sb(g, st, "mtmp", [128, 512]) for _ in range(NL)]
        sqs = [sb(g, st, "msq", [128, 512]) for _ in range(NL)]
        sss = [sb(g, st, "mss", [128, 8]) for _ in range(NL)]
        rstds = [sb(g, st, "mrs", [128, 8]) for _ in range(NL)]

        def lane(i):
          tmp, sq, ss, rstd = tmps[i], sqs[i], sss[i], rstds[i]
          for ti in range(i, g.NT, NL):
            rows = slice(ti * 128, (ti + 1) * 128)
            S.dma("sp", qin[i][:, :, :], g.QML[rows, :].rearrange("p (h d) -> p h d", h=4), writes=[qin[i].b])
            S.dma("sp", kvin[i][:, :, :], g.KVML[rows, :].rearrange("p (h d) -> p h d", h=4), writes=[kvin[i].b])
            S.dma("sp", krin[i][:, 0, :], g.Z[rows, O_KR:O_KR + 64], writes=[krin[i].b])
            rope = None
            if ti >= g.NTC:
                p0 = (ti - g.NTC) * 128
                S.dma("sp", cs[i][:, :], g.ropeM[0, p0:p0 + 128, :], writes=[cs[i].b])
                S.dma("sp", sn[i][:, :], g.ropeM[1, p0:p0 + 128, :], writes=[sn[i].b])
                rope = (cs[i], sn[i], 16)
            norm_rope(g, qin[i][:, :, 0:128], qin[i].b, 4, 128, gq_n[:, :].unsqueeze(1).broadcast_to([128, 4, 128]), gq_n.b,
                      xqn[i][:, :, :], xqn[i].b, tmp, sq, ss, rstd)
            norm_rope(g, qin[i][:, :, 128:192], qin[i].b, 4, 64, gq_r[:, :].unsqueeze(1).broadcast_to([128, 4, 64]), gq_r.b,
                      xqr[i][:, :, :], xqr[i].b, tmp, sq, ss, rstd, rope=rope)
            norm_rope(g, kvin[i][:, :, 0:128], kvin[i].b, 4, 128, gk_n[:, :].unsqueeze(1).broadcast_to([128, 4, 128]), gk_n.b,
                      xkn[i][:, :, :], xkn[i].b, tmp, sq, ss, rstd)
            norm_rope(g, krin[i][:, :, :], krin[i].b, 1, 64, gk_r[:, :].unsqueeze(1), gk_r.b,
                      xkr[i][:, :, :], xkr[i].b, tmp, sq, ss, rstd, rope=rope)
            transpose_slots(g, xqn[i], [0, 1, 2, 3], 128, tqn[i], 0)
            transpose_slots(g, xqr[i], [0, 1, 2, 3], 64, tqr[i], 0)
            transpose_slots(g, xkn[i], [0, 1, 2, 3], 128, tkn[i], 0)
            transpose_slots(g, xkr[i], [0], 64, tkr[i], 0)
            cols = slice(ti * 128, (ti + 1) * 128)
            S.dma("pool", g.MQN[:, :, cols].rearrange("s d t -> d s t"), tqn[i][:, :, :], reads=[tqn[i].b])
            S.dma("pool", g.MQR[:, :, cols].rearrange("s d t -> d s t"), tqr[i][:, :, :], reads=[tqr[i].b])
            S.dma("pool", g.MKN[:, :, cols].rearrange("s d t -> d s t"), tkn[i][:, :, :], reads=[tkn[i].b])
            S.dma("pool", g.MKR[:, :, cols].rearrange("s d t -> d s t"), tkr[i][:, :, :], reads=[tkr[i].b])
        S.run_lanes([(lambda i=i: lane(i)) for i in range(NL)])
    S.barrier()
    heads = []
    for h in range(4):
        heads.append(dict(kparts=[(g.MKN[h], 128), (g.MKR[0], 64)], qparts=[[(g.MQN[h], 128), (g.MQR[h], 64)]],
                          v=g.KVML[:, h * 256 + 128:h * 256 + 256], ycols=[1536 + h * 128], blocks=dense_blocks(g)))
    attention(g, heads, 192 ** -0.5)


RQ_R, RQ_V, RQ_LW, RQ_K, RQ_KK, RQ_KKA = 0, 1, 2, 3, 4, 5
RQ_SCAN = [RQ_R, RQ_LW, RQ_K, RQ_V, RQ_KK, RQ_KKA]
NRW = 2176


def stage_rwkv_prep(g, l):
    S = g.S
    NT, NTC = g.NT, g.NTC
    with ExitStack() as st:
        MU = [sb(g, st, "rmu", [128, NRW]) for _ in range(2)]
        w0 = [sb(g, st, "rw0", [128, 512]) for _ in range(2)]
        a0 = [sb(g, st, "ra0", [128, 512]) for _ in range(2)]
        w2 = [sb(g, st, "rw2", [96, 512]) for _ in range(2)]
        a2 = [sb(g, st, "ra2", [96, 512]) for _ in range(2)]
        g2 = sb(g, st, "rg2", [128, 2, 512])
        kk_ = sb(g, st, "rkk", [128, 512])
        ka = sb(g, st, "rka", [128, 512])
        omka = sb(g, st, "romka", [128, 512])
        for d in range(2):
            S.op("dve", lambda e: e.memset(MU[d][:, :], 0.0), writes=[MU[d].b])
            bcast_row(g, "sp", MU[d][:, 0:1792], MU[d].b, g.rwkv_mu[l, d, 0:1792])
            bcast_row(g, "sp", MU[d][:, 1792 + d * 96:1792 + (d + 1) * 96], MU[d].b, g.rwkv_mu[l, d, 1792:1888])
            bcast_row(g, "sp", MU[d][:, 1984 + d * 96:1984 + (d + 1) * 96], MU[d].b, g.rwkv_mu[l, d, 1888:1984])
            bcast_row(g, "sp", w0[d][:, :], w0[d].b, g.rwkv_w0[l, d, :])
            bcast_row(g, "sp", a0[d][:, :], a0[d].b, g.rwkv_a0[l, d, :])
            S.dma("sp", w2[d][:, :], g.rwkv_w2[l, d], writes=[w2[d].b])
            S.dma("sp", a2[d][:, :], g.rwkv_a2[l, d], writes=[a2[d].b])
        S.dma("sp", g2[:, :, :], g.rwkv_g2[l].rearrange("(kc p) n -> p kc n", p=128), writes=[g2.b])
        bcast_row(g, "sp", kk_[:, :], kk_.b, g.rwkv_k_k[l, :])
        bcast_row(g, "sp", ka[:, :], ka.b, g.rwkv_k_a[l, :])
        S.op("dve", lambda e: e.tensor_scalar(out=omka[:, :], in0=ka[:, :], scalar1=-1.0, scalar2=1.0, op0=ALU.mult, op1=ALU.add),
             reads=[ka.b], writes=[omka.b])
        LT = []
        for d in range(2):
            LT.append(dict(
                zc=[sb(g, st, "rzc", [128, NRW]) for _ in range(2)],
                zz=sb(g, st, "rzs", [128, NRW]),
                u=sb(g, st, "ru", [128, NRW]),
                tw=sb(g, st, "rtw", [128, 96]),
                sg=sb(g, st, "rsg", [128, 256]),
                tT=sb(g, st, "rtT", [128, 4, 128]),
                aa=sb(g, st, "raa", [128, 512]),
                t1=sb(g, st, "rt1", [128, 512]),
                t2=sb(g, st, "rt2", [128, 512]),
                ss=sb(g, st, "rss", [128, 8]),
                ob=[sb(g, st, "rob", [128, 4, 512]) for _ in range(2)],
                og=[sb(g, st, "rog", [128, 512]) for _ in range(2)]))

        def lane(d):
            lt = LT[d]
            zz, u, tw, sg, tT, aa, t1, t2, ss = (lt[k] for k in ["zz", "u", "tw", "sg", "tT", "aa", "t1", "t2", "ss"])
            for ti in range(NT):
                z = lt["zc"][ti % 2]
                r0 = ti * 128
                S.dma("sp", z[:, :], g.Z[r0:r0 + 128, O_RKVG:O_RKVG + NRW], writes=[z.b])
                if True:
                    if d == 0:
                        if ti in (0, NTC):
                            S.dma("sp", zz[1:128, :], g.Z[r0:r0 + 127, O_RKVG:O_RKVG + NRW], writes=[zz.b])
                            S.dma("sp", zz[0:1, :], g.zrow[0:1, :], writes=[zz.b])
                        else:
                            S.dma("sp", zz[:, :], g.Z[r0 - 1:r0 + 127, O_RKVG:O_RKVG + NRW], writes=[zz.b])
                    else:
                        if ti in (NTC - 1, NT - 1):
                            S.dma("sp", zz[0:127, :], g.Z[r0 + 1:r0 + 128, O_RKVG:O_RKVG + NRW], writes=[zz.b])
                            S.dma("sp", zz[127:128, :], g.zrow[0:1, :], writes=[zz.b])
                        else:
                            S.dma("sp", zz[:, :], g.Z[r0 + 1:r0 + 129, O_RKVG:O_RKVG + NRW], writes=[zz.b])
                    S.op("dve", lambda e: e.tensor_tensor(out=u[:, :], in0=zz[:, :], in1=z[:, :], op=ALU.subtract), reads=[zz.b, z.b], writes=[u.b])
                    S.op("dve", lambda e: e.tensor_tensor(out=u[:, :], in0=u[:, :], in1=MU[d][:, :], op=ALU.mult), reads=[u.b, MU[d].b], writes=[u.b])
                    S.op("dve", lambda e: e.tensor_tensor(out=u[:, :], in0=u[:, :], in1=z[:, :], op=ALU.add), reads=[u.b, z.b], writes=[u.b])
                    ur, uk, uv, ugd = u[:, 0:512], u[:, 512:1024], u[:, 1024:1536], u[:, 1536:1792]
                    uwd = u[:, 1792 + d * 96:1792 + (d + 1) * 96]
                    uad = u[:, 1984 + d * 96:1984 + (d + 1) * 96]
                    ob = lt["ob"][ti % 2]
                    o_g = lt["og"][ti % 2]

                    class _V:
                        def __init__(self, j):
                            self.j = j
                            self.b = ob.b

                        def __getitem__(self, idx):
                            return ob[:, self.j, :]
                    o_lw, o_k, o_kk, o_kka = _V(0), _V(1), _V(2), _V(3)
                    rows = slice(r0, r0 + 128)
                    S.dma("pool", g.RW[d, 0:2, rows, :].rearrange("q t c -> t q c"),
                          u[:, 0:2048].rearrange("p (a c) -> p a c", a=2)[:, :, 0:512], reads=[u.b])
                    S.op("act", lambda e: e.activation(out=tw[:, :], in_=uwd, func=AF.Tanh), reads=[u.b], writes=[tw.b])
                    S.op("act", lambda e: e.activation(out=sg[:, :], in_=ugd, func=AF.Sigmoid), reads=[u.b], writes=[sg.b])
                    ps = next_psum(g)
                    S.op("pe", lambda e: e.transpose(ps[0:96, 0:128], tw[:, :], g.ident[:]), reads=[tw.b, g.ident.b], writes=[ps.b])
                    S.op("pe", lambda e: e.transpose(ps[0:96, 128:256], uad, g.ident[:]), reads=[u.b, g.ident.b], writes=[ps.b])
                    copy_op(g, "act", tT[0:96, 0:2, :], ps[0:96, 0:256].rearrange("p (k t) -> p k t", k=2), [ps.b], [tT.b])
                    ps = next_psum(g)
                    S.op("pe", lambda e: e.transpose(ps[:, 0:128], sg[:, 0:128], g.ident[:]), reads=[sg.b, g.ident.b], writes=[ps.b])
                    S.op("pe", lambda e: e.transpose(ps[:, 128:256], sg[:, 128:256], g.ident[:]), reads=[sg.b, g.ident.b], writes=[ps.b])
                    copy_op(g, "dve", tT[:, 2:4, :], ps[:, 0:256].rearrange("p (k t) -> p k t", k=2), [ps.b], [tT.b])
                    psw = next_psum(g)
                    S.op("pe", lambda e: e.matmul(psw[:, :], lhsT=tT[0:96, 0, :], rhs=w2[d][:, :], start=True, stop=True), reads=[tT.b, w2[d].b], writes=[psw.b])
                    S.op("dve", lambda e: e.tensor_tensor(out=t1[:, :], in0=psw[:, :], in1=w0[d][:, :], op=ALU.add), reads=[psw.b, w0[d].b], writes=[t1.b])
                    S.op("act", lambda e: e.activation(out=t1[:, :], in_=t1[:, :], func=AF.Sigmoid), reads=[t1.b], writes=[t1.b])
                    S.op("act", lambda e: e.mul(out=o_lw[:, :], in_=t1[:, :], mul=-0.6065306597126334), reads=[t1.b], writes=[o_lw.b])
                    psa = next_psum(g)
                    S.op("pe", lambda e: e.matmul(psa[:, :], lhsT=tT[0:96, 1, :], rhs=a2[d][:, :], start=True, stop=True), reads=[tT.b, a2[d].b], writes=[psa.b])
                    S.op("dve", lambda e: e.tensor_tensor(out=aa[:, :], in0=psa[:, :], in1=a0[d][:, :], op=ALU.add), reads=[psa.b, a0[d].b], writes=[aa.b])
                    S.op("act", lambda e: e.activation(out=aa[:, :], in_=aa[:, :], func=AF.Sigmoid), reads=[aa.b], writes=[aa.b])
                    psg = next_psum(g)
                    for kc in range(2):
                        S.op("pe", lambda e: e.matmul(psg[:, :], lhsT=tT[:, 2 + kc, :], rhs=g2[:, kc, :], start=(kc == 0), stop=(kc == 1)),
                             reads=[tT.b, g2.b], writes=[psg.b])
                    copy_op(g, "act", o_g[:, :], psg[:, :], [psg.b], [o_g.b])
                    S.dma("pool", g.RG[d, rows, :], o_g[:, :], reads=[o_g.b])
                    S.op("dve", lambda e: e.tensor_tensor(out=t1[:, :], in0=uk, in1=kk_[:, :], op=ALU.mult), reads=[u.b, kk_.b], writes=[t1.b])
                    S.op("act", lambda e: e.activation(out=t2[:, :], in_=t1[:, :], func=AF.Square), reads=[t1.b], writes=[t2.b])
                    S.op("dve", lambda e: e.tensor_reduce(out=ss[:, 0:8], in_=t2[:, :].rearrange("p (h k) -> p h k", h=8), axis=AX.X, op=ALU.add),
                         reads=[t2.b], writes=[ss.b])
                    S.op("act", lambda e: e.activation(out=ss[:, 0:8], in_=ss[:, 0:8], func=AF.Sqrt), reads=[ss.b], writes=[ss.b])
                    S.op("dve", lambda e: e.tensor_scalar(out=ss[:, 0:8], in0=ss[:, 0:8], scalar1=1e-12, scalar2=None, op0=ALU.max), reads=[ss.b], writes=[ss.b])
                    S.op("dve", lambda e: e.reciprocal(out=ss[:, 0:8], in_=ss[:, 0:8]), reads=[ss.b], writes=[ss.b])
                    S.op("dve", lambda e: e.tensor_tensor(out=o_kk[:, :].rearrange("p (h k) -> p h k", h=8), in0=t1[:, :].rearrange("p (h k) -> p h k", h=8),
                                                          in1=ss[:, 0:8].unsqueeze(2).broadcast_to([128, 8, 64]), op=ALU.mult),
                         reads=[t1.b, ss.b], writes=[o_kk.b])
                    S.op("dve", lambda e: e.tensor_tensor(out=o_kka[:, :], in0=o_kk[:, :], in1=aa[:, :], op=ALU.mult), reads=[o_kk.b, aa.b], writes=[o_kka.b])
                    S.op("dve", lambda e: e.tensor_tensor(out=t2[:, :], in0=aa[:, :], in1=ka[:, :], op=ALU.mult), reads=[aa.b, ka.b], writes=[t2.b])
                    S.op("dve", lambda e: e.tensor_tensor(out=t2[:, :], in0=t2[:, :], in1=omka[:, :], op=ALU.add), reads=[t2.b, omka.b], writes=[t2.b])
                    S.op("dve", lambda e: e.tensor_tensor(out=o_k[:, :], in0=t2[:, :], in1=uk, op=ALU.mult), reads=[t2.b, u.b], writes=[o_k.b])
                    S.dma("pool", g.RW[d, 2:6, rows, :].rearrange("q t c -> t q c"), ob[:, :, :], reads=[ob.b])
        S.run_lanes([(lambda d=d: lane(d)) for d in range(2)])
    S.barrier()


def stage_rwkv_scan(g, l):
    S = g.S
    C = 64
    nch = g.T // C
    nch_c = g.LC // C
    ident64 = g.ident[0:64, 0:64]

    def mm(ps_ap, psb, lhsT, rhs, reads, start=True, stop=True):
        S.op("pe", lambda e: e.matmul(ps_ap, lhsT=lhsT, rhs=rhs, start=start, stop=stop), reads=reads, writes=[psb])

    with ExitStack() as st:
        names = ["Lsb", "eL", "enL", "eLx", "eEnd", "Rt", "Kt", "Kk", "Ak", "Kh", "Ah"]
        LT = []
        for d in range(2):
            LT.append(dict(
                ST=sb(g, st, "sST", [64, 8, 64]),
                qin=[sb(g, st, "sq", [64, 512]) for _ in range(6)],
                v2=sb(g, st, "sv2", [128, 512]),
                W={n: sb(g, st, "s" + n, [64, 512]) for n in names},
                pCT=sb(g, st, "spCT", [64, 8]),
                XT1=sb(g, st, "sXT1", [64, 8, 128]), XT2=sb(g, st, "sXT2", [64, 8, 128]),
                PRs=sb(g, st, "sPRs", [128, 8, 128]),
                Nn=[sb(g, st, "sN", [64, 8, 64]) for _ in range(2)],
                NTt=[sb(g, st, "sNT", [64, 8, 64]) for _ in range(2)],
                Qq=[sb(g, st, "sQ", [64, 8, 64]) for _ in range(2)],
                W1T=sb(g, st, "sW1T", [64, 8, 64]), MV=sb(g, st, "sMV", [64, 8, 64]), W2=sb(g, st, "sW2", [64, 8, 64]),
                Y0=sb(g, st, "sY0", [64, 8, 64]), D0=sb(g, st, "sD0", [64, 8, 64]), U=sb(g, st, "sU", [64, 8, 64]),
                Yo=[sb(g, st, "sYo", [64, 512]) for _ in range(2)], tmp=sb(g, st, "stmp", [64, 8, 64])))
            S.op("dve", lambda e: e.memset(LT[d]["ST"][:, :, :], 0.0), writes=[LT[d]["ST"].b])

        def lane(d):
            lt = LT[d]
            W, pCT, XT1, XT2, PRs, Nn, NTt, Qq = (lt[k] for k in ["W", "pCT", "XT1", "XT2", "PRs", "Nn", "NTt", "Qq"])
            W1T, MV, W2, Y0, D0, U, Yo, tmp = (lt[k] for k in ["W1T", "MV", "W2", "Y0", "D0", "U", "Yo", "tmp"])
            ST = {d: lt["ST"]}
            it = 0
            if d == 0:
                order = list(range(nch))
            else:
                order = list(range(nch_c - 1, -1, -1)) + list(range(nch - 1, nch_c - 1, -1))
            tri = g.tri[0:64, d, :]
            mk = g.mk[:, d, :]
            nmts = g.nmts[0:64, d, :]
            for c in order:
                it += 1
                q = lt["qin"]
                vv = lt["v2"]
                rows = slice(c * C, (c + 1) * C)
                for qi in range(6):
                    S.dma("sp", q[qi][:, :], g.RW[d, RQ_SCAN[qi], rows, :], writes=[q[qi].b])
                S.dma("sp", vv[64:128, :], g.RW[d, RQ_V, rows, :], writes=[vv.b])
                r_, lw, k_, v_, kk, kka = q
                psL = next_psum(g)
                mm(psL[0:64, :], psL.b, tri, lw[:, :], [g.tri.b, lw.b])
                psE = next_psum(g)
                mm(psE[0:64, :], psE.b, g.ones64[0:64, :], lw[:, :], [g.ones64.b, lw.b])
                psP = next_psum(g)
                for h in range(8):
                    mm(psP[0:64, h:h + 1], psP.b, lw[:, h * 64:(h + 1) * 64], g.ones64[0:64, 0:1], [lw.b, g.ones64.b])
                S.op("act", lambda e: e.activation(out=pCT[:, :], in_=psP[0:64, 0:8], func=AF.Exp), reads=[psP.b], writes=[pCT.b])
                S.op("act", lambda e: e.activation(out=W["eL"][:, :], in_=psL[0:64, :], func=AF.Exp), reads=[psL.b], writes=[W["eL"].b])
                S.op("act", lambda e: e.activation(out=W["enL"][:, :], in_=psL[0:64, :], func=AF.Exp, scale=-1.0), reads=[psL.b], writes=[W["enL"].b])
                S.op("dve", lambda e: e.tensor_tensor(out=W["eLx"][:, :], in0=psL[0:64, :], in1=lw[:, :], op=ALU.subtract), reads=[psL.b, lw.b], writes=[W["eLx"].b])
                S.op("act", lambda e: e.activation(out=W["eLx"][:, :], in_=W["eLx"][:, :], func=AF.Exp), reads=[W["eLx"].b], writes=[W["eLx"].b])
                copy_op(g, "dve", W["Lsb"][:, :], psL[0:64, :], [psL.b], [W["Lsb"].b])
                S.op("dve", lambda e: e.tensor_tensor(out=W["eEnd"][:, :], in0=psE[0:64, :], in1=W["Lsb"][:, :], op=ALU.subtract),
                     reads=[psE.b, W["Lsb"].b], writes=[W["eEnd"].b])
                S.op("act", lambda e: e.activation(out=W["eEnd"][:, :], in_=W["eEnd"][:, :], func=AF.Exp), reads=[W["eEnd"].b], writes=[W["eEnd"].b])
                for (o, a, b) in [("Rt", r_, "eL"), ("Kt", kk, "eLx"), ("Kk", k_, "enL"), ("Ak", kka, "enL"), ("Kh", k_, "eEnd"), ("Ah", kka, "eEnd")]:
                    S.op("dve", lambda e: e.tensor_tensor(out=W[o][:, :], in0=a[:, :], in1=W[b][:, :], op=ALU.mult), reads=[a.b, W[b].b], writes=[W[o].b])
                for (XT, n0, n1) in [(XT1, "Ak", "Kk"), (XT2, "Kt", "Rt")]:
                    for hg in range(2):
                        ps = next_psum(g)
                        for hh in range(4):
                            h = hg * 4 + hh
                            for j, nm in enumerate([n0, n1]):
                                S.op("pe", lambda e: e.transpose(ps[0:64, hh * 128 + j * 64:hh * 128 + (j + 1) * 64], W[nm][:, h * 64:(h + 1) * 64], ident64),
                                     reads=[W[nm].b, g.ident.b], writes=[ps.b])
                        copy_op(g, evac_engine(g), XT[:, hg * 4:(hg + 1) * 4, :], ps[0:64, :].rearrange("p (h x) -> p h x", h=4), [ps.b], [XT.b])
                for hg in range(2):
                    ps = next_psum(g)
                    for hh in range(4):
                        h = hg * 4 + hh
                        mm(ps[:, hh * 128:(hh + 1) * 128], ps.b, XT1[:, h, :], XT2[:, h, :], [XT1.b, XT2.b])
                    S.op("dve", lambda e: e.tensor_tensor(out=PRs[:, hg * 4:(hg + 1) * 4, :], in0=ps[:, :].rearrange("p (h x) -> p h x", h=4),
                                                          in1=mk.unsqueeze(1).broadcast_to([128, 4, 128]), op=ALU.mult),
                         reads=[ps.b, g.mk.b], writes=[PRs.b])
                ps = next_psum(g)
                for h in range(8):
                    mm(ps[0:64, h * 64:(h + 1) * 64], ps.b, XT2[:, h, 0:64], XT1[:, h, 0:64], [XT1.b, XT2.b])
                N, NT_, Q = Nn[0], NTt[0], Qq[0]
                S.op("dve", lambda e: e.tensor_tensor(out=NT_[:, :, :], in0=ps[0:64, :].rearrange("p (h x) -> p h x", h=8),
                                                      in1=nmts.unsqueeze(1).broadcast_to([64, 8, 64]), op=ALU.mult),
                     reads=[ps.b, g.nmts.b], writes=[NT_.b])
                S.op("dve", lambda e: e.tensor_scalar(out=N[:, :, :], in0=PRs[0:64, :, 0:64], scalar1=-1.0, scalar2=None, op0=ALU.mult), reads=[PRs.b], writes=[N.b])
                S.op("dve", lambda e: e.tensor_tensor(out=Q[:, :, :], in0=N[:, :, :], in1=ident64.unsqueeze(1).broadcast_to([64, 8, 64]), op=ALU.add),
                     reads=[N.b, g.ident.b], writes=[Q.b])
                cur = 0
                for lev in range(5):
                    N, NT_, Q = Nn[cur], NTt[cur], Qq[cur]
                    N2, NT2, Q2 = Nn[1 - cur], NTt[1 - cur], Qq[1 - cur]
                    psn = next_psum(g)
                    pst = next_psum(g)
                    for h in range(8):
                        if lev < 4:
                            mm(psn[0:64, h * 64:(h + 1) * 64], psn.b, NT_[:, h, :], N[:, h, :], [NT_.b, N.b])
                        mm(pst[0:64, h * 64:(h + 1) * 64], pst.b, N[:, h, :], NT_[:, h, :], [NT_.b, N.b])
                    if lev < 4:
                        copy_op(g, "act", N2[:, :, :], psn[0:64, :].rearrange("p (h x) -> p h x", h=8), [psn.b], [N2.b])
                    copy_op(g, "dve", NT2[:, :, :], pst[0:64, :].rearrange("p (h x) -> p h x", h=8), [pst.b], [NT2.b])
                    psq = next_psum(g)
                    for h in range(8):
                        mm(psq[0:64, h * 64:(h + 1) * 64], psq.b, NT2[:, h, :], Q[:, h, :], [NT2.b, Q.b])
                    S.op("dve", lambda e: e.tensor_tensor(out=Q2[:, :, :], in0=psq[0:64, :].rearrange("p (h x) -> p h x", h=8), in1=Q[:, :, :], op=ALU.add),
                         reads=[psq.b, Q.b], writes=[Q2.b])
                    cur = 1 - cur
                Q = Qq[cur]
                ps1 = next_psum(g)
                ps2 = next_psum(g)
                ps3 = next_psum(g)
                ps4 = next_psum(g)
                for h in range(8):
                    hs = slice(h * 64, (h + 1) * 64)
                    mm(ps1[0:64, hs], ps1.b, W["Kt"][:, hs], Q[:, h, :], [W["Kt"].b, Q.b])
                    mm(ps2[0:64, hs], ps2.b, PRs[64:128, h, 0:64], vv[64:128, hs], [PRs.b, vv.b])
                    mm(ps3[0:64, hs], ps3.b, PRs[64:128, h, 64:128], vv[64:128, hs], [PRs.b, vv.b])
                    mm(ps4[0:64, hs], ps4.b, W["Kh"][:, hs], v_[:, hs], [W["Kh"].b, v_.b])
                copy_op(g, "act", W1T[:, :, :], ps1[0:64, :].rearrange("p (h x) -> p h x", h=8), [ps1.b], [W1T.b])
                copy_op(g, "dve", MV[:, :, :], ps2[0:64, :].rearrange("p (h x) -> p h x", h=8), [ps2.b], [MV.b])
                copy_op(g, "act", Y0[:, :, :], ps3[0:64, :].rearrange("p (h x) -> p h x", h=8), [ps3.b], [Y0.b])
                copy_op(g, "dve", D0[:, :, :], ps4[0:64, :].rearrange("p (h x) -> p h x", h=8), [ps4.b], [D0.b])
                ps5 = next_psum(g)
                for h in range(8):
                    mm(ps5[0:64, h * 64:(h + 1) * 64], ps5.b, Q[:, h, :], MV[:, h, :], [Q.b, MV.b])
                copy_op(g, "act", W2[:, :, :], ps5[0:64, :].rearrange("p (h x) -> p h x", h=8), [ps5.b], [W2.b])
                st_ = ST[d]
                psu = next_psum(g)
                for h in range(8):
                    mm(psu[0:64, h * 64:(h + 1) * 64], psu.b, W1T[:, h, :], st_[:, h, :], [W1T.b, st_.b])
                S.op("dve", lambda e: e.scalar_tensor_tensor(out=U[:, :, :], in0=psu[0:64, :].rearrange("p (h x) -> p h x", h=8), scalar=-1.0,
                                                             in1=W2[:, :, :], op0=ALU.mult, op1=ALU.subtract),
                     reads=[psu.b, W2.b], writes=[U.b])
                psy = next_psum(g)
                for h in range(8):
                    hs = slice(h * 64, (h + 1) * 64)
                    mm(psy[0:64, hs], psy.b, XT2[:, h, 64:128], st_[:, h, :], [XT2.b, st_.b], start=True, stop=False)
                    mm(psy[0:64, hs], psy.b, PRs[0:64, h, 64:128], U[:, h, :], [PRs.b, U.b], start=False, stop=True)
                yo = Yo[it % 2]
                S.op("dve", lambda e: e.tensor_tensor(out=yo[:, :], in0=psy[0:64, :], in1=Y0[:, :, :].rearrange("p h x -> p (h x)"), op=ALU.add),
                     reads=[psy.b, Y0.b], writes=[yo.b])
                S.dma("pool", g.YR[d, rows, :], yo[:, :], reads=[yo.b])
                psd = next_psum(g)
                for h in range(8):
                    hs = slice(h * 64, (h + 1) * 64)
                    mm(psd[0:64, hs], psd.b, W["Ah"][:, hs], U[:, h, :], [W["Ah"].b, U.b])
                S.op("dve", lambda e: e.tensor_tensor(out=tmp[:, :, :], in0=st_[:, :, :], in1=pCT[:, 0:8].unsqueeze(2).broadcast_to([64, 8, 64]), op=ALU.mult),
                     reads=[st_.b, pCT.b], writes=[tmp.b])
                S.op("dve", lambda e: e.tensor_tensor(out=tmp[:, :, :], in0=tmp[:, :, :], in1=D0[:, :, :], op=ALU.add), reads=[tmp.b, D0.b], writes=[tmp.b])
                S.op("dve", lambda e: e.tensor_tensor(out=st_[:, :, :], in0=psd[0:64, :].rearrange("p (h x) -> p h x", h=8), in1=tmp[:, :, :], op=ALU.add),
                     reads=[psd.b, tmp.b], writes=[st_.b])
        S.run_lanes([(lambda d=d: lane(d)) for d in range(2)])
    S.barrier()


def stage_rwkv_out(g, l):
    S = g.S
    with ExitStack() as st:
        lw_ = sb(g, st, "olw", [128, 512])
        lb_ = sb(g, st, "olb", [128, 512])
        rk_ = sb(g, st, "ork", [128, 512])
        bcast_row(g, "sp", lw_[:, :], lw_.b, g.rwkv_lnx_w[l, :])
        bcast_row(g, "sp", lb_[:, :], lb_.b, g.rwkv_lnx_b[l, :])
        bcast_row(g, "sp", rk_[:, :], rk_.b, g.rwkv_r_k[l].rearrange("h k -> (h k)"))
        NL = 4
        LT = [dict(ins=[[sb(g, st, "oin", [128, 512]) for _ in range(5)] for _ in range(2)],
                   t1=sb(g, st, "ot1", [128, 512]), t2=sb(g, st, "ot2", [128, 512]),
                   acc=[sb(g, st, "oacc", [128, 512]) for _ in range(2)],
                   ss=sb(g, st, "oss", [128, 8]), s2=sb(g, st, "os2", [128, 8])) for _ in range(NL)]
        h8 = lambda ap: ap.rearrange("p (h k) -> p h k", h=8)
        b8 = lambda t_: t_[:, 0:8].unsqueeze(2).broadcast_to([128, 8, 64])

        def lane(li):
          lt = LT[li]
          ins, t1, t2, acc, ss, s2 = (lt[k] for k in ["ins", "t1", "t2", "acc", "ss", "s2"])
          it = 0
          for kk_i, ti in enumerate(range(li, g.NT, NL)):
            rows = slice(ti * 128, (ti + 1) * 128)
            a_ = acc[kk_i % 2]
            for d in range(2):
                it += 1
                y, r_, k_, v_, g_ = ins[it % 2]
                S.dma("sp", y[:, :], g.YR[d, rows, :], writes=[y.b])
                S.dma("sp", r_[:, :], g.RW[d, RQ_R, rows, :], writes=[r_.b])
                S.dma("sp", k_[:, :], g.RW[d, RQ_K, rows, :], writes=[k_.b])
                S.dma("sp", v_[:, :], g.RW[d, RQ_V, rows, :], writes=[v_.b])
                S.dma("sp", g_[:, :], g.RG[d, rows, :], writes=[g_.b])
                S.op("dve", lambda e: e.tensor_reduce(out=ss[:, 0:8], in_=h8(y[:, :]), axis=AX.X, op=ALU.add), reads=[y.b], writes=[ss.b])
                S.op("dve", lambda e: e.tensor_scalar(out=ss[:, 0:8], in0=ss[:, 0:8], scalar1=-1.0 / 64, scalar2=None, op0=ALU.mult), reads=[ss.b], writes=[ss.b])
                S.op("dve", lambda e: e.tensor_tensor(out=h8(t1[:, :]), in0=h8(y[:, :]), in1=b8(ss), op=ALU.add), reads=[y.b, ss.b], writes=[t1.b])
                S.op("act", lambda e: e.activation(out=t2[:, :], in_=t1[:, :], func=AF.Square), reads=[t1.b], writes=[t2.b])
                S.op("dve", lambda e: e.tensor_reduce(out=s2[:, 0:8], in_=h8(t2[:, :]), axis=AX.X, op=ALU.add), reads=[t2.b], writes=[s2.b])
                S.op("dve", lambda e: e.tensor_scalar(out=s2[:, 0:8], in0=s2[:, 0:8], scalar1=1.0 / 64, scalar2=LNX_EPS, op0=ALU.mult, op1=ALU.add),
                     reads=[s2.b], writes=[s2.b])
                S.op("act", lambda e: e.activation(out=s2[:, 0:8], in_=s2[:, 0:8], func=AF.Sqrt), reads=[s2.b], writes=[s2.b])
                S.op("dve", lambda e: e.reciprocal(out=s2[:, 0:8], in_=s2[:, 0:8]), reads=[s2.b], writes=[s2.b])
                S.op("dve", lambda e: e.tensor_tensor(out=h8(t1[:, :]), in0=h8(t1[:, :]), in1=b8(s2), op=ALU.mult), reads=[t1.b, s2.b], writes=[t1.b])
                S.op("dve", lambda e: e.tensor_tensor(out=t1[:, :], in0=t1[:, :], in1=lw_[:, :], op=ALU.mult), reads=[t1.b, lw_.b], writes=[t1.b])
                S.op("dve", lambda e: e.tensor_tensor(out=t1[:, :], in0=t1[:, :], in1=lb_[:, :], op=ALU.add), reads=[t1.b, lb_.b], writes=[t1.b])
                S.op("pool", lambda e: e.tensor_tensor(out=t2[:, :], in0=r_[:, :], in1=k_[:, :], op=ALU.mult), reads=[r_.b, k_.b], writes=[t2.b])
                S.op("pool", lambda e: e.tensor_tensor(out=t2[:, :], in0=t2[:, :], in1=rk_[:, :], op=ALU.mult), reads=[t2.b, rk_.b], writes=[t2.b])
                S.op("dve", lambda e: e.tensor_reduce(out=ss[:, 0:8], in_=h8(t2[:, :]), axis=AX.X, op=ALU.add), reads=[t2.b], writes=[ss.b])
                S.op("dve", lambda e: e.tensor_tensor(out=h8(t2[:, :]), in0=h8(v_[:, :]), in1=b8(ss), op=ALU.mult), reads=[v_.b, ss.b], writes=[t2.b])
                S.op("dve", lambda e: e.tensor_tensor(out=t1[:, :], in0=t1[:, :], in1=t2[:, :], op=ALU.add), reads=[t1.b, t2.b], writes=[t1.b])
                if d == 0:
                    S.op("dve", lambda e: e.tensor_tensor(out=a_[:, :], in0=t1[:, :], in1=g_[:, :], op=ALU.mult), reads=[t1.b, g_.b], writes=[a_.b])
                else:
                    S.op("dve", lambda e: e.tensor_tensor(out=t1[:, :], in0=t1[:, :], in1=g_[:, :], op=ALU.mult), reads=[t1.b, g_.b], writes=[t1.b])
                    S.op("dve", lambda e: e.tensor_tensor(out=a_[:, :], in0=a_[:, :], in1=t1[:, :], op=ALU.add), reads=[a_.b, t1.b], writes=[a_.b])
            S.dma("pool", g.YMIX[rows, 1024:1536], a_[:, :], reads=[a_.b])
        S.run_lanes([(lambda li=li: lane(li)) for li in range(NL)])
    S.barrier()


def stage_merge(g, l):
    S = g.S
    with ExitStack() as st:
        gt = [[sb(g, st, "mgt", [128, 512]) for _ in range(4)] for _ in range(2)]
        t1 = sb(g, st, "mt1", [128, 512])
        cnt = [0]

        def epi(pss, o, ti, n0, nb):
            gs = gt[cnt[0] % 2]
            cnt[0] += 1
            for i in range(4):
                c0 = O_GATE + i * D + n0
                S.dma("sp", gs[i][:, 0:nb], g.Z[ti * 128:(ti + 1) * 128, c0:c0 + nb], writes=[gs[i].b])
                S.op("act", lambda e: e.activation(out=gs[i][:, 0:nb], in_=gs[i][:, 0:nb], func=AF.Sigmoid), reads=[gs[i].b], writes=[gs[i].b])
            S.op("dve", lambda e: e.tensor_tensor(out=o[:, 0:nb], in0=pss[0][:, 0:nb], in1=gs[0][:, 0:nb], op=ALU.mult), reads=[pss[0].b, gs[0].b], writes=[o.b])
            for i in range(1, 4):
                S.op("dve", lambda e: e.tensor_tensor(out=t1[:, 0:nb], in0=pss[i][:, 0:nb], in1=gs[i][:, 0:nb], op=ALU.mult), reads=[pss[i].b, gs[i].b], writes=[t1.b])
                S.op("dve", lambda e: e.tensor_tensor(out=o[:, 0:nb], in0=o[:, 0:nb], in1=t1[:, 0:nb], op=ALU.add), reads=[o.b, t1.b], writes=[o.b])
            S.dma("sp", g.MERGED[ti * 128:(ti + 1) * 128, n0:n0 + nb], o[:, 0:nb], reads=[o.b])
        linear(g, g.YMIX, g.w_branch[l].rearrange("b k n -> (b k) n"), None, g.T, D, D, G=12, NB=512, epi=epi,
               segs=[(0, 4), (4, 8), (8, 12), (12, 16)])


def stage_conv(g, l):
    S = g.S
    NT, NTC = g.NT, g.NTC
    NL = 4
    blocks = list(range(0, DFF, 512))
    with ExitStack() as st:
        LT = [dict(cw=sb(g, st, "ccw", [128, 3, 512]), cb=sb(g, st, "ccb", [128, 512]),
                   aw=[sb(g, st, "caw", [128, 512]) for _ in range(4)],
                   bw=[sb(g, st, "cbw", [128, 512]) for _ in range(2)],
                   acc=sb(g, st, "cacc", [128, 512]), t1=sb(g, st, "ct1", [128, 512]),
                   t2=sb(g, st, "ct2", [128, 512]), xb=sb(g, st, "cxb", [128, 512]),
                   out=[sb(g, st, "cout", [128, 512]) for _ in range(2)]) for _ in range(NL)]

        def lane(li):
            lt = LT[li]
            w_, b_, aw, bw, acc, t1 = lt["cw"], lt["cb"], lt["aw"], lt["bw"], lt["acc"], lt["t1"]
            lq = "sp" if li % 2 == 0 else "act"
            for n0 in blocks[li::NL]:
                for j in range(3):
                    bcast_row(g, lq, w_[:, j, :], w_.b, g.ffn_conv_w[l, j, n0:n0 + 512])
                bcast_row(g, lq, b_[:, :], b_.b, g.ffn_conv_b[l, n0:n0 + 512])
                for t0 in range(min(2, NT)):
                    S.dma(lq, aw[t0 % 4][:, :], g.AB[t0 * 128:(t0 + 1) * 128, n0:n0 + 512], writes=[aw[t0 % 4].b])
                for ti in range(NT):
                    r0 = ti * 128
                    if ti + 2 < NT:
                        t2_ = ti + 2
                        S.dma(lq, aw[t2_ % 4][:, :], g.AB[t2_ * 128:(t2_ + 1) * 128, n0:n0 + 512], writes=[aw[t2_ % 4].b])
                    bb = bw[ti % 2]
                    S.dma(lq, bb[:, :], g.AB[r0:r0 + 128, DFF + n0:DFF + n0 + 512], writes=[bb.b])
                    ac_ = aw[ti % 4]
                    has_prev = ti not in (0, NTC)
                    has_next = ti not in (NTC - 1, NT - 1)
                    psP = next_psum(g)
                    S.op("pe", lambda e: e.matmul(psP[:, :], lhsT=g.shm[:, 0, :], rhs=ac_[:, :], start=True, stop=not has_prev),
                         reads=[g.shm.b, ac_.b], writes=[psP.b])
                    if has_prev:
                        ap_ = aw[(ti - 1) % 4]
                        S.op("pe", lambda e: e.matmul(psP[:, :], lhsT=g.shm[:, 1, :], rhs=ap_[:, :], start=False, stop=True),
                             reads=[g.shm.b, ap_.b], writes=[psP.b])
                    psN = next_psum(g)
                    S.op("pe", lambda e: e.matmul(psN[:, :], lhsT=g.shm[:, 2, :], rhs=ac_[:, :], start=True, stop=not has_next),
                         reads=[g.shm.b, ac_.b], writes=[psN.b])
                    if has_next:
                        an_ = aw[(ti + 1) % 4]
                        S.op("pe", lambda e: e.matmul(psN[:, :], lhsT=g.shm[:, 3, :], rhs=an_[:, :], start=False, stop=True),
                             reads=[g.shm.b, an_.b], writes=[psN.b])
                    t2 = lt["t2"]
                    xb = lt["xb"]
                    S.op("dve", lambda e: e.tensor_tensor(out=acc[:, :], in0=psP[:, :], in1=w_[:, 0, :], op=ALU.mult), reads=[psP.b, w_.b], writes=[acc.b])
                    S.op("pool", lambda e: e.tensor_tensor(out=t1[:, :], in0=ac_[:, :], in1=w_[:, 1, :], op=ALU.mult), reads=[ac_.b, w_.b], writes=[t1.b])
                    S.op("pool", lambda e: e.tensor_tensor(out=t1[:, :], in0=t1[:, :], in1=b_[:, :], op=ALU.add), reads=[t1.b, b_.b], writes=[t1.b])
                    S.op("dve", lambda e: e.tensor_tensor(out=t2[:, :], in0=psN[:, :], in1=w_[:, 2, :], op=ALU.mult), reads=[psN.b, w_.b], writes=[t2.b])
                    S.op("dve", lambda e: e.tensor_tensor(out=acc[:, :], in0=acc[:, :], in1=t2[:, :], op=ALU.add), reads=[acc.b, t2.b], writes=[acc.b])
                    S.op("dve", lambda e: e.tensor_tensor(out=acc[:, :], in0=acc[:, :], in1=t1[:, :], op=ALU.add), reads=[acc.b, t1.b], writes=[acc.b])
                    S.op("pool", lambda e: e.tensor_tensor(out=xb[:, :], in0=acc[:, :], in1=bb[:, :], op=ALU.mult), reads=[acc.b, bb.b], writes=[xb.b])
                    S.op("act", lambda e: e.activation(out=t2[:, :], in_=acc[:, :], func=AF.Square), reads=[acc.b], writes=[t2.b])
                    S.op("dve", lambda e: e.tensor_scalar(out=t2[:, :], in0=t2[:, :], scalar1=0.044715, scalar2=1.0, op0=ALU.mult, op1=ALU.add), reads=[t2.b], writes=[t2.b])
                    S.op("dve", lambda e: e.tensor_tensor(out=t2[:, :], in0=t2[:, :], in1=acc[:, :], op=ALU.mult), reads=[t2.b, acc.b], writes=[t2.b])
                    S.op("act", lambda e: e.activation(out=t2[:, :], in_=t2[:, :], func=AF.Sigmoid, scale=1.5957691216057308), reads=[t2.b], writes=[t2.b])
                    o = lt["out"][ti % 2]
                    S.op("dve", lambda e: e.tensor_tensor(out=o[:, :], in0=t2[:, :], in1=xb[:, :], op=ALU.mult), reads=[t2.b, xb.b], writes=[o.b])
                    S.dma("pool", g.GG[r0:r0 + 128, n0:n0 + 512], o[:, :], reads=[o.b])
        S.run_lanes([(lambda li=li: lane(li)) for li in range(NL)])
    S.barrier()


def build(cfg):
    TL, LC, DEPTH = cfg["TL"], cfg["LC"], cfg["DEPTH"]
    T = TL + LC
    nc = bass.Bass("TRN2", target_bir_lowering=False)
    g = Ctx()
    g.nc = nc
    g.uid = 0
    g.ev = 0
    g.pi = 0
    g.T, g.TL, g.LC, g.NT, g.NTC = T, TL, LC, T // 128, LC // 128
    g.S = S = Sync(nc)

    def din(name, shape):
        return nc.dram_tensor(name, list(shape), F32, kind="ExternalInput").ap()

    def dscr(name, shape):
        return nc.dram_tensor(name, list(shape), F32, kind="Internal").ap()

    L = DEPTH
    g.x_in = din("x", [TL, D])
    g.ctx_in = din("ctx", [LC, D])
    g.cvec = din("cvec", [2, D])
    g.ada_w = din("ada_w", [L, D, 6 * D])
    g.ada_b = din("ada_b", [L, 6 * D])
    g.norm1_g = din("norm1_g", [L, D])
    g.norm2_g = din("norm2_g", [L, D])
    g.w_in = din("w_in", [L, D, IN_W])
    g.identd = din("ident", [128, 128])
    g.ga_q_norm = din("ga_q_norm", [L, 128])
    g.ga_k_norm = din("ga_k_norm", [L, 128])
    g.wa_q_norm = din("wa_q_norm", [L, 128])
    g.wa_k_norm = din("wa_k_norm", [L, 128])
    g.wa_sink = din("wa_sink", [L, 4])
    g.mla_cq_norm = din("mla_cq_norm", [L, 384])
    g.mla_ckv_norm = din("mla_ckv_norm", [L, 512])
    g.mla_w_uq = din("mla_w_uq", [L, 384, 768])
    g.mla_w_ukv = din("mla_w_ukv", [L, 512, 1024])
    g.mla_qn_norm = din("mla_qn_norm", [L, 128])
    g.mla_qr_norm = din("mla_qr_norm", [L, 64])
    g.mla_kn_norm = din("mla_kn_norm", [L, 128])
    g.mla_kr_norm = din("mla_kr_norm", [L, 64])
    g.rwkv_mu = din("rwkv_mu", [L, 2, 1984])
    g.rwkv_w0 = din("rwkv_w0", [L, 2, 512])
    g.rwkv_w2 = din("rwkv_w2", [L, 2, 96, 512])
    g.rwkv_a0 = din("rwkv_a0", [L, 2, 512])
    g.rwkv_a2 = din("rwkv_a2", [L, 2, 96, 512])
    g.rwkv_g2 = din("rwkv_g2", [L, 256, 512])
    g.rwkv_k_k = din("rwkv_k_k", [L, 512])
    g.rwkv_k_a = din("rwkv_k_a", [L, 512])
    g.rwkv_r_k = din("rwkv_r_k", [L, 8, 64])
    g.rwkv_lnx_w = din("rwkv_lnx_w", [L, 512])
    g.rwkv_lnx_b = din("rwkv_lnx_b", [L, 512])
    g.w_branch = din("w_branch", [L, 4, 512, D])
    g.w_out = din("w_out", [L, D, D])
    g.ffn_up = din("ffn_up", [L, D, 2 * DFF])
    g.ffn_conv_w = din("ffn_conv_w", [L, 3, DFF])
    g.ffn_conv_b = din("ffn_conv_b", [L, DFF])
    g.ffn_down = din("ffn_down", [L, DFF, D])
    g.trid = din("tri", [64, 2, 64])
    g.mkd = din("mk", [128, 2, 128])
    g.nmtsd = din("nmts", [64, 2, 64])
    g.zrow = din("zrow", [1, NRW])
    g.shmd = din("shm", [128, 4, 128])
    g.ropeA = din("ropeA", [2, TL, 128])
    g.ropeM = din("ropeM", [2, TL, 64])
    g.wmaskd = din("wmask", [128, 2, 128])
    g.y_out = nc.dram_tensor("y", [TL, D], F32, kind="ExternalOutput").ap()
    dbg = cfg.get("debug")
    g.XS = dscr("XS", [T, D])
    g.MODB = dscr("MODB", [2, 128, 6 * D])
    g.Z = dscr("Z", [T, IN_W])
    g.QKT = dscr("QKT", [12, 128, T])
    g.YMIX = dscr("YMIX", [T, D])
    g.RW = dscr("RW", [2, 6, T, 512])
    g.RG = dscr("RG", [2, T, 512])
    g.YR = dscr("YR", [2, T, 512])
    g.MERGED = dscr("MERGED", [T, D])
    g.AB = dscr("AB", [T, 2 * DFF])
    g.GG = dscr("GG", [T, DFF])
    g.QML = dscr("QML", [T, 768])
    g.KVML = dscr("KVML", [T, 1024])
    g.MQN = dscr("MQN", [4, 128, T])
    g.MQR = dscr("MQR", [4, 64, T])
    g.MKN = dscr("MKN", [4, 128, T])
    g.MKR = dscr("MKR", [1, 64, T])
    if dbg:
        g.dbg_z = nc.dram_tensor("dbg_z", [T, IN_W], F32, kind="ExternalOutput").ap()
        g.dbg_y = nc.dram_tensor("dbg_y", [T, D], F32, kind="ExternalOutput").ap()
        g.dbg_qkt = nc.dram_tensor("dbg_qkt", [12, 128, T], F32, kind="ExternalOutput").ap()
        g.dbg_xs = nc.dram_tensor("dbg_xs", [T, D], F32, kind="ExternalOutput").ap()

    with ExitStack() as es:
        g.psums = [Tl(es.enter_context(nc.psum_tensor("ps%d" % i, [128, 512], F32)), "ps%d" % i) for i in range(8)]
        g.ident = sb(g, es, "ident", [128, 128])
        S.dma("sp", g.ident[:, :], g.identd[:, :], writes=[g.ident.b])
        g.tri = sb(g, es, "tri", [64, 2, 64])
        g.mk = sb(g, es, "mk", [128, 2, 128])
        g.nmts = sb(g, es, "nmts", [64, 2, 64])
        g.ones64 = sb(g, es, "ones64", [64, 64])
        S.dma("sp", g.tri[:, :, :], g.trid[:, :, :], writes=[g.tri.b])
        S.dma("sp", g.mk[:, :, :], g.mkd[:, :, :], writes=[g.mk.b])
        S.dma("sp", g.nmts[:, :, :], g.nmtsd[:, :, :], writes=[g.nmts.b])
        S.op("dve", lambda e: e.memset(g.ones64[:, :], 1.0), writes=[g.ones64.b])
        g.shm = sb(g, es, "shm", [128, 4, 128])
        S.dma("sp", g.shm[:, :, :], g.shmd[:, :, :], writes=[g.shm.b])
        g.wmask = sb(g, es, "wmask", [128, 2, 128])
        S.dma("sp", g.wmask[:, :, :], g.wmaskd[:, :, :], writes=[g.wmask.b])
        S.dma("sp", g.XS[0:LC, :], g.ctx_in[:, :])
        S.dma("sp", g.XS[LC:T, :], g.x_in[:, :])
        g.cb = [sb(g, es, "cb", [128, 16, 128]) for _ in range(2)]
        with ExitStack() as st:
            cT = [sb(g, st, "cT", [128, 16]) for _ in range(2)]
            ones = sb(g, st, "ones", [128, 128])
            S.op("dve", lambda e: e.memset(ones[:, :], 1.0), writes=[ones.b])
            for r in range(2):
                S.dma("sp", cT[r][:, :], g.cvec[r, :].rearrange("(kc p) -> p kc", p=128), writes=[cT[r].b], allow_slow_non_contiguous=True)
                S.op("act", lambda e: e.activation(out=cT[r][:, :], in_=cT[r][:, :], func=AF.Silu), reads=[cT[r].b], writes=[cT[r].b])
                for kc in range(16):
                    S.op("dve", lambda e: e.tensor_scalar(out=g.cb[r][:, kc, :], in0=ones[:, :], scalar1=cT[r][:, kc:kc + 1], scalar2=None, op0=ALU.mult),
                         reads=[ones.b, cT[r].b], writes=[g.cb[r].b])
            S.barrier()
        for l in range(L):
            stage_mod(g, l)
            with ExitStack() as st:
                pro = make_norm_pro(g, st, l, g.norm1_g[l, :], D, 0)
                linear(g, g.XS, g.w_in[l], g.Z, T, D, IN_W, G=12, NB=512, pro=pro)
            if cfg.get("stop") == "z":
                break
            stage_attn_prep(g, l)
            stage_attn_gawa(g, l)
            if cfg.get("stop") == "gawa":
                break
            if cfg.get("stop") != "rwkv":
                stage_mla(g, l)
            if cfg.get("stop") == "mla":
                break
            stage_rwkv_prep(g, l)
            stage_rwkv_scan(g, l)
            stage_rwkv_out(g, l)
            if cfg.get("stop") == "rwkv":
                break
            stage_merge(g, l)
            with ExitStack() as st:
                epi = make_resid_epi(g, st, 2 * D)
                linear(g, g.MERGED, g.w_out[l], None, T, D, D, G=12, NB=512, epi=epi)
            if cfg.get("stop") == "attn":
                break
            with ExitStack() as st:
                pro = make_norm_pro(g, st, l, g.norm2_g[l, :], 4 * D, 3 * D)
                linear(g, g.XS, g.ffn_up[l], g.AB, T, D, 2 * DFF, G=12, NB=512, pro=pro)
            stage_conv(g, l)
            with ExitStack() as st:
                epi = make_resid_epi(g, st, 5 * D)
                linear(g, g.GG, g.ffn_down[l], None, T, DFF, D, G=6, NB=256, epi=epi)
        if dbg:
            S.dma("sp", g.dbg_z[:, :], g.Z[:, :])
            S.dma("sp", g.dbg_y[:, :], g.YMIX[:, :])
            S.dma("sp", g.dbg_qkt[:, :, :], g.QKT[:, :, :])
            S.dma("sp", g.dbg_xs[:, :], g.XS[:, :])
        S.dma("sp", g.y_out[:, :], g.XS[LC:T, :])
        S.barrier()
    return nc, g


_CACHE = {}


def rope_table(n_tokens, rot_dim):
    rows = n_tokens // GRID_W
    row = np.repeat(np.arange(rows), GRID_W).astype(np.float32)
    col = np.tile(np.arange(GRID_W), rows).astype(np.float32)
    quarter = rot_dim // 4
    inv_freq = (10000.0 ** (-np.arange(quarter, dtype=np.float32) / quarter)).astype(np.float32)
    ang_r = row[:, None] * inv_freq
    ang_c = col[:, None] * inv_freq
    ang = np.concatenate([ang_r, ang_r, ang_c, ang_c], axis=-1).astype(np.float32)
    sign = np.concatenate([-np.ones(quarter), np.ones(quarter), -np.ones(quarter), np.ones(quarter)]).astype(np.float32)
    return np.stack([np.cos(ang), np.sin(ang) * sign]).astype(np.float32)


def shift_mats():
    m = np.zeros((128, 4, 128), np.float32)
    for k in range(127):
        m[k, 0, k + 1] = 1.0
        m[k + 1, 2, k] = 1.0
    m[127, 1, 0] = 1.0
    m[0, 3, 127] = 1.0
    return m


def const_tables(TL):
    idx = np.arange(128)
    m_lo = (idx[None, :] <= idx[:, None]).astype(np.float32)
    m_hi = (idx[:, None] <= idx[None, :]).astype(np.float32)
    i64 = np.arange(64)
    tri = np.stack([(i64[:, None] <= i64[None, :]), (i64[:, None] >= i64[None, :])]).astype(np.float32)
    strict = tri - np.eye(64, dtype=np.float32)[None]
    half = np.concatenate([strict, tri], axis=2)
    mk = np.concatenate([half, half], axis=1)
    nmts = -np.transpose(strict, (0, 2, 1))
    return {
        "tri": np.ascontiguousarray(np.transpose(tri, (1, 0, 2))),
        "mk": np.ascontiguousarray(np.transpose(mk, (1, 0, 2))),
        "nmts": np.ascontiguousarray(np.transpose(nmts, (1, 0, 2))),
        "zrow": np.zeros((1, NRW), np.float32),
        "shm": shift_mats(),
        "ropeA": rope_table(TL, 128),
        "ropeM": rope_table(TL, 64),
        "wmask": np.ascontiguousarray(np.stack([m_lo, m_hi], axis=1)),
    }


def make_inputs_for_core(inputs, b, L):
    f = lambda a: np.ascontiguousarray(np.asarray(a, dtype=np.float32))
    m = {
        "x": f(inputs["x"][b]),
        "ctx": f(inputs["ctx"][b]),
        "cvec": f(np.stack([np.asarray(inputs["c"][b]), np.asarray(inputs["c_ctx"])])),
        "ident": np.eye(128, dtype=np.float32),
    }
    m.update(const_tables(np.asarray(inputs["x"]).shape[1]))
    for k in ["ada_w", "ada_b", "norm1_g", "norm2_g", "w_in", "ga_q_norm", "ga_k_norm", "wa_q_norm", "wa_k_norm", "wa_sink",
              "mla_cq_norm", "mla_ckv_norm", "mla_w_uq", "mla_w_ukv", "mla_qn_norm", "mla_qr_norm", "mla_kn_norm", "mla_kr_norm",
              "rwkv_mu", "rwkv_w0", "rwkv_w2", "rwkv_a0", "rwkv_a2", "rwkv_g2", "rwkv_k_k", "rwkv_k_a", "rwkv_r_k", "rwkv_lnx_w", "rwkv_lnx_b",
              "w_branch", "w_out", "ffn_up", "ffn_conv_w", "ffn_conv_b", "ffn_down"]:
        m[k] = f(inputs[k][:L])
    return m


def kernel(**inputs):
    x = np.asarray(inputs["x"])
    B, TL, _ = x.shape
    LC = np.asarray(inputs["ctx"]).shape[1]
    L = np.asarray(inputs["ada_w"]).shape[0]
    cfg = {"TL": TL, "LC": LC, "DEPTH": L}
    nc, g = build(cfg)
    n = 8
    in_maps = [make_inputs_for_core(inputs, c % B, L) for c in range(n)]
    res = run_bass_kernel_spmd(nc, in_maps, core_ids=list(range(n)))
    return np.stack([res.results[b]["y"] for b in range(B)]).astype(np.float32)
```

```python
from contextlib import ExitStack
import threading
import numpy as np
import concourse.bass as bass
import concourse.mybir as mybir
from concourse.bass_utils import run_bass_kernel_spmd

F32 = mybir.dt.float32
BF16 = mybir.dt.bfloat16
AF = mybir.ActivationFunctionType
ALU = mybir.AluOpType
AX = mybir.AxisListType

D = 2048
GRID_W = 64
EPS = 1e-6
IN_W = 13376
DFF = 5632
O_GAQ, O_GAK, O_GAV, O_WAQ, O_WAK, O_WAV = 0, 512, 768, 1024, 1536, 1792
O_RKVG, O_ZW, O_ZA, O_CQ, O_CKV, O_KR, O_GATE = 2048, 3840, 4032, 4224, 4608, 5120, 5184
LNX_EPS = 64e-5


class Buf:
    __slots__ = ("name", "w", "r")

    def __init__(self, name=""):
        self.name = name
        self.w = None
        self.r = {}


class Sync:
    MAXC = 30000

    def __init__(self, nc, n_dma_sems=48, same_engine_sync=True):
        self.nc = nc
        self.hw = {"pe": nc.tensor, "dve": nc.vector, "act": nc.scalar, "pool": nc.gpsimd, "sp": nc.sync}
        self.gen = {k: 0 for k in self.hw}
        self.sem = {k: nc.alloc_semaphore("sem_%s_0" % k) for k in self.hw}
        self.count = {k: 0 for k in self.hw}
        self.seen = {k: {} for k in self.hw}
        self.dma_sems = [nc.alloc_semaphore("dsem%d" % i) for i in range(n_dma_sems)]
        self.dma_uses = [0] * n_dma_sems
        self.dma_next = 0
        self.same_engine_sync = same_engine_sync
        self.latest = {}
        self._lane = None

    def run_lanes(self, fns):
        n = len(fns)
        if n == 1:
            fns[0]()
            return
        cv = threading.Condition()
        st = {"turn": 0, "alive": [True] * n, "err": None}
        ids = {}

        def nxt(i):
            for k in range(1, n + 1):
                j = (i + k) % n
                if st["alive"][j]:
                    return j
            return -1

        def worker(i):
            ids[threading.get_ident()] = i
            with cv:
                while st["turn"] != i:
                    cv.wait()
            try:
                fns[i]()
            except BaseException as e:
                st["err"] = e
            with cv:
                st["alive"][i] = False
                st["turn"] = nxt(i)
                cv.notify_all()

        self._lane = (cv, st, ids, nxt)
        self.lane_pi = {}
        threads = [threading.Thread(target=worker, args=(i,)) for i in range(n)]
        for t in threads:
            t.start()
        for t in threads:
            t.join()
        self._lane = None
        if st["err"] is not None:
            raise st["err"]

    def lane_index(self):
        if self._lane is None:
            return None
        i = self._lane[2].get(threading.get_ident())
        if i is None:
            return None
        return i, len(self._lane[1]["alive"])

    def lane_yield(self):
        if self._lane is None:
            return
        cv, st, ids, nxt = self._lane
        i = ids.get(threading.get_ident())
        if i is None:
            return
        with cv:
            j = nxt(i)
            if j == i or j < 0:
                return
            st["turn"] = j
            cv.notify_all()
            while st["turn"] != i:
                cv.wait()

    def _wait(self, eng, dep):
        key, sem, val = dep
        if self.seen[eng].get(key, 0) >= val:
            return
        self.hw[eng].wait_ge(sem, val)
        self.seen[eng][key] = val

    @staticmethod
    def _add(deps, d):
        if d is not None and deps.get(d[0], (0, 0, 0))[2] < d[2]:
            deps[d[0]] = d

    def _deps(self, reads, writes):
        deps = {}
        for b in reads:
            self._add(deps, b.w)
        for b in writes:
            self._add(deps, b.w)
            for d in b.r.values():
                self._add(deps, d)
        return deps

    def _mark(self, dep, reads, writes):
        for b in reads:
            b.r[dep[0]] = dep
        for b in writes:
            b.w = dep
            b.r = {}
        self.latest[dep[0]] = dep

    def op(self, eng, fn, reads=(), writes=()):
        for key, d in self._deps(reads, writes).items():
            if isinstance(key, tuple) and key[0] == eng and (eng == "pe" or not self.same_engine_sync):
                continue
            self._wait(eng, d)
        ins = fn(self.hw[eng])
        self.count[eng] += 1
        ins.then_inc(self.sem[eng], 1)
        self._mark(((eng, self.gen[eng]), self.sem[eng], self.count[eng]), reads, writes)
        if self.count[eng] >= self.MAXC:
            self.gen[eng] += 1
            self.sem[eng] = self.nc.alloc_semaphore("sem_%s_%d" % (eng, self.gen[eng]))
            self.count[eng] = 0
        self.lane_yield()
        return ins

    def dma(self, q, out, in_, reads=(), writes=(), **kw):
        i = self.dma_next
        self.dma_next = (self.dma_next + 1) % len(self.dma_sems)
        sem = self.dma_sems[i]
        key = "d%d" % i
        if self.dma_uses[i] > 0:
            self._wait(q, (key, sem, 16 * self.dma_uses[i]))
        for k, d in self._deps(reads, writes).items():
            self._wait(q, d)
        self.dma_uses[i] += 1
        ins = self.hw[q].dma_start(out=out, in_=in_, **kw)
        ins.then_inc(sem, 16)
        self._mark((key, sem, 16 * self.dma_uses[i]), reads, writes)
        self.lane_yield()
        return ins

    def barrier(self):
        for eng in self.hw:
            for dep in list(self.latest.values()):
                key = dep[0]
                if isinstance(key, tuple) and key[0] == "pe" and eng == "pe":
                    continue
                self._wait(eng, dep)


class Tl:
    def __init__(self, t, name):
        self.t = t
        self.b = Buf(name)

    def __getitem__(self, idx):
        return self.t[idx]


class Ctx:
    pass


def sb(g, st, name, shape, dtype=F32):
    g.uid += 1
    nm = "%s_%d" % (name, g.uid)
    return Tl(st.enter_context(g.nc.sbuf_tensor(nm, list(shape), dtype)), nm)


def evac_engine(g):
    g.ev += 1
    return "dve" if g.ev % 2 else "act"


def copy_op(g, eng, out, in_, reads, writes):
    if eng == "act":
        return g.S.op("act", lambda e: e.copy(out=out, in_=in_), reads=reads, writes=writes)
    return g.S.op(eng, lambda e: e.tensor_copy(out=out, in_=in_), reads=reads, writes=writes)


def next_psum(g):
    lane = g.S.lane_index()
    if lane is None:
        g.pi = (g.pi + 1) % len(g.psums)
        return g.psums[g.pi]
    i, n = lane
    lo, hi = i * 8 // n, (i + 1) * 8 // n
    k = g.S.lane_pi.get(i, 0)
    g.S.lane_pi[i] = k + 1
    return g.psums[lo + k % (hi - lo)]


def linear(g, x, w, y, T, K, N, G=8, NB=512, pro=None, epi=None, segs=None, wlist=None):
    S, nc = g.S, g.nc
    KC = K // 128
    assert K % 128 == 0 and T % 128 == 0
    NT = T // 128
    XW = min(K, 2048)
    if segs is None:
        segs = [(0, KC)]
    with ExitStack() as st:
        xt = sb(g, st, "xt", [128, KC, G * 128], BF16)
        xin = [sb(g, st, "xin", [128, XW]) for _ in range(2)]
        wt = [sb(g, st, "wt", [128, KC, NB], BF16) for _ in range(3)]
        ot = [sb(g, st, "ot", [128, NB]) for _ in range(3)]
        nxin = nw = no = 0
        for g0 in range(0, NT, G):
            gn = min(G, NT - g0)
            for gi in range(gn):
                ti = g0 + gi
                for c0 in range(0, K, XW):
                    cw = min(XW, K - c0)
                    xi = xin[nxin % 2]
                    nxin += 1
                    S.dma("sp", xi[:, 0:cw], x[ti * 128:(ti + 1) * 128, c0:c0 + cw], writes=[xi.b])
                    if pro is not None:
                        pro(xi, ti, cw)
                    for k4 in range(0, cw // 128, 4):
                        ps = next_psum(g)
                        nk = min(4, cw // 128 - k4)
                        for j in range(nk):
                            kk = k4 + j
                            S.op("pe", lambda e: e.transpose(ps[:, j * 128:(j + 1) * 128], xi[:, kk * 128:(kk + 1) * 128], g.ident[:]),
                                 reads=[xi.b, g.ident.b], writes=[ps.b])
                        kc0 = c0 // 128 + k4
                        copy_op(g, evac_engine(g), xt[:, kc0:kc0 + nk, gi * 128:(gi + 1) * 128],
                                ps[:, 0:nk * 128].rearrange("p (k t) -> p k t", k=nk), [ps.b], [xt.b])
            for n0 in range(0, N, NB):
                nb = min(NB, N - n0)
                wi = wt[nw % 3]
                nw += 1
                S.dma("pool", wi[:, :, 0:nb], w.rearrange("(kc p) n -> p kc n", p=128)[:, :, n0:n0 + nb], writes=[wi.b])
                for gi in range(gn):
                    ti = g0 + gi
                    pss = []
                    for (k0, k1) in segs:
                        ps = next_psum(g)
                        pss.append(ps)
                        for kc in range(k0, k1):
                            S.op("pe", lambda e: e.matmul(ps[:, 0:nb], lhsT=xt[:, kc, gi * 128:(gi + 1) * 128], rhs=wi[:, kc, 0:nb],
                                                          start=(kc == k0), stop=(kc == k1 - 1)),
                                 reads=[xt.b, wi.b], writes=[ps.b])
                    o = ot[no % 3]
                    no += 1
                    if epi is None:
                        copy_op(g, evac_engine(g), o[:, 0:nb], pss[0][:, 0:nb], [pss[0].b], [o.b])
                        S.dma("sp", y[ti * 128:(ti + 1) * 128, n0:n0 + nb], o[:, 0:nb], reads=[o.b])
                    else:
                        epi(pss, o, ti, n0, nb)
    S.barrier()


def rms_rstd(g, st_tiles, x_ap, n, width, reads_b, sq, ss, rstd, eps=EPS):
    S = g.S
    S.op("act", lambda e: e.activation(out=sq[:, 0:n * width].rearrange("p (n w) -> p n w", n=n), in_=x_ap, func=AF.Square),
         reads=[reads_b], writes=[sq.b])
    S.op("dve", lambda e: e.tensor_reduce(out=ss[:, 0:n], in_=sq[:, 0:n * width].rearrange("p (n w) -> p n w", n=n), axis=AX.X, op=ALU.add),
         reads=[sq.b], writes=[ss.b])
    S.op("dve", lambda e: e.tensor_scalar(out=ss[:, 0:n], in0=ss[:, 0:n], scalar1=1.0 / width, scalar2=eps, op0=ALU.mult, op1=ALU.add),
         reads=[ss.b], writes=[ss.b])
    S.op("act", lambda e: e.activation(out=ss[:, 0:n], in_=ss[:, 0:n], func=AF.Sqrt), reads=[ss.b], writes=[ss.b])
    S.op("dve", lambda e: e.reciprocal(out=rstd[:, 0:n], in_=ss[:, 0:n]), reads=[ss.b], writes=[rstd.b])


def bcast_row(g, q, tile_ap, buf, dram_row_ap):
    g.S.dma(q, tile_ap, dram_row_ap.partition_broadcast(128), writes=[buf])


def stage_mod(g, l):
    S, nc = g.S, g.nc
    with ExitStack() as st:
        wt = [sb(g, st, "mw", [128, 16, 512]) for _ in range(2)]
        bt = [sb(g, st, "mb", [128, 512]) for _ in range(2)]
        ot = [sb(g, st, "mo", [128, 512]) for _ in range(3)]
        no = 0
        for bi, n0 in enumerate(range(0, 6 * D, 512)):
            wi = wt[bi % 2]
            bb = bt[bi % 2]
            S.dma("sp", wi[:, :, :], g.ada_w[l].rearrange("(kc p) n -> p kc n", p=128)[:, :, n0:n0 + 512], writes=[wi.b])
            bcast_row(g, "sp", bb[:, :], bb.b, g.ada_b[l, n0:n0 + 512])
            for r in range(2):
                ps = next_psum(g)
                for kc in range(16):
                    S.op("pe", lambda e: e.matmul(ps[:, :], lhsT=g.cb[r][:, kc, :], rhs=wi[:, kc, :], start=(kc == 0), stop=(kc == 15)),
                         reads=[g.cb[r].b, wi.b], writes=[ps.b])
                o = ot[no % 3]
                no += 1
                S.op("dve", lambda e: e.tensor_tensor(out=o[:, :], in0=ps[:, :], in1=bb[:, :], op=ALU.add), reads=[ps.b, bb.b], writes=[o.b])
                S.dma("pool", g.MODB[r, :, n0:n0 + 512], o[:, :], reads=[o.b])
    S.barrier()


def make_norm_pro(g, st, l, gain_dram, sc_off, sh_off):
    S = g.S
    A = [sb(g, st, "nA", [128, D]) for _ in range(2)]
    B = [sb(g, st, "nB", [128, D]) for _ in range(2)]
    sq = sb(g, st, "nsq", [128, D])
    ss = sb(g, st, "nss", [128, 1])
    rstd = sb(g, st, "nrs", [128, 1])
    with ExitStack() as st2:
        gt = sb(g, st2, "ng", [128, D])
        bcast_row(g, "sp", gt[:, :], gt.b, gain_dram)
        for r in range(2):
            S.dma("sp", A[r][:, :], g.MODB[r, :, sc_off:sc_off + D], writes=[A[r].b])
            S.dma("sp", B[r][:, :], g.MODB[r, :, sh_off:sh_off + D], writes=[B[r].b])
            S.op("dve", lambda e: e.scalar_tensor_tensor(out=A[r][:, :], in0=A[r][:, :], scalar=1.0, in1=gt[:, :], op0=ALU.add, op1=ALU.mult),
                 reads=[A[r].b, gt.b], writes=[A[r].b])
        S.barrier()

    def pro(xi, ti, cw):
        r = 1 if ti < g.NTC else 0
        rms_rstd(g, None, xi[:, 0:D].rearrange("p (n w) -> p n w", n=1), 1, D, xi.b, sq, ss, rstd)
        S.op("dve", lambda e: e.scalar_tensor_tensor(out=xi[:, 0:D], in0=xi[:, 0:D], scalar=rstd[:, 0:1], in1=A[r][:, :], op0=ALU.mult, op1=ALU.mult),
             reads=[xi.b, rstd.b, A[r].b], writes=[xi.b])
        S.op("dve", lambda e: e.tensor_tensor(out=xi[:, 0:D], in0=xi[:, 0:D], in1=B[r][:, :], op=ALU.add), reads=[xi.b, B[r].b], writes=[xi.b])
    return pro


def make_resid_epi(g, st, gate_off):
    S = g.S
    gts = [sb(g, st, "rg", [128, D]) for _ in range(2)]
    for r in range(2):
        S.dma("sp", gts[r][:, :], g.MODB[r, :, gate_off:gate_off + D], writes=[gts[r].b])
    xb = [sb(g, st, "rx", [128, 512]) for _ in range(3)]
    cnt = [0]

    def epi(pss, o, ti, n0, nb):
        r = 1 if ti < g.NTC else 0
        x = xb[cnt[0] % 3]
        cnt[0] += 1
        S.dma("act", x[:, 0:nb], g.XS[ti * 128:(ti + 1) * 128, n0:n0 + nb], writes=[x.b])
        S.op("dve", lambda e: e.tensor_tensor(out=o[:, 0:nb], in0=pss[0][:, 0:nb], in1=gts[r][:, n0:n0 + nb], op=ALU.mult),
             reads=[pss[0].b, gts[r].b], writes=[o.b])
        S.op("dve", lambda e: e.tensor_tensor(out=o[:, 0:nb], in0=o[:, 0:nb], in1=x[:, 0:nb], op=ALU.add), reads=[o.b, x.b], writes=[o.b])
        S.dma("sp", g.XS[ti * 128:(ti + 1) * 128, n0:n0 + nb], o[:, 0:nb], reads=[o.b])
    return epi


def norm_rope(g, src, src_b, n, w, gain_ap, gain_b, out, out_b, tmp, sq, ss, rstd, rope=None):
    S = g.S
    rms_rstd(g, None, src, n, w, src_b, sq, ss, rstd)
    dst = out if rope is None else tmp[:, 0:n * w].rearrange("p (n w) -> p n w", n=n)
    dst_b = out_b if rope is None else tmp.b
    S.op("dve", lambda e: e.tensor_tensor(out=dst, in0=src, in1=rstd[:, 0:n].unsqueeze(2).broadcast_to([128, n, w]), op=ALU.mult),
         reads=[src_b, rstd.b], writes=[dst_b])
    S.op("dve", lambda e: e.tensor_tensor(out=dst, in0=dst, in1=gain_ap, op=ALU.mult), reads=[dst_b, gain_b], writes=[dst_b])
    if rope is None:
        return
    cos, ssin, blk = rope
    nh = w // (2 * blk)
    xv = dst.rearrange("p n (h b k) -> p n h b k", h=nh, b=2, k=blk)
    sv = ssin[:, 0:w].rearrange("p (h b k) -> p h b k", h=nh, b=2, k=blk)
    sw = sq[:, 0:n * w].rearrange("p (n h b k) -> p n h b k", n=n, h=nh, b=2, k=blk)
    for n_i in range(n):
        for b in range(2):
            S.op("dve", lambda e: e.tensor_tensor(out=sw[:, n_i, :, b, :], in0=xv[:, n_i, :, 1 - b, :], in1=sv[:, :, b, :], op=ALU.mult),
                 reads=[dst_b, ssin.b], writes=[sq.b])
    S.op("dve", lambda e: e.tensor_tensor(out=dst, in0=dst, in1=cos[:, 0:w].unsqueeze(1).broadcast_to([128, n, w]), op=ALU.mult),
         reads=[dst_b, cos.b], writes=[dst_b])
    S.op("dve", lambda e: e.tensor_tensor(out=out, in0=dst, in1=sq[:, 0:n * w].rearrange("p (n w) -> p n w", n=n), op=ALU.add),
         reads=[dst_b, sq.b], writes=[out_b])


def transpose_slots(g, src_tl, slots, w, dst_tl, dst_slot0):
    S = g.S
    for i0 in range(0, len(slots), 4):
        grp = slots[i0:i0 + 4]
        ps = next_psum(g)
        for j, s_ in enumerate(grp):
            S.op("pe", lambda e: e.transpose(ps[0:w, j * 128:(j + 1) * 128], src_tl[:, s_, 0:w], g.ident[:]),
                 reads=[src_tl.b, g.ident.b], writes=[ps.b])
        copy_op(g, evac_engine(g), dst_tl[0:w, dst_slot0 + i0:dst_slot0 + i0 + len(grp), :],
                ps[0:w, 0:len(grp) * 128].rearrange("p (k t) -> p k t", k=len(grp)), [ps.b], [dst_tl.b])


def stage_attn_prep(g, l):
    S = g.S
    with ExitStack() as st:
        gain = sb(g, st, "apg", [128, 2, 6, 128])
        for gi, (qn, kn) in enumerate([(g.ga_q_norm, g.ga_k_norm), (g.wa_q_norm, g.wa_k_norm)]):
            for s_ in range(6):
                bcast_row(g, "sp", gain[:, gi, s_, :], gain.b, (qn if s_ < 4 else kn)[l, :])
        NL = 3
        LT = [dict(z=sb(g, st, "apz", [128, 16, 128]), x=sb(g, st, "apx", [128, 12, 128]), t_=sb(g, st, "apt", [128, 12, 128]),
                   cs=sb(g, st, "apc", [128, 128]), sn=sb(g, st, "aps", [128, 128]),
                   tmp=sb(g, st, "aptmp", [128, 6 * 128]), sq=sb(g, st, "apsq", [128, 6 * 128]),
                   ss=sb(g, st, "apss", [128, 8]), rstd=sb(g, st, "aprs", [128, 8])) for _ in range(NL)]

        def lane(li):
          lt = LT[li]
          tmp, sq, ss, rstd = lt["tmp"], lt["sq"], lt["ss"], lt["rstd"]
          cs = [lt["cs"], lt["cs"]]
          sn = [lt["sn"], lt["sn"]]
          for ti in range(li, g.NT, NL):
            z = lt["z"]
            x = lt["x"]
            t_ = lt["t_"]
            S.dma("sp", z[:, :, :], g.Z[ti * 128:(ti + 1) * 128, 0:2048].rearrange("p (s d) -> p s d", s=16), writes=[z.b])
            rope = None
            if ti >= g.NTC:
                c_, s_t = cs[ti % 2], sn[ti % 2]
                p0 = (ti - g.NTC) * 128
                S.dma("sp", c_[:, :], g.ropeA[0, p0:p0 + 128, :], writes=[c_.b])
                S.dma("sp", s_t[:, :], g.ropeA[1, p0:p0 + 128, :], writes=[s_t.b])
                rope = (c_, s_t, 32)
            for gi, base in enumerate([0, 8]):
                norm_rope(g, z[:, base:base + 6, :], z.b, 6, 128, gain[:, gi, :, :], gain.b,
                          x[:, gi * 6:(gi + 1) * 6, :], x.b, tmp, sq, ss, rstd, rope=rope)
            transpose_slots(g, x, list(range(12)), 128, t_, 0)
            S.dma("pool", g.QKT[:, :, ti * 128:(ti + 1) * 128].rearrange("s d t -> d s t"), t_[:, :, :], reads=[t_.b])
        S.run_lanes([(lambda li=li: lane(li)) for li in range(NL)])
    S.barrier()


def attention(g, heads, scale, sink_tl=None):
    S = g.S
    T, NT = g.T, g.NT
    po = g.psums[0:4]
    sps = g.psums[4:8]
    with ExitStack() as st:
        kts = [sb(g, st, "akt", [128, T], BF16) for _ in range(2)]
        vt = sb(g, st, "avt", [128, NT, 129], BF16)
        qts = [[sb(g, st, "aqt", [128, 512], BF16) for _ in range(2)] for _ in range(3)]
        pts = [sb(g, st, "apt", [128, 512], BF16) for _ in range(4)]
        yts = [sb(g, st, "ayt", [128, 128]) for _ in range(3)]
        den = sb(g, st, "aden", [128, 4])
        S.op("dve", lambda e: e.memset(vt[:, :, 128:129], 1.0), writes=[vt.b])
        nq = npt = ny = nsp = 0
        for hd in heads:
            for pi_, (kd_ap, kd) in enumerate(hd["kparts"]):
                S.dma("pool", kts[pi_][0:kd, :], kd_ap, writes=[kts[pi_].b])
            S.dma("pool", vt[:, :, 0:128], hd["v"].rearrange("(n p) d -> p n d", p=128), writes=[vt.b])
            for qh, qparts in enumerate(hd["qparts"]):
                for (q0, qlen, keys) in hd["blocks"]:
                    nqb = qlen // 128
                    qt = qts[nq % 3]
                    nq += 1
                    for pi_, (qd_ap, kd) in enumerate(qparts):
                        S.dma("pool", qt[pi_][0:kd, 0:qlen], qd_ap[:, q0:q0 + qlen], writes=[qt[pi_].b])
                    npart = len(qparts)

                    def scores(kt_):
                        nonlocal nsp
                        ps_ = sps[nsp % 4]
                        nsp += 1
                        for pi_, (qd_ap, kd) in enumerate(qparts):
                            S.op("pe", lambda e: e.matmul(ps_[:, 0:qlen], lhsT=kts[pi_][0:kd, kt_ * 128:(kt_ + 1) * 128], rhs=qt[pi_][0:kd, 0:qlen],
                                                          start=(pi_ == 0), stop=(pi_ == npart - 1)),
                                 reads=[kts[pi_].b, qt[pi_].b], writes=[ps_.b])
                        return ps_
                    ahead = [scores(keys[j][0]) for j in range(min(2, len(keys)))]
                    for idx, (kt, mask) in enumerate(keys):
                        ps = ahead.pop(0)
                        if idx + 2 < len(keys):
                            ahead.append(scores(keys[idx + 2][0]))
                        pt = pts[npt % 4]
                        npt += 1
                        S.op("act", lambda e: e.activation(out=pt[:, 0:qlen], in_=ps[:, 0:qlen], func=AF.Exp, scale=scale),
                             reads=[ps.b], writes=[pt.b])
                        if mask is not None:
                            S.op("dve", lambda e: e.tensor_tensor(out=pt[:, 0:qlen], in0=pt[:, 0:qlen], in1=g.wmask[:, mask, :], op=ALU.mult),
                                 reads=[pt.b, g.wmask.b], writes=[pt.b])
                        for qb in range(nqb):
                            S.op("pe", lambda e: e.matmul(po[qb][:, 0:129], lhsT=pt[:, qb * 128:(qb + 1) * 128], rhs=vt[:, kt, :],
                                                          start=(idx == 0), stop=(idx == len(keys) - 1)),
                                 reads=[pt.b, vt.b], writes=[po[qb].b])
                    for qb in range(nqb):
                        y = yts[ny % 3]
                        ny += 1
                        if hd.get("sink_idx") is not None:
                            si = hd["sink_idx"][qh]
                            S.op("dve", lambda e: e.tensor_tensor(out=den[:, 0:1], in0=po[qb][:, 128:129], in1=sink_tl[:, si:si + 1], op=ALU.add),
                                 reads=[po[qb].b, sink_tl.b], writes=[den.b])
                            S.op("dve", lambda e: e.reciprocal(out=den[:, 0:1], in_=den[:, 0:1]), reads=[den.b], writes=[den.b])
                        else:
                            S.op("dve", lambda e: e.reciprocal(out=den[:, 0:1], in_=po[qb][:, 128:129]), reads=[po[qb].b], writes=[den.b])
                        S.op("dve", lambda e: e.tensor_scalar(out=y[:, :], in0=po[qb][:, 0:128], scalar1=den[:, 0:1], scalar2=None, op0=ALU.mult),
                             reads=[po[qb].b, den.b], writes=[y.b])
                        r0 = q0 + qb * 128
                        yc = hd["ycols"][qh]
                        S.dma("sp", g.YMIX[r0:r0 + 128, yc:yc + 128], y[:, :], reads=[y.b])
    S.barrier()


def dense_blocks(g):
    blocks = [(0, g.LC, [(kt, None) for kt in range(g.NTC)])]
    for q0 in range(g.LC, g.T, 512):
        blocks.append((q0, min(512, g.T - q0), [(kt, None) for kt in range(g.NT)]))
    return blocks


def window_blocks(g):
    blocks = [(0, g.LC, [(kt, None) for kt in range(g.NTC)])]
    nblk = g.TL // 128
    for n in range(nblk):
        keys = [(kt, None) for kt in range(g.NTC)]
        if n > 0:
            keys.append((g.NTC + n - 1, 0))
        keys.append((g.NTC + n, None))
        if n < nblk - 1:
            keys.append((g.NTC + n + 1, 1))
        blocks.append((g.LC + n * 128, 128, keys))
    return blocks


def stage_attn_gawa(g, l):
    S = g.S
    heads = []
    for hk in range(2):
        heads.append(dict(kparts=[(g.QKT[4 + hk], 128)], qparts=[[(g.QKT[2 * hk + j], 128)] for j in range(2)],
                          v=g.Z[:, O_GAV + hk * 128:O_GAV + (hk + 1) * 128], ycols=[(2 * hk + j) * 128 for j in range(2)],
                          blocks=dense_blocks(g)))
    attention(g, heads, 128 ** -0.5)
    with ExitStack() as st:
        sink = sb(g, st, "sink", [128, 4])
        bcast_row(g, "sp", sink[:, :], sink.b, g.wa_sink[l, :])
        S.op("act", lambda e: e.activation(out=sink[:, :], in_=sink[:, :], func=AF.Exp), reads=[sink.b], writes=[sink.b])
        heads = []
        for hk in range(2):
            heads.append(dict(kparts=[(g.QKT[10 + hk], 128)], qparts=[[(g.QKT[6 + 2 * hk + j], 128)] for j in range(2)],
                              v=g.Z[:, O_WAV + hk * 128:O_WAV + (hk + 1) * 128], ycols=[512 + (2 * hk + j) * 128 for j in range(2)],
                              blocks=window_blocks(g), sink_idx=[2 * hk, 2 * hk + 1]))
        attention(g, heads, 128 ** -0.5, sink_tl=sink)


def make_rms_pro(g, st, gain_dram, K):
    S = g.S
    gt = sb(g, st, "pg", [128, K])
    sq = sb(g, st, "psq", [128, K])
    ss = sb(g, st, "pss", [128, 1])
    rstd = sb(g, st, "prs", [128, 1])
    bcast_row(g, "sp", gt[:, :], gt.b, gain_dram)

    def pro(xi, ti, cw):
        rms_rstd(g, None, xi[:, 0:K].rearrange("p (n w) -> p n w", n=1), 1, K, xi.b, sq, ss, rstd)
        S.op("dve", lambda e: e.scalar_tensor_tensor(out=xi[:, 0:K], in0=xi[:, 0:K], scalar=rstd[:, 0:1], in1=gt[:, :], op0=ALU.mult, op1=ALU.mult),
             reads=[xi.b, rstd.b, gt.b], writes=[xi.b])
    return pro


def stage_mla(g, l):
    S = g.S
    T = g.T
    with ExitStack() as st:
        pro = make_rms_pro(g, st, g.mla_cq_norm[l, :], 384)
        linear(g, g.Z[:, O_CQ:O_CQ + 384], g.mla_w_uq[l], g.QML, T, 384, 768, G=17, NB=512, pro=pro)
    with ExitStack() as st:
        pro = make_rms_pro(g, st, g.mla_ckv_norm[l, :], 512)
        linear(g, g.Z[:, O_CKV:O_CKV + 512], g.mla_w_ukv[l], g.KVML, T, 512, 1024, G=17, NB=512, pro=pro)
    with ExitStack() as st:
        gq_n = sb(g, st, "mgqn", [128, 128])
        gq_r = sb(g, st, "mgqr", [128, 64])
        gk_n = sb(g, st, "mgkn", [128, 128])
        gk_r = sb(g, st, "mgkr", [128, 64])
        bcast_row(g, "sp", gq_n[:, :], gq_n.b, g.mla_qn_norm[l, :])
        bcast_row(g, "sp", gq_r[:, :], gq_r.b, g.mla_qr_norm[l, :])
        bcast_row(g, "sp", gk_n[:, :], gk_n.b, g.mla_kn_norm[l, :])
        bcast_row(g, "sp", gk_r[:, :], gk_r.b, g.mla_kr_norm[l, :])
        NL = 3
        qin = [sb(g, st, "mq", [128, 4, 192]) for _ in range(NL)]
        kvin = [sb(g, st, "mkv", [128, 4, 256]) for _ in range(NL)]
        krin = [sb(g, st, "mkr", [128, 1, 64]) for _ in range(NL)]
        xqn = [sb(g, st, "xqn", [128, 4, 128]) for _ in range(NL)]
        xqr = [sb(g, st, "xqr", [128, 4, 64]) for _ in range(NL)]
        xkn = [sb(g, st, "xkn", [128, 4, 128]) for _ in range(NL)]
        xkr = [sb(g, st, "xkr", [128, 1, 64]) for _ in range(NL)]
        tqn = [sb(g, st, "tqn", [128, 4, 128]) for _ in range(NL)]
        tqr = [sb(g, st, "tqr", [64, 4, 128]) for _ in range(NL)]
        tkn = [sb(g, st, "tkn", [128, 4, 128]) for _ in range(NL)]
        tkr = [sb(g, st, "tkr", [64, 1, 128]) for _ in range(NL)]
        cs = [sb(g, st, "mc", [128, 64]) for _ in range(NL)]
        sn = [sb(g, st, "ms", [128, 64]) for _ in range(NL)]
        tmps = [sb(g, st, "mtmp", [128, 512]) for _ in range(NL)]
        sqs = [sb(g, st, "msq", [128, 512]) for _ in range(NL)]
        sss = [sb(g, st, "mss", [128, 8]) for _ in range(NL)]
        rstds = [sb(g, st, "mrs", [128, 8]) for _ in range(NL)]

        def lane(i):
          tmp, sq, ss, rstd = tmps[i], sqs[i], sss[i], rstds[i]
          for ti in range(i, g.NT, NL):
            rows = slice(ti * 128, (ti + 1) * 128)
            S.dma("sp", qin[i][:, :, :], g.QML[rows, :].rearrange("p (h d) -> p h d", h=4), writes=[qin[i].b])
            S.dma("sp", kvin[i][:, :, :], g.KVML[rows, :].rearrange("p (h d) -> p h d", h=4), writes=[kvin[i].b])
            S.dma("sp", krin[i][:, 0, :], g.Z[rows, O_KR:O_KR + 64], writes=[krin[i].b])
            rope = None
            if ti >= g.NTC:
                p0 = (ti - g.NTC) * 128
                S.dma("sp", cs[i][:, :], g.ropeM[0, p0:p0 + 128, :], writes=[cs[i].b])
                S.dma("sp", sn[i][:, :], g.ropeM[1, p0:p0 + 128, :], writes=[sn[i].b])
                rope = (cs[i], sn[i], 16)
            norm_rope(g, qin[i][:, :, 0:128], qin[i].b, 4, 128, gq_n[:, :].unsqueeze(1).broadcast_to([128, 4, 128]), gq_n.b,
                      xqn[i][:, :, :], xqn[i].b, tmp, sq, ss, rstd)
            norm_rope(g, qin[i][:, :, 128:192], qin[i].b, 4, 64, gq_r[:, :].unsqueeze(1).broadcast_to([128, 4, 64]), gq_r.b,
                      xqr[i][:, :, :], xqr[i].b, tmp, sq, ss, rstd, rope=rope)
            norm_rope(g, kvin[i][:, :, 0:128], kvin[i].b, 4, 128, gk_n[:, :].unsqueeze(1).broadcast_to([128, 4, 128]), gk_n.b,
                      xkn[i][:, :, :], xkn[i].b, tmp, sq, ss, rstd)
            norm_rope(g, krin[i][:, :, :], krin[i].b, 1, 64, gk_r[:, :].unsqueeze(1), gk_r.b,
                      xkr[i][:, :, :], xkr[i].b, tmp, sq, ss, rstd, rope=rope)
            transpose_slots(g, xqn[i], [0, 1, 2, 3], 128, tqn[i], 0)
            transpose_slots(g, xqr[i], [0, 1, 2, 3], 64, tqr[i], 0)
            transpose_slots(g, xkn[i], [0, 1, 2, 3], 128, tkn[i], 0)
            transpose_slots(g, xkr[i], [0], 64, tkr[i], 0)
            cols = slice(ti * 128, (ti + 1) * 128)
            S.dma("pool", g.MQN[:, :, cols].rearrange("s d t -> d s t"), tqn[i][:, :, :], reads=[tqn[i].b])
            S.dma("pool", g.MQR[:, :, cols].rearrange("s d t -> d s t"), tqr[i][:, :, :], reads=[tqr[i].b])
            S.dma("pool", g.MKN[:, :, cols].rearrange("s d t -> d s t"), tkn[i][:, :, :], reads=[tkn[i].b])
            S.dma("pool", g.MKR[:, :, cols].rearrange("s d t -> d s t"), tkr[i][:, :, :], reads=[tkr[i].b])
        S.run_lanes([(lambda i=i: lane(i)) for i in range(NL)])
    S.barrier()
    heads = []
    for h in range(4):
        heads.append(dict(kparts=[(g.MKN[h], 128), (g.MKR[0], 64)], qparts=[[(g.MQN[h], 128), (g.MQR[h], 64)]],
                          v=g.KVML[:, h * 256 + 128:h * 256 + 256], ycols=[1536 + h * 128], blocks=dense_blocks(g)))
    attention(g, heads, 192 ** -0.5)


RQ_R, RQ_V, RQ_LW, RQ_K, RQ_KK, RQ_KKA = 0, 1, 2, 3, 4, 5
RQ_SCAN = [RQ_R, RQ_LW, RQ_K, RQ_V, RQ_KK, RQ_KKA]
NRW = 2176


def stage_rwkv_prep(g, l):
    S = g.S
    NT, NTC = g.NT, g.NTC
    with ExitStack() as st:
        MU = [sb(g, st, "rmu", [128, NRW]) for _ in range(2)]
        w0 = [sb(g, st, "rw0", [128, 512]) for _ in range(2)]
        a0 = [sb(g, st, "ra0", [128, 512]) for _ in range(2)]
        w2 = [sb(g, st, "rw2", [96, 512]) for _ in range(2)]
        a2 = [sb(g, st, "ra2", [96, 512]) for _ in range(2)]
        g2 = sb(g, st, "rg2", [128, 2, 512])
        kk_ = sb(g, st, "rkk", [128, 512])
        ka = sb(g, st, "rka", [128, 512])
        omka = sb(g, st, "romka", [128, 512])
        for d in range(2):
            S.op("dve", lambda e: e.memset(MU[d][:, :], 0.0), writes=[MU[d].b])
            bcast_row(g, "sp", MU[d][:, 0:1792], MU[d].b, g.rwkv_mu[l, d, 0:1792])
            bcast_row(g, "sp", MU[d][:, 1792 + d * 96:1792 + (d + 1) * 96], MU[d].b, g.rwkv_mu[l, d, 1792:1888])
            bcast_row(g, "sp", MU[d][:, 1984 + d * 96:1984 + (d + 1) * 96], MU[d].b, g.rwkv_mu[l, d, 1888:1984])
            bcast_row(g, "sp", w0[d][:, :], w0[d].b, g.rwkv_w0[l, d, :])
            bcast_row(g, "sp", a0[d][:, :], a0[d].b, g.rwkv_a0[l, d, :])
            S.dma("sp", w2[d][:, :], g.rwkv_w2[l, d], writes=[w2[d].b])
            S.dma("sp", a2[d][:, :], g.rwkv_a2[l, d], writes=[a2[d].b])
        S.dma("sp", g2[:, :, :], g.rwkv_g2[l].rearrange("(kc p) n -> p kc n", p=128), writes=[g2.b])
        bcast_row(g, "sp", kk_[:, :], kk_.b, g.rwkv_k_k[l, :])
        bcast_row(g, "sp", ka[:, :], ka.b, g.rwkv_k_a[l, :])
        S.op("dve", lambda e: e.tensor_scalar(out=omka[:, :], in0=ka[:, :], scalar1=-1.0, scalar2=1.0, op0=ALU.mult, op1=ALU.add),
             reads=[ka.b], writes=[omka.b])
        LT = []
        for d in range(2):
            LT.append(dict(
                zc=[sb(g, st, "rzc", [128, NRW]) for _ in range(2)],
                zz=sb(g, st, "rzs", [128, NRW]),
                u=sb(g, st, "ru", [128, NRW]),
                tw=sb(g, st, "rtw", [128, 96]),
                sg=sb(g, st, "rsg", [128, 256]),
                tT=sb(g, st, "rtT", [128, 4, 128]),
                aa=sb(g, st, "raa", [128, 512]),
                t1=sb(g, st, "rt1", [128, 512]),
                t2=sb(g, st, "rt2", [128, 512]),
                ss=sb(g, st, "rss", [128, 8]),
                ob=[sb(g, st, "rob", [128, 4, 512]) for _ in range(2)],
                og=[sb(g, st, "rog", [128, 512]) for _ in range(2)]))

        def lane(d):
            lt = LT[d]
            zz, u, tw, sg, tT, aa, t1, t2, ss = (lt[k] for k in ["zz", "u", "tw", "sg", "tT", "aa", "t1", "t2", "ss"])
            for ti in range(NT):
                z = lt["zc"][ti % 2]
                r0 = ti * 128
                S.dma("sp", z[:, :], g.Z[r0:r0 + 128, O_RKVG:O_RKVG + NRW], writes=[z.b])
                if True:
                    if d == 0:
                        if ti in (0, NTC):
                            S.dma("sp", zz[1:128, :], g.Z[r0:r0 + 127, O_RKVG:O_RKVG + NRW], writes=[zz.b])
                            S.dma("sp", zz[0:1, :], g.zrow[0:1, :], writes=[zz.b])
                        else:
                            S.dma("sp", zz[:, :], g.Z[r0 - 1:r0 + 127, O_RKVG:O_RKVG + NRW], writes=[zz.b])
                    else:
                        if ti in (NTC - 1, NT - 1):
                            S.dma("sp", zz[0:127, :], g.Z[r0 + 1:r0 + 128, O_RKVG:O_RKVG + NRW], writes=[zz.b])
                            S.dma("sp", zz[127:128, :], g.zrow[0:1, :], writes=[zz.b])
                        else:
                            S.dma("sp", zz[:, :], g.Z[r0 + 1:r0 + 129, O_RKVG:O_RKVG + NRW], writes=[zz.b])
                    S.op("dve", lambda e: e.tensor_tensor(out=u[:, :], in0=zz[:, :], in1=z[:, :], op=ALU.subtract), reads=[zz.b, z.b], writes=[u.b])
                    S.op("dve", lambda e: e.tensor_tensor(out=u[:, :], in0=u[:, :], in1=MU[d][:, :], op=ALU.mult), reads=[u.b, MU[d].b], writes=[u.b])
                    S.op("dve", lambda e: e.tensor_tensor(out=u[:, :], in0=u[:, :], in1=z[:, :], op=ALU.add), reads=[u.b, z.b], writes=[u.b])
                    ur, uk, uv, ugd = u[:, 0:512], u[:, 512:1024], u[:, 1024:1536], u[:, 1536:1792]
                    uwd = u[:, 1792 + d * 96:1792 + (d + 1) * 96]
                    uad = u[:, 1984 + d * 96:1984 + (d + 1) * 96]
                    ob = lt["ob"][ti % 2]
                    o_g = lt["og"][ti % 2]

                    class _V:
                        def __init__(self, j):
                            self.j = j
                            self.b = ob.b

                        def __getitem__(self, idx):
                            return ob[:, self.j, :]
                    o_lw, o_k, o_kk, o_kka = _V(0), _V(1), _V(2), _V(3)
                    rows = slice(r0, r0 + 128)
                    S.dma("pool", g.RW[d, 0:2, rows, :].rearrange("q t c -> t q c"),
                          u[:, 0:2048].rearrange("p (a c) -> p a c", a=2)[:, :, 0:512], reads=[u.b])
                    S.op("act", lambda e: e.activation(out=tw[:, :], in_=uwd, func=AF.Tanh), reads=[u.b], writes=[tw.b])
                    S.op("act", lambda e: e.activation(out=sg[:, :], in_=ugd, func=AF.Sigmoid), reads=[u.b], writes=[sg.b])
                    ps = next_psum(g)
                    S.op("pe", lambda e: e.transpose(ps[0:96, 0:128], tw[:, :], g.ident[:]), reads=[tw.b, g.ident.b], writes=[ps.b])
                    S.op("pe", lambda e: e.transpose(ps[0:96, 128:256], uad, g.ident[:]), reads=[u.b, g.ident.b], writes=[ps.b])
                    copy_op(g, "act", tT[0:96, 0:2, :], ps[0:96, 0:256].rearrange("p (k t) -> p k t", k=2), [ps.b], [tT.b])
                    ps = next_psum(g)
                    S.op("pe", lambda e: e.transpose(ps[:, 0:128], sg[:, 0:128], g.ident[:]), reads=[sg.b, g.ident.b], writes=[ps.b])
                    S.op("pe", lambda e: e.transpose(ps[:, 128:256], sg[:, 128:256], g.ident[:]), reads=[sg.b, g.ident.b], writes=[ps.b])
                    copy_op(g, "dve", tT[:, 2:4, :], ps[:, 0:256].rearrange("p (k t) -> p k t", k=2), [ps.b], [tT.b])
                    psw = next_psum(g)
                    S.op("pe", lambda e: e.matmul(psw[:, :], lhsT=tT[0:96, 0, :], rhs=w2[d][:, :], start=True, stop=True), reads=[tT.b, w2[d].b], writes=[psw.b])
                    S.op("dve", lambda e: e.tensor_tensor(out=t1[:, :], in0=psw[:, :], in1=w0[d][:, :], op=ALU.add), reads=[psw.b, w0[d].b], writes=[t1.b])
                    S.op("act", lambda e: e.activation(out=t1[:, :], in_=t1[:, :], func=AF.Sigmoid), reads=[t1.b], writes=[t1.b])
                    S.op("act", lambda e: e.mul(out=o_lw[:, :], in_=t1[:, :], mul=-0.6065306597126334), reads=[t1.b], writes=[o_lw.b])
                    psa = next_psum(g)
                    S.op("pe", lambda e: e.matmul(psa[:, :], lhsT=tT[0:96, 1, :], rhs=a2[d][:, :], start=True, stop=True), reads=[tT.b, a2[d].b], writes=[psa.b])
                    S.op("dve", lambda e: e.tensor_tensor(out=aa[:, :], in0=psa[:, :], in1=a0[d][:, :], op=ALU.add), reads=[psa.b, a0[d].b], writes=[aa.b])
                    S.op("act", lambda e: e.activation(out=aa[:, :], in_=aa[:, :], func=AF.Sigmoid), reads=[aa.b], writes=[aa.b])
                    psg = next_psum(g)
                    for kc in range(2):
                        S.op("pe", lambda e: e.matmul(psg[:, :], lhsT=tT[:, 2 + kc, :], rhs=g2[:, kc, :], start=(kc == 0), stop=(kc == 1)),
                             reads=[tT.b, g2.b], writes=[psg.b])
                    copy_op(g, "act", o_g[:, :], psg[:, :], [psg.b], [o_g.b])
                    S.dma("pool", g.RG[d, rows, :], o_g[:, :], reads=[o_g.b])
                    S.op("dve", lambda e: e.tensor_tensor(out=t1[:, :], in0=uk, in1=kk_[:, :], op=ALU.mult), reads=[u.b, kk_.b], writes=[t1.b])
                    S.op("act", lambda e: e.activation(out=t2[:, :], in_=t1[:, :], func=AF.Square), reads=[t1.b], writes=[t2.b])
                    S.op("dve", lambda e: e.tensor_reduce(out=ss[:, 0:8], in_=t2[:, :].rearrange("p (h k) -> p h k", h=8), axis=AX.X, op=ALU.add),
                         reads=[t2.b], writes=[ss.b])
                    S.op("act", lambda e: e.activation(out=ss[:, 0:8], in_=ss[:, 0:8], func=AF.Sqrt), reads=[ss.b], writes=[ss.b])
                    S.op("dve", lambda e: e.tensor_scalar(out=ss[:, 0:8], in0=ss[:, 0:8], scalar1=1e-12, scalar2=None, op0=ALU.max), reads=[ss.b], writes=[ss.b])
                    S.op("dve", lambda e: e.reciprocal(out=ss[:, 0:8], in_=ss[:, 0:8]), reads=[ss.b], writes=[ss.b])
                    S.op("dve", lambda e: e.tensor_tensor(out=o_kk[:, :].rearrange("p (h k) -> p h k", h=8), in0=t1[:, :].rearrange("p (h k) -> p h k", h=8),
                                                          in1=ss[:, 0:8].unsqueeze(2).broadcast_to([128, 8, 64]), op=ALU.mult),
                         reads=[t1.b, ss.b], writes=[o_kk.b])
                    S.op("dve", lambda e: e.tensor_tensor(out=o_kka[:, :], in0=o_kk[:, :], in1=aa[:, :], op=ALU.mult), reads=[o_kk.b, aa.b], writes=[o_kka.b])
                    S.op("dve", lambda e: e.tensor_tensor(out=t2[:, :], in0=aa[:, :], in1=ka[:, :], op=ALU.mult), reads=[aa.b, ka.b], writes=[t2.b])
                    S.op("dve", lambda e: e.tensor_tensor(out=t2[:, :], in0=t2[:, :], in1=omka[:, :], op=ALU.add), reads=[t2.b, omka.b], writes=[t2.b])
                    S.op("dve", lambda e: e.tensor_tensor(out=o_k[:, :], in0=t2[:, :], in1=uk, op=ALU.mult), reads=[t2.b, u.b], writes=[o_k.b])
                    S.dma("pool", g.RW[d, 2:6, rows, :].rearrange("q t c -> t q c"), ob[:, :, :], reads=[ob.b])
        S.run_lanes([(lambda d=d: lane(d)) for d in range(2)])
    S.barrier()


def stage_rwkv_scan(g, l):
    S = g.S
    C = 64
    nch = g.T // C
    nch_c = g.LC // C
    ident64 = g.ident[0:64, 0:64]

    def mm(ps_ap, psb, lhsT, rhs, reads, start=True, stop=True):
        S.op("pe", lambda e: e.matmul(ps_ap, lhsT=lhsT, rhs=rhs, start=start, stop=stop), reads=reads, writes=[psb])

    with ExitStack() as st:
        names = ["Lsb", "eL", "enL", "eLx", "eEnd", "Rt", "Kt", "Kk", "Ak", "Kh", "Ah"]
        LT = []
        for d in range(2):
            LT.append(dict(
                ST=sb(g, st, "sST", [64, 8, 64]),
                qin=[sb(g, st, "sq", [64, 512]) for _ in range(6)],
                v2=sb(g, st, "sv2", [128, 512]),
                W={n: sb(g, st, "s" + n, [64, 512]) for n in names},
                pCT=sb(g, st, "spCT", [64, 8]),
                XT1=sb(g, st, "sXT1", [64, 8, 128]), XT2=sb(g, st, "sXT2", [64, 8, 128]),
                PRs=sb(g, st, "sPRs", [128, 8, 128]),
                Nn=[sb(g, st, "sN", [64, 8, 64]) for _ in range(2)],
                NTt=[sb(g, st, "sNT", [64, 8, 64]) for _ in range(2)],
                Qq=[sb(g, st, "sQ", [64, 8, 64]) for _ in range(2)],
                W1T=sb(g, st, "sW1T", [64, 8, 64]), MV=sb(g, st, "sMV", [64, 8, 64]), W2=sb(g, st, "sW2", [64, 8, 64]),
                Y0=sb(g, st, "sY0", [64, 8, 64]), D0=sb(g, st, "sD0", [64, 8, 64]), U=sb(g, st, "sU", [64, 8, 64]),
                Yo=[sb(g, st, "sYo", [64, 512]) for _ in range(2)], tmp=sb(g, st, "stmp", [64, 8, 64])))
            S.op("dve", lambda e: e.memset(LT[d]["ST"][:, :, :], 0.0), writes=[LT[d]["ST"].b])

        def lane(d):
            lt = LT[d]
            W, pCT, XT1, XT2, PRs, Nn, NTt, Qq = (lt[k] for k in ["W", "pCT", "XT1", "XT2", "PRs", "Nn", "NTt", "Qq"])
            W1T, MV, W2, Y0, D0, U, Yo, tmp = (lt[k] for k in ["W1T", "MV", "W2", "Y0", "D0", "U", "Yo", "tmp"])
            ST = {d: lt["ST"]}
            it = 0
            if d == 0:
                order = list(range(nch))
            else:
                order = list(range(nch_c - 1, -1, -1)) + list(range(nch - 1, nch_c - 1, -1))
            tri = g.tri[0:64, d, :]
            mk = g.mk[:, d, :]
            nmts = g.nmts[0:64, d, :]
            for c in order:
                it += 1
                q = lt["qin"]
                vv = lt["v2"]
                rows = slice(c * C, (c + 1) * C)
                for qi in range(6):
                    S.dma("sp", q[qi][:, :], g.RW[d, RQ_SCAN[qi], rows, :], writes=[q[qi].b])
                S.dma("sp", vv[64:128, :], g.RW[d, RQ_V, rows, :], writes=[vv.b])
                r_, lw, k_, v_, kk, kka = q
                psL = next_psum(g)
                mm(psL[0:64, :], psL.b, tri, lw[:, :], [g.tri.b, lw.b])
                psE = next_psum(g)
                mm(psE[0:64, :], psE.b, g.ones64[0:64, :], lw[:, :], [g.ones64.b, lw.b])
                psP = next_psum(g)
                for h in range(8):
                    mm(psP[0:64, h:h + 1], psP.b, lw[:, h * 64:(h + 1) * 64], g.ones64[0:64, 0:1], [lw.b, g.ones64.b])
                S.op("act", lambda e: e.activation(out=pCT[:, :], in_=psP[0:64, 0:8], func=AF.Exp), reads=[psP.b], writes=[pCT.b])
                S.op("act", lambda e: e.activation(out=W["eL"][:, :], in_=psL[0:64, :], func=AF.Exp), reads=[psL.b], writes=[W["eL"].b])
                S.op("act", lambda e: e.activation(out=W["enL"][:, :], in_=psL[0:64, :], func=AF.Exp, scale=-1.0), reads=[psL.b], writes=[W["enL"].b])
                S.op("dve", lambda e: e.tensor_tensor(out=W["eLx"][:, :], in0=psL[0:64, :], in1=lw[:, :], op=ALU.subtract), reads=[psL.b, lw.b], writes=[W["eLx"].b])
                S.op("act", lambda e: e.activation(out=W["eLx"][:, :], in_=W["eLx"][:, :], func=AF.Exp), reads=[W["eLx"].b], writes=[W["eLx"].b])
                copy_op(g, "dve", W["Lsb"][:, :], psL[0:64, :], [psL.b], [W["Lsb"].b])
                S.op("dve", lambda e: e.tensor_tensor(out=W["eEnd"][:, :], in0=psE[0:64, :], in1=W["Lsb"][:, :], op=ALU.subtract),
                     reads=[psE.b, W["Lsb"].b], writes=[W["eEnd"].b])
                S.op("act", lambda e: e.activation(out=W["eEnd"][:, :], in_=W["eEnd"][:, :], func=AF.Exp), reads=[W["eEnd"].b], writes=[W["eEnd"].b])
                for (o, a, b) in [("Rt", r_, "eL"), ("Kt", kk, "eLx"), ("Kk", k_, "enL"), ("Ak", kka, "enL"), ("Kh", k_, "eEnd"), ("Ah", kka, "eEnd")]:
                    S.op("dve", lambda e: e.tensor_tensor(out=W[o][:, :], in0=a[:, :], in1=W[b][:, :], op=ALU.mult), reads=[a.b, W[b].b], writes=[W[o].b])
                for (XT, n0, n1) in [(XT1, "Ak", "Kk"), (XT2, "Kt", "Rt")]:
                    for hg in range(2):
                        ps = next_psum(g)
                        for hh in range(4):
                            h = hg * 4 + hh
                            for j, nm in enumerate([n0, n1]):
                                S.op("pe", lambda e: e.transpose(ps[0:64, hh * 128 + j * 64:hh * 128 + (j + 1) * 64], W[nm][:, h * 64:(h + 1) * 64], ident64),
                                     reads=[W[nm].b, g.ident.b], writes=[ps.b])
                        copy_op(g, evac_engine(g), XT[:, hg * 4:(hg + 1) * 4, :], ps[0:64, :].rearrange("p (h x) -> p h x", h=4), [ps.b], [XT.b])
                for hg in range(2):
                    ps = next_psum(g)
                    for hh in range(4):
                        h = hg * 4 + hh
                        mm(ps[:, hh * 128:(hh + 1) * 128], ps.b, XT1[:, h, :], XT2[:, h, :], [XT1.b, XT2.b])
                    S.op("dve", lambda e: e.tensor_tensor(out=PRs[:, hg * 4:(hg + 1) * 4, :], in0=ps[:, :].rearrange("p (h x) -> p h x", h=4),
                                                          in1=mk.unsqueeze(1).broadcast_to([128, 4, 128]), op=ALU.mult),
                         reads=[ps.b, g.mk.b], writes=[PRs.b])
                ps = next_psum(g)
                for h in range(8):
                    mm(ps[0:64, h * 64:(h + 1) * 64], ps.b, XT2[:, h, 0:64], XT1[:, h, 0:64], [XT1.b, XT2.b])
                N, NT_, Q = Nn[0], NTt[0], Qq[0]
                S.op("dve", lambda e: e.tensor_tensor(out=NT_[:, :, :], in0=ps[0:64, :].rearrange("p (h x) -> p h x", h=8),
                                                      in1=nmts.unsqueeze(1).broadcast_to([64, 8, 64]), op=ALU.mult),
                     reads=[ps.b, g.nmts.b], writes=[NT_.b])
                S.op("dve", lambda e: e.tensor_scalar(out=N[:, :, :], in0=PRs[0:64, :, 0:64], scalar1=-1.0, scalar2=None, op0=ALU.mult), reads=[PRs.b], writes=[N.b])
                S.op("dve", lambda e: e.tensor_tensor(out=Q[:, :, :], in0=N[:, :, :], in1=ident64.unsqueeze(1).broadcast_to([64, 8, 64]), op=ALU.add),
                     reads=[N.b, g.ident.b], writes=[Q.b])
                cur = 0
                for lev in range(5):
                    N, NT_, Q = Nn[cur], NTt[cur], Qq[cur]
                    N2, NT2, Q2 = Nn[1 - cur], NTt[1 - cur], Qq[1 - cur]
                    psn = next_psum(g)
                    pst = next_psum(g)
                    for h in range(8):
                        if lev < 4:
                            mm(psn[0:64, h * 64:(h + 1) * 64], psn.b, NT_[:, h, :], N[:, h, :], [NT_.b, N.b])
                        mm(pst[0:64, h * 64:(h + 1) * 64], pst.b, N[:, h, :], NT_[:, h, :], [NT_.b, N.b])
                    if lev < 4:
                        copy_op(g, "act", N2[:, :, :], psn[0:64, :].rearrange("p (h x) -> p h x", h=8), [psn.b], [N2.b])
                    copy_op(g, "dve", NT2[:, :, :], pst[0:64, :].rearrange("p (h x) -> p h x", h=8), [pst.b], [NT2.b])
                    psq = next_psum(g)
                    for h in range(8):
                        mm(psq[0:64, h * 64:(h + 1) * 64], psq.b, NT2[:, h, :], Q[:, h, :], [NT2.b, Q.b])
                    S.op("dve", lambda e: e.tensor_tensor(out=Q2[:, :, :], in0=psq[0:64, :].rearrange("p (h x) -> p h x", h=8), in1=Q[:, :, :], op=ALU.add),
                         reads=[psq.b, Q.b], writes=[Q2.b])
                    cur = 1 - cur
                Q = Qq[cur]
                ps1 = next_psum(g)
                ps2 = next_psum(g)
                ps3 = next_psum(g)
                ps4 = next_psum(g)
                for h in range(8):
                    hs = slice(h * 64, (h + 1) * 64)
                    mm(ps1[0:64, hs], ps1.b, W["Kt"][:, hs], Q[:, h, :], [W["Kt"].b, Q.b])
                    mm(ps2[0:64, hs], ps2.b, PRs[64:128, h, 0:64], vv[64:128, hs], [PRs.b, vv.b])
                    mm(ps3[0:64, hs], ps3.b, PRs[64:128, h, 64:128], vv[64:128, hs], [PRs.b, vv.b])
                    mm(ps4[0:64, hs], ps4.b, W["Kh"][:, hs], v_[:, hs], [W["Kh"].b, v_.b])
                copy_op(g, "act", W1T[:, :, :], ps1[0:64, :].rearrange("p (h x) -> p h x", h=8), [ps1.b], [W1T.b])
                copy_op(g, "dve", MV[:, :, :], ps2[0:64, :].rearrange("p (h x) -> p h x", h=8), [ps2.b], [MV.b])
                copy_op(g, "act", Y0[:, :, :], ps3[0:64, :].rearrange("p (h x) -> p h x", h=8), [ps3.b], [Y0.b])
                copy_op(g, "dve", D0[:, :, :], ps4[0:64, :].rearrange("p (h x) -> p h x", h=8), [ps4.b], [D0.b])
                ps5 = next_psum(g)
                for h in range(8):
                    mm(ps5[0:64, h * 64:(h + 1) * 64], ps5.b, Q[:, h, :], MV[:, h, :], [Q.b, MV.b])
                copy_op(g, "act", W2[:, :, :], ps5[0:64, :].rearrange("p (h x) -> p h x", h=8), [ps5.b], [W2.b])
                st_ = ST[d]
                psu = next_psum(g)
                for h in range(8):
                    mm(psu[0:64, h * 64:(h + 1) * 64], psu.b, W1T[:, h, :], st_[:, h, :], [W1T.b, st_.b])
                S.op("dve", lambda e: e.scalar_tensor_tensor(out=U[:, :, :], in0=psu[0:64, :].rearrange("p (h x) -> p h x", h=8), scalar=-1.0,
                                                             in1=W2[:, :, :], op0=ALU.mult, op1=ALU.subtract),
                     reads=[psu.b, W2.b], writes=[U.b])
                psy = next_psum(g)
                for h in range(8):
                    hs = slice(h * 64, (h + 1) * 64)
                    mm(psy[0:64, hs], psy.b, XT2[:, h, 64:128], st_[:, h, :], [XT2.b, st_.b], start=True, stop=False)
                    mm(psy[0:64, hs], psy.b, PRs[0:64, h, 64:128], U[:, h, :], [PRs.b, U.b], start=False, stop=True)
                yo = Yo[it % 2]
                S.op("dve", lambda e: e.tensor_tensor(out=yo[:, :], in0=psy[0:64, :], in1=Y0[:, :, :].rearrange("p h x -> p (h x)"), op=ALU.add),
                     reads=[psy.b, Y0.b], writes=[yo.b])
                S.dma("pool", g.YR[d, rows, :], yo[:, :], reads=[yo.b])
                psd = next_psum(g)
                for h in range(8):
                    hs = slice(h * 64, (h + 1) * 64)
                    mm(psd[0:64, hs], psd.b, W["Ah"][:, hs], U[:, h, :], [W["Ah"].b, U.b])
                S.op("dve", lambda e: e.tensor_tensor(out=tmp[:, :, :], in0=st_[:, :, :], in1=pCT[:, 0:8].unsqueeze(2).broadcast_to([64, 8, 64]), op=ALU.mult),
                     reads=[st_.b, pCT.b], writes=[tmp.b])
                S.op("dve", lambda e: e.tensor_tensor(out=tmp[:, :, :], in0=tmp[:, :, :], in1=D0[:, :, :], op=ALU.add), reads=[tmp.b, D0.b], writes=[tmp.b])
                S.op("dve", lambda e: e.tensor_tensor(out=st_[:, :, :], in0=psd[0:64, :].rearrange("p (h x) -> p h x", h=8), in1=tmp[:, :, :], op=ALU.add),
                     reads=[psd.b, tmp.b], writes=[st_.b])
        S.run_lanes([(lambda d=d: lane(d)) for d in range(2)])
    S.barrier()


def stage_rwkv_out(g, l):
    S = g.S
    with ExitStack() as st:
        lw_ = sb(g, st, "olw", [128, 512])
        lb_ = sb(g, st, "olb", [128, 512])
        rk_ = sb(g, st, "ork", [128, 512])
        bcast_row(g, "sp", lw_[:, :], lw_.b, g.rwkv_lnx_w[l, :])
        bcast_row(g, "sp", lb_[:, :], lb_.b, g.rwkv_lnx_b[l, :])
        bcast_row(g, "sp", rk_[:, :], rk_.b, g.rwkv_r_k[l].rearrange("h k -> (h k)"))
        NL = 4
        LT = [dict(ins=[[sb(g, st, "oin", [128, 512]) for _ in range(5)] for _ in range(2)],
                   t1=sb(g, st, "ot1", [128, 512]), t2=sb(g, st, "ot2", [128, 512]),
                   acc=[sb(g, st, "oacc", [128, 512]) for _ in range(2)],
                   ss=sb(g, st, "oss", [128, 8]), s2=sb(g, st, "os2", [128, 8])) for _ in range(NL)]
        h8 = lambda ap: ap.rearrange("p (h k) -> p h k", h=8)
        b8 = lambda t_: t_[:, 0:8].unsqueeze(2).broadcast_to([128, 8, 64])

        def lane(li):
          lt = LT[li]
          ins, t1, t2, acc, ss, s2 = (lt[k] for k in ["ins", "t1", "t2", "acc", "ss", "s2"])
          it = 0
          for kk_i, ti in enumerate(range(li, g.NT, NL)):
            rows = slice(ti * 128, (ti + 1) * 128)
            a_ = acc[kk_i % 2]
            for d in range(2):
                it += 1
                y, r_, k_, v_, g_ = ins[it % 2]
                S.dma("sp", y[:, :], g.YR[d, rows, :], writes=[y.b])
                S.dma("sp", r_[:, :], g.RW[d, RQ_R, rows, :], writes=[r_.b])
                S.dma("sp", k_[:, :], g.RW[d, RQ_K, rows, :], writes=[k_.b])
                S.dma("sp", v_[:, :], g.RW[d, RQ_V, rows, :], writes=[v_.b])
                S.dma("sp", g_[:, :], g.RG[d, rows, :], writes=[g_.b])
                S.op("dve", lambda e: e.tensor_reduce(out=ss[:, 0:8], in_=h8(y[:, :]), axis=AX.X, op=ALU.add), reads=[y.b], writes=[ss.b])
                S.op("dve", lambda e: e.tensor_scalar(out=ss[:, 0:8], in0=ss[:, 0:8], scalar1=-1.0 / 64, scalar2=None, op0=ALU.mult), reads=[ss.b], writes=[ss.b])
                S.op("dve", lambda e: e.tensor_tensor(out=h8(t1[:, :]), in0=h8(y[:, :]), in1=b8(ss), op=ALU.add), reads=[y.b, ss.b], writes=[t1.b])
                S.op("act", lambda e: e.activation(out=t2[:, :], in_=t1[:, :], func=AF.Square), reads=[t1.b], writes=[t2.b])
                S.op("dve", lambda e: e.tensor_reduce(out=s2[:, 0:8], in_=h8(t2[:, :]), axis=AX.X, op=ALU.add), reads=[t2.b], writes=[s2.b])
                S.op("dve", lambda e: e.tensor_scalar(out=s2[:, 0:8], in0=s2[:, 0:8], scalar1=1.0 / 64, scalar2=LNX_EPS, op0=ALU.mult, op1=ALU.add),
                     reads=[s2.b], writes=[s2.b])
                S.op("act", lambda e: e.activation(out=s2[:, 0:8], in_=s2[:, 0:8], func=AF.Sqrt), reads=[s2.b], writes=[s2.b])
                S.op("dve", lambda e: e.reciprocal(out=s2[:, 0:8], in_=s2[:, 0:8]), reads=[s2.b], writes=[s2.b])
                S.op("dve", lambda e: e.tensor_tensor(out=h8(t1[:, :]), in0=h8(t1[:, :]), in1=b8(s2), op=ALU.mult), reads=[t1.b, s2.b], writes=[t1.b])
                S.op("dve", lambda e: e.tensor_tensor(out=t1[:, :], in0=t1[:, :], in1=lw_[:, :], op=ALU.mult), reads=[t1.b, lw_.b], writes=[t1.b])
                S.op("dve", lambda e: e.tensor_tensor(out=t1[:, :], in0=t1[:, :], in1=lb_[:, :], op=ALU.add), reads=[t1.b, lb_.b], writes=[t1.b])
                S.op("pool", lambda e: e.tensor_tensor(out=t2[:, :], in0=r_[:, :], in1=k_[:, :], op=ALU.mult), reads=[r_.b, k_.b], writes=[t2.b])
                S.op("pool", lambda e: e.tensor_tensor(out=t2[:, :], in0=t2[:, :], in1=rk_[:, :], op=ALU.mult), reads=[t2.b, rk_.b], writes=[t2.b])
                S.op("dve", lambda e: e.tensor_reduce(out=ss[:, 0:8], in_=h8(t2[:, :]), axis=AX.X, op=ALU.add), reads=[t2.b], writes=[ss.b])
                S.op("dve", lambda e: e.tensor_tensor(out=h8(t2[:, :]), in0=h8(v_[:, :]), in1=b8(ss), op=ALU.mult), reads=[v_.b, ss.b], writes=[t2.b])
                S.op("dve", lambda e: e.tensor_tensor(out=t1[:, :], in0=t1[:, :], in1=t2[:, :], op=ALU.add), reads=[t1.b, t2.b], writes=[t1.b])
                if d == 0:
                    S.op("dve", lambda e: e.tensor_tensor(out=a_[:, :], in0=t1[:, :], in1=g_[:, :], op=ALU.mult), reads=[t1.b, g_.b], writes=[a_.b])
                else:
                    S.op("dve", lambda e: e.tensor_tensor(out=t1[:, :], in0=t1[:, :], in1=g_[:, :], op=ALU.mult), reads=[t1.b, g_.b], writes=[t1.b])
                    S.op("dve", lambda e: e.tensor_tensor(out=a_[:, :], in0=a_[:, :], in1=t1[:, :], op=ALU.add), reads=[a_.b, t1.b], writes=[a_.b])
            S.dma("pool", g.YMIX[rows, 1024:1536], a_[:, :], reads=[a_.b])
        S.run_lanes([(lambda li=li: lane(li)) for li in range(NL)])
    S.barrier()


def stage_merge(g, l):
    S = g.S
    with ExitStack() as st:
        gt = [sb(g, st, "mgt", [128, 4, 512]) for _ in range(3)]
        t1 = sb(g, st, "mt1", [128, 512])
        cnt = [0]
        gv = g.Z[:, O_GATE:O_GATE + 4 * D].rearrange("t (i n) -> t i n", i=4)

        def epi(pss, o, ti, n0, nb):
            gs = gt[cnt[0] % 3]
            cnt[0] += 1
            S.dma("sp", gs[:, :, 0:nb], gv[ti * 128:(ti + 1) * 128, :, n0:n0 + nb], writes=[gs.b])
            S.op("act", lambda e: e.activation(out=gs[:, :, 0:nb], in_=gs[:, :, 0:nb], func=AF.Sigmoid), reads=[gs.b], writes=[gs.b])
            S.op("dve", lambda e: e.tensor_tensor(out=o[:, 0:nb], in0=pss[0][:, 0:nb], in1=gs[:, 0, 0:nb], op=ALU.mult), reads=[pss[0].b, gs.b], writes=[o.b])
            for i in range(1, 4):
                S.op("dve", lambda e: e.tensor_tensor(out=t1[:, 0:nb], in0=pss[i][:, 0:nb], in1=gs[:, i, 0:nb], op=ALU.mult), reads=[pss[i].b, gs.b], writes=[t1.b])
                S.op("dve", lambda e: e.tensor_tensor(out=o[:, 0:nb], in0=o[:, 0:nb], in1=t1[:, 0:nb], op=ALU.add), reads=[o.b, t1.b], writes=[o.b])
            S.dma("sp", g.MERGED[ti * 128:(ti + 1) * 128, n0:n0 + nb], o[:, 0:nb], reads=[o.b])
        linear(g, g.YMIX, g.w_branch[l].rearrange("b k n -> (b k) n"), None, g.T, D, D, G=12, NB=512, epi=epi,
               segs=[(0, 4), (4, 8), (8, 12), (12, 16)])


def stage_conv(g, l):
    S = g.S
    NT, NTC = g.NT, g.NTC
    NL = 4
    blocks = list(range(0, DFF, 512))
    with ExitStack() as st:
        LT = [dict(cw=sb(g, st, "ccw", [128, 3, 512]), cb=sb(g, st, "ccb", [128, 512]),
                   aw=[sb(g, st, "caw", [128, 512]) for _ in range(4)],
                   bw=[sb(g, st, "cbw", [128, 512]) for _ in range(2)],
                   acc=sb(g, st, "cacc", [128, 512]), t1=sb(g, st, "ct1", [128, 512]),
                   t2=sb(g, st, "ct2", [128, 512]), xb=sb(g, st, "cxb", [128, 512]),
                   out=[sb(g, st, "cout", [128, 512]) for _ in range(2)]) for _ in range(NL)]

        def lane(li):
            lt = LT[li]
            w_, b_, aw, bw, acc, t1 = lt["cw"], lt["cb"], lt["aw"], lt["bw"], lt["acc"], lt["t1"]
            lq = "sp" if li % 2 == 0 else "act"
            for n0 in blocks[li::NL]:
                for j in range(3):
                    bcast_row(g, lq, w_[:, j, :], w_.b, g.ffn_conv_w[l, j, n0:n0 + 512])
                bcast_row(g, lq, b_[:, :], b_.b, g.ffn_conv_b[l, n0:n0 + 512])
                for t0 in range(min(2, NT)):
                    S.dma(lq, aw[t0 % 4][:, :], g.AB[t0 * 128:(t0 + 1) * 128, n0:n0 + 512], writes=[aw[t0 % 4].b])
                for ti in range(NT):
                    r0 = ti * 128
                    if ti + 2 < NT:
                        t2_ = ti + 2
                        S.dma(lq, aw[t2_ % 4][:, :], g.AB[t2_ * 128:(t2_ + 1) * 128, n0:n0 + 512], writes=[aw[t2_ % 4].b])
                    bb = bw[ti % 2]
                    S.dma(lq, bb[:, :], g.AB[r0:r0 + 128, DFF + n0:DFF + n0 + 512], writes=[bb.b])
                    ac_ = aw[ti % 4]
                    has_prev = ti not in (0, NTC)
                    has_next = ti not in (NTC - 1, NT - 1)
                    psP = next_psum(g)
                    S.op("pe", lambda e: e.matmul(psP[:, :], lhsT=g.shm[:, 0, :], rhs=ac_[:, :], start=True, stop=not has_prev),
                         reads=[g.shm.b, ac_.b], writes=[psP.b])
                    if has_prev:
                        ap_ = aw[(ti - 1) % 4]
                        S.op("pe", lambda e: e.matmul(psP[:, :], lhsT=g.shm[:, 1, :], rhs=ap_[:, :], start=False, stop=True),
                             reads=[g.shm.b, ap_.b], writes=[psP.b])
                    psN = next_psum(g)
                    S.op("pe", lambda e: e.matmul(psN[:, :], lhsT=g.shm[:, 2, :], rhs=ac_[:, :], start=True, stop=not has_next),
                         reads=[g.shm.b, ac_.b], writes=[psN.b])
                    if has_next:
                        an_ = aw[(ti + 1) % 4]
                        S.op("pe", lambda e: e.matmul(psN[:, :], lhsT=g.shm[:, 3, :], rhs=an_[:, :], start=False, stop=True),
                             reads=[g.shm.b, an_.b], writes=[psN.b])
                    t2 = lt["t2"]
                    xb = lt["xb"]
                    S.op("dve", lambda e: e.tensor_tensor(out=acc[:, :], in0=psP[:, :], in1=w_[:, 0, :], op=ALU.mult), reads=[psP.b, w_.b], writes=[acc.b])
                    S.op("pool", lambda e: e.tensor_tensor(out=t1[:, :], in0=ac_[:, :], in1=w_[:, 1, :], op=ALU.mult), reads=[ac_.b, w_.b], writes=[t1.b])
                    S.op("pool", lambda e: e.tensor_tensor(out=t1[:, :], in0=t1[:, :], in1=b_[:, :], op=ALU.add), reads=[t1.b, b_.b], writes=[t1.b])
                    S.op("dve", lambda e: e.tensor_tensor(out=t2[:, :], in0=psN[:, :], in1=w_[:, 2, :], op=ALU.mult), reads=[psN.b, w_.b], writes=[t2.b])
                    S.op("dve", lambda e: e.tensor_tensor(out=acc[:, :], in0=acc[:, :], in1=t2[:, :], op=ALU.add), reads=[acc.b, t2.b], writes=[acc.b])
                    S.op("dve", lambda e: e.tensor_tensor(out=acc[:, :], in0=acc[:, :], in1=t1[:, :], op=ALU.add), reads=[acc.b, t1.b], writes=[acc.b])
                    S.op("pool", lambda e: e.tensor_tensor(out=xb[:, :], in0=acc[:, :], in1=bb[:, :], op=ALU.mult), reads=[acc.b, bb.b], writes=[xb.b])
                    S.op("act", lambda e: e.activation(out=t2[:, :], in_=acc[:, :], func=AF.Square), reads=[acc.b], writes=[t2.b])
                    S.op("dve", lambda e: e.tensor_scalar(out=t2[:, :], in0=t2[:, :], scalar1=0.044715, scalar2=1.0, op0=ALU.mult, op1=ALU.add), reads=[t2.b], writes=[t2.b])
                    S.op("dve", lambda e: e.tensor_tensor(out=t2[:, :], in0=t2[:, :], in1=acc[:, :], op=ALU.mult), reads=[t2.b, acc.b], writes=[t2.b])
                    S.op("act", lambda e: e.activation(out=t2[:, :], in_=t2[:, :], func=AF.Sigmoid, scale=1.5957691216057308), reads=[t2.b], writes=[t2.b])
                    o = lt["out"][ti % 2]
                    S.op("dve", lambda e: e.tensor_tensor(out=o[:, :], in0=t2[:, :], in1=xb[:, :], op=ALU.mult), reads=[t2.b, xb.b], writes=[o.b])
                    S.dma("pool", g.GG[r0:r0 + 128, n0:n0 + 512], o[:, :], reads=[o.b])
        S.run_lanes([(lambda li=li: lane(li)) for li in range(NL)])
    S.barrier()


def build(cfg):
    TL, LC, DEPTH = cfg["TL"], cfg["LC"], cfg["DEPTH"]
    T = TL + LC
    nc = bass.Bass("TRN2", target_bir_lowering=False)
    g = Ctx()
    g.nc = nc
    g.uid = 0
    g.ev = 0
    g.pi = 0
    g.T, g.TL, g.LC, g.NT, g.NTC = T, TL, LC, T // 128, LC // 128
    g.S = S = Sync(nc)

    def din(name, shape):
        return nc.dram_tensor(name, list(shape), F32, kind="ExternalInput").ap()

    def dscr(name, shape):
        return nc.dram_tensor(name, list(shape), F32, kind="Internal").ap()

    L = DEPTH
    g.x_in = din("x", [TL, D])
    g.ctx_in = din("ctx", [LC, D])
    g.cvec = din("cvec", [2, D])
    g.ada_w = din("ada_w", [L, D, 6 * D])
    g.ada_b = din("ada_b", [L, 6 * D])
    g.norm1_g = din("norm1_g", [L, D])
    g.norm2_g = din("norm2_g", [L, D])
    g.w_in = din("w_in", [L, D, IN_W])
    g.identd = din("ident", [128, 128])
    g.ga_q_norm = din("ga_q_norm", [L, 128])
    g.ga_k_norm = din("ga_k_norm", [L, 128])
    g.wa_q_norm = din("wa_q_norm", [L, 128])
    g.wa_k_norm = din("wa_k_norm", [L, 128])
    g.wa_sink = din("wa_sink", [L, 4])
    g.mla_cq_norm = din("mla_cq_norm", [L, 384])
    g.mla_ckv_norm = din("mla_ckv_norm", [L, 512])
    g.mla_w_uq = din("mla_w_uq", [L, 384, 768])
    g.mla_w_ukv = din("mla_w_ukv", [L, 512, 1024])
    g.mla_qn_norm = din("mla_qn_norm", [L, 128])
    g.mla_qr_norm = din("mla_qr_norm", [L, 64])
    g.mla_kn_norm = din("mla_kn_norm", [L, 128])
    g.mla_kr_norm = din("mla_kr_norm", [L, 64])
    g.rwkv_mu = din("rwkv_mu", [L, 2, 1984])
    g.rwkv_w0 = din("rwkv_w0", [L, 2, 512])
    g.rwkv_w2 = din("rwkv_w2", [L, 2, 96, 512])
    g.rwkv_a0 = din("rwkv_a0", [L, 2, 512])
    g.rwkv_a2 = din("rwkv_a2", [L, 2, 96, 512])
    g.rwkv_g2 = din("rwkv_g2", [L, 256, 512])
    g.rwkv_k_k = din("rwkv_k_k", [L, 512])
    g.rwkv_k_a = din("rwkv_k_a", [L, 512])
    g.rwkv_r_k = din("rwkv_r_k", [L, 8, 64])
    g.rwkv_lnx_w = din("rwkv_lnx_w", [L, 512])
    g.rwkv_lnx_b = din("rwkv_lnx_b", [L, 512])
    g.w_branch = din("w_branch", [L, 4, 512, D])
    g.w_out = din("w_out", [L, D, D])
    g.ffn_up = din("ffn_up", [L, D, 2 * DFF])
    g.ffn_conv_w = din("ffn_conv_w", [L, 3, DFF])
    g.ffn_conv_b = din("ffn_conv_b", [L, DFF])
    g.ffn_down = din("ffn_down", [L, DFF, D])
    g.trid = din("tri", [64, 2, 64])
    g.mkd = din("mk", [128, 2, 128])
    g.nmtsd = din("nmts", [64, 2, 64])
    g.zrow = din("zrow", [1, NRW])
    g.shmd = din("shm", [128, 4, 128])
    g.ropeA = din("ropeA", [2, TL, 128])
    g.ropeM = din("ropeM", [2, TL, 64])
    g.wmaskd = din("wmask", [128, 2, 128])
    g.y_out = nc.dram_tensor("y", [TL, D], F32, kind="ExternalOutput").ap()
    dbg = cfg.get("debug")
    g.XS = dscr("XS", [T, D])
    g.MODB = dscr("MODB", [2, 128, 6 * D])
    g.Z = dscr("Z", [T, IN_W])
    g.QKT = dscr("QKT", [12, 128, T])
    g.YMIX = dscr("YMIX", [T, D])
    g.RW = dscr("RW", [2, 6, T, 512])
    g.RG = dscr("RG", [2, T, 512])
    g.YR = dscr("YR", [2, T, 512])
    g.MERGED = dscr("MERGED", [T, D])
    g.AB = dscr("AB", [T, 2 * DFF])
    g.GG = dscr("GG", [T, DFF])
    g.QML = dscr("QML", [T, 768])
    g.KVML = dscr("KVML", [T, 1024])
    g.MQN = dscr("MQN", [4, 128, T])
    g.MQR = dscr("MQR", [4, 64, T])
    g.MKN = dscr("MKN", [4, 128, T])
    g.MKR = dscr("MKR", [1, 64, T])
    if dbg:
        g.dbg_z = nc.dram_tensor("dbg_z", [T, IN_W], F32, kind="ExternalOutput").ap()
        g.dbg_y = nc.dram_tensor("dbg_y", [T, D], F32, kind="ExternalOutput").ap()
        g.dbg_qkt = nc.dram_tensor("dbg_qkt", [12, 128, T], F32, kind="ExternalOutput").ap()
        g.dbg_xs = nc.dram_tensor("dbg_xs", [T, D], F32, kind="ExternalOutput").ap()

    with ExitStack() as es:
        g.psums = [Tl(es.enter_context(nc.psum_tensor("ps%d" % i, [128, 512], F32)), "ps%d" % i) for i in range(8)]
        g.ident = sb(g, es, "ident", [128, 128])
        S.dma("sp", g.ident[:, :], g.identd[:, :], writes=[g.ident.b])
        g.tri = sb(g, es, "tri", [64, 2, 64])
        g.mk = sb(g, es, "mk", [128, 2, 128])
        g.nmts = sb(g, es, "nmts", [64, 2, 64])
        g.ones64 = sb(g, es, "ones64", [64, 64])
        S.dma("sp", g.tri[:, :, :], g.trid[:, :, :], writes=[g.tri.b])
        S.dma("sp", g.mk[:, :, :], g.mkd[:, :, :], writes=[g.mk.b])
        S.dma("sp", g.nmts[:, :, :], g.nmtsd[:, :, :], writes=[g.nmts.b])
        S.op("dve", lambda e: e.memset(g.ones64[:, :], 1.0), writes=[g.ones64.b])
        g.shm = sb(g, es, "shm", [128, 4, 128])
        S.dma("sp", g.shm[:, :, :], g.shmd[:, :, :], writes=[g.shm.b])
        g.wmask = sb(g, es, "wmask", [128, 2, 128])
        S.dma("sp", g.wmask[:, :, :], g.wmaskd[:, :, :], writes=[g.wmask.b])
        S.dma("sp", g.XS[0:LC, :], g.ctx_in[:, :])
        S.dma("sp", g.XS[LC:T, :], g.x_in[:, :])
        g.cb = [sb(g, es, "cb", [128, 16, 128]) for _ in range(2)]
        with ExitStack() as st:
            cT = [sb(g, st, "cT", [128, 16]) for _ in range(2)]
            ones = sb(g, st, "ones", [128, 128])
            S.op("dve", lambda e: e.memset(ones[:, :], 1.0), writes=[ones.b])
            for r in range(2):
                S.dma("sp", cT[r][:, :], g.cvec[r, :].rearrange("(kc p) -> p kc", p=128), writes=[cT[r].b], allow_slow_non_contiguous=True)
                S.op("act", lambda e: e.activation(out=cT[r][:, :], in_=cT[r][:, :], func=AF.Silu), reads=[cT[r].b], writes=[cT[r].b])
                for kc in range(16):
                    S.op("dve", lambda e: e.tensor_scalar(out=g.cb[r][:, kc, :], in0=ones[:, :], scalar1=cT[r][:, kc:kc + 1], scalar2=None, op0=ALU.mult),
                         reads=[ones.b, cT[r].b], writes=[g.cb[r].b])
            S.barrier()
        for l in range(L):
            stage_mod(g, l)
            with ExitStack() as st:
                pro = make_norm_pro(g, st, l, g.norm1_g[l, :], D, 0)
                linear(g, g.XS, g.w_in[l], g.Z, T, D, IN_W, G=12, NB=512, pro=pro)
            if cfg.get("stop") == "z":
                break
            stage_attn_prep(g, l)
            stage_attn_gawa(g, l)
            if cfg.get("stop") == "gawa":
                break
            if cfg.get("stop") != "rwkv":
                stage_mla(g, l)
            if cfg.get("stop") == "mla":
                break
            stage_rwkv_prep(g, l)
            stage_rwkv_scan(g, l)
            stage_rwkv_out(g, l)
            if cfg.get("stop") == "rwkv":
                break
            stage_merge(g, l)
            with ExitStack() as st:
                epi = make_resid_epi(g, st, 2 * D)
                linear(g, g.MERGED, g.w_out[l], None, T, D, D, G=12, NB=512, epi=epi)
            if cfg.get("stop") == "attn":
                break
            with ExitStack() as st:
                pro = make_norm_pro(g, st, l, g.norm2_g[l, :], 4 * D, 3 * D)
                linear(g, g.XS, g.ffn_up[l], g.AB, T, D, 2 * DFF, G=12, NB=512, pro=pro)
            stage_conv(g, l)
            with ExitStack() as st:
                epi = make_resid_epi(g, st, 5 * D)
                linear(g, g.GG, g.ffn_down[l], None, T, DFF, D, G=6, NB=256, epi=epi)
        if dbg:
            S.dma("sp", g.dbg_z[:, :], g.Z[:, :])
            S.dma("sp", g.dbg_y[:, :], g.YMIX[:, :])
            S.dma("sp", g.dbg_qkt[:, :, :], g.QKT[:, :, :])
            S.dma("sp", g.dbg_xs[:, :], g.XS[:, :])
        S.dma("sp", g.y_out[:, :], g.XS[LC:T, :])
        S.barrier()
    return nc, g


_CACHE = {}


def rope_table(n_tokens, rot_dim):
    rows = n_tokens // GRID_W
    row = np.repeat(np.arange(rows), GRID_W).astype(np.float32)
    col = np.tile(np.arange(GRID_W), rows).astype(np.float32)
    quarter = rot_dim // 4
    inv_freq = (10000.0 ** (-np.arange(quarter, dtype=np.float32) / quarter)).astype(np.float32)
    ang_r = row[:, None] * inv_freq
    ang_c = col[:, None] * inv_freq
    ang = np.concatenate([ang_r, ang_r, ang_c, ang_c], axis=-1).astype(np.float32)
    sign = np.concatenate([-np.ones(quarter), np.ones(quarter), -np.ones(quarter), np.ones(quarter)]).astype(np.float32)
    return np.stack([np.cos(ang), np.sin(ang) * sign]).astype(np.float32)


def shift_mats():
    m = np.zeros((128, 4, 128), np.float32)
    for k in range(127):
        m[k, 0, k + 1] = 1.0
        m[k + 1, 2, k] = 1.0
    m[127, 1, 0] = 1.0
    m[0, 3, 127] = 1.0
    return m


def const_tables(TL):
    idx = np.arange(128)
    m_lo = (idx[None, :] <= idx[:, None]).astype(np.float32)
    m_hi = (idx[:, None] <= idx[None, :]).astype(np.float32)
    i64 = np.arange(64)
    tri = np.stack([(i64[:, None] <= i64[None, :]), (i64[:, None] >= i64[None, :])]).astype(np.float32)
    strict = tri - np.eye(64, dtype=np.float32)[None]
    half = np.concatenate([strict, tri], axis=2)
    mk = np.concatenate([half, half], axis=1)
    nmts = -np.transpose(strict, (0, 2, 1))
    return {
        "tri": np.ascontiguousarray(np.transpose(tri, (1, 0, 2))),
        "mk": np.ascontiguousarray(np.transpose(mk, (1, 0, 2))),
        "nmts": np.ascontiguousarray(np.transpose(nmts, (1, 0, 2))),
        "zrow": np.zeros((1, NRW), np.float32),
        "shm": shift_mats(),
        "ropeA": rope_table(TL, 128),
        "ropeM": rope_table(TL, 64),
        "wmask": np.ascontiguousarray(np.stack([m_lo, m_hi], axis=1)),
    }


def make_inputs_for_core(inputs, b, L):
    f = lambda a: np.ascontiguousarray(np.asarray(a, dtype=np.float32))
    m = {
        "x": f(inputs["x"][b]),
        "ctx": f(inputs["ctx"][b]),
        "cvec": f(np.stack([np.asarray(inputs["c"][b]), np.asarray(inputs["c_ctx"])])),
        "ident": np.eye(128, dtype=np.float32),
    }
    m.update(const_tables(np.asarray(inputs["x"]).shape[1]))
    for k in ["ada_w", "ada_b", "norm1_g", "norm2_g", "w_in", "ga_q_norm", "ga_k_norm", "wa_q_norm", "wa_k_norm", "wa_sink",
              "mla_cq_norm", "mla_ckv_norm", "mla_w_uq", "mla_w_ukv", "mla_qn_norm", "mla_qr_norm", "mla_kn_norm", "mla_kr_norm",
              "rwkv_mu", "rwkv_w0", "rwkv_w2", "rwkv_a0", "rwkv_a2", "rwkv_g2", "rwkv_k_k", "rwkv_k_a", "rwkv_r_k", "rwkv_lnx_w", "rwkv_lnx_b",
              "w_branch", "w_out", "ffn_up", "ffn_conv_w", "ffn_conv_b", "ffn_down"]:
        m[k] = f(inputs[k][:L])
    return m


def kernel(**inputs):
    x = np.asarray(inputs["x"])
    B, TL, _ = x.shape
    LC = np.asarray(inputs["ctx"]).shape[1]
    L = np.asarray(inputs["ada_w"]).shape[0]
    cfg = {"TL": TL, "LC": LC, "DEPTH": L}
    nc, g = build(cfg)
    n = 8
    in_maps = [make_inputs_for_core(inputs, c % B, L) for c in range(n)]
    res = run_bass_kernel_spmd(nc, in_maps, core_ids=list(range(n)))
    return np.stack([res.results[b]["y"] for b in range(B)]).astype(np.float32)
```

```python
from contextlib import ExitStack
import threading
import numpy as np
import concourse.bass as bass
import concourse.mybir as mybir
from concourse.bass_utils import run_bass_kernel_spmd

F32 = mybir.dt.float32
BF16 = mybir.dt.bfloat16
AF = mybir.ActivationFunctionType
ALU = mybir.AluOpType
AX = mybir.AxisListType

D = 2048
GRID_W = 64
EPS = 1e-6
IN_W = 13376
DFF = 5632
O_GAQ, O_GAK, O_GAV, O_WAQ, O_WAK, O_WAV = 0, 512, 768, 1024, 1536, 1792
O_RKVG, O_ZW, O_ZA, O_CQ, O_CKV, O_KR, O_GATE = 2048, 3840, 4032, 4224, 4608, 5120, 5184
LNX_EPS = 64e-5


class Buf:
    __slots__ = ("name", "w", "r")

    def __init__(self, name=""):
        self.name = name
        self.w = None
        self.r = {}


class Sync:
    MAXC = 30000

    def __init__(self, nc, n_dma_sems=48, same_engine_sync=True):
        self.nc = nc
        self.hw = {"pe": nc.tensor, "dve": nc.vector, "act": nc.scalar, "pool": nc.gpsimd, "sp": nc.sync}
        self.gen = {k: 0 for k in self.hw}
        self.sem = {k: nc.alloc_semaphore("sem_%s_0" % k) for k in self.hw}
        self.count = {k: 0 for k in self.hw}
        self.seen = {k: {} for k in self.hw}
        self.dma_sems = [nc.alloc_semaphore("dsem%d" % i) for i in range(n_dma_sems)]
        self.dma_uses = [0] * n_dma_sems
        self.dma_next = 0
        self.same_engine_sync = same_engine_sync
        self.latest = {}
        self._lane = None

    def run_lanes(self, fns):
        n = len(fns)
        if n == 1:
            fns[0]()
            return
        cv = threading.Condition()
        st = {"turn": 0, "alive": [True] * n, "err": None}
        ids = {}

        def nxt(i):
            for k in range(1, n + 1):
                j = (i + k) % n
                if st["alive"][j]:
                    return j
            return -1

        def worker(i):
            ids[threading.get_ident()] = i
            with cv:
                while st["turn"] != i:
                    cv.wait()
            try:
                fns[i]()
            except BaseException as e:
                st["err"] = e
            with cv:
                st["alive"][i] = False
                st["turn"] = nxt(i)
                cv.notify_all()

        self._lane = (cv, st, ids, nxt)
        self.lane_pi = {}
        threads = [threading.Thread(target=worker, args=(i,)) for i in range(n)]
        for t in threads:
            t.start()
        for t in threads:
            t.join()
        self._lane = None
        if st["err"] is not None:
            raise st["err"]

    def lane_index(self):
        if self._lane is None:
            return None
        i = self._lane[2].get(threading.get_ident())
        if i is None:
            return None
        return i, len(self._lane[1]["alive"])

    def lane_yield(self):
        if self._lane is None:
            return
        cv, st, ids, nxt = self._lane
        i = ids.get(threading.get_ident())
        if i is None:
            return
        with cv:
            j = nxt(i)
            if j == i or j < 0:
                return
            st["turn"] = j
            cv.notify_all()
            while st["turn"] != i:
                cv.wait()

    def _wait(self, eng, dep):
        key, sem, val = dep
        if self.seen[eng].get(key, 0) >= val:
            return
        self.hw[eng].wait_ge(sem, val)
        self.seen[eng][key] = val

    @staticmethod
    def _add(deps, d):
        if d is not None and deps.get(d[0], (0, 0, 0))[2] < d[2]:
            deps[d[0]] = d

    def _deps(self, reads, writes):
        deps = {}
        for b in reads:
            self._add(deps, b.w)
        for b in writes:
            self._add(deps, b.w)
            for d in b.r.values():
                self._add(deps, d)
        return deps

    def _mark(self, dep, reads, writes):
        for b in reads:
            b.r[dep[0]] = dep
        for b in writes:
            b.w = dep
            b.r = {}
        self.latest[dep[0]] = dep

    def op(self, eng, fn, reads=(), writes=()):
        for key, d in self._deps(reads, writes).items():
            if isinstance(key, tuple) and key[0] == eng and (eng == "pe" or not self.same_engine_sync):
                continue
            self._wait(eng, d)
        ins = fn(self.hw[eng])
        self.count[eng] += 1
        ins.then_inc(self.sem[eng], 1)
        self._mark(((eng, self.gen[eng]), self.sem[eng], self.count[eng]), reads, writes)
        if self.count[eng] >= self.MAXC:
            self.gen[eng] += 1
            self.sem[eng] = self.nc.alloc_semaphore("sem_%s_%d" % (eng, self.gen[eng]))
            self.count[eng] = 0
        self.lane_yield()
        return ins

    def dma(self, q, out, in_, reads=(), writes=(), **kw):
        i = self.dma_next
        self.dma_next = (self.dma_next + 1) % len(self.dma_sems)
        sem = self.dma_sems[i]
        key = "d%d" % i
        if self.dma_uses[i] > 0:
            self._wait(q, (key, sem, 16 * self.dma_uses[i]))
        for k, d in self._deps(reads, writes).items():
            self._wait(q, d)
        self.dma_uses[i] += 1
        ins = self.hw[q].dma_start(out=out, in_=in_, **kw)
        ins.then_inc(sem, 16)
        self._mark((key, sem, 16 * self.dma_uses[i]), reads, writes)
        self.lane_yield()
        return ins

    def barrier(self):
        for eng in self.hw:
            for dep in list(self.latest.values()):
                key = dep[0]
                if isinstance(key, tuple) and key[0] == "pe" and eng == "pe":
                    continue
                self._wait(eng, dep)


class Tl:
    def __init__(self, t, name):
        self.t = t
        self.b = Buf(name)

    def __getitem__(self, idx):
        return self.t[idx]


class Ctx:
    pass


def sb(g, st, name, shape, dtype=F32):
    g.uid += 1
    nm = "%s_%d" % (name, g.uid)
    return Tl(st.enter_context(g.nc.sbuf_tensor(nm, list(shape), dtype)), nm)


def evac_engine(g):
    g.ev += 1
    return "dve" if g.ev % 2 else "act"


def copy_op(g, eng, out, in_, reads, writes):
    if eng == "act":
        return g.S.op("act", lambda e: e.copy(out=out, in_=in_), reads=reads, writes=writes)
    return g.S.op(eng, lambda e: e.tensor_copy(out=out, in_=in_), reads=reads, writes=writes)


def next_psum(g):
    lane = g.S.lane_index()
    if lane is None:
        g.pi = (g.pi + 1) % len(g.psums)
        return g.psums[g.pi]
    i, n = lane
    lo, hi = i * 8 // n, (i + 1) * 8 // n
    k = g.S.lane_pi.get(i, 0)
    g.S.lane_pi[i] = k + 1
    return g.psums[lo + k % (hi - lo)]


def linear(g, x, w, y, T, K, N, G=8, NB=512, pro=None, epi=None, segs=None, wlist=None):
    S, nc = g.S, g.nc
    KC = K // 128
    assert K % 128 == 0 and T % 128 == 0
    NT = T // 128
    XW = min(K, 2048)
    if segs is None:
        segs = [(0, KC)]
    with ExitStack() as st:
        xt = sb(g, st, "xt", [128, KC, G * 128], BF16)
        xin = [sb(g, st, "xin", [128, XW]) for _ in range(2)]
        wt = [sb(g, st, "wt", [128, KC, NB], BF16) for _ in range(3)]
        ot = [sb(g, st, "ot", [128, NB]) for _ in range(3)]
        nxin = nw = no = 0
        for g0 in range(0, NT, G):
            gn = min(G, NT - g0)
            for gi in range(gn):
                ti = g0 + gi
                for c0 in range(0, K, XW):
                    cw = min(XW, K - c0)
                    xi = xin[nxin % 2]
                    nxin += 1
                    S.dma("sp", xi[:, 0:cw], x[ti * 128:(ti + 1) * 128, c0:c0 + cw], writes=[xi.b])
                    if pro is not None:
                        pro(xi, ti, cw)
                    for k4 in range(0, cw // 128, 4):
                        ps = next_psum(g)
                        nk = min(4, cw // 128 - k4)
                        for j in range(nk):
                            kk = k4 + j
                            S.op("pe", lambda e: e.transpose(ps[:, j * 128:(j + 1) * 128], xi[:, kk * 128:(kk + 1) * 128], g.ident[:]),
                                 reads=[xi.b, g.ident.b], writes=[ps.b])
                        kc0 = c0 // 128 + k4
                        copy_op(g, evac_engine(g), xt[:, kc0:kc0 + nk, gi * 128:(gi + 1) * 128],
                                ps[:, 0:nk * 128].rearrange("p (k t) -> p k t", k=nk), [ps.b], [xt.b])
            for n0 in range(0, N, NB):
                nb = min(NB, N - n0)
                wi = wt[nw % 3]
                nw += 1
                S.dma("pool", wi[:, :, 0:nb], w.rearrange("(kc p) n -> p kc n", p=128)[:, :, n0:n0 + nb], writes=[wi.b])
                for gi in range(gn):
                    ti = g0 + gi
                    pss = []
                    for (k0, k1) in segs:
                        ps = next_psum(g)
                        pss.append(ps)
                        for kc in range(k0, k1):
                            S.op("pe", lambda e: e.matmul(ps[:, 0:nb], lhsT=xt[:, kc, gi * 128:(gi + 1) * 128], rhs=wi[:, kc, 0:nb],
                                                          start=(kc == k0), stop=(kc == k1 - 1)),
                                 reads=[xt.b, wi.b], writes=[ps.b])
                    o = ot[no % 3]
                    no += 1
                    if epi is None:
                        copy_op(g, evac_engine(g), o[:, 0:nb], pss[0][:, 0:nb], [pss[0].b], [o.b])
                        S.dma("sp", y[ti * 128:(ti + 1) * 128, n0:n0 + nb], o[:, 0:nb], reads=[o.b])
                    else:
                        epi(pss, o, ti, n0, nb)
    S.barrier()


def rms_rstd(g, st_tiles, x_ap, n, width, reads_b, sq, ss, rstd, eps=EPS):
    S = g.S
    S.op("act", lambda e: e.activation(out=sq[:, 0:n * width].rearrange("p (n w) -> p n w", n=n), in_=x_ap, func=AF.Square),
         reads=[reads_b], writes=[sq.b])
    S.op("dve", lambda e: e.tensor_reduce(out=ss[:, 0:n], in_=sq[:, 0:n * width].rearrange("p (n w) -> p n w", n=n), axis=AX.X, op=ALU.add),
         reads=[sq.b], writes=[ss.b])
    S.op("dve", lambda e: e.tensor_scalar(out=ss[:, 0:n], in0=ss[:, 0:n], scalar1=1.0 / width, scalar2=eps, op0=ALU.mult, op1=ALU.add),
         reads=[ss.b], writes=[ss.b])
    S.op("act", lambda e: e.activation(out=ss[:, 0:n], in_=ss[:, 0:n], func=AF.Sqrt), reads=[ss.b], writes=[ss.b])
    S.op("dve", lambda e: e.reciprocal(out=rstd[:, 0:n], in_=ss[:, 0:n]), reads=[ss.b], writes=[rstd.b])


def bcast_row(g, q, tile_ap, buf, dram_row_ap):
    g.S.dma(q, tile_ap, dram_row_ap.partition_broadcast(128), writes=[buf])


def stage_mod(g, l):
    S, nc = g.S, g.nc
    with ExitStack() as st:
        wt = [sb(g, st, "mw", [128, 16, 512]) for _ in range(2)]
        bt = [sb(g, st, "mb", [128, 512]) for _ in range(2)]
        ot = [sb(g, st, "mo", [128, 512]) for _ in range(3)]
        no = 0
        for bi, n0 in enumerate(range(0, 6 * D, 512)):
            wi = wt[bi % 2]
            bb = bt[bi % 2]
            S.dma("sp", wi[:, :, :], g.ada_w[l].rearrange("(kc p) n -> p kc n", p=128)[:, :, n0:n0 + 512], writes=[wi.b])
            bcast_row(g, "sp", bb[:, :], bb.b, g.ada_b[l, n0:n0 + 512])
            for r in range(2):
                ps = next_psum(g)
                for kc in range(16):
                    S.op("pe", lambda e: e.matmul(ps[:, :], lhsT=g.cb[r][:, kc, :], rhs=wi[:, kc, :], start=(kc == 0), stop=(kc == 15)),
                         reads=[g.cb[r].b, wi.b], writes=[ps.b])
                o = ot[no % 3]
                no += 1
                S.op("dve", lambda e: e.tensor_tensor(out=o[:, :], in0=ps[:, :], in1=bb[:, :], op=ALU.add), reads=[ps.b, bb.b], writes=[o.b])
                S.dma("pool", g.MODB[r, :, n0:n0 + 512], o[:, :], reads=[o.b])
    S.barrier()


def make_norm_pro(g, st, l, gain_dram, sc_off, sh_off):
    S = g.S
    A = [sb(g, st, "nA", [128, D]) for _ in range(2)]
    B = [sb(g, st, "nB", [128, D]) for _ in range(2)]
    sq = sb(g, st, "nsq", [128, D])
    ss = sb(g, st, "nss", [128, 1])
    rstd = sb(g, st, "nrs", [128, 1])
    with ExitStack() as st2:
        gt = sb(g, st2, "ng", [128, D])
        bcast_row(g, "sp", gt[:, :], gt.b, gain_dram)
        for r in range(2):
            S.dma("sp", A[r][:, :], g.MODB[r, :, sc_off:sc_off + D], writes=[A[r].b])
            S.dma("sp", B[r][:, :], g.MODB[r, :, sh_off:sh_off + D], writes=[B[r].b])
            S.op("dve", lambda e: e.scalar_tensor_tensor(out=A[r][:, :], in0=A[r][:, :], scalar=1.0, in1=gt[:, :], op0=ALU.add, op1=ALU.mult),
                 reads=[A[r].b, gt.b], writes=[A[r].b])
        S.barrier()

    def pro(xi, ti, cw):
        r = 1 if ti < g.NTC else 0
        rms_rstd(g, None, xi[:, 0:D].rearrange("p (n w) -> p n w", n=1), 1, D, xi.b, sq, ss, rstd)
        S.op("dve", lambda e: e.scalar_tensor_tensor(out=xi[:, 0:D], in0=xi[:, 0:D], scalar=rstd[:, 0:1], in1=A[r][:, :], op0=ALU.mult, op1=ALU.mult),
             reads=[xi.b, rstd.b, A[r].b], writes=[xi.b])
        S.op("dve", lambda e: e.tensor_tensor(out=xi[:, 0:D], in0=xi[:, 0:D], in1=B[r][:, :], op=ALU.add), reads=[xi.b, B[r].b], writes=[xi.b])
    return pro


def make_resid_epi(g, st, gate_off):
    S = g.S
    gts = [sb(g, st, "rg", [128, D]) for _ in range(2)]
    for r in range(2):
        S.dma("sp", gts[r][:, :], g.MODB[r, :, gate_off:gate_off + D], writes=[gts[r].b])
    xb = [sb(g, st, "rx", [128, 512]) for _ in range(3)]
    cnt = [0]

    def epi(pss, o, ti, n0, nb):
        r = 1 if ti < g.NTC else 0
        x = xb[cnt[0] % 3]
        cnt[0] += 1
        S.dma("sp", x[:, 0:nb], g.XS[ti * 128:(ti + 1) * 128, n0:n0 + nb], writes=[x.b])
        S.op("dve", lambda e: e.tensor_tensor(out=o[:, 0:nb], in0=pss[0][:, 0:nb], in1=gts[r][:, n0:n0 + nb], op=ALU.mult),
             reads=[pss[0].b, gts[r].b], writes=[o.b])
        S.op("dve", lambda e: e.tensor_tensor(out=o[:, 0:nb], in0=o[:, 0:nb], in1=x[:, 0:nb], op=ALU.add), reads=[o.b, x.b], writes=[o.b])
        S.dma("sp", g.XS[ti * 128:(ti + 1) * 128, n0:n0 + nb], o[:, 0:nb], reads=[o.b])
    return epi


def norm_rope(g, src, src_b, n, w, gain_ap, gain_b, out, out_b, tmp, sq, ss, rstd, rope=None):
    S = g.S
    rms_rstd(g, None, src, n, w, src_b, sq, ss, rstd)
    dst = out if rope is None else tmp[:, 0:n * w].rearrange("p (n w) -> p n w", n=n)
    dst_b = out_b if rope is None else tmp.b
    S.op("dve", lambda e: e.tensor_tensor(out=dst, in0=src, in1=rstd[:, 0:n].unsqueeze(2).broadcast_to([128, n, w]), op=ALU.mult),
         reads=[src_b, rstd.b], writes=[dst_b])
    S.op("dve", lambda e: e.tensor_tensor(out=dst, in0=dst, in1=gain_ap, op=ALU.mult), reads=[dst_b, gain_b], writes=[dst_b])
    if rope is None:
        return
    cos, ssin, blk = rope
    nh = w // (2 * blk)
    xv = dst.rearrange("p n (h b k) -> p n h b k", h=nh, b=2, k=blk)
    sv = ssin[:, 0:w].rearrange("p (h b k) -> p h b k", h=nh, b=2, k=blk)
    sw = sq[:, 0:n * w].rearrange("p (n h b k) -> p n h b k", n=n, h=nh, b=2, k=blk)
    for n_i in range(n):
        for b in range(2):
            S.op("dve", lambda e: e.tensor_tensor(out=sw[:, n_i, :, b, :], in0=xv[:, n_i, :, 1 - b, :], in1=sv[:, :, b, :], op=ALU.mult),
                 reads=[dst_b, ssin.b], writes=[sq.b])
    S.op("dve", lambda e: e.tensor_tensor(out=dst, in0=dst, in1=cos[:, 0:w].unsqueeze(1).broadcast_to([128, n, w]), op=ALU.mult),
         reads=[dst_b, cos.b], writes=[dst_b])
    S.op("dve", lambda e: e.tensor_tensor(out=out, in0=dst, in1=sq[:, 0:n * w].rearrange("p (n w) -> p n w", n=n), op=ALU.add),
         reads=[dst_b, sq.b], writes=[out_b])


def transpose_slots(g, src_tl, slots, w, dst_tl, dst_slot0):
    S = g.S
    for i0 in range(0, len(slots), 4):
        grp = slots[i0:i0 + 4]
        ps = next_psum(g)
        for j, s_ in enumerate(grp):
            S.op("pe", lambda e: e.transpose(ps[0:w, j * 128:(j + 1) * 128], src_tl[:, s_, 0:w], g.ident[:]),
                 reads=[src_tl.b, g.ident.b], writes=[ps.b])
        copy_op(g, evac_engine(g), dst_tl[0:w, dst_slot0 + i0:dst_slot0 + i0 + len(grp), :],
                ps[0:w, 0:len(grp) * 128].rearrange("p (k t) -> p k t", k=len(grp)), [ps.b], [dst_tl.b])


def stage_attn_prep(g, l):
    S = g.S
    with ExitStack() as st:
        gain = sb(g, st, "apg", [128, 2, 6, 128])
        for gi, (qn, kn) in enumerate([(g.ga_q_norm, g.ga_k_norm), (g.wa_q_norm, g.wa_k_norm)]):
            for s_ in range(6):
                bcast_row(g, "sp", gain[:, gi, s_, :], gain.b, (qn if s_ < 4 else kn)[l, :])
        NL = 3
        LT = [dict(z=sb(g, st, "apz", [128, 16, 128]), x=sb(g, st, "apx", [128, 12, 128]), t_=sb(g, st, "apt", [128, 12, 128]),
                   cs=sb(g, st, "apc", [128, 128]), sn=sb(g, st, "aps", [128, 128]),
                   tmp=sb(g, st, "aptmp", [128, 6 * 128]), sq=sb(g, st, "apsq", [128, 6 * 128]),
                   ss=sb(g, st, "apss", [128, 8]), rstd=sb(g, st, "aprs", [128, 8])) for _ in range(NL)]

        def lane(li):
          lt = LT[li]
          tmp, sq, ss, rstd = lt["tmp"], lt["sq"], lt["ss"], lt["rstd"]
          cs = [lt["cs"], lt["cs"]]
          sn = [lt["sn"], lt["sn"]]
          for ti in range(li, g.NT, NL):
            z = lt["z"]
            x = lt["x"]
            t_ = lt["t_"]
            S.dma("sp", z[:, :, :], g.Z[ti * 128:(ti + 1) * 128, 0:2048].rearrange("p (s d) -> p s d", s=16), writes=[z.b])
            rope = None
            if ti >= g.NTC:
                c_, s_t = cs[ti % 2], sn[ti % 2]
                p0 = (ti - g.NTC) * 128
                S.dma("sp", c_[:, :], g.ropeA[0, p0:p0 + 128, :], writes=[c_.b])
                S.dma("sp", s_t[:, :], g.ropeA[1, p0:p0 + 128, :], writes=[s_t.b])
                rope = (c_, s_t, 32)
            for gi, base in enumerate([0, 8]):
                norm_rope(g, z[:, base:base + 6, :], z.b, 6, 128, gain[:, gi, :, :], gain.b,
                          x[:, gi * 6:(gi + 1) * 6, :], x.b, tmp, sq, ss, rstd, rope=rope)
            transpose_slots(g, x, list(range(12)), 128, t_, 0)
            S.dma("pool", g.QKT[:, :, ti * 128:(ti + 1) * 128].rearrange("s d t -> d s t"), t_[:, :, :], reads=[t_.b])
        S.run_lanes([(lambda li=li: lane(li)) for li in range(NL)])
    S.barrier()


def attention(g, heads, scale, sink_tl=None):
    S = g.S
    T, NT = g.T, g.NT
    po = g.psums[0:4]
    sps = g.psums[4:8]
    with ExitStack() as st:
        kts = [sb(g, st, "akt", [128, T], BF16) for _ in range(2)]
        vt = sb(g, st, "avt", [128, NT, 129], BF16)
        qts = [[sb(g, st, "aqt", [128, 512], BF16) for _ in range(2)] for _ in range(3)]
        pts = [sb(g, st, "apt", [128, 512], BF16) for _ in range(4)]
        yts = [sb(g, st, "ayt", [128, 128]) for _ in range(3)]
        den = sb(g, st, "aden", [128, 4])
        S.op("dve", lambda e: e.memset(vt[:, :, 128:129], 1.0), writes=[vt.b])
        nq = npt = ny = nsp = 0
        for hd in heads:
            for pi_, (kd_ap, kd) in enumerate(hd["kparts"]):
                S.dma("pool", kts[pi_][0:kd, :], kd_ap, writes=[kts[pi_].b])
            S.dma("pool", vt[:, :, 0:128], hd["v"].rearrange("(n p) d -> p n d", p=128), writes=[vt.b])
            for qh, qparts in enumerate(hd["qparts"]):
                for (q0, qlen, keys) in hd["blocks"]:
                    nqb = qlen // 128
                    qt = qts[nq % 3]
                    nq += 1
                    for pi_, (qd_ap, kd) in enumerate(qparts):
                        S.dma("pool", qt[pi_][0:kd, 0:qlen], qd_ap[:, q0:q0 + qlen], writes=[qt[pi_].b])
                    npart = len(qparts)

                    def scores(kt_):
                        nonlocal nsp
                        ps_ = sps[nsp % 4]
                        nsp += 1
                        for pi_, (qd_ap, kd) in enumerate(qparts):
                            S.op("pe", lambda e: e.matmul(ps_[:, 0:qlen], lhsT=kts[pi_][0:kd, kt_ * 128:(kt_ + 1) * 128], rhs=qt[pi_][0:kd, 0:qlen],
                                                          start=(pi_ == 0), stop=(pi_ == npart - 1)),
                                 reads=[kts[pi_].b, qt[pi_].b], writes=[ps_.b])
                        return ps_
                    ahead = [scores(keys[j][0]) for j in range(min(2, len(keys)))]
                    for idx, (kt, mask) in enumerate(keys):
                        ps = ahead.pop(0)
                        if idx + 2 < len(keys):
                            ahead.append(scores(keys[idx + 2][0]))
                        pt = pts[npt % 4]
                        npt += 1
                        S.op("act", lambda e: e.activation(out=pt[:, 0:qlen], in_=ps[:, 0:qlen], func=AF.Exp, scale=scale),
                             reads=[ps.b], writes=[pt.b])
                        if mask is not None:
                            S.op("dve", lambda e: e.tensor_tensor(out=pt[:, 0:qlen], in0=pt[:, 0:qlen], in1=g.wmask[:, mask, :], op=ALU.mult),
                                 reads=[pt.b, g.wmask.b], writes=[pt.b])
                        for qb in range(nqb):
                            S.op("pe", lambda e: e.matmul(po[qb][:, 0:129], lhsT=pt[:, qb * 128:(qb + 1) * 128], rhs=vt[:, kt, :],
                                                          start=(idx == 0), stop=(idx == len(keys) - 1)),
                                 reads=[pt.b, vt.b], writes=[po[qb].b])
                    for qb in range(nqb):
                        y = yts[ny % 3]
                        ny += 1
                        if hd.get("sink_idx") is not None:
                            si = hd["sink_idx"][qh]
                            S.op("dve", lambda e: e.tensor_tensor(out=den[:, 0:1], in0=po[qb][:, 128:129], in1=sink_tl[:, si:si + 1], op=ALU.add),
                                 reads=[po[qb].b, sink_tl.b], writes=[den.b])
                            S.op("dve", lambda e: e.reciprocal(out=den[:, 0:1], in_=den[:, 0:1]), reads=[den.b], writes=[den.b])
                        else:
                            S.op("dve", lambda e: e.reciprocal(out=den[:, 0:1], in_=po[qb][:, 128:129]), reads=[po[qb].b], writes=[den.b])
                        S.op("dve", lambda e: e.tensor_scalar(out=y[:, :], in0=po[qb][:, 0:128], scalar1=den[:, 0:1], scalar2=None, op0=ALU.mult),
                             reads=[po[qb].b, den.b], writes=[y.b])
                        r0 = q0 + qb * 128
                        yc = hd["ycols"][qh]
                        S.dma("sp", g.YMIX[r0:r0 + 128, yc:yc + 128], y[:, :], reads=[y.b])
    S.barrier()


def dense_blocks(g):
    blocks = [(0, g.LC, [(kt, None) for kt in range(g.NTC)])]
    for q0 in range(g.LC, g.T, 512):
        blocks.append((q0, min(512, g.T - q0), [(kt, None) for kt in range(g.NT)]))
    return blocks


def window_blocks(g):
    blocks = [(0, g.LC, [(kt, None) for kt in range(g.NTC)])]
    nblk = g.TL // 128
    for n in range(nblk):
        keys = [(kt, None) for kt in range(g.NTC)]
        if n > 0:
            keys.append((g.NTC + n - 1, 0))
        keys.append((g.NTC + n, None))
        if n < nblk - 1:
            keys.append((g.NTC + n + 1, 1))
        blocks.append((g.LC + n * 128, 128, keys))
    return blocks


def stage_attn_gawa(g, l):
    S = g.S
    heads = []
    for hk in range(2):
        heads.append(dict(kparts=[(g.QKT[4 + hk], 128)], qparts=[[(g.QKT[2 * hk + j], 128)] for j in range(2)],
                          v=g.Z[:, O_GAV + hk * 128:O_GAV + (hk + 1) * 128], ycols=[(2 * hk + j) * 128 for j in range(2)],
                          blocks=dense_blocks(g)))
    attention(g, heads, 128 ** -0.5)
    with ExitStack() as st:
        sink = sb(g, st, "sink", [128, 4])
        bcast_row(g, "sp", sink[:, :], sink.b, g.wa_sink[l, :])
        S.op("act", lambda e: e.activation(out=sink[:, :], in_=sink[:, :], func=AF.Exp), reads=[sink.b], writes=[sink.b])
        heads = []
        for hk in range(2):
            heads.append(dict(kparts=[(g.QKT[10 + hk], 128)], qparts=[[(g.QKT[6 + 2 * hk + j], 128)] for j in range(2)],
                              v=g.Z[:, O_WAV + hk * 128:O_WAV + (hk + 1) * 128], ycols=[512 + (2 * hk + j) * 128 for j in range(2)],
                              blocks=window_blocks(g), sink_idx=[2 * hk, 2 * hk + 1]))
        attention(g, heads, 128 ** -0.5, sink_tl=sink)


def make_rms_pro(g, st, gain_dram, K):
    S = g.S
    gt = sb(g, st, "pg", [128, K])
    sq = sb(g, st, "psq", [128, K])
    ss = sb(g, st, "pss", [128, 1])
    rstd = sb(g, st, "prs", [128, 1])
    bcast_row(g, "sp", gt[:, :], gt.b, gain_dram)

    def pro(xi, ti, cw):
        rms_rstd(g, None, xi[:, 0:K].rearrange("p (n w) -> p n w", n=1), 1, K, xi.b, sq, ss, rstd)
        S.op("dve", lambda e: e.scalar_tensor_tensor(out=xi[:, 0:K], in0=xi[:, 0:K], scalar=rstd[:, 0:1], in1=gt[:, :], op0=ALU.mult, op1=ALU.mult),
             reads=[xi.b, rstd.b, gt.b], writes=[xi.b])
    return pro


def stage_mla(g, l):
    S = g.S
    T = g.T
    with ExitStack() as st:
        pro = make_rms_pro(g, st, g.mla_cq_norm[l, :], 384)
        linear(g, g.Z[:, O_CQ:O_CQ + 384], g.mla_w_uq[l], g.QML, T, 384, 768, G=17, NB=512, pro=pro)
    with ExitStack() as st:
        pro = make_rms_pro(g, st, g.mla_ckv_norm[l, :], 512)
        linear(g, g.Z[:, O_CKV:O_CKV + 512], g.mla_w_ukv[l], g.KVML, T, 512, 1024, G=17, NB=512, pro=pro)
    with ExitStack() as st:
        gq_n = sb(g, st, "mgqn", [128, 128])
        gq_r = sb(g, st, "mgqr", [128, 64])
        gk_n = sb(g, st, "mgkn", [128, 128])
        gk_r = sb(g, st, "mgkr", [128, 64])
        bcast_row(g, "sp", gq_n[:, :], gq_n.b, g.mla_qn_norm[l, :])
        bcast_row(g, "sp", gq_r[:, :], gq_r.b, g.mla_qr_norm[l, :])
        bcast_row(g, "sp", gk_n[:, :], gk_n.b, g.mla_kn_norm[l, :])
        bcast_row(g, "sp", gk_r[:, :], gk_r.b, g.mla_kr_norm[l, :])
        NL = 3
        qin = [sb(g, st, "mq", [128, 4, 192]) for _ in range(NL)]
        kvin = [sb(g, st, "mkv", [128, 4, 256]) for _ in range(NL)]
        krin = [sb(g, st, "mkr", [128, 1, 64]) for _ in range(NL)]
        xqn = [sb(g, st, "xqn", [128, 4, 128]) for _ in range(NL)]
        xqr = [sb(g, st, "xqr", [128, 4, 64]) for _ in range(NL)]
        xkn = [sb(g, st, "xkn", [128, 4, 128]) for _ in range(NL)]
        xkr = [sb(g, st, "xkr", [128, 1, 64]) for _ in range(NL)]
        tqn = [sb(g, st, "tqn", [128, 4, 128]) for _ in range(NL)]
        tqr = [sb(g, st, "tqr", [64, 4, 128]) for _ in range(NL)]
        tkn = [sb(g, st, "tkn", [128, 4, 128]) for _ in range(NL)]
        tkr = [sb(g, st, "tkr", [64, 1, 128]) for _ in range(NL)]
        cs = [sb(g, st, "mc", [128, 64]) for _ in range(NL)]
        sn = [sb(g, st, "ms", [128, 64]) for _ in range(NL)]
        tmps = [sb(g, st, "mtmp", [128, 512]) for _ in range(NL)]
        sqs = [sb(g, st, "msq", [128, 512]) for _ in range(NL)]
        sss = [sb(g, st, "mss", [128, 8]) for _ in range(NL)]
        rstds = [sb(g, st, "mrs", [128, 8]) for _ in range(NL)]

        def lane(i):
          tmp, sq, ss, rstd = tmps[i], sqs[i], sss[i], rstds[i]
          for ti in range(i, g.NT, NL):
            rows = slice(ti * 128, (ti + 1) * 128)
            S.dma("sp", qin[i][:, :, :], g.QML[rows, :].rearrange("p (h d) -> p h d", h=4), writes=[qin[i].b])
            S.dma("sp", kvin[i][:, :, :], g.KVML[rows, :].rearrange("p (h d) -> p h d", h=4), writes=[kvin[i].b])
            S.dma("sp", krin[i][:, 0, :], g.Z[rows, O_KR:O_KR + 64], writes=[krin[i].b])
            rope = None
            if ti >= g.NTC:
                p0 = (ti - g.NTC) * 128
                S.dma("sp", cs[i][:, :], g.ropeM[0, p0:p0 + 128, :], writes=[cs[i].b])
                S.dma("sp", sn[i][:, :], g.ropeM[1, p0:p0 + 128, :], writes=[sn[i].b])
                rope = (cs[i], sn[i], 16)
            norm_rope(g, qin[i][:, :, 0:128], qin[i].b, 4, 128, gq_n[:, :].unsqueeze(1).broadcast_to([128, 4, 128]), gq_n.b,
                      xqn[i][:, :, :], xqn[i].b, tmp, sq, ss, rstd)
            norm_rope(g, qin[i][:, :, 128:192], qin[i].b, 4, 64, gq_r[:, :].unsqueeze(1).broadcast_to([128, 4, 64]), gq_r.b,
                      xqr[i][:, :, :], xqr[i].b, tmp, sq, ss, rstd, rope=rope)
            norm_rope(g, kvin[i][:, :, 0:128], kvin[i].b, 4, 128, gk_n[:, :].unsqueeze(1).broadcast_to([128, 4, 128]), gk_n.b,
                      xkn[i][:, :, :], xkn[i].b, tmp, sq, ss, rstd)
            norm_rope(g, krin[i][:, :, :], krin[i].b, 1, 64, gk_r[:, :].unsqueeze(1), gk_r.b,
                      xkr[i][:, :, :], xkr[i].b, tmp, sq, ss, rstd, rope=rope)
            transpose_slots(g, xqn[i], [0, 1, 2, 3], 128, tqn[i], 0)
            transpose_slots(g, xqr[i], [0, 1, 2, 3], 64, tqr[i], 0)
            transpose_slots(g, xkn[i], [0, 1, 2, 3], 128, tkn[i], 0)
            transpose_slots(g, xkr[i], [0], 64, tkr[i], 0)
            cols = slice(ti * 128, (ti + 1) * 128)
            S.dma("pool", g.MQN[:, :, cols].rearrange("s d t -> d s t"), tqn[i][:, :, :], reads=[tqn[i].b])
            S.dma("pool", g.MQR[:, :, cols].rearrange("s d t -> d s t"), tqr[i][:, :, :], reads=[tqr[i].b])
            S.dma("pool", g.MKN[:, :, cols].rearrange("s d t -> d s t"), tkn[i][:, :, :], reads=[tkn[i].b])
            S.dma("pool", g.MKR[:, :, cols].rearrange("s d t -> d s t"), tkr[i][:, :, :], reads=[tkr[i].b])
        S.run_lanes([(lambda i=i: lane(i)) for i in range(NL)])
    S.barrier()
    heads = []
    for h in range(4):
        heads.append(dict(kparts=[(g.MKN[h], 128), (g.MKR[0], 64)], qparts=[[(g.MQN[h], 128), (g.MQR[h], 64)]],
                          v=g.KVML[:, h * 256 + 128:h * 256 + 256], ycols=[1536 + h * 128], blocks=dense_blocks(g)))
    attention(g, heads, 192 ** -0.5)


RQ_R, RQ_V, RQ_LW, RQ_K, RQ_KK, RQ_KKA = 0, 1, 2, 3, 4, 5
RQ_SCAN = [RQ_R, RQ_LW, RQ_K, RQ_V, RQ_KK, RQ_KKA]
NRW = 2176


def stage_rwkv_prep(g, l):
    S = g.S
    NT, NTC = g.NT, g.NTC
    with ExitStack() as st:
        MU = [sb(g, st, "rmu", [128, NRW]) for _ in range(2)]
        w0 = [sb(g, st, "rw0", [128, 512]) for _ in range(2)]
        a0 = [sb(g, st, "ra0", [128, 512]) for _ in range(2)]
        w2 = [sb(g, st, "rw2", [96, 512]) for _ in range(2)]
        a2 = [sb(g, st, "ra2", [96, 512]) for _ in range(2)]
        g2 = sb(g, st, "rg2", [128, 2, 512])
        kk_ = sb(g, st, "rkk", [128, 512])
        ka = sb(g, st, "rka", [128, 512])
        omka = sb(g, st, "romka", [128, 512])
        for d in range(2):
            S.op("dve", lambda e: e.memset(MU[d][:, :], 0.0), writes=[MU[d].b])
            bcast_row(g, "sp", MU[d][:, 0:1792], MU[d].b, g.rwkv_mu[l, d, 0:1792])
            bcast_row(g, "sp", MU[d][:, 1792 + d * 96:1792 + (d + 1) * 96], MU[d].b, g.rwkv_mu[l, d, 1792:1888])
            bcast_row(g, "sp", MU[d][:, 1984 + d * 96:1984 + (d + 1) * 96], MU[d].b, g.rwkv_mu[l, d, 1888:1984])
            bcast_row(g, "sp", w0[d][:, :], w0[d].b, g.rwkv_w0[l, d, :])
            bcast_row(g, "sp", a0[d][:, :], a0[d].b, g.rwkv_a0[l, d, :])
            S.dma("sp", w2[d][:, :], g.rwkv_w2[l, d], writes=[w2[d].b])
            S.dma("sp", a2[d][:, :], g.rwkv_a2[l, d], writes=[a2[d].b])
        S.dma("sp", g2[:, :, :], g.rwkv_g2[l].rearrange("(kc p) n -> p kc n", p=128), writes=[g2.b])
        bcast_row(g, "sp", kk_[:, :], kk_.b, g.rwkv_k_k[l, :])
        bcast_row(g, "sp", ka[:, :], ka.b, g.rwkv_k_a[l, :])
        S.op("dve", lambda e: e.tensor_scalar(out=omka[:, :], in0=ka[:, :], scalar1=-1.0, scalar2=1.0, op0=ALU.mult, op1=ALU.add),
             reads=[ka.b], writes=[omka.b])
        LT = []
        for d in range(2):
            LT.append(dict(
                zc=[sb(g, st, "rzc", [128, NRW]) for _ in range(2)],
                zz=sb(g, st, "rzs", [128, NRW]),
                u=sb(g, st, "ru", [128, NRW]),
                tw=sb(g, st, "rtw", [128, 96]),
                sg=sb(g, st, "rsg", [128, 256]),
                tT=sb(g, st, "rtT", [128, 4, 128]),
                aa=sb(g, st, "raa", [128, 512]),
                t1=sb(g, st, "rt1", [128, 512]),
                t2=sb(g, st, "rt2", [128, 512]),
                ss=sb(g, st, "rss", [128, 8]),
                ob=[sb(g, st, "rob", [128, 4, 512]) for _ in range(2)],
                og=[sb(g, st, "rog", [128, 512]) for _ in range(2)]))

        def lane(d):
            lt = LT[d]
            zz, u, tw, sg, tT, aa, t1, t2, ss = (lt[k] for k in ["zz", "u", "tw", "sg", "tT", "aa", "t1", "t2", "ss"])
            for ti in range(NT):
                z = lt["zc"][ti % 2]
                r0 = ti * 128
                S.dma("sp", z[:, :], g.Z[r0:r0 + 128, O_RKVG:O_RKVG + NRW], writes=[z.b])
                if True:
                    if d == 0:
                        if ti in (0, NTC):
                            S.dma("sp", zz[1:128, :], g.Z[r0:r0 + 127, O_RKVG:O_RKVG + NRW], writes=[zz.b])
                            S.dma("sp", zz[0:1, :], g.zrow[0:1, :], writes=[zz.b])
                        else:
                            S.dma("sp", zz[:, :], g.Z[r0 - 1:r0 + 127, O_RKVG:O_RKVG + NRW], writes=[zz.b])
                    else:
                        if ti in (NTC - 1, NT - 1):
                            S.dma("sp", zz[0:127, :], g.Z[r0 + 1:r0 + 128, O_RKVG:O_RKVG + NRW], writes=[zz.b])
                            S.dma("sp", zz[127:128, :], g.zrow[0:1, :], writes=[zz.b])
                        else:
                            S.dma("sp", zz[:, :], g.Z[r0 + 1:r0 + 129, O_RKVG:O_RKVG + NRW], writes=[zz.b])
                    S.op("dve", lambda e: e.tensor_tensor(out=u[:, :], in0=zz[:, :], in1=z[:, :], op=ALU.subtract), reads=[zz.b, z.b], writes=[u.b])
                    S.op("dve", lambda e: e.tensor_tensor(out=u[:, :], in0=u[:, :], in1=MU[d][:, :], op=ALU.mult), reads=[u.b, MU[d].b], writes=[u.b])
                    S.op("dve", lambda e: e.tensor_tensor(out=u[:, :], in0=u[:, :], in1=z[:, :], op=ALU.add), reads=[u.b, z.b], writes=[u.b])
                    ur, uk, uv, ugd = u[:, 0:512], u[:, 512:1024], u[:, 1024:1536], u[:, 1536:1792]
                    uwd = u[:, 1792 + d * 96:1792 + (d + 1) * 96]
                    uad = u[:, 1984 + d * 96:1984 + (d + 1) * 96]
                    ob = lt["ob"][ti % 2]
                    o_g = lt["og"][ti % 2]

                    class _V:
                        def __init__(self, j):
                            self.j = j
                            self.b = ob.b

                        def __getitem__(self, idx):
                            return ob[:, self.j, :]
                    o_lw, o_k, o_kk, o_kka = _V(0), _V(1), _V(2), _V(3)
                    rows = slice(r0, r0 + 128)
                    S.dma("pool", g.RW[d, 0:2, rows, :].rearrange("q t c -> t q c"),
                          u[:, 0:2048].rearrange("p (a c) -> p a c", a=2)[:, :, 0:512], reads=[u.b])
                    S.op("act", lambda e: e.activation(out=tw[:, :], in_=uwd, func=AF.Tanh), reads=[u.b], writes=[tw.b])
                    S.op("act", lambda e: e.activation(out=sg[:, :], in_=ugd, func=AF.Sigmoid), reads=[u.b], writes=[sg.b])
                    ps = next_psum(g)
                    S.op("pe", lambda e: e.transpose(ps[0:96, 0:128], tw[:, :], g.ident[:]), reads=[tw.b, g.ident.b], writes=[ps.b])
                    S.op("pe", lambda e: e.transpose(ps[0:96, 128:256], uad, g.ident[:]), reads=[u.b, g.ident.b], writes=[ps.b])
                    copy_op(g, "act", tT[0:96, 0:2, :], ps[0:96, 0:256].rearrange("p (k t) -> p k t", k=2), [ps.b], [tT.b])
                    ps = next_psum(g)
                    S.op("pe", lambda e: e.transpose(ps[:, 0:128], sg[:, 0:128], g.ident[:]), reads=[sg.b, g.ident.b], writes=[ps.b])
                    S.op("pe", lambda e: e.transpose(ps[:, 128:256], sg[:, 128:256], g.ident[:]), reads=[sg.b, g.ident.b], writes=[ps.b])
                    copy_op(g, "dve", tT[:, 2:4, :], ps[:, 0:256].rearrange("p (k t) -> p k t", k=2), [ps.b], [tT.b])
                    psw = next_psum(g)
                    S.op("pe", lambda e: e.matmul(psw[:, :], lhsT=tT[0:96, 0, :], rhs=w2[d][:, :], start=True, stop=True), reads=[tT.b, w2[d].b], writes=[psw.b])
                    S.op("dve", lambda e: e.tensor_tensor(out=t1[:, :], in0=psw[:, :], in1=w0[d][:, :], op=ALU.add), reads=[psw.b, w0[d].b], writes=[t1.b])
                    S.op("act", lambda e: e.activation(out=t1[:, :], in_=t1[:, :], func=AF.Sigmoid), reads=[t1.b], writes=[t1.b])
                    S.op("act", lambda e: e.mul(out=o_lw[:, :], in_=t1[:, :], mul=-0.6065306597126334), reads=[t1.b], writes=[o_lw.b])
                    psa = next_psum(g)
                    S.op("pe", lambda e: e.matmul(psa[:, :], lhsT=tT[0:96, 1, :], rhs=a2[d][:, :], start=True, stop=True), reads=[tT.b, a2[d].b], writes=[psa.b])
                    S.op("dve", lambda e: e.tensor_tensor(out=aa[:, :], in0=psa[:, :], in1=a0[d][:, :], op=ALU.add), reads=[psa.b, a0[d].b], writes=[aa.b])
                    S.op("act", lambda e: e.activation(out=aa[:, :], in_=aa[:, :], func=AF.Sigmoid), reads=[aa.b], writes=[aa.b])
                    psg = next_psum(g)
                    for kc in range(2):
                        S.op("pe", lambda e: e.matmul(psg[:, :], lhsT=tT[:, 2 + kc, :], rhs=g2[:, kc, :], start=(kc == 0), stop=(kc == 1)),
                             reads=[tT.b, g2.b], writes=[psg.b])
                    copy_op(g, "act", o_g[:, :], psg[:, :], [psg.b], [o_g.b])
                    S.dma("pool", g.RG[d, rows, :], o_g[:, :], reads=[o_g.b])
                    S.op("dve", lambda e: e.tensor_tensor(out=t1[:, :], in0=uk, in1=kk_[:, :], op=ALU.mult), reads=[u.b, kk_.b], writes=[t1.b])
                    S.op("act", lambda e: e.activation(out=t2[:, :], in_=t1[:, :], func=AF.Square), reads=[t1.b], writes=[t2.b])
                    S.op("dve", lambda e: e.tensor_reduce(out=ss[:, 0:8], in_=t2[:, :].rearrange("p (h k) -> p h k", h=8), axis=AX.X, op=ALU.add),
                         reads=[t2.b], writes=[ss.b])
                    S.op("act", lambda e: e.activation(out=ss[:, 0:8], in_=ss[:, 0:8], func=AF.Sqrt), reads=[ss.b], writes=[ss.b])
                    S.op("dve", lambda e: e.tensor_scalar(out=ss[:, 0:8], in0=ss[:, 0:8], scalar1=1e-12, scalar2=None, op0=ALU.max), reads=[ss.b], writes=[ss.b])
                    S.op("dve", lambda e: e.reciprocal(out=ss[:, 0:8], in_=ss[:, 0:8]), reads=[ss.b], writes=[ss.b])
                    S.op("dve", lambda e: e.tensor_tensor(out=o_kk[:, :].rearrange("p (h k) -> p h k", h=8), in0=t1[:, :].rearrange("p (h k) -> p h k", h=8),
                                                          in1=ss[:, 0:8].unsqueeze(2).broadcast_to([128, 8, 64]), op=ALU.mult),
                         reads=[t1.b, ss.b], writes=[o_kk.b])
                    S.op("dve", lambda e: e.tensor_tensor(out=o_kka[:, :], in0=o_kk[:, :], in1=aa[:, :], op=ALU.mult), reads=[o_kk.b, aa.b], writes=[o_kka.b])
                    S.op("dve", lambda e: e.tensor_tensor(out=t2[:, :], in0=aa[:, :], in1=ka[:, :], op=ALU.mult), reads=[aa.b, ka.b], writes=[t2.b])
                    S.op("dve", lambda e: e.tensor_tensor(out=t2[:, :], in0=t2[:, :], in1=omka[:, :], op=ALU.add), reads=[t2.b, omka.b], writes=[t2.b])
                    S.op("dve", lambda e: e.tensor_tensor(out=o_k[:, :], in0=t2[:, :], in1=uk, op=ALU.mult), reads=[t2.b, u.b], writes=[o_k.b])
                    S.dma("pool", g.RW[d, 2:6, rows, :].rearrange("q t c -> t q c"), ob[:, :, :], reads=[ob.b])
        S.run_lanes([(lambda d=d: lane(d)) for d in range(2)])
    S.barrier()


def stage_rwkv_scan(g, l):
    S = g.S
    C = 64
    nch = g.T // C
    nch_c = g.LC // C
    ident64 = g.ident[0:64, 0:64]

    def mm(ps_ap, psb, lhsT, rhs, reads, start=True, stop=True):
        S.op("pe", lambda e: e.matmul(ps_ap, lhsT=lhsT, rhs=rhs, start=start, stop=stop), reads=reads, writes=[psb])

    with ExitStack() as st:
        names = ["Lsb", "eL", "enL", "eLx", "eEnd", "Rt", "Kt", "Kk", "Ak", "Kh", "Ah"]
        LT = []
        for d in range(2):
            LT.append(dict(
                ST=sb(g, st, "sST", [64, 8, 64]),
                qin=[sb(g, st, "sq", [64, 512]) for _ in range(6)],
                v2=sb(g, st, "sv2", [128, 512]),
                W={n: sb(g, st, "s" + n, [64, 512]) for n in names},
                pCT=sb(g, st, "spCT", [64, 8]),
                XT1=sb(g, st, "sXT1", [64, 8, 128]), XT2=sb(g, st, "sXT2", [64, 8, 128]),
                PRs=sb(g, st, "sPRs", [128, 8, 128]),
                Nn=[sb(g, st, "sN", [64, 8, 64]) for _ in range(2)],
                NTt=[sb(g, st, "sNT", [64, 8, 64]) for _ in range(2)],
                Qq=[sb(g, st, "sQ", [64, 8, 64]) for _ in range(2)],
                W1T=sb(g, st, "sW1T", [64, 8, 64]), MV=sb(g, st, "sMV", [64, 8, 64]), W2=sb(g, st, "sW2", [64, 8, 64]),
                Y0=sb(g, st, "sY0", [64, 8, 64]), D0=sb(g, st, "sD0", [64, 8, 64]), U=sb(g, st, "sU", [64, 8, 64]),
                Yo=[sb(g, st, "sYo", [64, 512]) for _ in range(2)], tmp=sb(g, st, "stmp", [64, 8, 64])))
            S.op("dve", lambda e: e.memset(LT[d]["ST"][:, :, :], 0.0), writes=[LT[d]["ST"].b])

        def lane(d):
            lt = LT[d]
            W, pCT, XT1, XT2, PRs, Nn, NTt, Qq = (lt[k] for k in ["W", "pCT", "XT1", "XT2", "PRs", "Nn", "NTt", "Qq"])
            W1T, MV, W2, Y0, D0, U, Yo, tmp = (lt[k] for k in ["W1T", "MV", "W2", "Y0", "D0", "U", "Yo", "tmp"])
            ST = {d: lt["ST"]}
            it = 0
            if d == 0:
                order = list(range(nch))
            else:
                order = list(range(nch_c - 1, -1, -1)) + list(range(nch - 1, nch_c - 1, -1))
            tri = g.tri[0:64, d, :]
            mk = g.mk[:, d, :]
            nmts = g.nmts[0:64, d, :]
            for c in order:
                it += 1
                q = lt["qin"]
                vv = lt["v2"]
                rows = slice(c * C, (c + 1) * C)
                for qi in range(6):
                    S.dma("sp", q[qi][:, :], g.RW[d, RQ_SCAN[qi], rows, :], writes=[q[qi].b])
                S.dma("sp", vv[64:128, :], g.RW[d, RQ_V, rows, :], writes=[vv.b])
                r_, lw, k_, v_, kk, kka = q
                psL = next_psum(g)
                mm(psL[0:64, :], psL.b, tri, lw[:, :], [g.tri.b, lw.b])
                psE = next_psum(g)
                mm(psE[0:64, :], psE.b, g.ones64[0:64, :], lw[:, :], [g.ones64.b, lw.b])
                psP = next_psum(g)
                for h in range(8):
                    mm(psP[0:64, h:h + 1], psP.b, lw[:, h * 64:(h + 1) * 64], g.ones64[0:64, 0:1], [lw.b, g.ones64.b])
                S.op("act", lambda e: e.activation(out=pCT[:, :], in_=psP[0:64, 0:8], func=AF.Exp), reads=[psP.b], writes=[pCT.b])
                S.op("act", lambda e: e.activation(out=W["eL"][:, :], in_=psL[0:64, :], func=AF.Exp), reads=[psL.b], writes=[W["eL"].b])
                S.op("act", lambda e: e.activation(out=W["enL"][:, :], in_=psL[0:64, :], func=AF.Exp, scale=-1.0), reads=[psL.b], writes=[W["enL"].b])
                S.op("dve", lambda e: e.tensor_tensor(out=W["eLx"][:, :], in0=psL[0:64, :], in1=lw[:, :], op=ALU.subtract), reads=[psL.b, lw.b], writes=[W["eLx"].b])
                S.op("act", lambda e: e.activation(out=W["eLx"][:, :], in_=W["eLx"][:, :], func=AF.Exp), reads=[W["eLx"].b], writes=[W["eLx"].b])
                copy_op(g, "dve", W["Lsb"][:, :], psL[0:64, :], [psL.b], [W["Lsb"].b])
                S.op("dve", lambda e: e.tensor_tensor(out=W["eEnd"][:, :], in0=psE[0:64, :], in1=W["Lsb"][:, :], op=ALU.subtract),
                     reads=[psE.b, W["Lsb"].b], writes=[W["eEnd"].b])
                S.op("act", lambda e: e.activation(out=W["eEnd"][:, :], in_=W["eEnd"][:, :], func=AF.Exp), reads=[W["eEnd"].b], writes=[W["eEnd"].b])
                for (o, a, b) in [("Rt", r_, "eL"), ("Kt", kk, "eLx"), ("Kk", k_, "enL"), ("Ak", kka, "enL"), ("Kh", k_, "eEnd"), ("Ah", kka, "eEnd")]:
                    S.op("dve", lambda e: e.tensor_tensor(out=W[o][:, :], in0=a[:, :], in1=W[b][:, :], op=ALU.mult), reads=[a.b, W[b].b], writes=[W[o].b])
                for (XT, n0, n1) in [(XT1, "Ak", "Kk"), (XT2, "Kt", "Rt")]:
                    for hg in range(2):
                        ps = next_psum(g)
                        for hh in range(4):
                            h = hg * 4 + hh
                            for j, nm in enumerate([n0, n1]):
                                S.op("pe", lambda e: e.transpose(ps[0:64, hh * 128 + j * 64:hh * 128 + (j + 1) * 64], W[nm][:, h * 64:(h + 1) * 64], ident64),
                                     reads=[W[nm].b, g.ident.b], writes=[ps.b])
                        copy_op(g, evac_engine(g), XT[:, hg * 4:(hg + 1) * 4, :], ps[0:64, :].rearrange("p (h x) -> p h x", h=4), [ps.b], [XT.b])
                for hg in range(2):
                    ps = next_psum(g)
                    for hh in range(4):
                        h = hg * 4 + hh
                        mm(ps[:, hh * 128:(hh + 1) * 128], ps.b, XT1[:, h, :], XT2[:, h, :], [XT1.b, XT2.b])
                    S.op("dve", lambda e: e.tensor_tensor(out=PRs[:, hg * 4:(hg + 1) * 4, :], in0=ps[:, :].rearrange("p (h x) -> p h x", h=4),
                                                          in1=mk.unsqueeze(1).broadcast_to([128, 4, 128]), op=ALU.mult),
                         reads=[ps.b, g.mk.b], writes=[PRs.b])
                ps = next_psum(g)
                for h in range(8):
                    mm(ps[0:64, h * 64:(h + 1) * 64], ps.b, XT2[:, h, 0:64], XT1[:, h, 0:64], [XT1.b, XT2.b])
                N, NT_, Q = Nn[0], NTt[0], Qq[0]
                S.op("dve", lambda e: e.tensor_tensor(out=NT_[:, :, :], in0=ps[0:64, :].rearrange("p (h x) -> p h x", h=8),
                                                      in1=nmts.unsqueeze(1).broadcast_to([64, 8, 64]), op=ALU.mult),
                     reads=[ps.b, g.nmts.b], writes=[NT_.b])
                S.op("dve", lambda e: e.tensor_scalar(out=N[:, :, :], in0=PRs[0:64, :, 0:64], scalar1=-1.0, scalar2=None, op0=ALU.mult), reads=[PRs.b], writes=[N.b])
                S.op("dve", lambda e: e.tensor_tensor(out=Q[:, :, :], in0=N[:, :, :], in1=ident64.unsqueeze(1).broadcast_to([64, 8, 64]), op=ALU.add),
                     reads=[N.b, g.ident.b], writes=[Q.b])
                cur = 0
                for lev in range(5):
                    N, NT_, Q = Nn[cur], NTt[cur], Qq[cur]
                    N2, NT2, Q2 = Nn[1 - cur], NTt[1 - cur], Qq[1 - cur]
                    psn = next_psum(g)
                    pst = next_psum(g)
                    for h in range(8):
                        if lev < 4:
                            mm(psn[0:64, h * 64:(h + 1) * 64], psn.b, NT_[:, h, :], N[:, h, :], [NT_.b, N.b])
                        mm(pst[0:64, h * 64:(h + 1) * 64], pst.b, N[:, h, :], NT_[:, h, :], [NT_.b, N.b])
                    if lev < 4:
                        copy_op(g, "act", N2[:, :, :], psn[0:64, :].rearrange("p (h x) -> p h x", h=8), [psn.b], [N2.b])
                    copy_op(g, "dve", NT2[:, :, :], pst[0:64, :].rearrange("p (h x) -> p h x", h=8), [pst.b], [NT2.b])
                    psq = next_psum(g)
                    for h in range(8):
                        mm(psq[0:64, h * 64:(h + 1) * 64], psq.b, NT2[:, h, :], Q[:, h, :], [NT2.b, Q.b])
                    S.op("dve", lambda e: e.tensor_tensor(out=Q2[:, :, :], in0=psq[0:64, :].rearrange("p (h x) -> p h x", h=8), in1=Q[:, :, :], op=ALU.add),
                         reads=[psq.b, Q.b], writes=[Q2.b])
                    cur = 1 - cur
                Q = Qq[cur]
                ps1 = next_psum(g)
                ps2 = next_psum(g)
                ps3 = next_psum(g)
                ps4 = next_psum(g)
                for h in range(8):
                    hs = slice(h * 64, (h + 1) * 64)
                    mm(ps1[0:64, hs], ps1.b, W["Kt"][:, hs], Q[:, h, :], [W["Kt"].b, Q.b])
                    mm(ps2[0:64, hs], ps2.b, PRs[64:128, h, 0:64], vv[64:128, hs], [PRs.b, vv.b])
                    mm(ps3[0:64, hs], ps3.b, PRs[64:128, h, 64:128], vv[64:128, hs], [PRs.b, vv.b])
                    mm(ps4[0:64, hs], ps4.b, W["Kh"][:, hs], v_[:, hs], [W["Kh"].b, v_.b])
                copy_op(g, "act", W1T[:, :, :], ps1[0:64, :].rearrange("p (h x) -> p h x", h=8), [ps1.b], [W1T.b])
                copy_op(g, "dve", MV[:, :, :], ps2[0:64, :].rearrange("p (h x) -> p h x", h=8), [ps2.b], [MV.b])
                copy_op(g, "act", Y0[:, :, :], ps3[0:64, :].rearrange("p (h x) -> p h x", h=8), [ps3.b], [Y0.b])
                copy_op(g, "dve", D0[:, :, :], ps4[0:64, :].rearrange("p (h x) -> p h x", h=8), [ps4.b], [D0.b])
                ps5 = next_psum(g)
                for h in range(8):
                    mm(ps5[0:64, h * 64:(h + 1) * 64], ps5.b, Q[:, h, :], MV[:, h, :], [Q.b, MV.b])
                copy_op(g, "act", W2[:, :, :], ps5[0:64, :].rearrange("p (h x) -> p h x", h=8), [ps5.b], [W2.b])
                st_ = ST[d]
                psu = next_psum(g)
                for h in range(8):
                    mm(psu[0:64, h * 64:(h + 1) * 64], psu.b, W1T[:, h, :], st_[:, h, :], [W1T.b, st_.b])
                S.op("dve", lambda e: e.scalar_tensor_tensor(out=U[:, :, :], in0=psu[0:64, :].rearrange("p (h x) -> p h x", h=8), scalar=-1.0,
                                                             in1=W2[:, :, :], op0=ALU.mult, op1=ALU.subtract),
                     reads=[psu.b, W2.b], writes=[U.b])
                psy = next_psum(g)
                for h in range(8):
                    hs = slice(h * 64, (h + 1) * 64)
                    mm(psy[0:64, hs], psy.b, XT2[:, h, 64:128], st_[:, h, :], [XT2.b, st_.b], start=True, stop=False)
                    mm(psy[0:64, hs], psy.b, PRs[0:64, h, 64:128], U[:, h, :], [PRs.b, U.b], start=False, stop=True)
                yo = Yo[it % 2]
                S.op("dve", lambda e: e.tensor_tensor(out=yo[:, :], in0=psy[0:64, :], in1=Y0[:, :, :].rearrange("p h x -> p (h x)"), op=ALU.add),
                     reads=[psy.b, Y0.b], writes=[yo.b])
                S.dma("pool", g.YR[d, rows, :], yo[:, :], reads=[yo.b])
                psd = next_psum(g)
                for h in range(8):
                    hs = slice(h * 64, (h + 1) * 64)
                    mm(psd[0:64, hs], psd.b, W["Ah"][:, hs], U[:, h, :], [W["Ah"].b, U.b])
                S.op("dve", lambda e: e.tensor_tensor(out=tmp[:, :, :], in0=st_[:, :, :], in1=pCT[:, 0:8].unsqueeze(2).broadcast_to([64, 8, 64]), op=ALU.mult),
                     reads=[st_.b, pCT.b], writes=[tmp.b])
                S.op("dve", lambda e: e.tensor_tensor(out=tmp[:, :, :], in0=tmp[:, :, :], in1=D0[:, :, :], op=ALU.add), reads=[tmp.b, D0.b], writes=[tmp.b])
                S.op("dve", lambda e: e.tensor_tensor(out=st_[:, :, :], in0=psd[0:64, :].rearrange("p (h x) -> p h x", h=8), in1=tmp[:, :, :], op=ALU.add),
                     reads=[psd.b, tmp.b], writes=[st_.b])
        S.run_lanes([(lambda d=d: lane(d)) for d in range(2)])
    S.barrier()


def stage_rwkv_out(g, l):
    S = g.S
    with ExitStack() as st:
        lw_ = sb(g, st, "olw", [128, 512])
        lb_ = sb(g, st, "olb", [128, 512])
        rk_ = sb(g, st, "ork", [128, 512])
        bcast_row(g, "sp", lw_[:, :], lw_.b, g.rwkv_lnx_w[l, :])
        bcast_row(g, "sp", lb_[:, :], lb_.b, g.rwkv_lnx_b[l, :])
        bcast_row(g, "sp", rk_[:, :], rk_.b, g.rwkv_r_k[l].rearrange("h k -> (h k)"))
        NL = 4
        LT = [dict(ins=[[sb(g, st, "oin", [128, 512]) for _ in range(5)] for _ in range(2)],
                   t1=sb(g, st, "ot1", [128, 512]), t2=sb(g, st, "ot2", [128, 512]),
                   acc=[sb(g, st, "oacc", [128, 512]) for _ in range(2)],
                   ss=sb(g, st, "oss", [128, 8]), s2=sb(g, st, "os2", [128, 8])) for _ in range(NL)]
        h8 = lambda ap: ap.rearrange("p (h k) -> p h k", h=8)
        b8 = lambda t_: t_[:, 0:8].unsqueeze(2).broadcast_to([128, 8, 64])

        def lane(li):
          lt = LT[li]
          ins, t1, t2, acc, ss, s2 = (lt[k] for k in ["ins", "t1", "t2", "acc", "ss", "s2"])
          it = 0
          for kk_i, ti in enumerate(range(li, g.NT, NL)):
            rows = slice(ti * 128, (ti + 1) * 128)
            a_ = acc[kk_i % 2]
            for d in range(2):
                it += 1
                y, r_, k_, v_, g_ = ins[it % 2]
                S.dma("sp", y[:, :], g.YR[d, rows, :], writes=[y.b])
                S.dma("sp", r_[:, :], g.RW[d, RQ_R, rows, :], writes=[r_.b])
                S.dma("sp", k_[:, :], g.RW[d, RQ_K, rows, :], writes=[k_.b])
                S.dma("sp", v_[:, :], g.RW[d, RQ_V, rows, :], writes=[v_.b])
                S.dma("sp", g_[:, :], g.RG[d, rows, :], writes=[g_.b])
                S.op("dve", lambda e: e.tensor_reduce(out=ss[:, 0:8], in_=h8(y[:, :]), axis=AX.X, op=ALU.add), reads=[y.b], writes=[ss.b])
                S.op("dve", lambda e: e.tensor_scalar(out=ss[:, 0:8], in0=ss[:, 0:8], scalar1=-1.0 / 64, scalar2=None, op0=ALU.mult), reads=[ss.b], writes=[ss.b])
                S.op("dve", lambda e: e.tensor_tensor(out=h8(t1[:, :]), in0=h8(y[:, :]), in1=b8(ss), op=ALU.add), reads=[y.b, ss.b], writes=[t1.b])
                S.op("act", lambda e: e.activation(out=t2[:, :], in_=t1[:, :], func=AF.Square), reads=[t1.b], writes=[t2.b])
                S.op("dve", lambda e: e.tensor_reduce(out=s2[:, 0:8], in_=h8(t2[:, :]), axis=AX.X, op=ALU.add), reads=[t2.b], writes=[s2.b])
                S.op("dve", lambda e: e.tensor_scalar(out=s2[:, 0:8], in0=s2[:, 0:8], scalar1=1.0 / 64, scalar2=LNX_EPS, op0=ALU.mult, op1=ALU.add),
                     reads=[s2.b], writes=[s2.b])
                S.op("act", lambda e: e.activation(out=s2[:, 0:8], in_=s2[:, 0:8], func=AF.Sqrt), reads=[s2.b], writes=[s2.b])
                S.op("dve", lambda e: e.reciprocal(out=s2[:, 0:8], in_=s2[:, 0:8]), reads=[s2.b], writes=[s2.b])
                S.op("dve", lambda e: e.tensor_tensor(out=h8(t1[:, :]), in0=h8(t1[:, :]), in1=b8(s2), op=ALU.mult), reads=[t1.b, s2.b], writes=[t1.b])
                S.op("dve", lambda e: e.tensor_tensor(out=t1[:, :], in0=t1[:, :], in1=lw_[:, :], op=ALU.mult), reads=[t1.b, lw_.b], writes=[t1.b])
                S.op("dve", lambda e: e.tensor_tensor(out=t1[:, :], in0=t1[:, :], in1=lb_[:, :], op=ALU.add), reads=[t1.b, lb_.b], writes=[t1.b])
                S.op("pool", lambda e: e.tensor_tensor(out=t2[:, :], in0=r_[:, :], in1=k_[:, :], op=ALU.mult), reads=[r_.b, k_.b], writes=[t2.b])
                S.op("pool", lambda e: e.tensor_tensor(out=t2[:, :], in0=t2[:, :], in1=rk_[:, :], op=ALU.mult), reads=[t2.b, rk_.b], writes=[t2.b])
                S.op("dve", lambda e: e.tensor_reduce(out=ss[:, 0:8], in_=h8(t2[:, :]), axis=AX.X, op=ALU.add), reads=[t2.b], writes=[ss.b])
                S.op("dve", lambda e: e.tensor_tensor(out=h8(t2[:, :]), in0=h8(v_[:, :]), in1=b8(ss), op=ALU.mult), reads=[v_.b, ss.b], writes=[t2.b])
                S.op("dve", lambda e: e.tensor_tensor(out=t1[:, :], in0=t1[:, :], in1=t2[:, :], op=ALU.add), reads=[t1.b, t2.b], writes=[t1.b])
                if d == 0:
                    S.op("dve", lambda e: e.tensor_tensor(out=a_[:, :], in0=t1[:, :], in1=g_[:, :], op=ALU.mult), reads=[t1.b, g_.b], writes=[a_.b])
                else:
                    S.op("dve", lambda e: e.tensor_tensor(out=t1[:, :], in0=t1[:, :], in1=g_[:, :], op=ALU.mult), reads=[t1.b, g_.b], writes=[t1.b])
                    S.op("dve", lambda e: e.tensor_tensor(out=a_[:, :], in0=a_[:, :], in1=t1[:, :], op=ALU.add), reads=[a_.b, t1.b], writes=[a_.b])
            S.dma("pool", g.YMIX[rows, 1024:1536], a_[:, :], reads=[a_.b])
        S.run_lanes([(lambda li=li: lane(li)) for li in range(NL)])
    S.barrier()


def stage_merge(g, l):
    S = g.S
    NT = g.NT
    with ExitStack() as st:
        wf = sb(g, st, "mwf", [128, 16, D], BF16)
        wv = g.w_branch[l].rearrange("b k n -> (b k) n").rearrange("(kc p) n -> p kc n", p=128)
        for q4 in range(4):
            S.dma("pool", wf[:, q4 * 4:(q4 + 1) * 4, :], wv[:, q4 * 4:(q4 + 1) * 4, :], writes=[wf.b])
        xin = [sb(g, st, "mxin", [128, D]) for _ in range(2)]
        xt = [sb(g, st, "mxt", [128, 16, 128], BF16) for _ in range(2)]
        gs = [sb(g, st, "mgs", [128, 4, D]) for _ in range(2)]
        orow = [sb(g, st, "morow", [128, D]) for _ in range(2)]
        t1 = sb(g, st, "mt1", [128, 512])

        def prefetch(ti):
            rows = slice(ti * 128, (ti + 1) * 128)
            S.dma("sp", xin[ti % 2][:, :], g.YMIX[rows, :], writes=[xin[ti % 2].b])
            gq = gs[ti % 2]
            S.dma("sp", gq[:, :, :], g.Z[rows, O_GATE:O_GATE + 4 * D].rearrange("t (i n) -> t i n", i=4), writes=[gq.b])
            for i in range(4):
                S.op("act", lambda e: e.activation(out=gq[:, i, :], in_=gq[:, i, :], func=AF.Sigmoid), reads=[gq.b], writes=[gq.b])
        prefetch(0)
        for ti in range(NT):
            if ti + 1 < NT:
                prefetch(ti + 1)
            xi, x_t, gq, o = xin[ti % 2], xt[ti % 2], gs[ti % 2], orow[ti % 2]
            for k4 in range(0, 16, 4):
                ps = next_psum(g)
                for j in range(4):
                    S.op("pe", lambda e: e.transpose(ps[:, j * 128:(j + 1) * 128], xi[:, (k4 + j) * 128:(k4 + j + 1) * 128], g.ident[:]),
                         reads=[xi.b, g.ident.b], writes=[ps.b])
                copy_op(g, "dve" if (k4 // 4) % 2 else "act", x_t[:, k4:k4 + 4, :], ps[:, :].rearrange("p (k t) -> p k t", k=4), [ps.b], [x_t.b])
            for n0 in range(0, D, 512):
                pss = []
                for i in range(4):
                    ps = next_psum(g)
                    pss.append(ps)
                    for kc in range(4 * i, 4 * i + 4):
                        S.op("pe", lambda e: e.matmul(ps[:, :], lhsT=x_t[:, kc, :], rhs=wf[:, kc, n0:n0 + 512], start=(kc == 4 * i), stop=(kc == 4 * i + 3)),
                             reads=[x_t.b, wf.b], writes=[ps.b])
                ob = o[:, n0:n0 + 512]
                S.op("dve", lambda e: e.tensor_tensor(out=ob, in0=pss[0][:, :], in1=gq[:, 0, n0:n0 + 512], op=ALU.mult), reads=[pss[0].b, gq.b], writes=[o.b])
                for i in range(1, 4):
                    S.op("dve", lambda e: e.tensor_tensor(out=t1[:, :], in0=pss[i][:, :], in1=gq[:, i, n0:n0 + 512], op=ALU.mult), reads=[pss[i].b, gq.b], writes=[t1.b])
                    S.op("dve", lambda e: e.tensor_tensor(out=ob, in0=ob, in1=t1[:, :], op=ALU.add), reads=[o.b, t1.b], writes=[o.b])
            S.dma("pool", g.MERGED[ti * 128:(ti + 1) * 128, :], o[:, :], reads=[o.b])
    S.barrier()


def stage_conv(g, l):
    S = g.S
    NT, NTC = g.NT, g.NTC
    NL = 4
    blocks = list(range(0, DFF, 512))
    with ExitStack() as st:
        LT = [dict(cw=sb(g, st, "ccw", [128, 3, 512]), cb=sb(g, st, "ccb", [128, 512]),
                   aw=[sb(g, st, "caw", [128, 512]) for _ in range(4)],
                   bw=[sb(g, st, "cbw", [128, 512]) for _ in range(2)],
                   acc=sb(g, st, "cacc", [128, 512]), t1=sb(g, st, "ct1", [128, 512]),
                   t2=sb(g, st, "ct2", [128, 512]), xb=sb(g, st, "cxb", [128, 512]),
                   out=[sb(g, st, "cout", [128, 512]) for _ in range(2)]) for _ in range(NL)]

        def lane(li):
            lt = LT[li]
            w_, b_, aw, bw, acc, t1 = lt["cw"], lt["cb"], lt["aw"], lt["bw"], lt["acc"], lt["t1"]
            lq = "sp" if li % 2 == 0 else "act"
            for n0 in blocks[li::NL]:
                for j in range(3):
                    bcast_row(g, lq, w_[:, j, :], w_.b, g.ffn_conv_w[l, j, n0:n0 + 512])
                bcast_row(g, lq, b_[:, :], b_.b, g.ffn_conv_b[l, n0:n0 + 512])
                for t0 in range(min(2, NT)):
                    S.dma(lq, aw[t0 % 4][:, :], g.AB[t0 * 128:(t0 + 1) * 128, n0:n0 + 512], writes=[aw[t0 % 4].b])
                for ti in range(NT):
                    r0 = ti * 128
                    if ti + 2 < NT:
                        t2_ = ti + 2
                        S.dma(lq, aw[t2_ % 4][:, :], g.AB[t2_ * 128:(t2_ + 1) * 128, n0:n0 + 512], writes=[aw[t2_ % 4].b])
                    bb = bw[ti % 2]
                    S.dma(lq, bb[:, :], g.AB[r0:r0 + 128, DFF + n0:DFF + n0 + 512], writes=[bb.b])
                    ac_ = aw[ti % 4]
                    has_prev = ti not in (0, NTC)
                    has_next = ti not in (NTC - 1, NT - 1)
                    psP = next_psum(g)
                    S.op("pe", lambda e: e.matmul(psP[:, :], lhsT=g.shm[:, 0, :], rhs=ac_[:, :], start=True, stop=not has_prev),
                         reads=[g.shm.b, ac_.b], writes=[psP.b])
                    if has_prev:
                        ap_ = aw[(ti - 1) % 4]
                        S.op("pe", lambda e: e.matmul(psP[:, :], lhsT=g.shm[:, 1, :], rhs=ap_[:, :], start=False, stop=True),
                             reads=[g.shm.b, ap_.b], writes=[psP.b])
                    psN = next_psum(g)
                    S.op("pe", lambda e: e.matmul(psN[:, :], lhsT=g.shm[:, 2, :], rhs=ac_[:, :], start=True, stop=not has_next),
                         reads=[g.shm.b, ac_.b], writes=[psN.b])
                    if has_next:
                        an_ = aw[(ti + 1) % 4]
                        S.op("pe", lambda e: e.matmul(psN[:, :], lhsT=g.shm[:, 3, :], rhs=an_[:, :], start=False, stop=True),
                             reads=[g.shm.b, an_.b], writes=[psN.b])
                    t2 = lt["t2"]
                    xb = lt["xb"]
                    S.op("dve", lambda e: e.tensor_tensor(out=acc[:, :], in0=psP[:, :], in1=w_[:, 0, :], op=ALU.mult), reads=[psP.b, w_.b], writes=[acc.b])
                    S.op("pool", lambda e: e.tensor_tensor(out=t1[:, :], in0=ac_[:, :], in1=w_[:, 1, :], op=ALU.mult), reads=[ac_.b, w_.b], writes=[t1.b])
                    S.op("pool", lambda e: e.tensor_tensor(out=t1[:, :], in0=t1[:, :], in1=b_[:, :], op=ALU.add), reads=[t1.b, b_.b], writes=[t1.b])
                    S.op("dve", lambda e: e.tensor_tensor(out=t2[:, :], in0=psN[:, :], in1=w_[:, 2, :], op=ALU.mult), reads=[psN.b, w_.b], writes=[t2.b])
                    S.op("dve", lambda e: e.tensor_tensor(out=acc[:, :], in0=acc[:, :], in1=t2[:, :], op=ALU.add), reads=[acc.b, t2.b], writes=[acc.b])
                    S.op("dve", lambda e: e.tensor_tensor(out=acc[:, :], in0=acc[:, :], in1=t1[:, :], op=ALU.add), reads=[acc.b, t1.b], writes=[acc.b])
                    S.op("pool", lambda e: e.tensor_tensor(out=xb[:, :], in0=acc[:, :], in1=bb[:, :], op=ALU.mult), reads=[acc.b, bb.b], writes=[xb.b])
                    S.op("act", lambda e: e.activation(out=t2[:, :], in_=acc[:, :], func=AF.Square), reads=[acc.b], writes=[t2.b])
                    S.op("dve", lambda e: e.tensor_scalar(out=t2[:, :], in0=t2[:, :], scalar1=0.044715, scalar2=1.0, op0=ALU.mult, op1=ALU.add), reads=[t2.b], writes=[t2.b])
                    S.op("dve", lambda e: e.tensor_tensor(out=t2[:, :], in0=t2[:, :], in1=acc[:, :], op=ALU.mult), reads=[t2.b, acc.b], writes=[t2.b])
                    S.op("act", lambda e: e.activation(out=t2[:, :], in_=t2[:, :], func=AF.Sigmoid, scale=1.5957691216057308), reads=[t2.b], writes=[t2.b])
                    o = lt["out"][ti % 2]
                    S.op("dve", lambda e: e.tensor_tensor(out=o[:, :], in0=t2[:, :], in1=xb[:, :], op=ALU.mult), reads=[t2.b, xb.b], writes=[o.b])
                    S.dma("pool", g.GG[r0:r0 + 128, n0:n0 + 512], o[:, :], reads=[o.b])
        S.run_lanes([(lambda li=li: lane(li)) for li in range(NL)])
    S.barrier()


def build(cfg):
    TL, LC, DEPTH = cfg["TL"], cfg["LC"], cfg["DEPTH"]
    T = TL + LC
    nc = bass.Bass("TRN2", target_bir_lowering=False)
    g = Ctx()
    g.nc = nc
    g.uid = 0
    g.ev = 0
    g.pi = 0
    g.T, g.TL, g.LC, g.NT, g.NTC = T, TL, LC, T // 128, LC // 128
    g.S = S = Sync(nc)

    def din(name, shape):
        return nc.dram_tensor(name, list(shape), F32, kind="ExternalInput").ap()

    def dscr(name, shape):
        return nc.dram_tensor(name, list(shape), F32, kind="Internal").ap()

    L = DEPTH
    g.x_in = din("x", [TL, D])
    g.ctx_in = din("ctx", [LC, D])
    g.cvec = din("cvec", [2, D])
    g.ada_w = din("ada_w", [L, D, 6 * D])
    g.ada_b = din("ada_b", [L, 6 * D])
    g.norm1_g = din("norm1_g", [L, D])
    g.norm2_g = din("norm2_g", [L, D])
    g.w_in = din("w_in", [L, D, IN_W])
    g.identd = din("ident", [128, 128])
    g.ga_q_norm = din("ga_q_norm", [L, 128])
    g.ga_k_norm = din("ga_k_norm", [L, 128])
    g.wa_q_norm = din("wa_q_norm", [L, 128])
    g.wa_k_norm = din("wa_k_norm", [L, 128])
    g.wa_sink = din("wa_sink", [L, 4])
    g.mla_cq_norm = din("mla_cq_norm", [L, 384])
    g.mla_ckv_norm = din("mla_ckv_norm", [L, 512])
    g.mla_w_uq = din("mla_w_uq", [L, 384, 768])
    g.mla_w_ukv = din("mla_w_ukv", [L, 512, 1024])
    g.mla_qn_norm = din("mla_qn_norm", [L, 128])
    g.mla_qr_norm = din("mla_qr_norm", [L, 64])
    g.mla_kn_norm = din("mla_kn_norm", [L, 128])
    g.mla_kr_norm = din("mla_kr_norm", [L, 64])
    g.rwkv_mu = din("rwkv_mu", [L, 2, 1984])
    g.rwkv_w0 = din("rwkv_w0", [L, 2, 512])
    g.rwkv_w2 = din("rwkv_w2", [L, 2, 96, 512])
    g.rwkv_a0 = din("rwkv_a0", [L, 2, 512])
    g.rwkv_a2 = din("rwkv_a2", [L, 2, 96, 512])
    g.rwkv_g2 = din("rwkv_g2", [L, 256, 512])
    g.rwkv_k_k = din("rwkv_k_k", [L, 512])
    g.rwkv_k_a = din("rwkv_k_a", [L, 512])
    g.rwkv_r_k = din("rwkv_r_k", [L, 8, 64])
    g.rwkv_lnx_w = din("rwkv_lnx_w", [L, 512])
    g.rwkv_lnx_b = din("rwkv_lnx_b", [L, 512])
    g.w_branch = din("w_branch", [L, 4, 512, D])
    g.w_out = din("w_out", [L, D, D])
    g.ffn_up = din("ffn_up", [L, D, 2 * DFF])
    g.ffn_conv_w = din("ffn_conv_w", [L, 3, DFF])
    g.ffn_conv_b = din("ffn_conv_b", [L, DFF])
    g.ffn_down = din("ffn_down", [L, DFF, D])
    g.trid = din("tri", [64, 2, 64])
    g.mkd = din("mk", [128, 2, 128])
    g.nmtsd = din("nmts", [64, 2, 64])
    g.zrow = din("zrow", [1, NRW])
    g.shmd = din("shm", [128, 4, 128])
    g.ropeA = din("ropeA", [2, TL, 128])
    g.ropeM = din("ropeM", [2, TL, 64])
    g.wmaskd = din("wmask", [128, 2, 128])
    g.y_out = nc.dram_tensor("y", [TL, D], F32, kind="ExternalOutput").ap()
    dbg = cfg.get("debug")
    g.XS = dscr("XS", [T, D])
    g.MODB = dscr("MODB", [2, 128, 6 * D])
    g.Z = dscr("Z", [T, IN_W])
    g.QKT = dscr("QKT", [12, 128, T])
    g.YMIX = dscr("YMIX", [T, D])
    g.RW = dscr("RW", [2, 6, T, 512])
    g.RG = dscr("RG", [2, T, 512])
    g.YR = dscr("YR", [2, T, 512])
    g.MERGED = dscr("MERGED", [T, D])
    g.AB = dscr("AB", [T, 2 * DFF])
    g.GG = dscr("GG", [T, DFF])
    g.QML = dscr("QML", [T, 768])
    g.KVML = dscr("KVML", [T, 1024])
    g.MQN = dscr("MQN", [4, 128, T])
    g.MQR = dscr("MQR", [4, 64, T])
    g.MKN = dscr("MKN", [4, 128, T])
    g.MKR = dscr("MKR", [1, 64, T])
    if dbg:
        g.dbg_z = nc.dram_tensor("dbg_z", [T, IN_W], F32, kind="ExternalOutput").ap()
        g.dbg_y = nc.dram_tensor("dbg_y", [T, D], F32, kind="ExternalOutput").ap()
        g.dbg_qkt = nc.dram_tensor("dbg_qkt", [12, 128, T], F32, kind="ExternalOutput").ap()
        g.dbg_xs = nc.dram_tensor("dbg_xs", [T, D], F32, kind="ExternalOutput").ap()

    with ExitStack() as es:
        g.psums = [Tl(es.enter_context(nc.psum_tensor("ps%d" % i, [128, 512], F32)), "ps%d" % i) for i in range(8)]
        g.ident = sb(g, es, "ident", [128, 128])
        S.dma("sp", g.ident[:, :], g.identd[:, :], writes=[g.ident.b])
        g.tri = sb(g, es, "tri", [64, 2, 64])
        g.mk = sb(g, es, "mk", [128, 2, 128])
        g.nmts = sb(g, es, "nmts", [64, 2, 64])
        g.ones64 = sb(g, es, "ones64", [64, 64])
        S.dma("sp", g.tri[:, :, :], g.trid[:, :, :], writes=[g.tri.b])
        S.dma("sp", g.mk[:, :, :], g.mkd[:, :, :], writes=[g.mk.b])
        S.dma("sp", g.nmts[:, :, :], g.nmtsd[:, :, :], writes=[g.nmts.b])
        S.op("dve", lambda e: e.memset(g.ones64[:, :], 1.0), writes=[g.ones64.b])
        g.shm = sb(g, es, "shm", [128, 4, 128])
        S.dma("sp", g.shm[:, :, :], g.shmd[:, :, :], writes=[g.shm.b])
        g.wmask = sb(g, es, "wmask", [128, 2, 128])
        S.dma("sp", g.wmask[:, :, :], g.wmaskd[:, :, :], writes=[g.wmask.b])
        S.dma("sp", g.XS[0:LC, :], g.ctx_in[:, :])
        S.dma("sp", g.XS[LC:T, :], g.x_in[:, :])
        g.cb = [sb(g, es, "cb", [128, 16, 128]) for _ in range(2)]
        with ExitStack() as st:
            cT = [sb(g, st, "cT", [128, 16]) for _ in range(2)]
            ones = sb(g, st, "ones", [128, 128])
            S.op("dve", lambda e: e.memset(ones[:, :], 1.0), writes=[ones.b])
            for r in range(2):
                S.dma("sp", cT[r][:, :], g.cvec[r, :].rearrange("(kc p) -> p kc", p=128), writes=[cT[r].b], allow_slow_non_contiguous=True)
                S.op("act", lambda e: e.activation(out=cT[r][:, :], in_=cT[r][:, :], func=AF.Silu), reads=[cT[r].b], writes=[cT[r].b])
                for kc in range(16):
                    S.op("dve", lambda e: e.tensor_scalar(out=g.cb[r][:, kc, :], in0=ones[:, :], scalar1=cT[r][:, kc:kc + 1], scalar2=None, op0=ALU.mult),
                         reads=[ones.b, cT[r].b], writes=[g.cb[r].b])
            S.barrier()
        for l in range(L):
            stage_mod(g, l)
            with ExitStack() as st:
                pro = make_norm_pro(g, st, l, g.norm1_g[l, :], D, 0)
                linear(g, g.XS, g.w_in[l], g.Z, T, D, IN_W, G=12, NB=512, pro=pro)
            if cfg.get("stop") == "z":
                break
            stage_attn_prep(g, l)
            stage_attn_gawa(g, l)
            if cfg.get("stop") == "gawa":
                break
            if cfg.get("stop") != "rwkv":
                stage_mla(g, l)
            if cfg.get("stop") == "mla":
                break
            stage_rwkv_prep(g, l)
            stage_rwkv_scan(g, l)
            stage_rwkv_out(g, l)
            if cfg.get("stop") == "rwkv":
                break
            stage_merge(g, l)
            with ExitStack() as st:
                epi = make_resid_epi(g, st, 2 * D)
                linear(g, g.MERGED, g.w_out[l], None, T, D, D, G=12, NB=512, epi=epi)
            if cfg.get("stop") == "attn":
                break
            with ExitStack() as st:
                pro = make_norm_pro(g, st, l, g.norm2_g[l, :], 4 * D, 3 * D)
                linear(g, g.XS, g.ffn_up[l], g.AB, T, D, 2 * DFF, G=12, NB=512, pro=pro)
            stage_conv(g, l)
            with ExitStack() as st:
                epi = make_resid_epi(g, st, 5 * D)
                linear(g, g.GG, g.ffn_down[l], None, T, DFF, D, G=6, NB=256, epi=epi)
        if dbg:
            S.dma("sp", g.dbg_z[:, :], g.Z[:, :])
            S.dma("sp", g.dbg_y[:, :], g.YMIX[:, :])
            S.dma("sp", g.dbg_qkt[:, :, :], g.QKT[:, :, :])
            S.dma("sp", g.dbg_xs[:, :], g.XS[:, :])
        S.dma("sp", g.y_out[:, :], g.XS[LC:T, :])
        S.barrier()
    return nc, g


_CACHE = {}


def rope_table(n_tokens, rot_dim):
    rows = n_tokens // GRID_W
    row = np.repeat(np.arange(rows), GRID_W).astype(np.float32)
    col = np.tile(np.arange(GRID_W), rows).astype(np.float32)
    quarter = rot_dim // 4
    inv_freq = (10000.0 ** (-np.arange(quarter, dtype=np.float32) / quarter)).astype(np.float32)
    ang_r = row[:, None] * inv_freq
    ang_c = col[:, None] * inv_freq
    ang = np.concatenate([ang_r, ang_r, ang_c, ang_c], axis=-1).astype(np.float32)
    sign = np.concatenate([-np.ones(quarter), np.ones(quarter), -np.ones(quarter), np.ones(quarter)]).astype(np.float32)
    return np.stack([np.cos(ang), np.sin(ang) * sign]).astype(np.float32)


def shift_mats():
    m = np.zeros((128, 4, 128), np.float32)
    for k in range(127):
        m[k, 0, k + 1] = 1.0
        m[k + 1, 2, k] = 1.0
    m[127, 1, 0] = 1.0
    m[0, 3, 127] = 1.0
    return m


def const_tables(TL):
    idx = np.arange(128)
    m_lo = (idx[None, :] <= idx[:, None]).astype(np.float32)
    m_hi = (idx[:, None] <= idx[None, :]).astype(np.float32)
    i64 = np.arange(64)
    tri = np.stack([(i64[:, None] <= i64[None, :]), (i64[:, None] >= i64[None, :])]).astype(np.float32)
    strict = tri - np.eye(64, dtype=np.float32)[None]
    half = np.concatenate([strict, tri], axis=2)
    mk = np.concatenate([half, half], axis=1)
    nmts = -np.transpose(strict, (0, 2, 1))
    return {
        "tri": np.ascontiguousarray(np.transpose(tri, (1, 0, 2))),
        "mk": np.ascontiguousarray(np.transpose(mk, (1, 0, 2))),
        "nmts": np.ascontiguousarray(np.transpose(nmts, (1, 0, 2))),
        "zrow": np.zeros((1, NRW), np.float32),
        "shm": shift_mats(),
        "ropeA": rope_table(TL, 128),
        "ropeM": rope_table(TL, 64),
        "wmask": np.ascontiguousarray(np.stack([m_lo, m_hi], axis=1)),
    }


def make_inputs_for_core(inputs, b, L):
    f = lambda a: np.ascontiguousarray(np.asarray(a, dtype=np.float32))
    m = {
        "x": f(inputs["x"][b]),
        "ctx": f(inputs["ctx"][b]),
        "cvec": f(np.stack([np.asarray(inputs["c"][b]), np.asarray(inputs["c_ctx"])])),
        "ident": np.eye(128, dtype=np.float32),
    }
    m.update(const_tables(np.asarray(inputs["x"]).shape[1]))
    for k in ["ada_w", "ada_b", "norm1_g", "norm2_g", "w_in", "ga_q_norm", "ga_k_norm", "wa_q_norm", "wa_k_norm", "wa_sink",
              "mla_cq_norm", "mla_ckv_norm", "mla_w_uq", "mla_w_ukv", "mla_qn_norm", "mla_qr_norm", "mla_kn_norm", "mla_kr_norm",
              "rwkv_mu", "rwkv_w0", "rwkv_w2", "rwkv_a0", "rwkv_a2", "rwkv_g2", "rwkv_k_k", "rwkv_k_a", "rwkv_r_k", "rwkv_lnx_w", "rwkv_lnx_b",
              "w_branch", "w_out", "ffn_up", "ffn_conv_w", "ffn_conv_b", "ffn_down"]:
        m[k] = f(inputs[k][:L])
    return m


def kernel(**inputs):
    x = np.asarray(inputs["x"])
    B, TL, _ = x.shape
    LC = np.asarray(inputs["ctx"]).shape[1]
    L = np.asarray(inputs["ada_w"]).shape[0]
    cfg = {"TL": TL, "LC": LC, "DEPTH": L}
    nc, g = build(cfg)
    n = 8
    in_maps = [make_inputs_for_core(inputs, c % B, L) for c in range(n)]
    res = run_bass_kernel_spmd(nc, in_maps, core_ids=list(range(n)))
    return np.stack([res.results[b]["y"] for b in range(B)]).astype(np.float32)
```

```python
from contextlib import ExitStack
import threading
import numpy as np
import concourse.bass as bass
import concourse.mybir as mybir
from concourse.bass_utils import run_bass_kernel_spmd

F32 = mybir.dt.float32
BF16 = mybir.dt.bfloat16
AF = mybir.ActivationFunctionType
ALU = mybir.AluOpType
AX = mybir.AxisListType

D = 2048
GRID_W = 64
EPS = 1e-6
IN_W = 13376
DFF = 5632
O_GAQ, O_GAK, O_GAV, O_WAQ, O_WAK, O_WAV = 0, 512, 768, 1024, 1536, 1792
O_RKVG, O_ZW, O_ZA, O_CQ, O_CKV, O_KR, O_GATE = 2048, 3840, 4032, 4224, 4608, 5120, 5184
LNX_EPS = 64e-5


class Buf:
    __slots__ = ("name", "w", "r")

    def __init__(self, name=""):
        self.name = name
        self.w = None
        self.r = {}


class Sync:
    MAXC = 30000

    def __init__(self, nc, n_dma_sems=48, same_engine_sync=True):
        self.nc = nc
        self.hw = {"pe": nc.tensor, "dve": nc.vector, "act": nc.scalar, "pool": nc.gpsimd, "sp": nc.sync}
        self.gen = {k: 0 for k in self.hw}
        self.sem = {k: nc.alloc_semaphore("sem_%s_0" % k) for k in self.hw}
        self.count = {k: 0 for k in self.hw}
        self.seen = {k: {} for k in self.hw}
        self.dma_sems = [nc.alloc_semaphore("dsem%d" % i) for i in range(n_dma_sems)]
        self.dma_uses = [0] * n_dma_sems
        self.dma_next = 0
        self.same_engine_sync = same_engine_sync
        self.latest = {}
        self._lane = None

    def run_lanes(self, fns):
        n = len(fns)
        if n == 1:
            fns[0]()
            return
        cv = threading.Condition()
        st = {"turn": 0, "alive": [True] * n, "err": None}
        ids = {}

        def nxt(i):
            for k in range(1, n + 1):
                j = (i + k) % n
                if st["alive"][j]:
                    return j
            return -1

        def worker(i):
            ids[threading.get_ident()] = i
            with cv:
                while st["turn"] != i:
                    cv.wait()
            try:
                fns[i]()
            except BaseException as e:
                st["err"] = e
            with cv:
                st["alive"][i] = False
                st["turn"] = nxt(i)
                cv.notify_all()

        self._lane = (cv, st, ids, nxt)
        self.lane_pi = {}
        threads = [threading.Thread(target=worker, args=(i,)) for i in range(n)]
        for t in threads:
            t.start()
        for t in threads:
            t.join()
        self._lane = None
        if st["err"] is not None:
            raise st["err"]

    def lane_index(self):
        if self._lane is None:
            return None
        i = self._lane[2].get(threading.get_ident())
        if i is None:
            return None
        return i, len(self._lane[1]["alive"])

    def lane_yield(self):
        if self._lane is None:
            return
        cv, st, ids, nxt = self._lane
        i = ids.get(threading.get_ident())
        if i is None:
            return
        with cv:
            j = nxt(i)
            if j == i or j < 0:
                return
            st["turn"] = j
            cv.notify_all()
            while st["turn"] != i:
                cv.wait()

    def _wait(self, eng, dep):
        key, sem, val = dep
        if self.seen[eng].get(key, 0) >= val:
            return
        self.hw[eng].wait_ge(sem, val)
        self.seen[eng][key] = val

    @staticmethod
    def _add(deps, d):
        if d is not None and deps.get(d[0], (0, 0, 0))[2] < d[2]:
            deps[d[0]] = d

    def _deps(self, reads, writes):
        deps = {}
        for b in reads:
            self._add(deps, b.w)
        for b in writes:
            self._add(deps, b.w)
            for d in b.r.values():
                self._add(deps, d)
        return deps

    def _mark(self, dep, reads, writes):
        for b in reads:
            b.r[dep[0]] = dep
        for b in writes:
            b.w = dep
            b.r = {}
        self.latest[dep[0]] = dep

    def op(self, eng, fn, reads=(), writes=()):
        for key, d in self._deps(reads, writes).items():
            if isinstance(key, tuple) and key[0] == eng and (eng == "pe" or not self.same_engine_sync):
                continue
            self._wait(eng, d)
        ins = fn(self.hw[eng])
        self.count[eng] += 1
        ins.then_inc(self.sem[eng], 1)
        self._mark(((eng, self.gen[eng]), self.sem[eng], self.count[eng]), reads, writes)
        if self.count[eng] >= self.MAXC:
            self.gen[eng] += 1
            self.sem[eng] = self.nc.alloc_semaphore("sem_%s_%d" % (eng, self.gen[eng]))
            self.count[eng] = 0
        self.lane_yield()
        return ins

    def dma(self, q, out, in_, reads=(), writes=(), **kw):
        i = self.dma_next
        self.dma_next = (self.dma_next + 1) % len(self.dma_sems)
        sem = self.dma_sems[i]
        key = "d%d" % i
        if self.dma_uses[i] > 0:
            self._wait(q, (key, sem, 16 * self.dma_uses[i]))
        for k, d in self._deps(reads, writes).items():
            self._wait(q, d)
        self.dma_uses[i] += 1
        ins = self.hw[q].dma_start(out=out, in_=in_, **kw)
        ins.then_inc(sem, 16)
        self._mark((key, sem, 16 * self.dma_uses[i]), reads, writes)
        self.lane_yield()
        return ins

    def barrier(self):
        for eng in self.hw:
            for dep in list(self.latest.values()):
                key = dep[0]
                if isinstance(key, tuple) and key[0] == "pe" and eng == "pe":
                    continue
                self._wait(eng, dep)


class Tl:
    def __init__(self, t, name):
        self.t = t
        self.b = Buf(name)

    def __getitem__(self, idx):
        return self.t[idx]


class Ctx:
    pass


def sb(g, st, name, shape, dtype=F32):
    g.uid += 1
    nm = "%s_%d" % (name, g.uid)
    return Tl(st.enter_context(g.nc.sbuf_tensor(nm, list(shape), dtype)), nm)


def evac_engine(g):
    g.ev += 1
    return "dve" if g.ev % 2 else "act"


def copy_op(g, eng, out, in_, reads, writes):
    if eng == "act":
        return g.S.op("act", lambda e: e.copy(out=out, in_=in_), reads=reads, writes=writes)
    return g.S.op(eng, lambda e: e.tensor_copy(out=out, in_=in_), reads=reads, writes=writes)


def next_psum(g):
    lane = g.S.lane_index()
    if lane is None:
        g.pi = (g.pi + 1) % len(g.psums)
        return g.psums[g.pi]
    i, n = lane
    lo, hi = i * 8 // n, (i + 1) * 8 // n
    k = g.S.lane_pi.get(i, 0)
    g.S.lane_pi[i] = k + 1
    return g.psums[lo + k % (hi - lo)]


def linear(g, x, w, y, T, K, N, G=8, NB=512, pro=None, epi=None, segs=None, wlist=None):
    S, nc = g.S, g.nc
    KC = K // 128
    assert K % 128 == 0 and T % 128 == 0
    NT = T // 128
    XW = min(K, 2048)
    if segs is None:
        segs = [(0, KC)]
    with ExitStack() as st:
        xt = sb(g, st, "xt", [128, KC, G * 128], BF16)
        xin = [sb(g, st, "xin", [128, XW]) for _ in range(2)]
        wt = [sb(g, st, "wt", [128, KC, NB], BF16) for _ in range(3)]
        ot = [sb(g, st, "ot", [128, NB]) for _ in range(3)]
        nxin = nw = no = 0
        for g0 in range(0, NT, G):
            gn = min(G, NT - g0)
            for gi in range(gn):
                ti = g0 + gi
                for c0 in range(0, K, XW):
                    cw = min(XW, K - c0)
                    xi = xin[nxin % 2]
                    nxin += 1
                    S.dma("sp", xi[:, 0:cw], x[ti * 128:(ti + 1) * 128, c0:c0 + cw], writes=[xi.b])
                    if pro is not None:
                        pro(xi, ti, cw)
                    for k4 in range(0, cw // 128, 4):
                        ps = next_psum(g)
                        nk = min(4, cw // 128 - k4)
                        for j in range(nk):
                            kk = k4 + j
                            S.op("pe", lambda e: e.transpose(ps[:, j * 128:(j + 1) * 128], xi[:, kk * 128:(kk + 1) * 128], g.ident[:]),
                                 reads=[xi.b, g.ident.b], writes=[ps.b])
                        kc0 = c0 // 128 + k4
                        copy_op(g, evac_engine(g), xt[:, kc0:kc0 + nk, gi * 128:(gi + 1) * 128],
                                ps[:, 0:nk * 128].rearrange("p (k t) -> p k t", k=nk), [ps.b], [xt.b])
            for n0 in range(0, N, NB):
                nb = min(NB, N - n0)
                wi = wt[nw % 3]
                nw += 1
                S.dma("pool", wi[:, :, 0:nb], w.rearrange("(kc p) n -> p kc n", p=128)[:, :, n0:n0 + nb], writes=[wi.b])
                for gi in range(gn):
                    ti = g0 + gi
                    pss = []
                    for (k0, k1) in segs:
                        ps = next_psum(g)
                        pss.append(ps)
                        for kc in range(k0, k1):
                            S.op("pe", lambda e: e.matmul(ps[:, 0:nb], lhsT=xt[:, kc, gi * 128:(gi + 1) * 128], rhs=wi[:, kc, 0:nb],
                                                          start=(kc == k0), stop=(kc == k1 - 1)),
                                 reads=[xt.b, wi.b], writes=[ps.b])
                    o = ot[no % 3]
                    no += 1
                    if epi is None:
                        copy_op(g, evac_engine(g), o[:, 0:nb], pss[0][:, 0:nb], [pss[0].b], [o.b])
                        S.dma("sp", y[ti * 128:(ti + 1) * 128, n0:n0 + nb], o[:, 0:nb], reads=[o.b])
                    else:
                        epi(pss, o, ti, n0, nb)
    S.barrier()


def rms_rstd(g, st_tiles, x_ap, n, width, reads_b, sq, ss, rstd, eps=EPS):
    S = g.S
    S.op("act", lambda e: e.activation(out=sq[:, 0:n * width].rearrange("p (n w) -> p n w", n=n), in_=x_ap, func=AF.Square),
         reads=[reads_b], writes=[sq.b])
    S.op("dve", lambda e: e.tensor_reduce(out=ss[:, 0:n], in_=sq[:, 0:n * width].rearrange("p (n w) -> p n w", n=n), axis=AX.X, op=ALU.add),
         reads=[sq.b], writes=[ss.b])
    S.op("dve", lambda e: e.tensor_scalar(out=ss[:, 0:n], in0=ss[:, 0:n], scalar1=1.0 / width, scalar2=eps, op0=ALU.mult, op1=ALU.add),
         reads=[ss.b], writes=[ss.b])
    S.op("act", lambda e: e.activation(out=ss[:, 0:n], in_=ss[:, 0:n], func=AF.Sqrt), reads=[ss.b], writes=[ss.b])
    S.op("dve", lambda e: e.reciprocal(out=rstd[:, 0:n], in_=ss[:, 0:n]), reads=[ss.b], writes=[rstd.b])


def bcast_row(g, q, tile_ap, buf, dram_row_ap):
    g.S.dma(q, tile_ap, dram_row_ap.partition_broadcast(128), writes=[buf])


def stage_mod(g, l):
    S, nc = g.S, g.nc
    with ExitStack() as st:
        wt = [sb(g, st, "mw", [128, 16, 512]) for _ in range(2)]
        bt = [sb(g, st, "mb", [128, 512]) for _ in range(2)]
        ot = [sb(g, st, "mo", [128, 512]) for _ in range(3)]
        no = 0
        for bi, n0 in enumerate(range(0, 6 * D, 512)):
            wi = wt[bi % 2]
            bb = bt[bi % 2]
            S.dma("sp", wi[:, :, :], g.ada_w[l].rearrange("(kc p) n -> p kc n", p=128)[:, :, n0:n0 + 512], writes=[wi.b])
            bcast_row(g, "sp", bb[:, :], bb.b, g.ada_b[l, n0:n0 + 512])
            for r in range(2):
                ps = next_psum(g)
                for kc in range(16):
                    S.op("pe", lambda e: e.matmul(ps[:, :], lhsT=g.cb[r][:, kc, :], rhs=wi[:, kc, :], start=(kc == 0), stop=(kc == 15)),
                         reads=[g.cb[r].b, wi.b], writes=[ps.b])
                o = ot[no % 3]
                no += 1
                S.op("dve", lambda e: e.tensor_tensor(out=o[:, :], in0=ps[:, :], in1=bb[:, :], op=ALU.add), reads=[ps.b, bb.b], writes=[o.b])
                S.dma("pool", g.MODB[r, :, n0:n0 + 512], o[:, :], reads=[o.b])
    S.barrier()


def make_norm_pro(g, st, l, gain_dram, sc_off, sh_off):
    S = g.S
    A = [sb(g, st, "nA", [128, D]) for _ in range(2)]
    B = [sb(g, st, "nB", [128, D]) for _ in range(2)]
    sq = sb(g, st, "nsq", [128, D])
    ss = sb(g, st, "nss", [128, 1])
    rstd = sb(g, st, "nrs", [128, 1])
    with ExitStack() as st2:
        gt = sb(g, st2, "ng", [128, D])
        bcast_row(g, "sp", gt[:, :], gt.b, gain_dram)
        for r in range(2):
            S.dma("sp", A[r][:, :], g.MODB[r, :, sc_off:sc_off + D], writes=[A[r].b])
            S.dma("sp", B[r][:, :], g.MODB[r, :, sh_off:sh_off + D], writes=[B[r].b])
            S.op("dve", lambda e: e.scalar_tensor_tensor(out=A[r][:, :], in0=A[r][:, :], scalar=1.0, in1=gt[:, :], op0=ALU.add, op1=ALU.mult),
                 reads=[A[r].b, gt.b], writes=[A[r].b])
        S.barrier()

    def pro(xi, ti, cw):
        r = 1 if ti < g.NTC else 0
        rms_rstd(g, None, xi[:, 0:D].rearrange("p (n w) -> p n w", n=1), 1, D, xi.b, sq, ss, rstd)
        S.op("dve", lambda e: e.scalar_tensor_tensor(out=xi[:, 0:D], in0=xi[:, 0:D], scalar=rstd[:, 0:1], in1=A[r][:, :], op0=ALU.mult, op1=ALU.mult),
             reads=[xi.b, rstd.b, A[r].b], writes=[xi.b])
        S.op("dve", lambda e: e.tensor_tensor(out=xi[:, 0:D], in0=xi[:, 0:D], in1=B[r][:, :], op=ALU.add), reads=[xi.b, B[r].b], writes=[xi.b])
    return pro


def make_resid_epi(g, st, gate_off):
    S = g.S
    gts = [sb(g, st, "rg", [128, D]) for _ in range(2)]
    for r in range(2):
        S.dma("sp", gts[r][:, :], g.MODB[r, :, gate_off:gate_off + D], writes=[gts[r].b])
    xb = [sb(g, st, "rx", [128, 512]) for _ in range(3)]
    cnt = [0]

    def epi(pss, o, ti, n0, nb):
        r = 1 if ti < g.NTC else 0
        x = xb[cnt[0] % 3]
        cnt[0] += 1
        S.dma("sp", x[:, 0:nb], g.XS[ti * 128:(ti + 1) * 128, n0:n0 + nb], writes=[x.b])
        S.op("dve", lambda e: e.tensor_tensor(out=o[:, 0:nb], in0=pss[0][:, 0:nb], in1=gts[r][:, n0:n0 + nb], op=ALU.mult),
             reads=[pss[0].b, gts[r].b], writes=[o.b])
        S.op("dve", lambda e: e.tensor_tensor(out=o[:, 0:nb], in0=o[:, 0:nb], in1=x[:, 0:nb], op=ALU.add), reads=[o.b, x.b], writes=[o.b])
        S.dma("sp", g.XS[ti * 128:(ti + 1) * 128, n0:n0 + nb], o[:, 0:nb], reads=[o.b])
    return epi


def norm_rope(g, src, src_b, n, w, gain_ap, gain_b, out, out_b, tmp, sq, ss, rstd, rope=None):
    S = g.S
    rms_rstd(g, None, src, n, w, src_b, sq, ss, rstd)
    dst = out if rope is None else tmp[:, 0:n * w].rearrange("p (n w) -> p n w", n=n)
    dst_b = out_b if rope is None else tmp.b
    S.op("dve", lambda e: e.tensor_tensor(out=dst, in0=src, in1=rstd[:, 0:n].unsqueeze(2).broadcast_to([128, n, w]), op=ALU.mult),
         reads=[src_b, rstd.b], writes=[dst_b])
    S.op("dve", lambda e: e.tensor_tensor(out=dst, in0=dst, in1=gain_ap, op=ALU.mult), reads=[dst_b, gain_b], writes=[dst_b])
    if rope is None:
        return
    cos, ssin, blk = rope
    nh = w // (2 * blk)
    xv = dst.rearrange("p n (h b k) -> p n h b k", h=nh, b=2, k=blk)
    sv = ssin[:, 0:w].rearrange("p (h b k) -> p h b k", h=nh, b=2, k=blk)
    sw = sq[:, 0:n * w].rearrange("p (n h b k) -> p n h b k", n=n, h=nh, b=2, k=blk)
    for n_i in range(n):
        for b in range(2):
            S.op("dve", lambda e: e.tensor_tensor(out=sw[:, n_i, :, b, :], in0=xv[:, n_i, :, 1 - b, :], in1=sv[:, :, b, :], op=ALU.mult),
                 reads=[dst_b, ssin.b], writes=[sq.b])
    S.op("dve", lambda e: e.tensor_tensor(out=dst, in0=dst, in1=cos[:, 0:w].unsqueeze(1).broadcast_to([128, n, w]), op=ALU.mult),
         reads=[dst_b, cos.b], writes=[dst_b])
    S.op("dve", lambda e: e.tensor_tensor(out=out, in0=dst, in1=sq[:, 0:n * w].rearrange("p (n w) -> p n w", n=n), op=ALU.add),
         reads=[dst_b, sq.b], writes=[out_b])


def transpose_slots(g, src_tl, slots, w, dst_tl, dst_slot0):
    S = g.S
    for i0 in range(0, len(slots), 4):
        grp = slots[i0:i0 + 4]
        ps = next_psum(g)
        for j, s_ in enumerate(grp):
            S.op("pe", lambda e: e.transpose(ps[0:w, j * 128:(j + 1) * 128], src_tl[:, s_, 0:w], g.ident[:]),
                 reads=[src_tl.b, g.ident.b], writes=[ps.b])
        copy_op(g, evac_engine(g), dst_tl[0:w, dst_slot0 + i0:dst_slot0 + i0 + len(grp), :],
                ps[0:w, 0:len(grp) * 128].rearrange("p (k t) -> p k t", k=len(grp)), [ps.b], [dst_tl.b])


def stage_attn_prep(g, l):
    S = g.S
    with ExitStack() as st:
        gain = sb(g, st, "apg", [128, 2, 6, 128])
        for gi, (qn, kn) in enumerate([(g.ga_q_norm, g.ga_k_norm), (g.wa_q_norm, g.wa_k_norm)]):
            for s_ in range(6):
                bcast_row(g, "sp", gain[:, gi, s_, :], gain.b, (qn if s_ < 4 else kn)[l, :])
        NL = 3
        LT = [dict(z=sb(g, st, "apz", [128, 16, 128]), x=sb(g, st, "apx", [128, 12, 128]), t_=sb(g, st, "apt", [128, 12, 128]),
                   cs=sb(g, st, "apc", [128, 128]), sn=sb(g, st, "aps", [128, 128]),
                   tmp=sb(g, st, "aptmp", [128, 6 * 128]), sq=sb(g, st, "apsq", [128, 6 * 128]),
                   ss=sb(g, st, "apss", [128, 8]), rstd=sb(g, st, "aprs", [128, 8])) for _ in range(NL)]

        def lane(li):
          lt = LT[li]
          tmp, sq, ss, rstd = lt["tmp"], lt["sq"], lt["ss"], lt["rstd"]
          cs = [lt["cs"], lt["cs"]]
          sn = [lt["sn"], lt["sn"]]
          for ti in range(li, g.NT, NL):
            z = lt["z"]
            x = lt["x"]
            t_ = lt["t_"]
            S.dma("sp", z[:, :, :], g.Z[ti * 128:(ti + 1) * 128, 0:2048].rearrange("p (s d) -> p s d", s=16), writes=[z.b])
            rope = None
            if ti >= g.NTC:
                c_, s_t = cs[ti % 2], sn[ti % 2]
                p0 = (ti - g.NTC) * 128
                S.dma("sp", c_[:, :], g.ropeA[0, p0:p0 + 128, :], writes=[c_.b])
                S.dma("sp", s_t[:, :], g.ropeA[1, p0:p0 + 128, :], writes=[s_t.b])
                rope = (c_, s_t, 32)
            for gi, base in enumerate([0, 8]):
                norm_rope(g, z[:, base:base + 6, :], z.b, 6, 128, gain[:, gi, :, :], gain.b,
                          x[:, gi * 6:(gi + 1) * 6, :], x.b, tmp, sq, ss, rstd, rope=rope)
            transpose_slots(g, x, list(range(12)), 128, t_, 0)
            S.dma("pool", g.QKT[:, :, ti * 128:(ti + 1) * 128].rearrange("s d t -> d s t"), t_[:, :, :], reads=[t_.b])
        S.run_lanes([(lambda li=li: lane(li)) for li in range(NL)])
    S.barrier()


def attention(g, heads, scale, sink_tl=None):
    S = g.S
    T, NT = g.T, g.NT
    po = g.psums[0:4]
    sps = g.psums[4:8]
    with ExitStack() as st:
        kts = [sb(g, st, "akt", [128, T], BF16) for _ in range(2)]
        vt = sb(g, st, "avt", [128, NT, 129], BF16)
        qts = [[sb(g, st, "aqt", [128, 512], BF16) for _ in range(2)] for _ in range(3)]
        pts = [sb(g, st, "apt", [128, 512], BF16) for _ in range(4)]
        yts = [sb(g, st, "ayt", [128, 128]) for _ in range(3)]
        den = sb(g, st, "aden", [128, 4])
        S.op("dve", lambda e: e.memset(vt[:, :, 128:129], 1.0), writes=[vt.b])
        nq = npt = ny = nsp = 0
        for hd in heads:
            for pi_, (kd_ap, kd) in enumerate(hd["kparts"]):
                S.dma("pool", kts[pi_][0:kd, :], kd_ap, writes=[kts[pi_].b])
            S.dma("pool", vt[:, :, 0:128], hd["v"].rearrange("(n p) d -> p n d", p=128), writes=[vt.b])
            for qh, qparts in enumerate(hd["qparts"]):
                for (q0, qlen, keys) in hd["blocks"]:
                    nqb = qlen // 128
                    qt = qts[nq % 3]
                    nq += 1
                    for pi_, (qd_ap, kd) in enumerate(qparts):
                        S.dma("pool", qt[pi_][0:kd, 0:qlen], qd_ap[:, q0:q0 + qlen], writes=[qt[pi_].b])
                    npart = len(qparts)

                    def scores(kt_):
                        nonlocal nsp
                        ps_ = sps[nsp % 4]
                        nsp += 1
                        for pi_, (qd_ap, kd) in enumerate(qparts):
                            S.op("pe", lambda e: e.matmul(ps_[:, 0:qlen], lhsT=kts[pi_][0:kd, kt_ * 128:(kt_ + 1) * 128], rhs=qt[pi_][0:kd, 0:qlen],
                                                          start=(pi_ == 0), stop=(pi_ == npart - 1)),
                                 reads=[kts[pi_].b, qt[pi_].b], writes=[ps_.b])
                        return ps_
                    ahead = [scores(keys[j][0]) for j in range(min(2, len(keys)))]
                    for idx, (kt, mask) in enumerate(keys):
                        ps = ahead.pop(0)
                        if idx + 2 < len(keys):
                            ahead.append(scores(keys[idx + 2][0]))
                        pt = pts[npt % 4]
                        npt += 1
                        S.op("act", lambda e: e.activation(out=pt[:, 0:qlen], in_=ps[:, 0:qlen], func=AF.Exp, scale=scale),
                             reads=[ps.b], writes=[pt.b])
                        if mask is not None:
                            S.op("dve", lambda e: e.tensor_tensor(out=pt[:, 0:qlen], in0=pt[:, 0:qlen], in1=g.wmask[:, mask, :], op=ALU.mult),
                                 reads=[pt.b, g.wmask.b], writes=[pt.b])
                        for qb in range(nqb):
                            S.op("pe", lambda e: e.matmul(po[qb][:, 0:129], lhsT=pt[:, qb * 128:(qb + 1) * 128], rhs=vt[:, kt, :],
                                                          start=(idx == 0), stop=(idx == len(keys) - 1)),
                                 reads=[pt.b, vt.b], writes=[po[qb].b])
                    for qb in range(nqb):
                        y = yts[ny % 3]
                        ny += 1
                        if hd.get("sink_idx") is not None:
                            si = hd["sink_idx"][qh]
                            S.op("dve", lambda e: e.tensor_tensor(out=den[:, 0:1], in0=po[qb][:, 128:129], in1=sink_tl[:, si:si + 1], op=ALU.add),
                                 reads=[po[qb].b, sink_tl.b], writes=[den.b])
                            S.op("dve", lambda e: e.reciprocal(out=den[:, 0:1], in_=den[:, 0:1]), reads=[den.b], writes=[den.b])
                        else:
                            S.op("dve", lambda e: e.reciprocal(out=den[:, 0:1], in_=po[qb][:, 128:129]), reads=[po[qb].b], writes=[den.b])
                        S.op("dve", lambda e: e.tensor_scalar(out=y[:, :], in0=po[qb][:, 0:128], scalar1=den[:, 0:1], scalar2=None, op0=ALU.mult),
                             reads=[po[qb].b, den.b], writes=[y.b])
                        r0 = q0 + qb * 128
                        yc = hd["ycols"][qh]
                        S.dma("sp", g.YMIX[r0:r0 + 128, yc:yc + 128], y[:, :], reads=[y.b])
    S.barrier()


def dense_blocks(g):
    blocks = [(0, g.LC, [(kt, None) for kt in range(g.NTC)])]
    for q0 in range(g.LC, g.T, 512):
        blocks.append((q0, min(512, g.T - q0), [(kt, None) for kt in range(g.NT)]))
    return blocks


def window_blocks(g):
    blocks = [(0, g.LC, [(kt, None) for kt in range(g.NTC)])]
    nblk = g.TL // 128
    for n in range(nblk):
        keys = [(kt, None) for kt in range(g.NTC)]
        if n > 0:
            keys.append((g.NTC + n - 1, 0))
        keys.append((g.NTC + n, None))
        if n < nblk - 1:
            keys.append((g.NTC + n + 1, 1))
        blocks.append((g.LC + n * 128, 128, keys))
    return blocks


def stage_attn_gawa(g, l):
    S = g.S
    heads = []
    for hk in range(2):
        heads.append(dict(kparts=[(g.QKT[4 + hk], 128)], qparts=[[(g.QKT[2 * hk + j], 128)] for j in range(2)],
                          v=g.Z[:, O_GAV + hk * 128:O_GAV + (hk + 1) * 128], ycols=[(2 * hk + j) * 128 for j in range(2)],
                          blocks=dense_blocks(g)))
    attention(g, heads, 128 ** -0.5)
    with ExitStack() as st:
        sink = sb(g, st, "sink", [128, 4])
        bcast_row(g, "sp", sink[:, :], sink.b, g.wa_sink[l, :])
        S.op("act", lambda e: e.activation(out=sink[:, :], in_=sink[:, :], func=AF.Exp), reads=[sink.b], writes=[sink.b])
        heads = []
        for hk in range(2):
            heads.append(dict(kparts=[(g.QKT[10 + hk], 128)], qparts=[[(g.QKT[6 + 2 * hk + j], 128)] for j in range(2)],
                              v=g.Z[:, O_WAV + hk * 128:O_WAV + (hk + 1) * 128], ycols=[512 + (2 * hk + j) * 128 for j in range(2)],
                              blocks=window_blocks(g), sink_idx=[2 * hk, 2 * hk + 1]))
        attention(g, heads, 128 ** -0.5, sink_tl=sink)


def make_rms_pro(g, st, gain_dram, K):
    S = g.S
    gt = sb(g, st, "pg", [128, K])
    sq = sb(g, st, "psq", [128, K])
    ss = sb(g, st, "pss", [128, 1])
    rstd = sb(g, st, "prs", [128, 1])
    bcast_row(g, "sp", gt[:, :], gt.b, gain_dram)

    def pro(xi, ti, cw):
        rms_rstd(g, None, xi[:, 0:K].rearrange("p (n w) -> p n w", n=1), 1, K, xi.b, sq, ss, rstd)
        S.op("dve", lambda e: e.scalar_tensor_tensor(out=xi[:, 0:K], in0=xi[:, 0:K], scalar=rstd[:, 0:1], in1=gt[:, :], op0=ALU.mult, op1=ALU.mult),
             reads=[xi.b, rstd.b, gt.b], writes=[xi.b])
    return pro


def stage_mla(g, l):
    S = g.S
    T = g.T
    with ExitStack() as st:
        pro = make_rms_pro(g, st, g.mla_cq_norm[l, :], 384)
        linear(g, g.Z[:, O_CQ:O_CQ + 384], g.mla_w_uq[l], g.QML, T, 384, 768, G=17, NB=512, pro=pro)
    with ExitStack() as st:
        pro = make_rms_pro(g, st, g.mla_ckv_norm[l, :], 512)
        linear(g, g.Z[:, O_CKV:O_CKV + 512], g.mla_w_ukv[l], g.KVML, T, 512, 1024, G=17, NB=512, pro=pro)
    with ExitStack() as st:
        gq_n = sb(g, st, "mgqn", [128, 128])
        gq_r = sb(g, st, "mgqr", [128, 64])
        gk_n = sb(g, st, "mgkn", [128, 128])
        gk_r = sb(g, st, "mgkr", [128, 64])
        bcast_row(g, "sp", gq_n[:, :], gq_n.b, g.mla_qn_norm[l, :])
        bcast_row(g, "sp", gq_r[:, :], gq_r.b, g.mla_qr_norm[l, :])
        bcast_row(g, "sp", gk_n[:, :], gk_n.b, g.mla_kn_norm[l, :])
        bcast_row(g, "sp", gk_r[:, :], gk_r.b, g.mla_kr_norm[l, :])
        NL = 3
        qin = [sb(g, st, "mq", [128, 4, 192]) for _ in range(NL)]
        kvin = [sb(g, st, "mkv", [128, 4, 256]) for _ in range(NL)]
        krin = [sb(g, st, "mkr", [128, 1, 64]) for _ in range(NL)]
        xqn = [sb(g, st, "xqn", [128, 4, 128]) for _ in range(NL)]
        xqr = [sb(g, st, "xqr", [128, 4, 64]) for _ in range(NL)]
        xkn = [sb(g, st, "xkn", [128, 4, 128]) for _ in range(NL)]
        xkr = [sb(g, st, "xkr", [128, 1, 64]) for _ in range(NL)]
        tqn = [sb(g, st, "tqn", [128, 4, 128]) for _ in range(NL)]
        tqr = [sb(g, st, "tqr", [64, 4, 128]) for _ in range(NL)]
        tkn = [sb(g, st, "tkn", [128, 4, 128]) for _ in range(NL)]
        tkr = [sb(g, st, "tkr", [64, 1, 128]) for _ in range(NL)]
        cs = [sb(g, st, "mc", [128, 64]) for _ in range(NL)]
        sn = [sb(g, st, "ms", [128, 64]) for _ in range(NL)]
        tmps = [sb(g, st, "mtmp", [128, 512]) for _ in range(NL)]
        sqs = [sb(g, st, "msq", [128, 512]) for _ in range(NL)]
        sss = [sb(g, st, "mss", [128, 8]) for _ in range(NL)]
        rstds = [sb(g, st, "mrs", [128, 8]) for _ in range(NL)]

        def lane(i):
          tmp, sq, ss, rstd = tmps[i], sqs[i], sss[i], rstds[i]
          for ti in range(i, g.NT, NL):
            rows = slice(ti * 128, (ti + 1) * 128)
            S.dma("sp", qin[i][:, :, :], g.QML[rows, :].rearrange("p (h d) -> p h d", h=4), writes=[qin[i].b])
            S.dma("sp", kvin[i][:, :, :], g.KVML[rows, :].rearrange("p (h d) -> p h d", h=4), writes=[kvin[i].b])
            S.dma("sp", krin[i][:, 0, :], g.Z[rows, O_KR:O_KR + 64], writes=[krin[i].b])
            rope = None
            if ti >= g.NTC:
                p0 = (ti - g.NTC) * 128
                S.dma("sp", cs[i][:, :], g.ropeM[0, p0:p0 + 128, :], writes=[cs[i].b])
                S.dma("sp", sn[i][:, :], g.ropeM[1, p0:p0 + 128, :], writes=[sn[i].b])
                rope = (cs[i], sn[i], 16)
            norm_rope(g, qin[i][:, :, 0:128], qin[i].b, 4, 128, gq_n[:, :].unsqueeze(1).broadcast_to([128, 4, 128]), gq_n.b,
                      xqn[i][:, :, :], xqn[i].b, tmp, sq, ss, rstd)
            norm_rope(g, qin[i][:, :, 128:192], qin[i].b, 4, 64, gq_r[:, :].unsqueeze(1).broadcast_to([128, 4, 64]), gq_r.b,
                      xqr[i][:, :, :], xqr[i].b, tmp, sq, ss, rstd, rope=rope)
            norm_rope(g, kvin[i][:, :, 0:128], kvin[i].b, 4, 128, gk_n[:, :].unsqueeze(1).broadcast_to([128, 4, 128]), gk_n.b,
                      xkn[i][:, :, :], xkn[i].b, tmp, sq, ss, rstd)
            norm_rope(g, krin[i][:, :, :], krin[i].b, 1, 64, gk_r[:, :].unsqueeze(1), gk_r.b,
                      xkr[i][:, :, :], xkr[i].b, tmp, sq, ss, rstd, rope=rope)
            transpose_slots(g, xqn[i], [0, 1, 2, 3], 128, tqn[i], 0)
            transpose_slots(g, xqr[i], [0, 1, 2, 3], 64, tqr[i], 0)
            transpose_slots(g, xkn[i], [0, 1, 2, 3], 128, tkn[i], 0)
            transpose_slots(g, xkr[i], [0], 64, tkr[i], 0)
            cols = slice(ti * 128, (ti + 1) * 128)
            S.dma("pool", g.MQN[:, :, cols].rearrange("s d t -> d s t"), tqn[i][:, :, :], reads=[tqn[i].b])
            S.dma("pool", g.MQR[:, :, cols].rearrange("s d t -> d s t"), tqr[i][:, :, :], reads=[tqr[i].b])
            S.dma("pool", g.MKN[:, :, cols].rearrange("s d t -> d s t"), tkn[i][:, :, :], reads=[tkn[i].b])
            S.dma("pool", g.MKR[:, :, cols].rearrange("s d t -> d s t"), tkr[i][:, :, :], reads=[tkr[i].b])
        S.run_lanes([(lambda i=i: lane(i)) for i in range(NL)])
    S.barrier()
    heads = []
    for h in range(4):
        heads.append(dict(kparts=[(g.MKN[h], 128), (g.MKR[0], 64)], qparts=[[(g.MQN[h], 128), (g.MQR[h], 64)]],
                          v=g.KVML[:, h * 256 + 128:h * 256 + 256], ycols=[1536 + h * 128], blocks=dense_blocks(g)))
    attention(g, heads, 192 ** -0.5)


RQ_R, RQ_V, RQ_LW, RQ_K, RQ_KK, RQ_KKA = 0, 1, 2, 3, 4, 5
RQ_SCAN = [RQ_R, RQ_LW, RQ_K, RQ_V, RQ_KK, RQ_KKA]
NRW = 2176


def stage_rwkv_prep(g, l):
    S = g.S
    NT, NTC = g.NT, g.NTC
    with ExitStack() as st:
        MU = [sb(g, st, "rmu", [128, NRW]) for _ in range(2)]
        w0 = [sb(g, st, "rw0", [128, 512]) for _ in range(2)]
        a0 = [sb(g, st, "ra0", [128, 512]) for _ in range(2)]
        w2 = [sb(g, st, "rw2", [96, 512]) for _ in range(2)]
        a2 = [sb(g, st, "ra2", [96, 512]) for _ in range(2)]
        g2 = sb(g, st, "rg2", [128, 2, 512])
        kk_ = sb(g, st, "rkk", [128, 512])
        ka = sb(g, st, "rka", [128, 512])
        omka = sb(g, st, "romka", [128, 512])
        for d in range(2):
            S.op("dve", lambda e: e.memset(MU[d][:, :], 0.0), writes=[MU[d].b])
            bcast_row(g, "sp", MU[d][:, 0:1792], MU[d].b, g.rwkv_mu[l, d, 0:1792])
            bcast_row(g, "sp", MU[d][:, 1792 + d * 96:1792 + (d + 1) * 96], MU[d].b, g.rwkv_mu[l, d, 1792:1888])
            bcast_row(g, "sp", MU[d][:, 1984 + d * 96:1984 + (d + 1) * 96], MU[d].b, g.rwkv_mu[l, d, 1888:1984])
            bcast_row(g, "sp", w0[d][:, :], w0[d].b, g.rwkv_w0[l, d, :])
            bcast_row(g, "sp", a0[d][:, :], a0[d].b, g.rwkv_a0[l, d, :])
            S.dma("sp", w2[d][:, :], g.rwkv_w2[l, d], writes=[w2[d].b])
            S.dma("sp", a2[d][:, :], g.rwkv_a2[l, d], writes=[a2[d].b])
        S.dma("sp", g2[:, :, :], g.rwkv_g2[l].rearrange("(kc p) n -> p kc n", p=128), writes=[g2.b])
        bcast_row(g, "sp", kk_[:, :], kk_.b, g.rwkv_k_k[l, :])
        bcast_row(g, "sp", ka[:, :], ka.b, g.rwkv_k_a[l, :])
        S.op("dve", lambda e: e.tensor_scalar(out=omka[:, :], in0=ka[:, :], scalar1=-1.0, scalar2=1.0, op0=ALU.mult, op1=ALU.add),
             reads=[ka.b], writes=[omka.b])
        LT = []
        for d in range(2):
            LT.append(dict(
                zc=[sb(g, st, "rzc", [128, NRW]) for _ in range(2)],
                zz=sb(g, st, "rzs", [128, NRW]),
                u=sb(g, st, "ru", [128, NRW]),
                tw=sb(g, st, "rtw", [128, 96]),
                sg=sb(g, st, "rsg", [128, 256]),
                tT=sb(g, st, "rtT", [128, 4, 128]),
                aa=sb(g, st, "raa", [128, 512]),
                t1=sb(g, st, "rt1", [128, 512]),
                t2=sb(g, st, "rt2", [128, 512]),
                ss=sb(g, st, "rss", [128, 8]),
                ob=[sb(g, st, "rob", [128, 4, 512]) for _ in range(2)],
                og=[sb(g, st, "rog", [128, 512]) for _ in range(2)]))

        def lane(d):
            lt = LT[d]
            zz, u, tw, sg, tT, aa, t1, t2, ss = (lt[k] for k in ["zz", "u", "tw", "sg", "tT", "aa", "t1", "t2", "ss"])
            for ti in range(NT):
                z = lt["zc"][ti % 2]
                r0 = ti * 128
                S.dma("sp", z[:, :], g.Z[r0:r0 + 128, O_RKVG:O_RKVG + NRW], writes=[z.b])
                if True:
                    if d == 0:
                        if ti in (0, NTC):
                            S.dma("sp", zz[1:128, :], g.Z[r0:r0 + 127, O_RKVG:O_RKVG + NRW], writes=[zz.b])
                            S.dma("sp", zz[0:1, :], g.zrow[0:1, :], writes=[zz.b])
                        else:
                            S.dma("sp", zz[:, :], g.Z[r0 - 1:r0 + 127, O_RKVG:O_RKVG + NRW], writes=[zz.b])
                    else:
                        if ti in (NTC - 1, NT - 1):
                            S.dma("sp", zz[0:127, :], g.Z[r0 + 1:r0 + 128, O_RKVG:O_RKVG + NRW], writes=[zz.b])
                            S.dma("sp", zz[127:128, :], g.zrow[0:1, :], writes=[zz.b])
                        else:
                            S.dma("sp", zz[:, :], g.Z[r0 + 1:r0 + 129, O_RKVG:O_RKVG + NRW], writes=[zz.b])
                    S.op("dve", lambda e: e.tensor_tensor(out=u[:, :], in0=zz[:, :], in1=z[:, :], op=ALU.subtract), reads=[zz.b, z.b], writes=[u.b])
                    S.op("dve", lambda e: e.tensor_tensor(out=u[:, :], in0=u[:, :], in1=MU[d][:, :], op=ALU.mult), reads=[u.b, MU[d].b], writes=[u.b])
                    S.op("dve", lambda e: e.tensor_tensor(out=u[:, :], in0=u[:, :], in1=z[:, :], op=ALU.add), reads=[u.b, z.b], writes=[u.b])
                    ur, uk, uv, ugd = u[:, 0:512], u[:, 512:1024], u[:, 1024:1536], u[:, 1536:1792]
                    uwd = u[:, 1792 + d * 96:1792 + (d + 1) * 96]
                    uad = u[:, 1984 + d * 96:1984 + (d + 1) * 96]
                    ob = lt["ob"][ti % 2]
                    o_g = lt["og"][ti % 2]

                    class _V:
                        def __init__(self, j):
                            self.j = j
                            self.b = ob.b

                        def __getitem__(self, idx):
                            return ob[:, self.j, :]
                    o_lw, o_k, o_kk, o_kka = _V(0), _V(1), _V(2), _V(3)
                    rows = slice(r0, r0 + 128)
                    S.dma("pool", g.RW[d, 0:2, rows, :].rearrange("q t c -> t q c"),
                          u[:, 0:2048].rearrange("p (a c) -> p a c", a=2)[:, :, 0:512], reads=[u.b])
                    S.op("act", lambda e: e.activation(out=tw[:, :], in_=uwd, func=AF.Tanh), reads=[u.b], writes=[tw.b])
                    S.op("act", lambda e: e.activation(out=sg[:, :], in_=ugd, func=AF.Sigmoid), reads=[u.b], writes=[sg.b])
                    ps = next_psum(g)
                    S.op("pe", lambda e: e.transpose(ps[0:96, 0:128], tw[:, :], g.ident[:]), reads=[tw.b, g.ident.b], writes=[ps.b])
                    S.op("pe", lambda e: e.transpose(ps[0:96, 128:256], uad, g.ident[:]), reads=[u.b, g.ident.b], writes=[ps.b])
                    copy_op(g, "act", tT[0:96, 0:2, :], ps[0:96, 0:256].rearrange("p (k t) -> p k t", k=2), [ps.b], [tT.b])
                    ps = next_psum(g)
                    S.op("pe", lambda e: e.transpose(ps[:, 0:128], sg[:, 0:128], g.ident[:]), reads=[sg.b, g.ident.b], writes=[ps.b])
                    S.op("pe", lambda e: e.transpose(ps[:, 128:256], sg[:, 128:256], g.ident[:]), reads=[sg.b, g.ident.b], writes=[ps.b])
                    copy_op(g, "dve", tT[:, 2:4, :], ps[:, 0:256].rearrange("p (k t) -> p k t", k=2), [ps.b], [tT.b])
                    psw = next_psum(g)
                    S.op("pe", lambda e: e.matmul(psw[:, :], lhsT=tT[0:96, 0, :], rhs=w2[d][:, :], start=True, stop=True), reads=[tT.b, w2[d].b], writes=[psw.b])
                    S.op("dve", lambda e: e.tensor_tensor(out=t1[:, :], in0=psw[:, :], in1=w0[d][:, :], op=ALU.add), reads=[psw.b, w0[d].b], writes=[t1.b])
                    S.op("act", lambda e: e.activation(out=t1[:, :], in_=t1[:, :], func=AF.Sigmoid), reads=[t1.b], writes=[t1.b])
                    S.op("act", lambda e: e.mul(out=o_lw[:, :], in_=t1[:, :], mul=-0.6065306597126334), reads=[t1.b], writes=[o_lw.b])
                    psa = next_psum(g)
                    S.op("pe", lambda e: e.matmul(psa[:, :], lhsT=tT[0:96, 1, :], rhs=a2[d][:, :], start=True, stop=True), reads=[tT.b, a2[d].b], writes=[psa.b])
                    S.op("dve", lambda e: e.tensor_tensor(out=aa[:, :], in0=psa[:, :], in1=a0[d][:, :], op=ALU.add), reads=[psa.b, a0[d].b], writes=[aa.b])
                    S.op("act", lambda e: e.activation(out=aa[:, :], in_=aa[:, :], func=AF.Sigmoid), reads=[aa.b], writes=[aa.b])
                    psg = next_psum(g)
                    for kc in range(2):
                        S.op("pe", lambda e: e.matmul(psg[:, :], lhsT=tT[:, 2 + kc, :], rhs=g2[:, kc, :], start=(kc == 0), stop=(kc == 1)),
                             reads=[tT.b, g2.b], writes=[psg.b])
                    copy_op(g, "act", o_g[:, :], psg[:, :], [psg.b], [o_g.b])
                    S.dma("pool", g.RG[d, rows, :], o_g[:, :], reads=[o_g.b])
                    S.op("dve", lambda e: e.tensor_tensor(out=t1[:, :], in0=uk, in1=kk_[:, :], op=ALU.mult), reads=[u.b, kk_.b], writes=[t1.b])
                    S.op("act", lambda e: e.activation(out=t2[:, :], in_=t1[:, :], func=AF.Square), reads=[t1.b], writes=[t2.b])
                    S.op("dve", lambda e: e.tensor_reduce(out=ss[:, 0:8], in_=t2[:, :].rearrange("p (h k) -> p h k", h=8), axis=AX.X, op=ALU.add),
                         reads=[t2.b], writes=[ss.b])
                    S.op("act", lambda e: e.activation(out=ss[:, 0:8], in_=ss[:, 0:8], func=AF.Sqrt), reads=[ss.b], writes=[ss.b])
                    S.op("dve", lambda e: e.tensor_scalar(out=ss[:, 0:8], in0=ss[:, 0:8], scalar1=1e-12, scalar2=None, op0=ALU.max), reads=[ss.b], writes=[ss.b])
                    S.op("dve", lambda e: e.reciprocal(out=ss[:, 0:8], in_=ss[:, 0:8]), reads=[ss.b], writes=[ss.b])
                    S.op("dve", lambda e: e.tensor_tensor(out=o_kk[:, :].rearrange("p (h k) -> p h k", h=8), in0=t1[:, :].rearrange("p (h k) -> p h k", h=8),
                                                          in1=ss[:, 0:8].unsqueeze(2).broadcast_to([128, 8, 64]), op=ALU.mult),
                         reads=[t1.b, ss.b], writes=[o_kk.b])
                    S.op("dve", lambda e: e.tensor_tensor(out=o_kka[:, :], in0=o_kk[:, :], in1=aa[:, :], op=ALU.mult), reads=[o_kk.b, aa.b], writes=[o_kka.b])
                    S.op("dve", lambda e: e.tensor_tensor(out=t2[:, :], in0=aa[:, :], in1=ka[:, :], op=ALU.mult), reads=[aa.b, ka.b], writes=[t2.b])
                    S.op("dve", lambda e: e.tensor_tensor(out=t2[:, :], in0=t2[:, :], in1=omka[:, :], op=ALU.add), reads=[t2.b, omka.b], writes=[t2.b])
                    S.op("dve", lambda e: e.tensor_tensor(out=o_k[:, :], in0=t2[:, :], in1=uk, op=ALU.mult), reads=[t2.b, u.b], writes=[o_k.b])
                    S.dma("pool", g.RW[d, 2:6, rows, :].rearrange("q t c -> t q c"), ob[:, :, :], reads=[ob.b])
        S.run_lanes([(lambda d=d: lane(d)) for d in range(2)])
    S.barrier()


def stage_rwkv_scan(g, l):
    S = g.S
    C = 64
    nch = g.T // C
    nch_c = g.LC // C
    ident64 = g.ident[0:64, 0:64]

    def mm(ps_ap, psb, lhsT, rhs, reads, start=True, stop=True):
        S.op("pe", lambda e: e.matmul(ps_ap, lhsT=lhsT, rhs=rhs, start=start, stop=stop), reads=reads, writes=[psb])

    with ExitStack() as st:
        names = ["Lsb", "eL", "enL", "eLx", "eEnd", "Rt", "Kt", "Kk", "Ak", "Kh", "Ah"]
        LT = []
        for d in range(2):
            LT.append(dict(
                ST=sb(g, st, "sST", [64, 8, 64]),
                qin=[sb(g, st, "sq", [64, 512]) for _ in range(6)],
                v2=sb(g, st, "sv2", [128, 512]),
                W={n: sb(g, st, "s" + n, [64, 512]) for n in names},
                pCT=sb(g, st, "spCT", [64, 8]),
                XT1=sb(g, st, "sXT1", [64, 8, 128]), XT2=sb(g, st, "sXT2", [64, 8, 128]),
                PRs=sb(g, st, "sPRs", [128, 8, 128]),
                Nn=[sb(g, st, "sN", [64, 8, 64]) for _ in range(2)],
                NTt=[sb(g, st, "sNT", [64, 8, 64]) for _ in range(2)],
                Qq=[sb(g, st, "sQ", [64, 8, 64]) for _ in range(2)],
                W1T=sb(g, st, "sW1T", [64, 8, 64]), MV=sb(g, st, "sMV", [64, 8, 64]), W2=sb(g, st, "sW2", [64, 8, 64]),
                Y0=sb(g, st, "sY0", [64, 8, 64]), D0=sb(g, st, "sD0", [64, 8, 64]), U=sb(g, st, "sU", [64, 8, 64]),
                Yo=[sb(g, st, "sYo", [64, 512]) for _ in range(2)], tmp=sb(g, st, "stmp", [64, 8, 64])))
            S.op("dve", lambda e: e.memset(LT[d]["ST"][:, :, :], 0.0), writes=[LT[d]["ST"].b])

        def lane(d):
            lt = LT[d]
            W, pCT, XT1, XT2, PRs, Nn, NTt, Qq = (lt[k] for k in ["W", "pCT", "XT1", "XT2", "PRs", "Nn", "NTt", "Qq"])
            W1T, MV, W2, Y0, D0, U, Yo, tmp = (lt[k] for k in ["W1T", "MV", "W2", "Y0", "D0", "U", "Yo", "tmp"])
            ST = {d: lt["ST"]}
            it = 0
            if d == 0:
                order = list(range(nch))
            else:
                order = list(range(nch_c - 1, -1, -1)) + list(range(nch - 1, nch_c - 1, -1))
            tri = g.tri[0:64, d, :]
            mk = g.mk[:, d, :]
            nmts = g.nmts[0:64, d, :]
            for c in order:
                it += 1
                q = lt["qin"]
                vv = lt["v2"]
                rows = slice(c * C, (c + 1) * C)
                for qi in range(6):
                    S.dma("sp", q[qi][:, :], g.RW[d, RQ_SCAN[qi], rows, :], writes=[q[qi].b])
                S.dma("sp", vv[64:128, :], g.RW[d, RQ_V, rows, :], writes=[vv.b])
                r_, lw, k_, v_, kk, kka = q
                psL = next_psum(g)
                mm(psL[0:64, :], psL.b, tri, lw[:, :], [g.tri.b, lw.b])
                psE = next_psum(g)
                mm(psE[0:64, :], psE.b, g.ones64[0:64, :], lw[:, :], [g.ones64.b, lw.b])
                psP = next_psum(g)
                for h in range(8):
                    mm(psP[0:64, h:h + 1], psP.b, lw[:, h * 64:(h + 1) * 64], g.ones64[0:64, 0:1], [lw.b, g.ones64.b])
                S.op("act", lambda e: e.activation(out=pCT[:, :], in_=psP[0:64, 0:8], func=AF.Exp), reads=[psP.b], writes=[pCT.b])
                S.op("act", lambda e: e.activation(out=W["eL"][:, :], in_=psL[0:64, :], func=AF.Exp), reads=[psL.b], writes=[W["eL"].b])
                S.op("act", lambda e: e.activation(out=W["enL"][:, :], in_=psL[0:64, :], func=AF.Exp, scale=-1.0), reads=[psL.b], writes=[W["enL"].b])
                S.op("dve", lambda e: e.tensor_tensor(out=W["eLx"][:, :], in0=psL[0:64, :], in1=lw[:, :], op=ALU.subtract), reads=[psL.b, lw.b], writes=[W["eLx"].b])
                S.op("act", lambda e: e.activation(out=W["eLx"][:, :], in_=W["eLx"][:, :], func=AF.Exp), reads=[W["eLx"].b], writes=[W["eLx"].b])
                copy_op(g, "dve", W["Lsb"][:, :], psL[0:64, :], [psL.b], [W["Lsb"].b])
                S.op("dve", lambda e: e.tensor_tensor(out=W["eEnd"][:, :], in0=psE[0:64, :], in1=W["Lsb"][:, :], op=ALU.subtract),
                     reads=[psE.b, W["Lsb"].b], writes=[W["eEnd"].b])
                S.op("act", lambda e: e.activation(out=W["eEnd"][:, :], in_=W["eEnd"][:, :], func=AF.Exp), reads=[W["eEnd"].b], writes=[W["eEnd"].b])
                for (o, a, b) in [("Rt", r_, "eL"), ("Kt", kk, "eLx"), ("Kk", k_, "enL"), ("Ak", kka, "enL"), ("Kh", k_, "eEnd"), ("Ah", kka, "eEnd")]:
                    S.op("dve", lambda e: e.tensor_tensor(out=W[o][:, :], in0=a[:, :], in1=W[b][:, :], op=ALU.mult), reads=[a.b, W[b].b], writes=[W[o].b])
                for (XT, n0, n1) in [(XT1, "Ak", "Kk"), (XT2, "Kt", "Rt")]:
                    for hg in range(2):
                        ps = next_psum(g)
                        for hh in range(4):
                            h = hg * 4 + hh
                            for j, nm in enumerate([n0, n1]):
                                S.op("pe", lambda e: e.transpose(ps[0:64, hh * 128 + j * 64:hh * 128 + (j + 1) * 64], W[nm][:, h * 64:(h + 1) * 64], ident64),
                                     reads=[W[nm].b, g.ident.b], writes=[ps.b])
                        copy_op(g, evac_engine(g), XT[:, hg * 4:(hg + 1) * 4, :], ps[0:64, :].rearrange("p (h x) -> p h x", h=4), [ps.b], [XT.b])
                for hg in range(2):
                    ps = next_psum(g)
                    for hh in range(4):
                        h = hg * 4 + hh
                        mm(ps[:, hh * 128:(hh + 1) * 128], ps.b, XT1[:, h, :], XT2[:, h, :], [XT1.b, XT2.b])
                    S.op("dve", lambda e: e.tensor_tensor(out=PRs[:, hg * 4:(hg + 1) * 4, :], in0=ps[:, :].rearrange("p (h x) -> p h x", h=4),
                                                          in1=mk.unsqueeze(1).broadcast_to([128, 4, 128]), op=ALU.mult),
                         reads=[ps.b, g.mk.b], writes=[PRs.b])
                ps = next_psum(g)
                for h in range(8):
                    mm(ps[0:64, h * 64:(h + 1) * 64], ps.b, XT2[:, h, 0:64], XT1[:, h, 0:64], [XT1.b, XT2.b])
                N, NT_, Q = Nn[0], NTt[0], Qq[0]
                S.op("dve", lambda e: e.tensor_tensor(out=NT_[:, :, :], in0=ps[0:64, :].rearrange("p (h x) -> p h x", h=8),
                                                      in1=nmts.unsqueeze(1).broadcast_to([64, 8, 64]), op=ALU.mult),
                     reads=[ps.b, g.nmts.b], writes=[NT_.b])
                S.op("dve", lambda e: e.tensor_scalar(out=N[:, :, :], in0=PRs[0:64, :, 0:64], scalar1=-1.0, scalar2=None, op0=ALU.mult), reads=[PRs.b], writes=[N.b])
                S.op("dve", lambda e: e.tensor_tensor(out=Q[:, :, :], in0=N[:, :, :], in1=ident64.unsqueeze(1).broadcast_to([64, 8, 64]), op=ALU.add),
                     reads=[N.b, g.ident.b], writes=[Q.b])
                cur = 0
                for lev in range(5):
                    N, NT_, Q = Nn[cur], NTt[cur], Qq[cur]
                    N2, NT2, Q2 = Nn[1 - cur], NTt[1 - cur], Qq[1 - cur]
                    psn = next_psum(g)
                    pst = next_psum(g)
                    for h in range(8):
                        if lev < 4:
                            mm(psn[0:64, h * 64:(h + 1) * 64], psn.b, NT_[:, h, :], N[:, h, :], [NT_.b, N.b])
                        mm(pst[0:64, h * 64:(h + 1) * 64], pst.b, N[:, h, :], NT_[:, h, :], [NT_.b, N.b])
                    if lev < 4:
                        copy_op(g, "act", N2[:, :, :], psn[0:64, :].rearrange("p (h x) -> p h x", h=8), [psn.b], [N2.b])
                    copy_op(g, "dve", NT2[:, :, :], pst[0:64, :].rearrange("p (h x) -> p h x", h=8), [pst.b], [NT2.b])
                    psq = next_psum(g)
                    for h in range(8):
                        mm(psq[0:64, h * 64:(h + 1) * 64], psq.b, NT2[:, h, :], Q[:, h, :], [NT2.b, Q.b])
                    S.op("dve", lambda e: e.tensor_tensor(out=Q2[:, :, :], in0=psq[0:64, :].rearrange("p (h x) -> p h x", h=8), in1=Q[:, :, :], op=ALU.add),
                         reads=[psq.b, Q.b], writes=[Q2.b])
                    cur = 1 - cur
                Q = Qq[cur]
                ps1 = next_psum(g)
                ps2 = next_psum(g)
                ps3 = next_psum(g)
                ps4 = next_psum(g)
                for h in range(8):
                    hs = slice(h * 64, (h + 1) * 64)
                    mm(ps1[0:64, hs], ps1.b, W["Kt"][:, hs], Q[:, h, :], [W["Kt"].b, Q.b])
                    mm(ps2[0:64, hs], ps2.b, PRs[64:128, h, 0:64], vv[64:128, hs], [PRs.b, vv.b])
                    mm(ps3[0:64, hs], ps3.b, PRs[64:128, h, 64:128], vv[64:128, hs], [PRs.b, vv.b])
                    mm(ps4[0:64, hs], ps4.b, W["Kh"][:, hs], v_[:, hs], [W["Kh"].b, v_.b])
                copy_op(g, "act", W1T[:, :, :], ps1[0:64, :].rearrange("p (h x) -> p h x", h=8), [ps1.b], [W1T.b])
                copy_op(g, "dve", MV[:, :, :], ps2[0:64, :].rearrange("p (h x) -> p h x", h=8), [ps2.b], [MV.b])
                copy_op(g, "act", Y0[:, :, :], ps3[0:64, :].rearrange("p (h x) -> p h x", h=8), [ps3.b], [Y0.b])
                copy_op(g, "dve", D0[:, :, :], ps4[0:64, :].rearrange("p (h x) -> p h x", h=8), [ps4.b], [D0.b])
                ps5 = next_psum(g)
                for h in range(8):
                    mm(ps5[0:64, h * 64:(h + 1) * 64], ps5.b, Q[:, h, :], MV[:, h, :], [Q.b, MV.b])
                copy_op(g, "act", W2[:, :, :], ps5[0:64, :].rearrange("p (h x) -> p h x", h=8), [ps5.b], [W2.b])
                st_ = ST[d]
                psu = next_psum(g)
                for h in range(8):
                    mm(psu[0:64, h * 64:(h + 1) * 64], psu.b, W1T[:, h, :], st_[:, h, :], [W1T.b, st_.b])
                S.op("dve", lambda e: e.scalar_tensor_tensor(out=U[:, :, :], in0=psu[0:64, :].rearrange("p (h x) -> p h x", h=8), scalar=-1.0,
                                                             in1=W2[:, :, :], op0=ALU.mult, op1=ALU.subtract),
                     reads=[psu.b, W2.b], writes=[U.b])
                psy = next_psum(g)
                for h in range(8):
                    hs = slice(h * 64, (h + 1) * 64)
                    mm(psy[0:64, hs], psy.b, XT2[:, h, 64:128], st_[:, h, :], [XT2.b, st_.b], start=True, stop=False)
                    mm(psy[0:64, hs], psy.b, PRs[0:64, h, 64:128], U[:, h, :], [PRs.b, U.b], start=False, stop=True)
                yo = Yo[it % 2]
                S.op("dve", lambda e: e.tensor_tensor(out=yo[:, :], in0=psy[0:64, :], in1=Y0[:, :, :].rearrange("p h x -> p (h x)"), op=ALU.add),
                     reads=[psy.b, Y0.b], writes=[yo.b])
                S.dma("pool", g.YR[d, rows, :], yo[:, :], reads=[yo.b])
                psd = next_psum(g)
                for h in range(8):
                    hs = slice(h * 64, (h + 1) * 64)
                    mm(psd[0:64, hs], psd.b, W["Ah"][:, hs], U[:, h, :], [W["Ah"].b, U.b])
                S.op("dve", lambda e: e.tensor_tensor(out=tmp[:, :, :], in0=st_[:, :, :], in1=pCT[:, 0:8].unsqueeze(2).broadcast_to([64, 8, 64]), op=ALU.mult),
                     reads=[st_.b, pCT.b], writes=[tmp.b])
                S.op("dve", lambda e: e.tensor_tensor(out=tmp[:, :, :], in0=tmp[:, :, :], in1=D0[:, :, :], op=ALU.add), reads=[tmp.b, D0.b], writes=[tmp.b])
                S.op("dve", lambda e: e.tensor_tensor(out=st_[:, :, :], in0=psd[0:64, :].rearrange("p (h x) -> p h x", h=8), in1=tmp[:, :, :], op=ALU.add),
                     reads=[psd.b, tmp.b], writes=[st_.b])
        S.run_lanes([(lambda d=d: lane(d)) for d in range(2)])
    S.barrier()


def stage_rwkv_out(g, l):
    S = g.S
    with ExitStack() as st:
        lw_ = sb(g, st, "olw", [128, 512])
        lb_ = sb(g, st, "olb", [128, 512])
        rk_ = sb(g, st, "ork", [128, 512])
        bcast_row(g, "sp", lw_[:, :], lw_.b, g.rwkv_lnx_w[l, :])
        bcast_row(g, "sp", lb_[:, :], lb_.b, g.rwkv_lnx_b[l, :])
        bcast_row(g, "sp", rk_[:, :], rk_.b, g.rwkv_r_k[l].rearrange("h k -> (h k)"))
        NL = 4
        LT = [dict(ins=[[sb(g, st, "oin", [128, 512]) for _ in range(5)] for _ in range(2)],
                   t1=sb(g, st, "ot1", [128, 512]), t2=sb(g, st, "ot2", [128, 512]),
                   acc=[sb(g, st, "oacc", [128, 512]) for _ in range(2)],
                   ss=sb(g, st, "oss", [128, 8]), s2=sb(g, st, "os2", [128, 8])) for _ in range(NL)]
        h8 = lambda ap: ap.rearrange("p (h k) -> p h k", h=8)
        b8 = lambda t_: t_[:, 0:8].unsqueeze(2).broadcast_to([128, 8, 64])

        def lane(li):
          lt = LT[li]
          ins, t1, t2, acc, ss, s2 = (lt[k] for k in ["ins", "t1", "t2", "acc", "ss", "s2"])
          it = 0
          for kk_i, ti in enumerate(range(li, g.NT, NL)):
            rows = slice(ti * 128, (ti + 1) * 128)
            a_ = acc[kk_i % 2]
            for d in range(2):
                it += 1
                y, r_, k_, v_, g_ = ins[it % 2]
                S.dma("sp", y[:, :], g.YR[d, rows, :], writes=[y.b])
                S.dma("sp", r_[:, :], g.RW[d, RQ_R, rows, :], writes=[r_.b])
                S.dma("sp", k_[:, :], g.RW[d, RQ_K, rows, :], writes=[k_.b])
                S.dma("sp", v_[:, :], g.RW[d, RQ_V, rows, :], writes=[v_.b])
                S.dma("sp", g_[:, :], g.RG[d, rows, :], writes=[g_.b])
                S.op("dve", lambda e: e.tensor_reduce(out=ss[:, 0:8], in_=h8(y[:, :]), axis=AX.X, op=ALU.add), reads=[y.b], writes=[ss.b])
                S.op("dve", lambda e: e.tensor_scalar(out=ss[:, 0:8], in0=ss[:, 0:8], scalar1=-1.0 / 64, scalar2=None, op0=ALU.mult), reads=[ss.b], writes=[ss.b])
                S.op("dve", lambda e: e.tensor_tensor(out=h8(t1[:, :]), in0=h8(y[:, :]), in1=b8(ss), op=ALU.add), reads=[y.b, ss.b], writes=[t1.b])
                S.op("act", lambda e: e.activation(out=t2[:, :], in_=t1[:, :], func=AF.Square), reads=[t1.b], writes=[t2.b])
                S.op("dve", lambda e: e.tensor_reduce(out=s2[:, 0:8], in_=h8(t2[:, :]), axis=AX.X, op=ALU.add), reads=[t2.b], writes=[s2.b])
                S.op("dve", lambda e: e.tensor_scalar(out=s2[:, 0:8], in0=s2[:, 0:8], scalar1=1.0 / 64, scalar2=LNX_EPS, op0=ALU.mult, op1=ALU.add),
                     reads=[s2.b], writes=[s2.b])
                S.op("act", lambda e: e.activation(out=s2[:, 0:8], in_=s2[:, 0:8], func=AF.Sqrt), reads=[s2.b], writes=[s2.b])
                S.op("dve", lambda e: e.reciprocal(out=s2[:, 0:8], in_=s2[:, 0:8]), reads=[s2.b], writes=[s2.b])
                S.op("dve", lambda e: e.tensor_tensor(out=h8(t1[:, :]), in0=h8(t1[:, :]), in1=b8(s2), op=ALU.mult), reads=[t1.b, s2.b], writes=[t1.b])
                S.op("dve", lambda e: e.tensor_tensor(out=t1[:, :], in0=t1[:, :], in1=lw_[:, :], op=ALU.mult), reads=[t1.b, lw_.b], writes=[t1.b])
                S.op("dve", lambda e: e.tensor_tensor(out=t1[:, :], in0=t1[:, :], in1=lb_[:, :], op=ALU.add), reads=[t1.b, lb_.b], writes=[t1.b])
                S.op("pool", lambda e: e.tensor_tensor(out=t2[:, :], in0=r_[:, :], in1=k_[:, :], op=ALU.mult), reads=[r_.b, k_.b], writes=[t2.b])
                S.op("pool", lambda e: e.tensor_tensor(out=t2[:, :], in0=t2[:, :], in1=rk_[:, :], op=ALU.mult), reads=[t2.b, rk_.b], writes=[t2.b])
                S.op("dve", lambda e: e.tensor_reduce(out=ss[:, 0:8], in_=h8(t2[:, :]), axis=AX.X, op=ALU.add), reads=[t2.b], writes=[ss.b])
                S.op("dve", lambda e: e.tensor_tensor(out=h8(t2[:, :]), in0=h8(v_[:, :]), in1=b8(ss), op=ALU.mult), reads=[v_.b, ss.b], writes=[t2.b])
                S.op("dve", lambda e: e.tensor_tensor(out=t1[:, :], in0=t1[:, :], in1=t2[:, :], op=ALU.add), reads=[t1.b, t2.b], writes=[t1.b])
                if d == 0:
                    S.op("dve", lambda e: e.tensor_tensor(out=a_[:, :], in0=t1[:, :], in1=g_[:, :], op=ALU.mult), reads=[t1.b, g_.b], writes=[a_.b])
                else:
                    S.op("dve", lambda e: e.tensor_tensor(out=t1[:, :], in0=t1[:, :], in1=g_[:, :], op=ALU.mult), reads=[t1.b, g_.b], writes=[t1.b])
                    S.op("dve", lambda e: e.tensor_tensor(out=a_[:, :], in0=a_[:, :], in1=t1[:, :], op=ALU.add), reads=[a_.b, t1.b], writes=[a_.b])
            S.dma("pool", g.YMIX[rows, 1024:1536], a_[:, :], reads=[a_.b])
        S.run_lanes([(lambda li=li: lane(li)) for li in range(NL)])
    S.barrier()


def stage_merge(g, l):
    S = g.S
    NT = g.NT
    with ExitStack() as st:
        wf = sb(g, st, "mwf", [128, 16, D], BF16)
        wv = g.w_branch[l].rearrange("b k n -> (b k) n").rearrange("(kc p) n -> p kc n", p=128)
        for q4 in range(4):
            S.dma("pool", wf[:, q4 * 4:(q4 + 1) * 4, :], wv[:, q4 * 4:(q4 + 1) * 4, :], writes=[wf.b])
        xin = [sb(g, st, "mxin", [128, D]) for _ in range(2)]
        xt = [sb(g, st, "mxt", [128, 16, 128], BF16) for _ in range(2)]
        gs = [sb(g, st, "mgs", [128, 4, D]) for _ in range(2)]
        orow = [sb(g, st, "morow", [128, D]) for _ in range(2)]
        t1 = sb(g, st, "mt1", [128, 512])

        def prefetch(ti):
            rows = slice(ti * 128, (ti + 1) * 128)
            S.dma("sp", xin[ti % 2][:, :], g.YMIX[rows, :], writes=[xin[ti % 2].b])
            gq = gs[ti % 2]
            S.dma("sp", gq[:, :, :], g.Z[rows, O_GATE:O_GATE + 4 * D].rearrange("t (i n) -> t i n", i=4), writes=[gq.b])
            for i in range(4):
                S.op("act", lambda e: e.activation(out=gq[:, i, :], in_=gq[:, i, :], func=AF.Sigmoid), reads=[gq.b], writes=[gq.b])
        prefetch(0)
        for ti in range(NT):
            if ti + 1 < NT:
                prefetch(ti + 1)
            xi, x_t, gq, o = xin[ti % 2], xt[ti % 2], gs[ti % 2], orow[ti % 2]
            for k4 in range(0, 16, 4):
                ps = next_psum(g)
                for j in range(4):
                    S.op("pe", lambda e: e.transpose(ps[:, j * 128:(j + 1) * 128], xi[:, (k4 + j) * 128:(k4 + j + 1) * 128], g.ident[:]),
                         reads=[xi.b, g.ident.b], writes=[ps.b])
                copy_op(g, "dve" if (k4 // 4) % 2 else "act", x_t[:, k4:k4 + 4, :], ps[:, :].rearrange("p (k t) -> p k t", k=4), [ps.b], [x_t.b])
            for n0 in range(0, D, 512):
                pss = []
                for i in range(4):
                    ps = next_psum(g)
                    pss.append(ps)
                    for kc in range(4 * i, 4 * i + 4):
                        S.op("pe", lambda e: e.matmul(ps[:, :], lhsT=x_t[:, kc, :], rhs=wf[:, kc, n0:n0 + 512], start=(kc == 4 * i), stop=(kc == 4 * i + 3)),
                             reads=[x_t.b, wf.b], writes=[ps.b])
                ob = o[:, n0:n0 + 512]
                S.op("dve", lambda e: e.tensor_tensor(out=ob, in0=pss[0][:, :], in1=gq[:, 0, n0:n0 + 512], op=ALU.mult), reads=[pss[0].b, gq.b], writes=[o.b])
                for i in range(1, 4):
                    S.op("dve", lambda e: e.tensor_tensor(out=t1[:, :], in0=pss[i][:, :], in1=gq[:, i, n0:n0 + 512], op=ALU.mult), reads=[pss[i].b, gq.b], writes=[t1.b])
                    S.op("dve", lambda e: e.tensor_tensor(out=ob, in0=ob, in1=t1[:, :], op=ALU.add), reads=[o.b, t1.b], writes=[o.b])
            S.dma("pool", g.MERGED[ti * 128:(ti + 1) * 128, :], o[:, :], reads=[o.b])
    S.barrier()


def stage_conv(g, l):
    S = g.S
    NT, NTC = g.NT, g.NTC
    NL = 4
    blocks = list(range(0, DFF, 512))
    with ExitStack() as st:
        LT = [dict(cw=sb(g, st, "ccw", [128, 3, 512]), cb=sb(g, st, "ccb", [128, 512]),
                   aw=[sb(g, st, "caw", [128, 512]) for _ in range(4)],
                   bw=[sb(g, st, "cbw", [128, 512]) for _ in range(2)],
                   acc=sb(g, st, "cacc", [128, 512]), t1=sb(g, st, "ct1", [128, 512]),
                   t2=sb(g, st, "ct2", [128, 512]), xb=sb(g, st, "cxb", [128, 512]),
                   out=[sb(g, st, "cout", [128, 512]) for _ in range(2)]) for _ in range(NL)]

        def lane(li):
            lt = LT[li]
            w_, b_, aw, bw, acc, t1 = lt["cw"], lt["cb"], lt["aw"], lt["bw"], lt["acc"], lt["t1"]
            lq = "sp" if li % 2 == 0 else "act"
            for n0 in blocks[li::NL]:
                for j in range(3):
                    bcast_row(g, lq, w_[:, j, :], w_.b, g.ffn_conv_w[l, j, n0:n0 + 512])
                bcast_row(g, lq, b_[:, :], b_.b, g.ffn_conv_b[l, n0:n0 + 512])
                for t0 in range(min(2, NT)):
                    S.dma(lq, aw[t0 % 4][:, :], g.AB[t0 * 128:(t0 + 1) * 128, n0:n0 + 512], writes=[aw[t0 % 4].b])
                for ti in range(NT):
                    r0 = ti * 128
                    if ti + 2 < NT:
                        t2_ = ti + 2
                        S.dma(lq, aw[t2_ % 4][:, :], g.AB[t2_ * 128:(t2_ + 1) * 128, n0:n0 + 512], writes=[aw[t2_ % 4].b])
                    bb = bw[ti % 2]
                    S.dma(lq, bb[:, :], g.AB[r0:r0 + 128, DFF + n0:DFF + n0 + 512], writes=[bb.b])
                    ac_ = aw[ti % 4]
                    has_prev = ti not in (0, NTC)
                    has_next = ti not in (NTC - 1, NT - 1)
                    psP = next_psum(g)
                    S.op("pe", lambda e: e.matmul(psP[:, :], lhsT=g.shm[:, 0, :], rhs=ac_[:, :], start=True, stop=not has_prev),
                         reads=[g.shm.b, ac_.b], writes=[psP.b])
                    if has_prev:
                        ap_ = aw[(ti - 1) % 4]
                        S.op("pe", lambda e: e.matmul(psP[:, :], lhsT=g.shm[:, 1, :], rhs=ap_[:, :], start=False, stop=True),
                             reads=[g.shm.b, ap_.b], writes=[psP.b])
                    psN = next_psum(g)
                    S.op("pe", lambda e: e.matmul(psN[:, :], lhsT=g.shm[:, 2, :], rhs=ac_[:, :], start=True, stop=not has_next),
                         reads=[g.shm.b, ac_.b], writes=[psN.b])
                    if has_next:
                        an_ = aw[(ti + 1) % 4]
                        S.op("pe", lambda e: e.matmul(psN[:, :], lhsT=g.shm[:, 3, :], rhs=an_[:, :], start=False, stop=True),
                             reads=[g.shm.b, an_.b], writes=[psN.b])
                    t2 = lt["t2"]
                    xb = lt["xb"]
                    S.op("dve", lambda e: e.tensor_tensor(out=acc[:, :], in0=psP[:, :], in1=w_[:, 0, :], op=ALU.mult), reads=[psP.b, w_.b], writes=[acc.b])
                    S.op("dve", lambda e: e.tensor_tensor(out=t1[:, :], in0=ac_[:, :], in1=w_[:, 1, :], op=ALU.mult), reads=[ac_.b, w_.b], writes=[t1.b])
                    S.op("dve", lambda e: e.tensor_tensor(out=t1[:, :], in0=t1[:, :], in1=b_[:, :], op=ALU.add), reads=[t1.b, b_.b], writes=[t1.b])
                    S.op("dve", lambda e: e.tensor_tensor(out=t2[:, :], in0=psN[:, :], in1=w_[:, 2, :], op=ALU.mult), reads=[psN.b, w_.b], writes=[t2.b])
                    S.op("dve", lambda e: e.tensor_tensor(out=acc[:, :], in0=acc[:, :], in1=t2[:, :], op=ALU.add), reads=[acc.b, t2.b], writes=[acc.b])
                    S.op("dve", lambda e: e.tensor_tensor(out=acc[:, :], in0=acc[:, :], in1=t1[:, :], op=ALU.add), reads=[acc.b, t1.b], writes=[acc.b])
                    S.op("dve", lambda e: e.tensor_tensor(out=xb[:, :], in0=acc[:, :], in1=bb[:, :], op=ALU.mult), reads=[acc.b, bb.b], writes=[xb.b])
                    S.op("act", lambda e: e.activation(out=t2[:, :], in_=acc[:, :], func=AF.Square), reads=[acc.b], writes=[t2.b])
                    S.op("dve", lambda e: e.tensor_scalar(out=t2[:, :], in0=t2[:, :], scalar1=0.044715, scalar2=1.0, op0=ALU.mult, op1=ALU.add), reads=[t2.b], writes=[t2.b])
                    S.op("dve", lambda e: e.tensor_tensor(out=t2[:, :], in0=t2[:, :], in1=acc[:, :], op=ALU.mult), reads=[t2.b, acc.b], writes=[t2.b])
                    S.op("act", lambda e: e.activation(out=t2[:, :], in_=t2[:, :], func=AF.Sigmoid, scale=1.5957691216057308), reads=[t2.b], writes=[t2.b])
                    o = lt["out"][ti % 2]
                    S.op("dve", lambda e: e.tensor_tensor(out=o[:, :], in0=t2[:, :], in1=xb[:, :], op=ALU.mult), reads=[t2.b, xb.b], writes=[o.b])
                    S.dma("pool", g.GG[r0:r0 + 128, n0:n0 + 512], o[:, :], reads=[o.b])
        S.run_lanes([(lambda li=li: lane(li)) for li in range(NL)])
    S.barrier()


def build(cfg):
    TL, LC, DEPTH = cfg["TL"], cfg["LC"], cfg["DEPTH"]
    T = TL + LC
    nc = bass.Bass("TRN2", target_bir_lowering=False)
    g = Ctx()
    g.nc = nc
    g.uid = 0
    g.ev = 0
    g.pi = 0
    g.T, g.TL, g.LC, g.NT, g.NTC = T, TL, LC, T // 128, LC // 128
    g.S = S = Sync(nc)

    def din(name, shape):
        return nc.dram_tensor(name, list(shape), F32, kind="ExternalInput").ap()

    def dscr(name, shape):
        return nc.dram_tensor(name, list(shape), F32, kind="Internal").ap()

    L = DEPTH
    g.x_in = din("x", [TL, D])
    g.ctx_in = din("ctx", [LC, D])
    g.cvec = din("cvec", [2, D])
    g.ada_w = din("ada_w", [L, D, 6 * D])
    g.ada_b = din("ada_b", [L, 6 * D])
    g.norm1_g = din("norm1_g", [L, D])
    g.norm2_g = din("norm2_g", [L, D])
    g.w_in = din("w_in", [L, D, IN_W])
    g.identd = din("ident", [128, 128])
    g.ga_q_norm = din("ga_q_norm", [L, 128])
    g.ga_k_norm = din("ga_k_norm", [L, 128])
    g.wa_q_norm = din("wa_q_norm", [L, 128])
    g.wa_k_norm = din("wa_k_norm", [L, 128])
    g.wa_sink = din("wa_sink", [L, 4])
    g.mla_cq_norm = din("mla_cq_norm", [L, 384])
    g.mla_ckv_norm = din("mla_ckv_norm", [L, 512])
    g.mla_w_uq = din("mla_w_uq", [L, 384, 768])
    g.mla_w_ukv = din("mla_w_ukv", [L, 512, 1024])
    g.mla_qn_norm = din("mla_qn_norm", [L, 128])
    g.mla_qr_norm = din("mla_qr_norm", [L, 64])
    g.mla_kn_norm = din("mla_kn_norm", [L, 128])
    g.mla_kr_norm = din("mla_kr_norm", [L, 64])
    g.rwkv_mu = din("rwkv_mu", [L, 2, 1984])
    g.rwkv_w0 = din("rwkv_w0", [L, 2, 512])
    g.rwkv_w2 = din("rwkv_w2", [L, 2, 96, 512])
    g.rwkv_a0 = din("rwkv_a0", [L, 2, 512])
    g.rwkv_a2 = din("rwkv_a2", [L, 2, 96, 512])
    g.rwkv_g2 = din("rwkv_g2", [L, 256, 512])
    g.rwkv_k_k = din("rwkv_k_k", [L, 512])
    g.rwkv_k_a = din("rwkv_k_a", [L, 512])
    g.rwkv_r_k = din("rwkv_r_k", [L, 8, 64])
    g.rwkv_lnx_w = din("rwkv_lnx_w", [L, 512])
    g.rwkv_lnx_b = din("rwkv_lnx_b", [L, 512])
    g.w_branch = din("w_branch", [L, 4, 512, D])
    g.w_out = din("w_out", [L, D, D])
    g.ffn_up = din("ffn_up", [L, D, 2 * DFF])
    g.ffn_conv_w = din("ffn_conv_w", [L, 3, DFF])
    g.ffn_conv_b = din("ffn_conv_b", [L, DFF])
    g.ffn_down = din("ffn_down", [L, DFF, D])
    g.trid = din("tri", [64, 2, 64])
    g.mkd = din("mk", [128, 2, 128])
    g.nmtsd = din("nmts", [64, 2, 64])
    g.zrow = din("zrow", [1, NRW])
    g.shmd = din("shm", [128, 4, 128])
    g.ropeA = din("ropeA", [2, TL, 128])
    g.ropeM = din("ropeM", [2, TL, 64])
    g.wmaskd = din("wmask", [128, 2, 128])
    g.y_out = nc.dram_tensor("y", [TL, D], F32, kind="ExternalOutput").ap()
    dbg = cfg.get("debug")
    g.XS = dscr("XS", [T, D])
    g.MODB = dscr("MODB", [2, 128, 6 * D])
    g.Z = dscr("Z", [T, IN_W])
    g.QKT = dscr("QKT", [12, 128, T])
    g.YMIX = dscr("YMIX", [T, D])
    g.RW = dscr("RW", [2, 6, T, 512])
    g.RG = dscr("RG", [2, T, 512])
    g.YR = dscr("YR", [2, T, 512])
    g.MERGED = dscr("MERGED", [T, D])
    g.AB = dscr("AB", [T, 2 * DFF])
    g.GG = dscr("GG", [T, DFF])
    g.QML = dscr("QML", [T, 768])
    g.KVML = dscr("KVML", [T, 1024])
    g.MQN = dscr("MQN", [4, 128, T])
    g.MQR = dscr("MQR", [4, 64, T])
    g.MKN = dscr("MKN", [4, 128, T])
    g.MKR = dscr("MKR", [1, 64, T])
    if dbg:
        g.dbg_z = nc.dram_tensor("dbg_z", [T, IN_W], F32, kind="ExternalOutput").ap()
        g.dbg_y = nc.dram_tensor("dbg_y", [T, D], F32, kind="ExternalOutput").ap()
        g.dbg_qkt = nc.dram_tensor("dbg_qkt", [12, 128, T], F32, kind="ExternalOutput").ap()
        g.dbg_xs = nc.dram_tensor("dbg_xs", [T, D], F32, kind="ExternalOutput").ap()

    with ExitStack() as es:
        g.psums = [Tl(es.enter_context(nc.psum_tensor("ps%d" % i, [128, 512], F32)), "ps%d" % i) for i in range(8)]
        g.ident = sb(g, es, "ident", [128, 128])
        S.dma("sp", g.ident[:, :], g.identd[:, :], writes=[g.ident.b])
        g.tri = sb(g, es, "tri", [64, 2, 64])
        g.mk = sb(g, es, "mk", [128, 2, 128])
        g.nmts = sb(g, es, "nmts", [64, 2, 64])
        g.ones64 = sb(g, es, "ones64", [64, 64])
        S.dma("sp", g.tri[:, :, :], g.trid[:, :, :], writes=[g.tri.b])
        S.dma("sp", g.mk[:, :, :], g.mkd[:, :, :], writes=[g.mk.b])
        S.dma("sp", g.nmts[:, :, :], g.nmtsd[:, :, :], writes=[g.nmts.b])
        S.op("dve", lambda e: e.memset(g.ones64[:, :], 1.0), writes=[g.ones64.b])
        g.shm = sb(g, es, "shm", [128, 4, 128])
        S.dma("sp", g.shm[:, :, :], g.shmd[:, :, :], writes=[g.shm.b])
        g.wmask = sb(g, es, "wmask", [128, 2, 128])
        S.dma("sp", g.wmask[:, :, :], g.wmaskd[:, :, :], writes=[g.wmask.b])
        S.dma("sp", g.XS[0:LC, :], g.ctx_in[:, :])
        S.dma("sp", g.XS[LC:T, :], g.x_in[:, :])
        g.cb = [sb(g, es, "cb", [128, 16, 128]) for _ in range(2)]
        with ExitStack() as st:
            cT = [sb(g, st, "cT", [128, 16]) for _ in range(2)]
            ones = sb(g, st, "ones", [128, 128])
            S.op("dve", lambda e: e.memset(ones[:, :], 1.0), writes=[ones.b])
            for r in range(2):
                S.dma("sp", cT[r][:, :], g.cvec[r, :].rearrange("(kc p) -> p kc", p=128), writes=[cT[r].b], allow_slow_non_contiguous=True)
                S.op("act", lambda e: e.activation(out=cT[r][:, :], in_=cT[r][:, :], func=AF.Silu), reads=[cT[r].b], writes=[cT[r].b])
                for kc in range(16):
                    S.op("dve", lambda e: e.tensor_scalar(out=g.cb[r][:, kc, :], in0=ones[:, :], scalar1=cT[r][:, kc:kc + 1], scalar2=None, op0=ALU.mult),
                         reads=[ones.b, cT[r].b], writes=[g.cb[r].b])
            S.barrier()
        for l in range(L):
            stage_mod(g, l)
            with ExitStack() as st:
                pro = make_norm_pro(g, st, l, g.norm1_g[l, :], D, 0)
                linear(g, g.XS, g.w_in[l], g.Z, T, D, IN_W, G=12, NB=512, pro=pro)
            if cfg.get("stop") == "z":
                break
            stage_attn_prep(g, l)
            stage_attn_gawa(g, l)
            if cfg.get("stop") == "gawa":
                break
            if cfg.get("stop") != "rwkv":
                stage_mla(g, l)
            if cfg.get("stop") == "mla":
                break
            stage_rwkv_prep(g, l)
            stage_rwkv_scan(g, l)
            stage_rwkv_out(g, l)
            if cfg.get("stop") == "rwkv":
                break
            stage_merge(g, l)
            with ExitStack() as st:
                epi = make_resid_epi(g, st, 2 * D)
                linear(g, g.MERGED, g.w_out[l], None, T, D, D, G=12, NB=512, epi=epi)
            if cfg.get("stop") == "attn":
                break
            with ExitStack() as st:
                pro = make_norm_pro(g, st, l, g.norm2_g[l, :], 4 * D, 3 * D)
                linear(g, g.XS, g.ffn_up[l], g.AB, T, D, 2 * DFF, G=12, NB=512, pro=pro)
            stage_conv(g, l)
            with ExitStack() as st:
                epi = make_resid_epi(g, st, 5 * D)
                linear(g, g.GG, g.ffn_down[l], None, T, DFF, D, G=6, NB=256, epi=epi)
        if dbg:
            S.dma("sp", g.dbg_z[:, :], g.Z[:, :])
            S.dma("sp", g.dbg_y[:, :], g.YMIX[:, :])
            S.dma("sp", g.dbg_qkt[:, :, :], g.QKT[:, :, :])
            S.dma("sp", g.dbg_xs[:, :], g.XS[:, :])
        S.dma("sp", g.y_out[:, :], g.XS[LC:T, :])
        S.barrier()
    return nc, g


_CACHE = {}


def rope_table(n_tokens, rot_dim):
    rows = n_tokens // GRID_W
    row = np.repeat(np.arange(rows), GRID_W).astype(np.float32)
    col = np.tile(np.arange(GRID_W), rows).astype(np.float32)
    quarter = rot_dim // 4
    inv_freq = (10000.0 ** (-np.arange(quarter, dtype=np.float32) / quarter)).astype(np.float32)
    ang_r = row[:, None] * inv_freq
    ang_c = col[:, None] * inv_freq
    ang = np.concatenate([ang_r, ang_r, ang_c, ang_c], axis=-1).astype(np.float32)
    sign = np.concatenate([-np.ones(quarter), np.ones(quarter), -np.ones(quarter), np.ones(quarter)]).astype(np.float32)
    return np.stack([np.cos(ang), np.sin(ang) * sign]).astype(np.float32)


def shift_mats():
    m = np.zeros((128, 4, 128), np.float32)
    for k in range(127):
        m[k, 0, k + 1] = 1.0
        m[k + 1, 2, k] = 1.0
    m[127, 1, 0] = 1.0
    m[0, 3, 127] = 1.0
    return m


def const_tables(TL):
    idx = np.arange(128)
    m_lo = (idx[None, :] <= idx[:, None]).astype(np.float32)
    m_hi = (idx[:, None] <= idx[None, :]).astype(np.float32)
    i64 = np.arange(64)
    tri = np.stack([(i64[:, None] <= i64[None, :]), (i64[:, None] >= i64[None, :])]).astype(np.float32)
    strict = tri - np.eye(64, dtype=np.float32)[None]
    half = np.concatenate([strict, tri], axis=2)
    mk = np.concatenate([half, half], axis=1)
    nmts = -np.transpose(strict, (0, 2, 1))
    return {
        "tri": np.ascontiguousarray(np.transpose(tri, (1, 0, 2))),
        "mk": np.ascontiguousarray(np.transpose(mk, (1, 0, 2))),
        "nmts": np.ascontiguousarray(np.transpose(nmts, (1, 0, 2))),
        "zrow": np.zeros((1, NRW), np.float32),
        "shm": shift_mats(),
        "ropeA": rope_table(TL, 128),
        "ropeM": rope_table(TL, 64),
        "wmask": np.ascontiguousarray(np.stack([m_lo, m_hi], axis=1)),
    }


def make_inputs_for_core(inputs, b, L):
    f = lambda a: np.ascontiguousarray(np.asarray(a, dtype=np.float32))
    m = {
        "x": f(inputs["x"][b]),
        "ctx": f(inputs["ctx"][b]),
        "cvec": f(np.stack([np.asarray(inputs["c"][b]), np.asarray(inputs["c_ctx"])])),
        "ident": np.eye(128, dtype=np.float32),
    }
    m.update(const_tables(np.asarray(inputs["x"]).shape[1]))
    for k in ["ada_w", "ada_b", "norm1_g", "norm2_g", "w_in", "ga_q_norm", "ga_k_norm", "wa_q_norm", "wa_k_norm", "wa_sink",
              "mla_cq_norm", "mla_ckv_norm", "mla_w_uq", "mla_w_ukv", "mla_qn_norm", "mla_qr_norm", "mla_kn_norm", "mla_kr_norm",
              "rwkv_mu", "rwkv_w0", "rwkv_w2", "rwkv_a0", "rwkv_a2", "rwkv_g2", "rwkv_k_k", "rwkv_k_a", "rwkv_r_k", "rwkv_lnx_w", "rwkv_lnx_b",
              "w_branch", "w_out", "ffn_up", "ffn_conv_w", "ffn_conv_b", "ffn_down"]:
        m[k] = f(inputs[k][:L])
    return m


def kernel(**inputs):
    x = np.asarray(inputs["x"])
    B, TL, _ = x.shape
    LC = np.asarray(inputs["ctx"]).shape[1]
    L = np.asarray(inputs["ada_w"]).shape[0]
    cfg = {"TL": TL, "LC": LC, "DEPTH": L}
    nc, g = build(cfg)
    n = 8
    in_maps = [make_inputs_for_core(inputs, c % B, L) for c in range(n)]
    res = run_bass_kernel_spmd(nc, in_maps, core_ids=list(range(n)))
    return np.stack([res.results[b]["y"] for b in range(B)]).astype(np.float32)
```
